# Optimizing a Trainium2 kernel written in Bass

```python
import jax, jax.numpy as jnp
from jax import lax
import numpy as np

D_MODEL = 2048
BATCH = 16
SEQ = 2048
DEPTH = 2
DEC_BATCH = 16
DEC_SEQ = 64
PAST_LEN = 4096

CHUNK = 64
HEAD_DIM = 64
D_A = D_MODEL
N_HEADS_A = D_A // HEAD_DIM
R_W = 96
R_A = 96
R_G = 256
D_C = D_MODEL
CONV_K = 31
D_FF = ((8 * D_MODEL // 3 + 255) // 256) * 256
P_RWKV = 3 * D_A + R_W + R_A + R_G
P_TOT = P_RWKV + 2 * D_C + 2 * D_MODEL
EPS_RMS = 1e-6
EPS_LN = 1e-5
EPS_GN = 64e-5

kernel_name = "rwkv7_conformer_griffin_stream_step"


def rms_norm(x, g):
    xf = x.astype(jnp.float32)
    y = xf * lax.rsqrt(jnp.mean(xf * xf, axis=-1, keepdims=True) + EPS_RMS)
    return (y * g.astype(jnp.float32)).astype(x.dtype)


def swiglu_ffn(h, w1, w3, w2):
    return (jax.nn.silu(h @ w1) * (h @ w3)) @ w2


def wkv7_scan(S0, r, w, k, v, a, b):
    def step(S, inp):
        r_t, w_t, k_t, v_t, a_t, b_t = inp
        sa = jnp.einsum('bhij,bhj->bhi', S, a_t)
        S = (S * w_t[:, :, None, :] + sa[..., None] * b_t[:, :, None, :]
             + v_t[..., None] * k_t[:, :, None, :])
        y_t = jnp.einsum('bhij,bhj->bhi', S, r_t)
        return S, y_t
    xs = tuple(jnp.moveaxis(z, 1, 0) for z in (r, w, k, v, a, b))
    S, ys = lax.scan(step, S0, xs)
    return jnp.moveaxis(ys, 0, 1), S


def rwkv7_mix(p_a, shift_prev, wkv_prev, mu, w0, w_up, a0, a_up, g_up, k_k, k_a, r_k,
              gn_g, gn_b, w_o):
    f32 = jnp.float32
    B, T, _ = p_a.shape
    prev = jnp.concatenate([shift_prev[:, None].astype(p_a.dtype), p_a[:, :-1]], axis=1)
    xm = p_a + (prev - p_a) * mu
    r, k, v, xw, xa, xg = jnp.split(
        xm, [D_A, 2 * D_A, 3 * D_A, 3 * D_A + R_W, 3 * D_A + R_W + R_A], axis=-1)
    w_log = -jax.nn.softplus(-(w0 + jnp.tanh(xw) @ w_up).astype(f32)) - 0.5
    decay = jnp.exp(-jnp.exp(w_log))
    a = jax.nn.sigmoid((a0 + xa @ a_up).astype(f32))
    g = jax.nn.sigmoid(xg) @ g_up
    hs = lambda z: z.reshape(B, T, N_HEADS_A, HEAD_DIM)
    kk = hs(k.astype(f32) * k_k)
    kk = kk * lax.rsqrt(jnp.maximum(jnp.sum(kk * kk, axis=-1, keepdims=True), 1e-24))
    k = k.astype(f32) * (1.0 + (a - 1.0) * k_a)
    rh, kh, vh, ah = hs(r.astype(f32)), hs(k), hs(v.astype(f32)), hs(a)
    y, S = wkv7_scan(wkv_prev.astype(f32), rh, hs(decay), kh, vh, -kk, kk * ah)
    mean = jnp.mean(y, axis=-1, keepdims=True)
    var = jnp.mean(jnp.square(y - mean), axis=-1, keepdims=True)
    y = ((y - mean) * lax.rsqrt(var + EPS_GN)).reshape(B, T, D_A) * gn_g + gn_b
    bonus = jnp.sum(rh * kh * r_k, axis=-1, keepdims=True) * vh
    y = (y + bonus.reshape(B, T, D_A)).astype(p_a.dtype)
    return (y * g) @ w_o, S, p_a[:, -1]


def conformer_conv(p_c, conv_prev, b_in, conv_w, conv_b, ln_g, ln_b, w_o, b_o):
    val, gate = jnp.split(p_c + b_in, 2, axis=-1)
    glu = val * jax.nn.sigmoid(gate)
    xc = jnp.concatenate([conv_prev.astype(glu.dtype), glu], axis=1)
    dw = lax.conv_general_dilated(
        xc, conv_w.astype(xc.dtype)[:, None, :], window_strides=(1,), padding='VALID',
        dimension_numbers=('NWC', 'WIO', 'NWC'), feature_group_count=D_C) + conv_b
    f = dw.astype(jnp.float32)
    mean = jnp.mean(f, axis=-1, keepdims=True)
    var = jnp.mean(jnp.square(f - mean), axis=-1, keepdims=True)
    f = (f - mean) * lax.rsqrt(var + EPS_LN) * ln_g + ln_b
    h = jax.nn.silu(f).astype(p_c.dtype)
    return h @ w_o + b_o, xc[:, -(CONV_K - 1):]


def trunk_layer(x, wkv_prev, shift_prev, conv_prev, p, l):
    h = rms_norm(x, p['ffn1_norm'][l])
    x = x + 0.5 * swiglu_ffn(h, p['ffn1_w1'][l], p['ffn1_w3'][l], p['ffn1_w2'][l])
    u = rms_norm(x, p['mix_norm'][l])
    proj = u @ p['w_in'][l]
    p_a, p_c, p_g = jnp.split(proj, [P_RWKV, P_RWKV + 2 * D_C], axis=-1)
    y_a, wkv_new, shift_new = rwkv7_mix(
        p_a, shift_prev, wkv_prev, p['mu_shift'][l], p['w0'][l], p['w_up'][l], p['a0'][l],
        p['a_up'][l], p['g_up'][l], p['k_k'][l], p['k_a'][l], p['r_k'][l], p['gn_g'][l],
        p['gn_b'][l], p['w_o_a'][l])
    y_c, conv_new = conformer_conv(
        p_c, conv_prev, p['b_conv_in'][l], p['conv_w'][l], p['conv_b'][l], p['conv_ln_g'][l],
        p['conv_ln_b'][l], p['w_o_c'][l], p['b_o_c'][l])
    gate_a, gate_c = jnp.split(jax.nn.sigmoid(p_g + p['b_gate'][l]), 2, axis=-1)
    x = x + (gate_a * y_a + gate_c * y_c) @ p['w_out'][l]
    h = rms_norm(x, p['ffn2_norm'][l])
    x = x + 0.5 * swiglu_ffn(h, p['ffn2_w1'][l], p['ffn2_w3'][l], p['ffn2_w2'][l])
    return x, wkv_new, shift_new, conv_new


def run_trunk(x, wkv0, shift0, conv0, p, final_norm):
    wkvs, shifts, convs = [], [], []
    for l in range(DEPTH):
        x, s_wkv, s_shift, s_conv = trunk_layer(x, wkv0[l], shift0[l], conv0[l], p, l)
        wkvs.append(s_wkv)
        shifts.append(s_shift)
        convs.append(s_conv)
    return rms_norm(x, final_norm), jnp.stack(wkvs), jnp.stack(shifts), jnp.stack(convs)


def setup_inputs(seed: int = 0) -> dict:
    key = jax.random.key(seed)
    ks = iter(jax.random.split(key, 48))
    f32 = jnp.float32
    nrm = lambda shape, scale: scale * jax.random.normal(next(ks), shape, f32)
    gain = lambda shape: 1.0 + 0.01 * jax.random.normal(next(ks), shape, f32)
    L, D = DEPTH, D_MODEL
    return {
        "x_prompt": nrm((BATCH, SEQ, D), 1.0),
        "x_sample": nrm((DEC_BATCH, DEC_SEQ, D), 1.0),
        "state_wkv": nrm((L, DEC_BATCH, N_HEADS_A, HEAD_DIM, HEAD_DIM), 0.1),
        "state_shift": nrm((L, DEC_BATCH, P_RWKV), 1.0),
        "state_conv": nrm((L, DEC_BATCH, CONV_K - 1, D_C), 0.5),
        "ffn1_norm": gain((L, D)),
        "ffn1_w1": nrm((L, D, D_FF), D ** -0.5),
        "ffn1_w3": nrm((L, D, D_FF), D ** -0.5),
        "ffn1_w2": nrm((L, D_FF, D), D_FF ** -0.5),
        "mix_norm": gain((L, D)),
        "w_in": nrm((L, D, P_TOT), D ** -0.5),
        "mu_shift": jax.random.uniform(next(ks), (L, P_RWKV), f32),
        "w0": jax.random.uniform(next(ks), (L, D_A), f32, minval=-6.0, maxval=-1.0),
        "w_up": nrm((L, R_W, D_A), 0.5 * R_W ** -0.5),
        "a0": nrm((L, D_A), 0.5),
        "a_up": nrm((L, R_A, D_A), 0.5 * R_A ** -0.5),
        "g_up": nrm((L, R_G, D_A), R_G ** -0.5),
        "k_k": 0.85 + nrm((L, D_A), 0.05),
        "k_a": 1.0 + nrm((L, D_A), 0.05),
        "r_k": nrm((L, N_HEADS_A, HEAD_DIM), 0.1),
        "gn_g": gain((L, D_A)),
        "gn_b": nrm((L, D_A), 0.01),
        "w_o_a": nrm((L, D_A, D), D_A ** -0.5),
        "b_conv_in": nrm((L, 2 * D_C), 0.01),
        "conv_w": nrm((L, CONV_K, D_C), CONV_K ** -0.5),
        "conv_b": nrm((L, D_C), 0.01),
        "conv_ln_g": gain((L, D_C)),
        "conv_ln_b": nrm((L, D_C), 0.01),
        "w_o_c": nrm((L, D_C, D), D_C ** -0.5),
        "b_o_c": nrm((L, D), 0.01),
        "b_gate": nrm((L, 2 * D), 0.01),
        "w_out": nrm((L, D, D), D ** -0.5),
        "ffn2_norm": gain((L, D)),
        "ffn2_w1": nrm((L, D, D_FF), D ** -0.5),
        "ffn2_w3": nrm((L, D, D_FF), D ** -0.5),
        "ffn2_w2": nrm((L, D_FF, D), D_FF ** -0.5),
        "final_norm": gain((D,)),
    }


def reference(x_prompt, x_sample, state_wkv, state_shift, state_conv,
              ffn1_norm, ffn1_w1, ffn1_w3, ffn1_w2, mix_norm, w_in, mu_shift,
              w0, w_up, a0, a_up, g_up, k_k, k_a, r_k, gn_g, gn_b, w_o_a,
              b_conv_in, conv_w, conv_b, conv_ln_g, conv_ln_b, w_o_c, b_o_c,
              b_gate, w_out, ffn2_norm, ffn2_w1, ffn2_w3, ffn2_w2, final_norm):
    p = dict(ffn1_norm=ffn1_norm, ffn1_w1=ffn1_w1, ffn1_w3=ffn1_w3, ffn1_w2=ffn1_w2,
             mix_norm=mix_norm, w_in=w_in, mu_shift=mu_shift, w0=w0, w_up=w_up, a0=a0,
             a_up=a_up, g_up=g_up, k_k=k_k, k_a=k_a, r_k=r_k, gn_g=gn_g, gn_b=gn_b,
             w_o_a=w_o_a, b_conv_in=b_conv_in, conv_w=conv_w, conv_b=conv_b,
             conv_ln_g=conv_ln_g, conv_ln_b=conv_ln_b, w_o_c=w_o_c, b_o_c=b_o_c,
             b_gate=b_gate, w_out=w_out, ffn2_norm=ffn2_norm, ffn2_w1=ffn2_w1,
             ffn2_w3=ffn2_w3, ffn2_w2=ffn2_w2)
    Bp = x_prompt.shape[0]
    wkv0 = jnp.zeros((DEPTH, Bp, N_HEADS_A, HEAD_DIM, HEAD_DIM), jnp.float32)
    shift0 = jnp.zeros((DEPTH, Bp, P_RWKV), x_prompt.dtype)
    conv0 = jnp.zeros((DEPTH, Bp, CONV_K - 1, D_C), x_prompt.dtype)
    y_prompt, wkv_p, shift_p, conv_p = run_trunk(x_prompt, wkv0, shift0, conv0, p, final_norm)
    y_sample, wkv_s, shift_s, conv_s = run_trunk(x_sample, state_wkv, state_shift, state_conv,
                                                 p, final_norm)
    return (y_prompt, y_sample, wkv_p, shift_p, conv_p, wkv_s, shift_s, conv_s)
```

```python
import contextlib
import numpy as np
import concourse.bass as bass
import concourse.mybir as mybir
from concourse.bass_utils import run_bass_kernel_spmd

F32 = mybir.dt.float32
BF16 = mybir.dt.bfloat16
ALU = mybir.AluOpType
AF = mybir.ActivationFunctionType

D = 2048
KC = 16
FF = 5632
FC = 44
NH = 32
HD = 64
R_W, R_A, R_G = 96, 96, 256
P_RWKV = 3 * D + R_W + R_A + R_G
P_TOT = P_RWKV + 2 * D + 2 * D
CONV_K = 31
EPS_RMS = 1e-6
EPS_LN = 1e-5
EPS_GN = 64e-5

ENGS = ("pe", "act", "dve", "pool", "sp")


class Op:
    __slots__ = ("eng", "fn", "deps", "dma", "sem", "sig", "sigval", "gidx")


class Prog:
    def __init__(self):
        self.ops = {e: [] for e in ENGS}
        self.lastw = {}
        self.readers = {}
        self.n = 0
        self.dma_slots = {"sp": ["d_sp%d" % i for i in range(8)],
                          "pool": ["d_pl%d" % i for i in range(8)],
                          "act": ["d_ac%d" % i for i in range(4)]}
        self.dma_rr = {"sp": 0, "pool": 0, "act": 0}
        self.slot_last = {}
        self.slot_count = {}

    def add(self, eng, fn, r=(), w=(), dma=False):
        op = Op()
        op.eng, op.fn, op.dma, op.sig, op.sigval = eng, fn, dma, False, 0
        op.gidx = self.n
        self.n += 1
        deps = {}
        for k in r:
            lw = self.lastw.get(k)
            if lw is not None:
                deps[id(lw)] = lw
        for k in w:
            lw = self.lastw.get(k)
            if lw is not None:
                deps[id(lw)] = lw
            rd = self.readers.get(k)
            if rd:
                for o in rd[0].values():
                    deps[id(o)] = o
                for o in rd[1]:
                    deps[id(o)] = o
        if dma:
            slots = self.dma_slots[eng]
            s = slots[self.dma_rr[eng] % len(slots)]
            self.dma_rr[eng] += 1
            op.sem = s
            prev = self.slot_last.get(s)
            if prev is not None:
                deps[id(prev)] = prev
            self.slot_last[s] = op
            self.slot_count[s] = self.slot_count.get(s, 0) + 1
            op.sigval = 16 * self.slot_count[s]
            op.sig = True
        else:
            op.sem = "c_" + eng
        for k in w:
            self.lastw[k] = op
            self.readers[k] = ({}, [])
        for k in r:
            rd = self.readers.get(k)
            if rd is None:
                rd = ({}, [])
                self.readers[k] = rd
            if dma:
                rd[1].append(op)
            else:
                rd[0][eng] = op
        deps.pop(id(op), None)
        dl = []
        for d in deps.values():
            if (not d.dma) and (not dma) and d.eng == eng and eng == "pe":
                continue
            dl.append(d)
        op.deps = dl
        self.ops[eng].append(op)
        return op

    def finalize(self):
        for e in ENGS:
            for op in self.ops[e]:
                for d in op.deps:
                    d.sig = True
        for e in ENGS:
            c = 0
            for op in self.ops[e]:
                if op.dma:
                    continue
                if op.sig:
                    c += 1
                    op.sigval = c

    def sem_names(self):
        names = ["c_" + e for e in ("pe", "act", "dve", "pool")]
        for e in ("sp", "pool", "act"):
            names += self.dma_slots[e]
        return names

    def emit(self, e, eng, sems, final_wait=False):
        waited = {}
        for op in self.ops[e]:
            for d in op.deps:
                if waited.get(d.sem, 0) < d.sigval:
                    eng.wait_ge(sems[d.sem], d.sigval)
                    waited[d.sem] = d.sigval
            inst = op.fn(eng)
            if op.dma:
                inst.then_inc(sems[op.sem], 16)
            elif op.sig:
                inst.then_inc(sems[op.sem], 1)
        if final_wait:
            for s, c in self.slot_count.items():
                if waited.get(s, 0) < 16 * c:
                    eng.wait_ge(sems[s], 16 * c)


RG = [(g * 128, 128) for g in range(48)] + [(6144, 96), (6240, 96), (6336, 128), (6464, 128)]
NRG = len(RG)
CDEC = float(np.exp(-0.5))
WNAMES = [("ffn1_norm", [D]), ("ffn1_w1", [D, FF]), ("ffn1_w3", [D, FF]), ("ffn1_w2", [FF, D]),
          ("mix_norm", [D]), ("w_in", [D, P_TOT]), ("mu_shift", [P_RWKV]), ("w0", [D]),
          ("w_up", [R_W, D]), ("a0", [D]), ("a_up", [R_A, D]), ("g_up", [R_G, D]), ("k_k", [D]),
          ("k_a", [D]), ("r_k", [D]), ("gn_g", [D]), ("gn_b", [D]), ("w_o_a", [D, D]),
          ("b_conv_in", [2 * D]), ("conv_w", [CONV_K, D]), ("conv_b", [D]), ("conv_ln_g", [D]),
          ("conv_ln_b", [D]), ("w_o_c", [D, D]), ("b_o_c", [D]), ("b_gate", [2 * D]),
          ("w_out", [D, D]), ("ffn2_norm", [D]), ("ffn2_w1", [D, FF]), ("ffn2_w3", [D, FF]),
          ("ffn2_w2", [FF, D])]


def build_program(cfg):
    n_pseq, plen, n_sseq, slen = cfg["n_pseq"], cfg["plen"], cfg["n_sseq"], cfg["slen"]
    L = cfg["depth"]
    phases = cfg.get("phases", ("ffn1", "mix", "ffn2"))
    TT = 512
    NSEG = max(1, n_sseq)

    nc = bass.Bass("TRN2", target_bir_lowering=False)

    def din(name, shape):
        return nc.dram_tensor(name, list(shape), F32, kind="ExternalInput").ap()

    def dout(name, shape):
        return nc.dram_tensor(name, list(shape), F32, kind="ExternalOutput").ap()

    x_p = din("x_p", [n_pseq, plen, D])
    x_s = din("x_s", [max(1, n_sseq), slen, D])
    st_wkv = din("state_wkv", [L, max(1, n_sseq), NH, HD, HD])
    st_shift = din("state_shift", [L, max(1, n_sseq), P_RWKV])
    st_conv = din("state_conv", [L, max(1, n_sseq), CONV_K - 1, D])
    W = {}
    for nm, shp in WNAMES:
        W[nm] = din(nm, [L] + shp)
    W["final_norm"] = din("final_norm", [1, D])
    y_p = dout("y_p", [n_pseq, plen, D])
    y_s = dout("y_s", [max(1, n_sseq), slen, D])
    o_wkv = {"p": dout("wkv_p", [L, n_pseq, NH, HD, HD]), "s": dout("wkv_s", [L, max(1, n_sseq), NH, HD, HD])}
    o_shift = {"p": dout("shift_p", [L, n_pseq, P_RWKV]), "s": dout("shift_s", [L, max(1, n_sseq), P_RWKV])}
    o_conv = {"p": dout("conv_p", [L, n_pseq, CONV_K - 1, D]),
              "s": dout("conv_s", [L, max(1, n_sseq), CONV_K - 1, D])}

    P = Prog()
    st = contextlib.ExitStack()

    def sb(name, shape, dt):
        return st.enter_context(nc.sbuf_tensor(name, list(shape), dt))

    X = sb("X", [128, KC, TT], F32)
    H = sb("H", [128, KC, TT], BF16)
    HID = sb("HID", [128, FC * TT], BF16)
    HIDf = HID.bitcast(F32)
    NWU = 52
    WBUF = sb("WBUF", [128, NWU * 512], BF16)
    SQ = [sb("SQ%d" % i, [128, TT], BF16) for i in range(2)]
    SIL = [sb("SIL%d" % i, [128, TT], F32) for i in range(2)]
    RSTD = sb("RSTD", [128, TT], F32)
    TMPF = [sb("TMPF%d" % i, [128, TT], F32) for i in range(2)]
    ONESB = sb("ONESB", [128, 128], BF16)
    ONESF = sb("ONESF", [128, 128], F32)
    IDF = sb("IDF", [128, 128], F32)
    IDB = sb("IDB", [128, 128], BF16)
    BLKONES = sb("BLKONES", [128, 128], BF16)
    MASK2 = sb("MASK2", [128, 2, 128], F32)
    MASKL = sb("MASKL", [128, 128], F32)
    NV = sb("NV", [128, 2 * L + 1, KC], F32)
    PS = [st.enter_context(nc.psum_tensor("PS%d" % i, [128, 512], F32)) for i in range(7)]
    PSB = st.enter_context(nc.psum_tensor("PSB", [128, 1024], BF16))

    cvcols = {}
    off = 0
    for nm, n in [("mix_norm", 16), ("mu", NRG), ("omu", NRG), ("w0", 16), ("a0", 16), ("k_k", 16), ("k_a", 16),
                  ("r_k", 16), ("gn_g", 16), ("gn_b", 16), ("b_ci", 32), ("conv_w", 31 * 16), ("conv_b", 16),
                  ("ln_g", 16), ("ln_b", 16), ("b_oc", 16), ("b_gate", 32)]:
        cvcols[nm] = off
        off += n
    NCV = off
    CV = [sb("CV%d" % l, [128, NCV], F32) for l in range(L)]
    ST = [sb("ST%d" % l, [128, 16, 64], F32) for l in range(L)]
    SBD = sb("SBD", [128, 16, 128], BF16)
    SBLK = sb("SBLK", [128, 128], F32)
    STG = sb("STG", [128, 64], F32)
    OSTG = sb("OSTG", [128, 64], F32)
    SH = [[sb("SH%d_%d" % (l, s), [128, NRG], F32) for s in range(NSEG)] for l in range(L)]
    CT = [[sb("CT%d_%d" % (l, s), [128, 16, 30], BF16) for s in range(NSEG)] for l in range(L)]
    LORA = sb("LORA", [128, 4, TT], BF16)
    LUB = [sb("LUB%d" % i, [128, 512], BF16) for i in range(2)]
    TOK = sb("TOK", [128, 4, 3, 128], BF16)
    DG = sb("DG", [128, 8, 128], BF16)
    WCL = sb("WCL", [128, 4], F32)
    MA = [sb("MA%d" % h, [128, 2, 128], BF16) for h in range(2)]
    MB = [sb("MB%d" % h, [128, 2, 128], BF16) for h in range(2)]
    MC = [sb("MC%d" % h, [128, 128], BF16) for h in range(2)]
    TT_ = [[sb("TI%d_%d" % (h, i), [128, 128], BF16) for i in range(2)] for h in range(2)]
    PP = [[sb("PP%d_%d" % (h, i), [128, 256], BF16) for i in range(2)] for h in range(2)]
    XT = sb("XT", [128, 128], BF16)
    UT = sb("UT", [128, 128], BF16)
    YN = sb("YN", [128, 128], F32)
    Y1 = sb("Y1", [128, 128], F32)
    BNS = sb("BNS", [128, 2, 6], F32)
    MV = sb("MV", [128, 2, 2], F32)
    RS2 = sb("RS2", [128, 2], F32)

    def hid_keys(b0, nbytes):
        return [("HID", i) for i in range(b0 // 1024, (b0 + nbytes + 1023) // 1024)]

    def psk(bank, c0=0, n=512):
        return [("PS", bank)]

    def hbf(b0, n):
        return HID[:, b0 // 2:b0 // 2 + n]

    def hf32(b0, n):
        return HIDf[:, b0 // 4:b0 // 4 + n]

    ring = [0]

    def walloc(nunits):
        if ring[0] + nunits > NWU:
            ring[0] = 0
        u0 = ring[0]
        ring[0] += nunits
        return u0, [("WB", u) for u in range(u0, u0 + nunits)]

    def wview(u0, kc, ncols):
        return WBUF[:, u0 * 512:u0 * 512 + kc * ncols].rearrange("p (kc n) -> p kc n", kc=kc)

    def load_w(src3, kc, ncols):
        nun = (kc * ncols + 511) // 512
        u0, keys = walloc(nun)
        v = wview(u0, kc, ncols)
        P.add("pool", (lambda e, v=v, src3=src3: e.dma_start(out=v, in_=src3)), w=keys, dma=True)
        return v, keys

    def cvc(l, nm, i=0):
        c = cvcols[nm] + i
        return CV[l][:, c:c + 1]

    def cvk(l, nm):
        return ("CV", l, nm)

    P.add("dve", lambda e: e.memset(ONESB[:], 1.0), w=[("ONESB",)])
    P.add("dve", lambda e: e.memset(ONESF[:], 1.0), w=[("ONESF",)])
    P.add("pool", lambda e: e.memset(IDF[:], 1.0), w=[("IDF",)])
    P.add("pool", lambda e: e.affine_select(out=IDF[:], in_=IDF[:], pattern=[[-1, 128]],
                                            compare_op=ALU.is_equal, fill=0.0, base=0,
                                            channel_multiplier=1), r=[("IDF",)], w=[("IDF",)])
    P.add("dve", lambda e: e.tensor_copy(out=IDB[:], in_=IDF[:]), r=[("IDF",)], w=[("IDB",)])
    P.add("dve", lambda e: e.memset(BLKONES[:], 0.0), w=[("BLKONES",)])
    P.add("dve", lambda e: e.memset(BLKONES[0:64, 0:64], 1.0), w=[("BLKONES",)])
    P.add("dve", lambda e: e.memset(BLKONES[64:128, 64:128], 1.0), w=[("BLKONES",)])
    P.add("dve", lambda e: e.memset(SBLK[:], 0.0), w=[("SBLK",)])
    for i_ in range(2):
        P.add("dve", (lambda e, i_=i_: e.memset(LUB[i_][:], 0.0)), w=[("LUB", i_)])
    P.add("pool", lambda e: e.memset(MASK2[:], 1.0), w=[("MASK2",)])
    P.add("pool", lambda e: e.affine_select(out=MASK2[:, 0, :], in_=MASK2[:, 0, :], pattern=[[1, 128]],
                                            compare_op=ALU.is_gt, fill=0.0, base=0, channel_multiplier=-1),
          r=[("MASK2",)], w=[("MASK2",)])
    P.add("pool", lambda e: e.affine_select(out=MASK2[:, 1, :], in_=MASK2[:, 1, :], pattern=[[1, 128]],
                                            compare_op=ALU.is_ge, fill=0.0, base=0, channel_multiplier=-1),
          r=[("MASK2",)], w=[("MASK2",)])
    P.add("pool", lambda e: e.memset(MASKL[:], 1.0), w=[("MASKL",)])
    P.add("pool", lambda e: e.affine_select(out=MASKL[:], in_=MASKL[:], pattern=[[-1, 128]],
                                            compare_op=ALU.is_gt, fill=0.0, base=0, channel_multiplier=1),
          r=[("MASKL",)], w=[("MASKL",)])

    def load_vec(dst_tile, col0, vec1d, n, key):
        src = vec1d.rearrange("(c p) -> p c", p=128)
        P.add("sp", (lambda e, src=src: e.dma_start(out=dst_tile[:, col0:col0 + n], in_=src)), w=[key], dma=True)

    def load_rg(dst_tile, col0, vec1d, key, store=False):
        parts = [(dst_tile[:, col0:col0 + 48], vec1d[0:6144].rearrange("(c p) -> p c", p=128)),
                 (dst_tile[0:96, col0 + 48:col0 + 49], vec1d[6144:6240].rearrange("(c p) -> p c", p=96)),
                 (dst_tile[0:96, col0 + 49:col0 + 50], vec1d[6240:6336].rearrange("(c p) -> p c", p=96)),
                 (dst_tile[:, col0 + 50:col0 + 52], vec1d[6336:6592].rearrange("(c p) -> p c", p=128))]
        for (sbap, drap) in parts:
            if store:
                P.add("sp", (lambda e, sbap=sbap, drap=drap: e.dma_start(out=drap, in_=sbap)), r=[key], dma=True)
            else:
                P.add("sp", (lambda e, sbap=sbap, drap=drap: e.dma_start(out=sbap, in_=drap)), w=[key], dma=True)

    nv_idx = {}
    i = 0
    for l in range(L):
        for nm in ("ffn1_norm", "ffn2_norm"):
            nv_idx[(nm, l)] = i
            load_vec(NV[:, i, :], 0, W[nm][l], 16, ("NV", i))
            i += 1
    nv_idx[("final_norm", 0)] = i
    load_vec(NV[:, i, :], 0, W["final_norm"][0], 16, ("NV", i))
    if "mix" in phases:
        for l in range(L):
            for nm, src, n in [("mix_norm", "mix_norm", 16), ("w0", "w0", 16), ("a0", "a0", 16), ("k_k", "k_k", 16),
                               ("k_a", "k_a", 16), ("r_k", "r_k", 16), ("gn_g", "gn_g", 16), ("gn_b", "gn_b", 16),
                               ("b_ci", "b_conv_in", 32), ("conv_b", "conv_b", 16), ("ln_g", "conv_ln_g", 16),
                               ("ln_b", "conv_ln_b", 16), ("b_oc", "b_o_c", 16), ("b_gate", "b_gate", 32)]:
                load_vec(CV[l], cvcols[nm], W[src][l], n, cvk(l, nm))
            for k in range(CONV_K):
                load_vec(CV[l], cvcols["conv_w"] + k * 16, W["conv_w"][l, k], 16, cvk(l, "conv_w"))
            P.add("dve", (lambda e, l=l: e.memset(CV[l][:, cvcols["mu"]:cvcols["mu"] + NRG], 0.0)), w=[cvk(l, "mu")])
            load_rg(CV[l], cvcols["mu"], W["mu_shift"][l], cvk(l, "mu"))
            P.add("dve", (lambda e, l=l: e.tensor_scalar(
                out=CV[l][:, cvcols["omu"]:cvcols["omu"] + NRG], in0=CV[l][:, cvcols["mu"]:cvcols["mu"] + NRG],
                scalar1=-1.0, scalar2=1.0, op0=ALU.mult, op1=ALU.add)), r=[cvk(l, "mu")], w=[cvk(l, "omu")])

    def load_x(xsrc_rows, ntok):
        nb = len(xsrc_rows)
        IO = HIDf[:, 0:nb * D].rearrange("p (a b) -> p a b", a=nb)
        for tb, (src, n) in enumerate(xsrc_rows):
            P.add("sp", (lambda e, tb=tb, src=src, n=n: e.dma_start(out=IO[0:n, tb, :], in_=src)),
                  w=hid_keys(tb * 8192, 8192), dma=True)
        for c in range(KC):
            bank = c % 2
            for tb, (src, n) in enumerate(xsrc_rows):
                P.add("pe", (lambda e, tb=tb, n=n, c=c, bank=bank: e.transpose(
                    PS[bank][:, tb * 128:tb * 128 + n], IO[0:n, tb, c * 128:(c + 1) * 128],
                    IDF[0:n, 0:n])),
                    r=hid_keys(tb * 8192, 8192) + [("IDF",)], w=psk(bank))
            P.add("act" if c % 2 else "dve",
                  (lambda e, c=c, bank=bank: (e.activation(out=X[:, c, 0:ntok], in_=PS[bank][:, 0:ntok],
                                                           func=AF.Copy)
                                              if c % 2 else
                                              e.tensor_copy(out=X[:, c, 0:ntok], in_=PS[bank][:, 0:ntok]))),
                  r=psk(bank), w=[("X", c)])

    def rms_stats(ntok):
        for c in range(KC):
            s = c % 2
            P.add("act", (lambda e, c=c, s=s: e.activation(out=SQ[s][:, 0:ntok], in_=X[:, c, 0:ntok],
                                                           func=AF.Square)),
                  r=[("X", c)], w=[("SQ", s)])
            P.add("pe", (lambda e, c=c, s=s: e.matmul(PS[6][:, 0:ntok], lhsT=ONESB[:], rhs=SQ[s][:, 0:ntok],
                                                      start=(c == 0), stop=(c == KC - 1))),
                  r=[("SQ", s), ("ONESB",)], w=psk(6))
        P.add("act", (lambda e: e.activation(out=RSTD[:, 0:ntok], in_=PS[6][:, 0:ntok], func=AF.Sqrt,
                                             scale=1.0 / D, bias=EPS_RMS)),
              r=psk(6), w=[("RSTD",)])
        P.add("dve", (lambda e: e.reciprocal(out=RSTD[:, 0:ntok], in_=RSTD[:, 0:ntok])),
              r=[("RSTD",)], w=[("RSTD",)])

    def rmsnorm(ntok, gap_fn, gkey):
        rms_stats(ntok)
        for c in range(KC):
            P.add("dve", (lambda e, c=c: e.scalar_tensor_tensor(
                out=H[:, c, 0:ntok], in0=X[:, c, 0:ntok], scalar=gap_fn(c),
                in1=RSTD[:, 0:ntok], op0=ALU.mult, op1=ALU.mult)),
                r=[("X", c), ("RSTD",), gkey], w=[("H", c)])

    def proj16(bank, ntok, wv, wkeys, col0, m, rows=128):
        for kc in range(KC):
            P.add("pe", (lambda e, kc=kc: e.matmul(
                PS[bank][0:rows, 0:ntok], lhsT=wv[:, kc, col0:col0 + rows], rhs=H[:, kc, 0:ntok],
                start=(kc == 0), stop=(kc == KC - 1))),
                r=wkeys + [("H", kc)], w=psk(bank))

    def ffn(ntok, l, pre):
        w1 = W[pre + "_w1"][l].rearrange("(kc p) n -> p kc n", p=128)
        w3 = W[pre + "_w3"][l].rearrange("(kc p) n -> p kc n", p=128)
        w2 = W[pre + "_w2"][l].rearrange("(kc p) n -> p kc n", p=128)
        for fb in range(FF // 512):
            v1, k1 = load_w(w1[:, :, fb * 512:(fb + 1) * 512], KC, 512)
            v3, k3 = load_w(w3[:, :, fb * 512:(fb + 1) * 512], KC, 512)
            for m in range(4):
                f = fb * 4 + m
                ba, bb = 2 * (f % 2), 2 * (f % 2) + 1
                proj16(ba, ntok, v1, k1, m * 128, m)
                proj16(bb, ntok, v3, k3, m * 128, m)
                sl = f % 2
                P.add("act", (lambda e, sl=sl, ba=ba: e.activation(out=SIL[sl][:, 0:ntok], in_=PS[ba][:, 0:ntok],
                                                                   func=AF.Silu)),
                      r=psk(ba), w=[("SIL", sl)])
                P.add("dve", (lambda e, sl=sl, bb=bb, f=f: e.tensor_tensor(
                    out=HID[:, f * TT:f * TT + ntok], in0=SIL[sl][:, 0:ntok], in1=PS[bb][:, 0:ntok], op=ALU.mult)),
                    r=[("SIL", sl)] + psk(bb), w=[("HID", f)])
        for dg in range(4):
            banks = [0, 1, 2, 3] if dg % 2 == 0 else [4, 5, 6, 3]
            for fq in range(4):
                v2, k2 = load_w(w2[:, fq * 11:(fq + 1) * 11, dg * 512:(dg + 1) * 512], 11, 512)
                for dd in range(4):
                    for ff in range(11):
                        f = fq * 11 + ff
                        P.add("pe", (lambda e, dd=dd, ff=ff, f=f, v2=v2, bk=banks[dd]: e.matmul(
                            PS[bk][:, 0:ntok], lhsT=v2[:, ff, dd * 128:(dd + 1) * 128],
                            rhs=HID[:, f * TT:f * TT + ntok], start=(f == 0), stop=(f == FC - 1))),
                            r=k2 + [("HID", f)], w=psk(banks[dd]))
            for dd in range(4):
                c = dg * 4 + dd
                P.add("dve", (lambda e, c=c, bk=banks[dd]: e.scalar_tensor_tensor(
                    out=X[:, c, 0:ntok], in0=PS[bk][:, 0:ntok], scalar=0.5, in1=X[:, c, 0:ntok],
                    op0=ALU.mult, op1=ALU.add)),
                    r=psk(banks[dd]) + [("X", c)], w=[("X", c)])

    def add_out_proj(ntok, l, zsrc_fn, zkeys_fn):
        wo = W["w_out"][l].rearrange("(kc p) n -> p kc n", p=128)
        for blk in range(4):
            v, keys = load_w(wo[:, :, blk * 512:(blk + 1) * 512], KC, 512)
            for m in range(4):
                c = blk * 4 + m
                bank = c % 2
                for kc in range(KC):
                    P.add("pe", (lambda e, kc=kc, m=m, v=v, bank=bank: e.matmul(
                        PS[bank][:, 0:ntok], lhsT=v[:, kc, m * 128:(m + 1) * 128], rhs=zsrc_fn(kc),
                        start=(kc == 0), stop=(kc == KC - 1))),
                        r=keys + zkeys_fn(kc), w=psk(bank))
                P.add("dve", (lambda e, c=c, bank=bank: e.tensor_tensor(
                    out=X[:, c, 0:ntok], in0=PS[bank][:, 0:ntok], in1=X[:, c, 0:ntok], op=ALU.add)),
                    r=psk(bank) + [("X", c)], w=[("X", c)])

    def shift_evac(l, bank, rows, g, out_ap, out_keys, segs, ntok):
        mu = CV[l][0:rows, cvcols["mu"] + g:cvcols["mu"] + g + 1]
        omu = CV[l][0:rows, cvcols["omu"] + g:cvcols["omu"] + g + 1]
        ps = PS[bank]
        P.add("act", (lambda e: e.activation(out=out_ap[0:rows, 0:ntok], in_=ps[0:rows, 0:ntok], func=AF.Copy,
                                             scale=omu)),
              r=psk(bank) + [cvk(l, "omu")], w=out_keys)
        se = cfg.get("se", 99)
        for s, (c0, n) in enumerate(segs):
            if se <= 1:
                break
            P.add("dve", (lambda e, c0=c0, n=n: e.scalar_tensor_tensor(
                out=out_ap[0:rows, c0 + 1:c0 + n], in0=ps[0:rows, c0:c0 + n - 1], scalar=mu,
                in1=out_ap[0:rows, c0 + 1:c0 + n], op0=ALU.mult, op1=ALU.add)),
                r=psk(bank) + out_keys + [cvk(l, "mu")], w=out_keys)
            if se <= 2:
                continue
            P.add("dve", (lambda e, c0=c0, s=s: e.scalar_tensor_tensor(
                out=out_ap[0:rows, c0:c0 + 1], in0=SH[l][s][0:rows, g:g + 1], scalar=mu,
                in1=out_ap[0:rows, c0:c0 + 1], op0=ALU.mult, op1=ALU.add)),
                r=[("SH", l, s, g), cvk(l, "mu")] + out_keys, w=out_keys)
            if se <= 3:
                continue
            P.add("dve", (lambda e, c0=c0, n=n, s=s: e.tensor_copy(
                out=SH[l][s][0:rows, g:g + 1], in_=ps[0:rows, c0 + n - 1:c0 + n])),
                r=psk(bank), w=[("SH", l, s, g)])

    def mix(l, ti):
        ntok, segs, C, nq = ti["ntok"], ti["segs"], ti["C"], ti["nq"]
        kind = ti["kind"]
        stop = cfg.get("stop", 99)
        win = W["w_in"][l].rearrange("(kc p) n -> p kc n", p=128)
        nseg = len(segs)
        for s in range(nseg):
            if kind == "p":
                if ti["first"]:
                    P.add("dve", (lambda e, s=s: e.memset(SH[l][s][:], 0.0)),
                          w=[("SH", l, s, g) for g in range(NRG)])
                    P.add("dve", (lambda e, s=s: e.memset(CT[l][s][:], 0.0)), w=[("CT", l, s)])
            else:
                P.add("dve", (lambda e, s=s: e.memset(SH[l][s][:], 0.0)), w=[("SH", l, s, g) for g in range(NRG)])
                parts_key = "SHLOAD"
                vec = st_shift[l, ti["sq"][s]]
                for (sbap, drap) in [
                        (SH[l][s][:, 0:48], vec[0:6144].rearrange("(c p) -> p c", p=128)),
                        (SH[l][s][0:96, 48:49], vec[6144:6240].rearrange("(c p) -> p c", p=96)),
                        (SH[l][s][0:96, 49:50], vec[6240:6336].rearrange("(c p) -> p c", p=96)),
                        (SH[l][s][:, 50:52], vec[6336:6592].rearrange("(c p) -> p c", p=128))]:
                    P.add("sp", (lambda e, sbap=sbap, drap=drap: e.dma_start(out=sbap, in_=drap)),
                          w=[("SH", l, s, g) for g in range(NRG)], dma=True)
                stg = hf32(20480, D)
                P.add("sp", (lambda e, s=s, stg=stg: e.dma_start(out=stg[0:30, :], in_=st_conv[l, ti["sq"][s]])),
                      w=hid_keys(20480, 8192), dma=True)
                for c in range(KC):
                    P.add("pe", (lambda e, c=c, stg=stg: e.transpose(
                        PS[c % 2][:, 0:30], stg[0:30, c * 128:(c + 1) * 128], IDF[0:30, 0:30])),
                        r=hid_keys(20480, 8192) + [("IDF",)], w=psk(c % 2, 0, 30))
                    P.add("dve", (lambda e, c=c, s=s: e.tensor_copy(out=CT[l][s][:, c, :], in_=PS[c % 2][:, 0:30])),
                          r=psk(c % 2, 0, 30), w=[("CT", l, s)])
        if stop <= 0:
            return
        rmsnorm(ntok, lambda c: cvc(l, "mix_norm", c), cvk(l, "mix_norm"))

        if stop <= 1:
            return
        GW = sum(30 + n for (_, n) in segs)
        gbase = []
        o = 0
        for (_, n) in segs:
            gbase.append(o)
            o += 30 + n
        GLU = HID[:, 0:16 * GW].rearrange("p (c w) -> p c w", c=16)

        def glu_keys(c):
            return hid_keys(c * GW * 2, GW * 2)
        DWB0 = 12288

        def dw_ap(c):
            return hf32(DWB0 + c * ntok * 4, ntok)

        def dw_keys(c):
            return hid_keys(DWB0 + c * ntok * 4, ntok * 4)
        need_cs = (kind == "s") or ti["last"]
        CTF0 = 40960
        CTF = hf32(CTF0, 16 * nseg * 30).rearrange("p (c s t) -> p c s t", c=16, s=nseg)
        for s in range(nseg):
            P.add("dve", (lambda e, s=s: e.tensor_copy(out=GLU[:, :, gbase[s]:gbase[s] + 30], in_=CT[l][s][:, :, :])),
                  r=[("CT", l, s)], w=hid_keys(0, 16 * GW * 2))
        for blk in range(4):
            c0w = P_RWKV + blk * 512
            vv, kv = load_w(win[:, :, c0w:c0w + 512], KC, 512)
            vg, kg = load_w(win[:, :, c0w + D:c0w + D + 512], KC, 512)
            for m in range(4):
                c = blk * 4 + m
                ba, bb = 2 * (c % 2), 2 * (c % 2) + 1
                proj16(ba, ntok, vv, kv, m * 128, m)
                proj16(bb, ntok, vg, kg, m * 128, m)
                sl = c % 2
                P.add("act", (lambda e, sl=sl, bb=bb, c=c: e.activation(
                    out=SIL[sl][:, 0:ntok], in_=PS[bb][:, 0:ntok], func=AF.Sigmoid, bias=cvc(l, "b_ci", 16 + c))),
                    r=psk(bb) + [cvk(l, "b_ci")], w=[("SIL", sl)])
                for s, (c0, n) in enumerate(segs):
                    P.add("dve", (lambda e, sl=sl, ba=ba, c=c, c0=c0, n=n, s=s: e.scalar_tensor_tensor(
                        out=GLU[:, c, gbase[s] + 30:gbase[s] + 30 + n], in0=PS[ba][:, c0:c0 + n],
                        scalar=cvc(l, "b_ci", c), in1=SIL[sl][:, c0:c0 + n], op0=ALU.add, op1=ALU.mult)),
                        r=psk(ba) + [("SIL", sl), cvk(l, "b_ci")], w=glu_keys(c))
                    if need_cs:
                        P.add("dve", (lambda e, sl=sl, ba=ba, c=c, c0=c0, n=n, s=s: e.scalar_tensor_tensor(
                            out=CTF[:, c, s, :], in0=PS[ba][:, c0 + n - 30:c0 + n],
                            scalar=cvc(l, "b_ci", c), in1=SIL[sl][:, c0 + n - 30:c0 + n], op0=ALU.add, op1=ALU.mult)),
                            r=psk(ba) + [("SIL", sl), cvk(l, "b_ci")], w=hid_keys(CTF0, 16 * nseg * 120))
        if stop <= 2:
            return
        for s, (c0, n) in enumerate(segs):
            P.add("dve", (lambda e, s=s, n=n: e.tensor_copy(out=CT[l][s][:, :, :],
                                                            in_=GLU[:, :, gbase[s] + n:gbase[s] + n + 30])),
                  r=hid_keys(0, 16 * GW * 2), w=[("CT", l, s)])
        if stop <= 3:
            return
        if need_cs:
            stg = hf32(20480, D)
            for s in range(nseg):
                for c in range(KC):
                    P.add("pe", (lambda e, c=c, s=s: e.transpose(
                        PS[4 + c % 2][0:30, 0:128], CTF[:, c, s, :], IDF[:, :])),
                        r=hid_keys(CTF0, 16 * nseg * 120) + [("IDF",)], w=psk(4 + c % 2, 0, 128))
                    P.add("dve", (lambda e, c=c, stg=stg: e.tensor_copy(out=stg[0:30, c * 128:(c + 1) * 128],
                                                                        in_=PS[4 + c % 2][0:30, 0:128])),
                          r=psk(4 + c % 2, 0, 128), w=hid_keys(20480, 8192))
                dst = o_conv[kind][l, ti["sq"][s]]
                P.add("sp", (lambda e, dst=dst, stg=stg: e.dma_start(out=dst, in_=stg[0:30, :])),
                      r=hid_keys(20480, 8192), dma=True)
        if stop <= 4:
            return
        dgc = [0]
        S1B, S2B = 5, 6
        for ci, c in enumerate(range(KC - 1, -1, -1)):
            cbanks = [(2 * (ci % 2)) + s for s in range(nseg)]
            for k in range(CONV_K):
                slot = dgc[0] % 8
                dgc[0] += 1
                eng = "act" if k % 2 else "pool"
                if eng == "act":
                    P.add("act", (lambda e, slot=slot, k=k, c=c: e.activation(
                        out=DG[:, slot, :], in_=IDB[:], func=AF.Copy, scale=cvc(l, "conv_w", k * 16 + c))),
                        r=[("IDB",), cvk(l, "conv_w")], w=[("DG", slot)])
                else:
                    P.add("pool", (lambda e, slot=slot, k=k, c=c: e.tensor_scalar(
                        out=DG[:, slot, :], in0=IDB[:], scalar1=cvc(l, "conv_w", k * 16 + c), scalar2=None,
                        op0=ALU.mult)),
                        r=[("IDB",), cvk(l, "conv_w")], w=[("DG", slot)])
                for s, (c0, n) in enumerate(segs):
                    P.add("pe", (lambda e, slot=slot, k=k, c=c, s=s, c0=c0, n=n, bk=cbanks[s]: e.matmul(
                        PS[bk][:, 0:n], lhsT=DG[:, slot, :], rhs=GLU[:, c, gbase[s] + k:gbase[s] + k + n],
                        start=(k == 0), stop=(k == CONV_K - 1))),
                        r=[("DG", slot)] + glu_keys(c), w=psk(cbanks[s], 0, n))
            for s, (c0, n) in enumerate(segs):
                bk = cbanks[s]
                P.add("act", (lambda e, c=c, c0=c0, n=n, bk=bk: e.activation(
                    out=dw_ap(c)[:, c0:c0 + n], in_=PS[bk][:, 0:n], func=AF.Identity, bias=cvc(l, "conv_b", c))),
                    r=psk(bk, 0, n) + [cvk(l, "conv_b")], w=dw_keys(c))
                P.add("act", (lambda e, c=c, c0=c0, n=n, bk=bk, ci=ci: e.activation(
                    out=SQ[1][:, c0:c0 + n], in_=PS[bk][:, 0:n], func=AF.Square, bias=cvc(l, "conv_b", c))),
                    r=psk(bk, 0, n) + [cvk(l, "conv_b")], w=[("SQ", 1)])
            P.add("dve", (lambda e, c=c: e.tensor_copy(out=SQ[0][:, 0:ntok], in_=dw_ap(c)[:, 0:ntok])),
                  r=dw_keys(c), w=[("SQ", 0)])
            P.add("pe", (lambda e, ci=ci: e.matmul(PS[S1B][:, 0:ntok], lhsT=ONESB[:], rhs=SQ[0][:, 0:ntok],
                                                   start=(ci == 0), stop=(ci == KC - 1))),
                  r=[("SQ", 0), ("ONESB",)], w=psk(S1B))
            P.add("pe", (lambda e, ci=ci: e.matmul(PS[S2B][:, 0:ntok], lhsT=ONESB[:], rhs=SQ[1][:, 0:ntok],
                                                   start=(ci == 0), stop=(ci == KC - 1))),
                  r=[("SQ", 1), ("ONESB",)], w=psk(S2B))
        if stop <= 5:
            return
        P.add("act", (lambda e: e.activation(out=SIL[0][:, 0:ntok], in_=PS[S1B][:, 0:ntok], func=AF.Copy,
                                             scale=1.0 / D)), r=psk(S1B), w=[("SIL", 0)])
        P.add("dve", (lambda e: e.tensor_tensor(out=TMPF[0][:, 0:ntok], in0=SIL[0][:, 0:ntok], in1=SIL[0][:, 0:ntok],
                                                op=ALU.mult)), r=[("SIL", 0)], w=[("TMPF", 0)])
        P.add("dve", (lambda e: e.scalar_tensor_tensor(out=SIL[1][:, 0:ntok], in0=PS[S2B][:, 0:ntok],
                                                       scalar=1.0 / D, in1=TMPF[0][:, 0:ntok],
                                                       op0=ALU.mult, op1=ALU.subtract)),
              r=psk(S2B) + [("TMPF", 0)], w=[("SIL", 1)])
        P.add("act", (lambda e: e.activation(out=SIL[1][:, 0:ntok], in_=SIL[1][:, 0:ntok], func=AF.Sqrt,
                                             bias=EPS_LN)), r=[("SIL", 1)], w=[("SIL", 1)])
        P.add("dve", (lambda e: e.reciprocal(out=SIL[1][:, 0:ntok], in_=SIL[1][:, 0:ntok])),
              r=[("SIL", 1)], w=[("SIL", 1)])
        HC0 = 0

        def hc_ap(c):
            return hbf(HC0 + c * ntok * 2, ntok)

        def hc_keys(c):
            return hid_keys(HC0 + c * ntok * 2, ntok * 2)
        for c in range(KC):
            t = TMPF[c % 2]
            P.add("dve", (lambda e, c=c, t=t: e.tensor_tensor(out=t[:, 0:ntok], in0=dw_ap(c)[:, 0:ntok],
                                                              in1=SIL[0][:, 0:ntok], op=ALU.subtract)),
                  r=dw_keys(c) + [("SIL", 0)], w=[("TMPF", c % 2)])
            P.add("dve", (lambda e, c=c, t=t: e.tensor_tensor(out=t[:, 0:ntok], in0=t[:, 0:ntok],
                                                              in1=SIL[1][:, 0:ntok], op=ALU.mult)),
                  r=[("TMPF", c % 2), ("SIL", 1)], w=[("TMPF", c % 2)])
            P.add("act", (lambda e, c=c, t=t: e.activation(out=hc_ap(c), in_=t[:, 0:ntok], func=AF.Silu,
                                                           scale=cvc(l, "ln_g", c), bias=cvc(l, "ln_b", c))),
                  r=[("TMPF", c % 2), cvk(l, "ln_g"), cvk(l, "ln_b")], w=hc_keys(c))
        if stop <= 6:
            return
        ZC0 = 16384

        def zc_ap(c):
            return hbf(ZC0 + c * ntok * 2, ntok)

        def zc_keys(c):
            return hid_keys(ZC0 + c * ntok * 2, ntok * 2)
        woc = W["w_o_c"][l].rearrange("(kc p) n -> p kc n", p=128)
        for blk in range(4):
            vo, ko = load_w(woc[:, :, blk * 512:(blk + 1) * 512], KC, 512)
            cg = P_RWKV + 2 * D + D + blk * 512
            vg, kg = load_w(win[:, :, cg:cg + 512], KC, 512)
            for m in range(4):
                c = blk * 4 + m
                ba, bb = 2 * (c % 2), 2 * (c % 2) + 1
                for kc in range(KC):
                    P.add("pe", (lambda e, kc=kc, m=m, vo=vo, ba=ba: e.matmul(
                        PS[ba][:, 0:ntok], lhsT=vo[:, kc, m * 128:(m + 1) * 128], rhs=hc_ap(kc),
                        start=(kc == 0), stop=(kc == KC - 1))),
                        r=ko + hc_keys(kc), w=psk(ba))
                proj16(bb, ntok, vg, kg, m * 128, m)
                sl = c % 2
                P.add("act", (lambda e, sl=sl, bb=bb, c=c: e.activation(
                    out=SIL[sl][:, 0:ntok], in_=PS[bb][:, 0:ntok], func=AF.Sigmoid, bias=cvc(l, "b_gate", 16 + c))),
                    r=psk(bb) + [cvk(l, "b_gate")], w=[("SIL", sl)])
                P.add("dve", (lambda e, sl=sl, ba=ba, c=c: e.scalar_tensor_tensor(
                    out=zc_ap(c), in0=PS[ba][:, 0:ntok], scalar=cvc(l, "b_oc", c), in1=SIL[sl][:, 0:ntok],
                    op0=ALU.add, op1=ALU.mult)),
                    r=psk(ba) + [("SIL", sl), cvk(l, "b_oc")], w=zc_keys(c))
        add_out_proj(ntok, l, zc_ap, zc_keys)

        if stop <= 7:
            return
        YA0 = 0

        def ya_ap(c):
            return hbf(YA0 + c * ntok * 2, ntok)

        def ya_keys(c):
            return hid_keys(YA0 + c * ntok * 2, ntok * 2)
        B0 = 16384
        offs = {}
        for i_, nm in enumerate(["RM", "KM", "VM", "AL", "SG", "LC", "E0", "E1", "KK", "T"]):
            offs[nm] = B0 + i_ * 2048
        offs["AR"] = B0 + 20480
        offs["BT"] = offs["AR"] + 2048
        offs["KT"] = offs["BT"] + 1024
        offs["VB"] = offs["KT"] + 1024
        offs["BHF"] = offs["VB"] + 1024
        offs["KHF"] = offs["BHF"] + 1024

        def f32t(nm):
            return hf32(offs[nm], TT), hid_keys(offs[nm], 2048)

        def bft(nm):
            return hbf(offs[nm], TT), hid_keys(offs[nm], 1024)
        RM, kRM = f32t("RM")
        KM, kKM = f32t("KM")
        VM, kVM = f32t("VM")
        AL, kAL = f32t("AL")
        SG, kSG = f32t("SG")
        LC, kLC = f32t("LC")
        E0, kE0 = f32t("E0")
        E1, kE1 = f32t("E1")
        KK, kKK = f32t("KK")
        T_, kT = f32t("T")
        AR = hbf(offs["AR"], 2 * TT).rearrange("p (a t) -> p a t", a=2)
        kAR = hid_keys(offs["AR"], 2048)
        BT, kBT = bft("BT")
        KT, kKT = bft("KT")
        VB, kVB = bft("VB")
        BHF, kBHF = bft("BHF")
        KHF, kKHF = bft("KHF")

        vs, ks = load_w(win[:, :, 6144:6272], KC, 128)
        vsa, ksa = load_w(win[:, :, 6240:6368], KC, 128)
        vgx, kgx = load_w(win[:, :, 6336:6592], KC, 256)
        sub = cfg.get("sub", 99)
        if sub <= 1:
            return
        proj16(0, ntok, vs, ks, 0, 0)
        if sub <= 2:
            return
        shift_evac(l, 0, 128, 48, T_, kT, segs, ntok)
        if sub <= 3:
            return
        P.add("act", (lambda e: e.activation(out=LORA[:, 0, 0:ntok], in_=T_[:, 0:ntok], func=AF.Tanh)),
              r=kT, w=[("LORA", 0)])
        proj16(1, ntok, vsa, ksa, 0, 0)
        shift_evac(l, 1, 128, 49, E0, kE0, segs, ntok)
        P.add("dve", (lambda e: e.tensor_copy(out=LORA[:, 1, 0:ntok], in_=E0[:, 0:ntok])),
              r=kE0, w=[("LORA", 1)])
        for j in range(2):
            tt = E1 if j == 0 else KK
            kt = kE1 if j == 0 else kKK
            proj16(2 + j, ntok, vgx, kgx, j * 128, 0)
            shift_evac(l, 2 + j, 128, 50 + j, tt, kt, segs, ntok)
            P.add("act", (lambda e, j=j, tt=tt: e.activation(out=LORA[:, 2 + j, 0:ntok], in_=tt[:, 0:ntok],
                                                             func=AF.Sigmoid)),
                  r=kt, w=[("LORA", 2 + j)])

        if stop <= 8:
            return
        wup = W["w_up"][l]
        aup = W["a_up"][l]
        gup = W["g_up"][l].rearrange("(kc p) n -> p kc n", p=128)
        for c in range(KC):
            u0, kw = walloc(12)
            wv = wview(u0, KC, 384)
            for j in range(3):
                P.add("pool", (lambda e, j=j, wv=wv, c=c: e.dma_start(
                    out=wv[:, :, j * 128:(j + 1) * 128], in_=win[:, :, j * D + c * 128:j * D + (c + 1) * 128])),
                    w=kw, dma=True)
            LU = LUB[c % 2]
            kl = [("LUB", c % 2)]
            P.add("pool", (lambda e, LU=LU, c=c: e.dma_start(out=LU[0:96, 0:128], in_=wup[:, c * 128:(c + 1) * 128])),
                  w=kl, dma=True)
            P.add("pool", (lambda e, LU=LU, c=c: e.dma_start(out=LU[0:96, 128:256], in_=aup[:, c * 128:(c + 1) * 128])),
                  w=kl, dma=True)
            P.add("pool", (lambda e, LU=LU, c=c: e.dma_start(
                out=LU[:, 256:512].rearrange("p (k n) -> p k n", k=2), in_=gup[:, :, c * 128:(c + 1) * 128])),
                w=kl, dma=True)
            for j, (dst, kd) in enumerate([(RM, kRM), (KM, kKM), (VM, kVM)]):
                proj16(j, ntok, wv, kw, j * 128, 0)
                shift_evac(l, j, 128, j * 16 + c, dst, kd, segs, ntok)
            P.add("pe", (lambda e, LU=LU: e.matmul(PS[3][:, 0:ntok], lhsT=LU[:, 0:128], rhs=LORA[:, 0, 0:ntok],
                                                   start=True, stop=True)),
                  r=kl + [("LORA", 0)], w=psk(3))
            P.add("pe", (lambda e, LU=LU: e.matmul(PS[4][:, 0:ntok], lhsT=LU[:, 128:256], rhs=LORA[:, 1, 0:ntok],
                                                   start=True, stop=True)),
                  r=kl + [("LORA", 1)], w=psk(4))
            for j in range(2):
                P.add("pe", (lambda e, LU=LU, j=j: e.matmul(
                    PS[5][:, 0:ntok], lhsT=LU[:, 256 + j * 128:256 + (j + 1) * 128], rhs=LORA[:, 2 + j, 0:ntok],
                    start=(j == 0), stop=(j == 1))),
                    r=kl + [("LORA", 2 + j)], w=psk(5))
            P.add("act", (lambda e, c=c: e.activation(out=SG[:, 0:ntok], in_=PS[3][:, 0:ntok], func=AF.Sigmoid,
                                                      bias=cvc(l, "w0", c))),
                  r=psk(3) + [cvk(l, "w0")], w=kSG)
            P.add("act", (lambda e, c=c: e.activation(out=AL[:, 0:ntok], in_=PS[4][:, 0:ntok], func=AF.Sigmoid,
                                                      bias=cvc(l, "a0", c))),
                  r=psk(4) + [cvk(l, "a0")], w=kAL)
            for q in range(nq):
                P.add("dve", (lambda e, q=q: e.tensor_tensor_scan(
                    out=LC[:, q * C:(q + 1) * C], data0=ONESF[:, 0:C], data1=SG[:, q * C:(q + 1) * C], initial=0.0,
                    op0=ALU.mult, op1=ALU.add)),
                    r=kSG + [("ONESF",)], w=kLC)
            P.add("dve", (lambda e: e.tensor_tensor(out=SG[:, 0:ntok], in0=LC[:, 0:ntok], in1=SG[:, 0:ntok],
                                                    op=ALU.subtract)), r=kLC + kSG, w=kSG)
            LCend = LC[:, 0:ntok].rearrange("p (q t) -> p q t", t=C)[:, :, C - 1]
            P.add("act", (lambda e, LCend=LCend: e.activation(out=WCL[:, 0:nq], in_=LCend, func=AF.Exp, scale=-CDEC)),
                  r=kLC, w=[("WCL",)])
            P.add("dve", (lambda e, c=c: e.tensor_scalar(out=KK[:, 0:ntok], in0=KM[:, 0:ntok], scalar1=cvc(l, "k_k", c),
                                                         scalar2=None, op0=ALU.mult)),
                  r=kKM + [cvk(l, "k_k")], w=kKK)
            P.add("act", (lambda e: e.activation(out=SQ[0][:, 0:ntok], in_=KK[:, 0:ntok], func=AF.Square)),
                  r=kKK, w=[("SQ", 0)])
            P.add("pe", (lambda e: e.matmul(PS[6][:, 0:ntok], lhsT=BLKONES[:], rhs=SQ[0][:, 0:ntok], start=True,
                                            stop=True)), r=[("SQ", 0), ("BLKONES",)], w=psk(6))
            P.add("dve", (lambda e: e.tensor_scalar(out=T_[:, 0:ntok], in0=PS[6][:, 0:ntok], scalar1=1e-24, scalar2=None,
                                                    op0=ALU.max)), r=psk(6), w=kT)
            P.add("act", (lambda e: e.activation(out=T_[:, 0:ntok], in_=T_[:, 0:ntok], func=AF.Sqrt)), r=kT, w=kT)
            P.add("dve", (lambda e: e.reciprocal(out=T_[:, 0:ntok], in_=T_[:, 0:ntok])), r=kT, w=kT)
            P.add("dve", (lambda e: e.tensor_tensor(out=KK[:, 0:ntok], in0=KK[:, 0:ntok], in1=T_[:, 0:ntok],
                                                    op=ALU.mult)), r=kKK + kT, w=kKK)
            P.add("dve", (lambda e, c=c: e.tensor_scalar(out=T_[:, 0:ntok], in0=AL[:, 0:ntok], scalar1=-1.0,
                                                         scalar2=cvc(l, "k_a", c), op0=ALU.add, op1=ALU.mult)),
                  r=kAL + [cvk(l, "k_a")], w=kT)
            P.add("dve", (lambda e: e.scalar_tensor_tensor(out=KM[:, 0:ntok], in0=T_[:, 0:ntok], scalar=1.0,
                                                           in1=KM[:, 0:ntok], op0=ALU.add, op1=ALU.mult)),
                  r=kT + kKM, w=kKM)
            P.add("dve", (lambda e: e.tensor_tensor(out=AL[:, 0:ntok], in0=KK[:, 0:ntok], in1=AL[:, 0:ntok],
                                                    op=ALU.mult)), r=kKK + kAL, w=kAL)
            P.add("dve", (lambda e, c=c: e.scalar_tensor_tensor(out=SQ[1][:, 0:ntok], in0=RM[:, 0:ntok],
                                                                scalar=cvc(l, "r_k", c), in1=KM[:, 0:ntok],
                                                                op0=ALU.mult, op1=ALU.mult)),
                  r=kRM + kKM + [cvk(l, "r_k")], w=[("SQ", 1)])
            P.add("pe", (lambda e: e.matmul(PS[6][:, 0:ntok], lhsT=BLKONES[:], rhs=SQ[1][:, 0:ntok], start=True,
                                            stop=True)), r=[("SQ", 1), ("BLKONES",)], w=psk(6))
            P.add("act", (lambda e: e.activation(out=VB[:, 0:ntok], in_=VM[:, 0:ntok], func=AF.Copy)), r=kVM, w=kVB)
            P.add("dve", (lambda e: e.tensor_tensor(out=VM[:, 0:ntok], in0=PS[6][:, 0:ntok], in1=VM[:, 0:ntok],
                                                    op=ALU.mult)), r=psk(6) + kVM + kVB, w=kVM)
            P.add("act", (lambda e: e.activation(out=E0[:, 0:ntok], in_=LC[:, 0:ntok], func=AF.Exp, scale=-CDEC)),
                  r=kLC, w=kE0)
            P.add("dve", (lambda e: e.tensor_tensor(out=AR[:, 1, 0:ntok], in0=RM[:, 0:ntok], in1=E0[:, 0:ntok],
                                                    op=ALU.mult)), r=kRM + kE0, w=kAR)
            P.add("act", (lambda e: e.activation(out=E1[:, 0:ntok], in_=LC[:, 0:ntok], func=AF.Exp, scale=CDEC)),
                  r=kLC, w=kE1)
            P.add("dve", (lambda e: e.tensor_tensor(out=KT[:, 0:ntok], in0=KM[:, 0:ntok], in1=E1[:, 0:ntok],
                                                    op=ALU.mult)), r=kKM + kE1, w=kKT)
            P.add("dve", (lambda e: e.tensor_tensor(out=BT[:, 0:ntok], in0=AL[:, 0:ntok], in1=E1[:, 0:ntok],
                                                    op=ALU.mult)), r=kAL + kE1, w=kBT)
            P.add("act", (lambda e: e.activation(out=E0[:, 0:ntok], in_=SG[:, 0:ntok], func=AF.Exp, scale=-CDEC)),
                  r=kSG + kAR, w=kE0)
            P.add("dve", (lambda e: e.scalar_tensor_tensor(out=AR[:, 0, 0:ntok], in0=KK[:, 0:ntok], scalar=-1.0,
                                                           in1=E0[:, 0:ntok], op0=ALU.mult, op1=ALU.mult)),
                  r=kKK + kE0, w=kAR)
            for q in range(nq):
                P.add("act", (lambda e, q=q: e.activation(out=BHF[:, q * C:(q + 1) * C], in_=BT[:, q * C:(q + 1) * C],
                                                          func=AF.Copy, scale=WCL[:, q:q + 1])),
                      r=kBT + [("WCL",)], w=kBHF)
                P.add("act", (lambda e, q=q: e.activation(out=KHF[:, q * C:(q + 1) * C], in_=KT[:, q * C:(q + 1) * C],
                                                          func=AF.Copy, scale=WCL[:, q:q + 1])),
                      r=kKT + [("WCL",)], w=kKHF)
            for q in range(nq):
                for j, (src, ksrc) in enumerate([(VB, kVB), (BHF, kBHF), (KHF, kKHF)]):
                    P.add("pe", (lambda e, q=q, j=j, src=src: e.transpose(
                        PSB[0:C, j * 128:(j + 1) * 128], src[:, q * C:(q + 1) * C], IDB[:, :])),
                        r=ksrc + [("IDB",)], w=[("PSB",)])
                P.add("dve", (lambda e, q=q: e.tensor_copy(
                    out=TOK[0:C, q, :, :], in_=PSB[0:C, 0:384].rearrange("p (a b) -> p a b", a=3))),
                    r=[("PSB",)], w=[("TOK", q)])
            for q in range(nq if stop > 9 else 0):
                wkv_chunk(l, c, q, C, ti, AR, kAR, BT, kBT, KT, kKT, VM, kVM, ya_ap, ya_keys)
        if stop <= 10:
            return
        ZA0 = 16384

        def za_ap(c):
            return hbf(ZA0 + c * ntok * 2, ntok)

        def za_keys(c):
            return hid_keys(ZA0 + c * ntok * 2, ntok * 2)
        woa = W["w_o_a"][l].rearrange("(kc p) n -> p kc n", p=128)
        for blk in range(4):
            vo, ko = load_w(woa[:, :, blk * 512:(blk + 1) * 512], KC, 512)
            cg = P_RWKV + 2 * D + blk * 512
            vg, kg = load_w(win[:, :, cg:cg + 512], KC, 512)
            for m in range(4):
                c = blk * 4 + m
                ba, bb = 2 * (c % 2), 2 * (c % 2) + 1
                for kc in range(KC):
                    P.add("pe", (lambda e, kc=kc, m=m, vo=vo, ba=ba: e.matmul(
                        PS[ba][:, 0:ntok], lhsT=vo[:, kc, m * 128:(m + 1) * 128], rhs=ya_ap(kc),
                        start=(kc == 0), stop=(kc == KC - 1))),
                        r=ko + ya_keys(kc), w=psk(ba))
                proj16(bb, ntok, vg, kg, m * 128, m)
                sl = c % 2
                P.add("act", (lambda e, sl=sl, bb=bb, c=c: e.activation(
                    out=SIL[sl][:, 0:ntok], in_=PS[bb][:, 0:ntok], func=AF.Sigmoid, bias=cvc(l, "b_gate", c))),
                    r=psk(bb) + [cvk(l, "b_gate")], w=[("SIL", sl)])
                P.add("dve", (lambda e, sl=sl, ba=ba, c=c: e.tensor_tensor(
                    out=za_ap(c), in0=PS[ba][:, 0:ntok], in1=SIL[sl][:, 0:ntok], op=ALU.mult)),
                    r=psk(ba) + [("SIL", sl)], w=za_keys(c))
        add_out_proj(ntok, l, za_ap, za_keys)
        if kind == "s" or ti["last"]:
            for s in range(nseg):
                vec = o_shift[kind][l, ti["sq"][s]]
                for (sbap, drap) in [
                        (SH[l][s][:, 0:48], vec[0:6144].rearrange("(c p) -> p c", p=128)),
                        (SH[l][s][0:96, 48:49], vec[6144:6240].rearrange("(c p) -> p c", p=96)),
                        (SH[l][s][0:96, 49:50], vec[6240:6336].rearrange("(c p) -> p c", p=96)),
                        (SH[l][s][:, 50:52], vec[6336:6592].rearrange("(c p) -> p c", p=128))]:
                    P.add("sp", (lambda e, sbap=sbap, drap=drap: e.dma_start(out=drap, in_=sbap)),
                          r=[("SH", l, s, g) for g in range(NRG)], dma=True)

    def wkv_chunk(l, c, q, C, ti, AR, kAR, BT, kBT, KT, kKT, VM, kVM, ya_ap, ya_keys):
        kind = ti["kind"]
        cs = slice(q * C, (q + 1) * C)
        nit = int(np.log2(C)) - 1
        kST = ("ST", l, c)
        kSBD = ("SBD", c)
        fresh = (kind == "p" and ti["first"] and q == 0)
        si = cfg.get("si", 0)
        if si == 1 and kind == "s":
            fresh = True
        if si == 2 and not (fresh or kind == "s"):
            pass
        elif fresh:
            P.add("dve", (lambda e: e.memset(ST[l][:, c, :], 0.0)), w=[kST])
            P.add("dve", (lambda e: e.memset(SBD[:, c, :], 0.0)), w=[kSBD])
        elif kind == "s":
            b = ti["sq"][q]
            P.add("sp", (lambda e, b=b: e.dma_start(
                out=STG[:, :], in_=st_wkv[l, b, 2 * c:2 * c + 2].rearrange("h i j -> (h i) j"))),
                w=[("STG",)], dma=True)
            for h in range(2):
                P.add("dve", (lambda e, h=h: e.tensor_copy(out=SBLK[h * 64:(h + 1) * 64, h * 64:(h + 1) * 64],
                                                           in_=STG[h * 64:(h + 1) * 64, :])),
                      r=[("STG",)], w=[("SBLK",)])
            P.add("pe", (lambda e: e.transpose(PS[1][:, 0:128], SBLK[:, :], IDF[:, :])),
                  r=[("SBLK",), ("IDF",)], w=psk(1, 0, 128))
            for h in range(2):
                P.add("dve", (lambda e, h=h: e.tensor_copy(out=ST[l][h * 64:(h + 1) * 64, c, :],
                                                           in_=PS[1][h * 64:(h + 1) * 64, h * 64:(h + 1) * 64])),
                      r=psk(1, 0, 128), w=[kST])
            P.add("dve", (lambda e: e.tensor_copy(out=SBD[:, c, :], in_=PS[1][:, 0:128])),
                  r=psk(1, 0, 128), w=[kSBD])
        elif q == 0:
            P.add("dve", (lambda e: e.memset(SBD[:, c, :], 0.0)), w=[kSBD])
            for h in range(2):
                P.add("act", (lambda e, h=h: e.activation(out=SBD[h * 64:(h + 1) * 64, c, h * 64:(h + 1) * 64],
                                                          in_=ST[l][h * 64:(h + 1) * 64, c, :], func=AF.Copy)),
                      r=[kST], w=[kSBD])
        wk = cfg.get("wk", 99)
        if wk <= 1:
            return
        VTq = TOK[0:C, q, 0, :]
        BHq = TOK[0:C, q, 1, :]
        KHq = TOK[0:C, q, 2, :]
        kTOK = [("TOK", q)]
        for h in range(2):
            hs = slice(h * 64, (h + 1) * 64)
            bA = 0 if h == 0 else 3
            bC = 1 if h == 0 else 4
            psA = PS[bA][0:C, 0:2 * C].rearrange("p (a t) -> p a t", a=2)
            psB = PS[bA][0:C, 256:256 + 2 * C].rearrange("p (a t) -> p a t", a=2)
            P.add("pe", (lambda e, hs=hs, psA=psA: e.matmul(psA, lhsT=BT[hs, cs], rhs=AR[hs, :, cs], start=True,
                                                            stop=True)), r=kBT + kAR, w=psk(bA, 0, 256))
            P.add("pe", (lambda e, hs=hs, psB=psB: e.matmul(psB, lhsT=KT[hs, cs], rhs=AR[hs, :, cs], start=True,
                                                            stop=True)), r=kKT + kAR, w=psk(bA, 256, 256))
            P.add("pe", (lambda e, hs=hs, bC=bC: e.matmul(PS[bC][0:C, 0:C], lhsT=AR[hs, 0, cs], rhs=BT[hs, cs], start=True,
                                                   stop=True)), r=kBT + kAR, w=psk(bC, 0, 128))
            P.add("dve", (lambda e, h=h, psA=psA: e.tensor_tensor(out=MA[h][0:C, :, 0:C], in0=psA,
                                                                  in1=MASK2[0:C, :, 0:C], op=ALU.mult)),
                  r=psk(bA, 0, 256) + [("MASK2",)], w=[("MA", h)])
            P.add("dve", (lambda e, h=h, psB=psB: e.tensor_tensor(out=MB[h][0:C, :, 0:C], in0=psB,
                                                                  in1=MASK2[0:C, :, 0:C], op=ALU.mult)),
                  r=psk(bA, 256, 256) + [("MASK2",)], w=[("MB", h)])
            P.add("dve", (lambda e, h=h, bC=bC: e.tensor_tensor(out=MC[h][0:C, 0:C], in0=PS[bC][0:C, 0:C],
                                                         in1=MASKL[0:C, 0:C], op=ALU.mult)),
                  r=psk(bC, 0, 128) + [("MASKL",)], w=[("MC", h)])
            if wk <= 2:
                cur0 = cur1 = 0
                continue
            P.add("dve", (lambda e, h=h: e.tensor_tensor(out=TT_[h][0][0:C, 0:C], in0=MA[h][0:C, 0, 0:C],
                                                         in1=IDB[0:C, 0:C], op=ALU.add)),
                  r=[("MA", h), ("IDB",)], w=[("TI", h, 0)])
            cur = 0
            Pm, PTm = MA[h][0:C, 0, 0:C], MC[h][0:C, 0:C]
            kP = [("MA", h), ("MC", h)]
            for it in range(min(nit, cfg.get('nitmax', 99))):
                nxt = it % 2
                psP = PS[1][0:C, 256:256 + 2 * C].rearrange("p (a t) -> p a t", a=2)
                last = (it == nit - 1)
                its = cfg.get("itstep", 99)
                if not last:
                    P.add("pe", (lambda e, Pm=Pm, PTm=PTm, psP=psP: e.matmul(psP[:, 0, :], lhsT=PTm, rhs=Pm, start=True,
                                                                             stop=True)), r=kP, w=psk(1, 256, 256))
                if its <= 1:
                    continue
                P.add("pe", (lambda e, Pm=Pm, PTm=PTm, psP=psP: e.matmul(psP[:, 1, :], lhsT=Pm, rhs=PTm, start=True,
                                                                         stop=True)), r=kP, w=psk(1, 256, 256))
                if its <= 2:
                    continue
                P.add("dve", (lambda e, h=h, nxt=nxt: e.tensor_copy(out=PP[h][nxt][0:C, 0:2 * C],
                                                                    in_=PS[1][0:C, 256:256 + 2 * C])),
                      r=psk(1, 256, 256), w=[("PP", h, nxt)])
                if its <= 3:
                    continue
                Pm, PTm = PP[h][nxt][0:C, 0:C], PP[h][nxt][0:C, C:2 * C]
                kP = [("PP", h, nxt)]
                Tc = TT_[h][cur][0:C, 0:C]
                P.add("pe", (lambda e, PTm=PTm, Tc=Tc: e.matmul(PS[1][0:C, 128:128 + C], lhsT=PTm, rhs=Tc, start=True,
                                                                stop=True)),
                      r=kP + [("TI", h, cur)], w=psk(1, 128, 128))
                if its <= 4:
                    continue
                P.add("dve", (lambda e, h=h, cur=cur, Tc=Tc: e.tensor_tensor(
                    out=TT_[h][1 - cur][0:C, 0:C], in0=PS[1][0:C, 128:128 + C], in1=Tc, op=ALU.add)),
                    r=psk(1, 128, 128) + [("TI", h, cur)], w=[("TI", h, 1 - cur)])
                cur = 1 - cur
            if h == 0:
                cur0 = cur
            else:
                cur1 = cur
        Tfin = [TT_[0][cur0][0:C, 0:C], TT_[1][cur1][0:C, 0:C]]
        kTfin = [("TI", 0, cur0), ("TI", 1, cur1)]
        if wk <= 3:
            return
        P.add("pe", (lambda e: e.matmul(PS[2][0:C, 0:128], lhsT=AR[:, 0, cs], rhs=SBD[:, c, :], start=True, stop=False)),
              r=kAR + [kSBD], w=psk(2, 0, 128))
        for h in range(2):
            hc = slice(h * 64, (h + 1) * 64)
            P.add("pe", (lambda e, h=h, hc=hc: e.matmul(PS[2][0:C, hc], lhsT=MB[h][0:C, 0, 0:C], rhs=VTq[:, hc],
                                                        start=False, stop=(h == 1))),
                  r=[("MB", h)] + kTOK, w=psk(2, 0, 128))
        P.add("dve", (lambda e: e.tensor_copy(out=XT[0:C, :], in_=PS[2][0:C, 0:128])),
              r=psk(2, 0, 128), w=[("XT",)])
        for h in range(2):
            hc = slice(h * 64, (h + 1) * 64)
            P.add("pe", (lambda e, h=h, hc=hc: e.matmul(PS[2][0:C, 128 + h * 64:128 + (h + 1) * 64], lhsT=Tfin[h],
                                                        rhs=XT[0:C, hc], start=True, stop=True)),
                  r=[kTfin[h], ("XT",)], w=psk(2, 128, 128))
        P.add("dve", (lambda e: e.tensor_copy(out=UT[0:C, :], in_=PS[2][0:C, 128:256])),
              r=psk(2, 128, 128), w=[("UT",)])
        if wk <= 4:
            return
        yv = cfg.get("yv", 99)
        P.add("pe", (lambda e: e.matmul(PS[6][0:C, 0:128], lhsT=AR[:, 1, cs], rhs=SBD[:, c, :], start=True,
                                        stop=(yv <= 1))),
              r=kAR + [kSBD], w=psk(6, 0, 128))
        for h in range(2):
            if yv <= 1:
                break
            hc = slice(h * 64, (h + 1) * 64)
            oc = slice(h * 64, (h + 1) * 64)
            P.add("pe", (lambda e, h=h, hc=hc, oc=oc: e.matmul(PS[6][0:C, oc], lhsT=MA[h][0:C, 1, 0:C], rhs=UT[0:C, hc],
                                                               start=False, stop=(yv <= 2))),
                  r=[("MA", h), ("UT",)], w=psk(6, 0, 128))
            if yv <= 2:
                continue
            P.add("pe", (lambda e, h=h, hc=hc, oc=oc: e.matmul(PS[6][0:C, oc], lhsT=MB[h][0:C, 1, 0:C], rhs=VTq[:, hc],
                                                               start=False, stop=(h == 1))),
                  r=[("MB", h)] + kTOK, w=psk(6, 0, 128))
        w5 = cfg.get("w5", 99)
        if w5 <= 1:
            return
        P.add("pe", (lambda e: e.matmul(PS[6][:, 128:256], lhsT=BHq, rhs=UT[0:C, :], start=True, stop=False)),
              r=kTOK + [("UT",)], w=psk(6, 128, 128))
        P.add("pe", (lambda e: e.matmul(PS[6][:, 128:256], lhsT=KHq, rhs=VTq, start=False, stop=True)),
              r=kTOK, w=psk(6, 128, 128))
        if w5 <= 2:
            return
        for h in range(2):
            hs = slice(h * 64, (h + 1) * 64)
            P.add("dve", (lambda e, h=h, hs=hs: e.scalar_tensor_tensor(
                out=ST[l][hs, c, :], in0=ST[l][hs, c, :], scalar=WCL[hs, q:q + 1],
                in1=PS[6][hs, 128 + h * 64:128 + (h + 1) * 64], op0=ALU.mult, op1=ALU.add)),
                r=[kST, ("WCL",)] + psk(6, 128, 128) + [kSBD], w=[kST])
            if w5 <= 3:
                continue
            P.add("act", (lambda e, h=h, hs=hs: e.activation(out=SBD[hs, c, h * 64:(h + 1) * 64], in_=ST[l][hs, c, :],
                                                             func=AF.Copy)), r=[kST], w=[kSBD])
        if wk <= 5:
            return
        for h in range(2):
            P.add("dve", (lambda e, h=h: e.bn_stats(out=BNS[0:C, h, :], in_=PS[6][0:C, h * 64:(h + 1) * 64])),
                  r=psk(6, 0, 128), w=[("BNS", h)])
            P.add("dve", (lambda e, h=h: e.bn_aggr(out=MV[0:C, h, :], in_=BNS[0:C, h, :])),
                  r=[("BNS", h)], w=[("MV", h)])
        P.add("act", (lambda e: e.activation(out=RS2[0:C, :], in_=MV[0:C, :, 1], func=AF.Sqrt, bias=EPS_GN)),
              r=[("MV", 0), ("MV", 1)], w=[("RS2",)])
        P.add("dve", (lambda e: e.reciprocal(out=RS2[0:C, :], in_=RS2[0:C, :])), r=[("RS2",)], w=[("RS2",)])
        for h in range(2):
            P.add("dve", (lambda e, h=h: e.tensor_scalar(
                out=YN[0:C, h * 64:(h + 1) * 64], in0=PS[6][0:C, h * 64:(h + 1) * 64],
                scalar1=MV[0:C, h, 0:1], scalar2=RS2[0:C, h:h + 1], op0=ALU.subtract, op1=ALU.mult)),
                r=psk(6, 0, 128) + [("MV", h), ("RS2",)], w=[("YN",)])
        if wk <= 6:
            return
        P.add("pe", (lambda e: e.transpose(PS[3][:, 0:C], YN[0:C, :], IDF[0:C, 0:C])), r=[("YN",), ("IDF",)],
              w=psk(3, 0, 128))
        P.add("act", (lambda e: e.activation(out=Y1[:, 0:C], in_=PS[3][:, 0:C], func=AF.Identity,
                                             scale=cvc(l, "gn_g", c), bias=cvc(l, "gn_b", c))),
              r=psk(3, 0, 128) + [cvk(l, "gn_g"), cvk(l, "gn_b")], w=[("Y1",)])
        P.add("dve", (lambda e: e.tensor_tensor(out=Y1[:, 0:C], in0=Y1[:, 0:C], in1=VM[:, cs], op=ALU.add)),
              r=[("Y1",)] + kVM, w=[("Y1",)])
        P.add("dve", (lambda e: e.tensor_tensor(out=ya_ap(c)[:, cs], in0=Y1[:, 0:C], in1=PS[5][:, cs], op=ALU.mult)),
              r=[("Y1",)] + psk(5), w=ya_keys(c))
        if wk <= 7:
            return
        if (kind == "s") or (ti["last"] and q == ti["nq"] - 1):
            b = ti["sq"][q] if kind == "s" else ti["sq"][0]
            for h in range(2):
                hs = slice(h * 64, (h + 1) * 64)
                P.add("dve", (lambda e, h=h, hs=hs: e.tensor_copy(out=SBLK[hs, h * 64:(h + 1) * 64], in_=ST[l][hs, c, :])),
                      r=[kST], w=[("SBLK",)])
            P.add("pe", (lambda e: e.transpose(PS[1][:, 0:128], SBLK[:, :], IDF[:, :])), r=[("SBLK",), ("IDF",)],
                  w=psk(1, 0, 128))
            for h in range(2):
                P.add("dve", (lambda e, h=h: e.tensor_copy(out=OSTG[h * 64:(h + 1) * 64, :],
                                                           in_=PS[1][h * 64:(h + 1) * 64, h * 64:(h + 1) * 64])),
                      r=psk(1, 0, 128), w=[("OSTG",)])
            P.add("sp", (lambda e, b=b: e.dma_start(
                out=o_wkv[kind][l, b, 2 * c:2 * c + 2].rearrange("h i j -> (h i) j"), in_=OSTG[:, :])),
                r=[("OSTG",)], dma=True)

    def final_and_store(ydst_rows, ntok):
        gidx = nv_idx[("final_norm", 0)]
        rms_stats(ntok)
        YF = HIDf[:, 0:KC * TT].rearrange("p (c t) -> p c t", c=KC)
        for c in range(KC):
            P.add("dve", (lambda e, c=c: e.scalar_tensor_tensor(
                out=YF[:, c, 0:ntok], in0=X[:, c, 0:ntok], scalar=NV[:, gidx, c:c + 1],
                in1=RSTD[:, 0:ntok], op0=ALU.mult, op1=ALU.mult)),
                r=[("X", c), ("RSTD",), ("NV", gidx)], w=hid_keys(c * 2048, 2048))
        IO = HIDf[:, KC * TT:KC * TT + D]
        for tb, (dst, n) in enumerate(ydst_rows):
            for cg in range(4):
                bank = cg % 2
                for cc in range(4):
                    c = cg * 4 + cc
                    P.add("pe", (lambda e, c=c, cc=cc, tb=tb, n=n, bank=bank: e.transpose(
                        PS[bank][0:n, cc * 128:(cc + 1) * 128], YF[:, c, tb * 128:tb * 128 + n], IDF[:, :])),
                        r=hid_keys(c * 2048, 2048) + [("IDF",)], w=psk(bank))
                P.add("act" if cg % 2 else "dve",
                      (lambda e, cg=cg, n=n, bank=bank: (
                          e.activation(out=IO[0:n, cg * 512:(cg + 1) * 512], in_=PS[bank][0:n, :], func=AF.Copy)
                          if cg % 2 else
                          e.tensor_copy(out=IO[0:n, cg * 512:(cg + 1) * 512], in_=PS[bank][0:n, :]))),
                      r=psk(bank), w=hid_keys(32768 + cg * 2048, 2048))
            P.add("sp", (lambda e, dst=dst, n=n: e.dma_start(out=dst, in_=IO[0:n, :])),
                  r=hid_keys(32768, 8192), dma=True)

    tiles = []
    for sq in range(n_pseq):
        nt = plen // TT
        for it in range(nt):
            tiles.append(dict(kind="p", sq=[sq], t0=it * TT, ntok=TT, segs=[(0, TT)], C=128, nq=TT // 128,
                              first=(it == 0), last=(it == nt - 1)))
    if n_sseq and not cfg.get("nos", 0):
        tiles.append(dict(kind="s", sq=list(range(n_sseq)), t0=0, ntok=n_sseq * slen,
                          segs=[(s * slen, slen) for s in range(n_sseq)], C=slen, nq=n_sseq, first=True, last=True))

    for ti in tiles:
        ntok = ti["ntok"]
        if ti["kind"] == "p":
            sq, t0 = ti["sq"][0], ti["t0"]
            rows = [(x_p[sq, t0 + tb * 128:t0 + (tb + 1) * 128, :], 128) for tb in range(TT // 128)]
            orow = [(y_p[sq, t0 + tb * 128:t0 + (tb + 1) * 128, :], 128) for tb in range(TT // 128)]
        else:
            xs2 = x_s.rearrange("b t d -> (b t) d")
            ys2 = y_s.rearrange("b t d -> (b t) d")
            rows, orow = [], []
            for r0 in range(0, ntok, 128):
                n = min(128, ntok - r0)
                rows.append((xs2[r0:r0 + n, :], n))
                orow.append((ys2[r0:r0 + n, :], n))
        load_x(rows, ntok)
        for l in range(L):
            if "ffn1" in phases:
                gi = nv_idx[("ffn1_norm", l)]
                rmsnorm(ntok, (lambda c, gi=gi: NV[:, gi, c:c + 1]), ("NV", gi))
                ffn(ntok, l, "ffn1")
            if "mix" in phases:
                mix(l, ti)
            if "ffn2" in phases:
                gi = nv_idx[("ffn2_norm", l)]
                rmsnorm(ntok, (lambda c, gi=gi: NV[:, gi, c:c + 1]), ("NV", gi))
                ffn(ntok, l, "ffn2")
        final_and_store(orow, ntok)

    P.finalize()
    sems = {nm: st.enter_context(nc.semaphore(nm)) for nm in P.sem_names()}
    with nc.allow_non_contiguous_dma(reason="per-feature vectors / small state layouts"):
        with nc.Block() as block:
            @block.tensor
            def _(eng):
                P.emit("pe", eng, sems)

            @block.scalar
            def _(eng):
                P.emit("act", eng, sems)

            @block.vector
            def _(eng):
                P.emit("dve", eng, sems)

            @block.gpsimd
            def _(eng):
                P.emit("pool", eng, sems)

            @block.sync
            def _(eng):
                P.emit("sp", eng, sems, final_wait=True)
    st.close()
    return nc, P


FULL_CFG = dict(n_pseq=2, plen=2048, n_sseq=2, slen=64, depth=2)
_CACHE = {}


def kernel(**inputs):
    cfg = dict(FULL_CFG)
    if "nc" not in _CACHE:
        _CACHE["nc"] = build_program(cfg)[0]
    nc = _CACHE["nc"]
    n = 8
    f = lambda a: np.ascontiguousarray(a, dtype=np.float32)
    shared = {nm: f(inputs[nm]) for nm, _ in WNAMES if nm != "r_k"}
    shared["r_k"] = f(inputs["r_k"]).reshape(2, D)
    shared["final_norm"] = f(inputs["final_norm"]).reshape(1, D)
    xp, xs = f(inputs["x_prompt"]), f(inputs["x_sample"])
    swkv, sshift, sconv = f(inputs["state_wkv"]), f(inputs["state_shift"]), f(inputs["state_conv"])
    in_maps = []
    for i in range(n):
        m = dict(shared)
        m["x_p"] = xp[2 * i:2 * i + 2]
        m["x_s"] = xs[2 * i:2 * i + 2]
        m["state_wkv"] = f(swkv[:, 2 * i:2 * i + 2])
        m["state_shift"] = f(sshift[:, 2 * i:2 * i + 2])
        m["state_conv"] = f(sconv[:, 2 * i:2 * i + 2])
        in_maps.append(m)
    res = run_bass_kernel_spmd(nc, in_maps, core_ids=list(range(n)))
    cat0 = lambda k: np.concatenate([r[k] for r in res.results], axis=0)
    cat1 = lambda k: np.concatenate([r[k] for r in res.results], axis=1)
    return (cat0("y_p"), cat0("y_s"), cat1("wkv_p"), cat1("shift_p"), cat1("conv_p"),
            cat1("wkv_s"), cat1("shift_s"), cat1("conv_s"))
```

```python
import contextlib
import numpy as np
import concourse.bass as bass
import concourse.mybir as mybir
from concourse.bass_utils import run_bass_kernel_spmd

F32 = mybir.dt.float32
BF16 = mybir.dt.bfloat16
ALU = mybir.AluOpType
AF = mybir.ActivationFunctionType

D = 2048
KC = 16
FF = 5632
FC = 44
NH = 32
HD = 64
R_W, R_A, R_G = 96, 96, 256
P_RWKV = 3 * D + R_W + R_A + R_G
P_TOT = P_RWKV + 2 * D + 2 * D
CONV_K = 31
EPS_RMS = 1e-6
EPS_LN = 1e-5
EPS_GN = 64e-5

ENGS = ("pe", "act", "dve", "pool", "sp")


class Op:
    __slots__ = ("eng", "fn", "deps", "dma", "sem", "sig", "sigval", "gidx")


class Prog:
    def __init__(self):
        self.ops = {e: [] for e in ENGS}
        self.lastw = {}
        self.readers = {}
        self.n = 0
        self.dma_slots = {"sp": ["d_sp%d" % i for i in range(8)],
                          "pool": ["d_pl%d" % i for i in range(8)],
                          "act": ["d_ac%d" % i for i in range(4)]}
        self.dma_rr = {"sp": 0, "pool": 0, "act": 0}
        self.slot_last = {}
        self.slot_count = {}

    def add(self, eng, fn, r=(), w=(), dma=False):
        op = Op()
        op.eng, op.fn, op.dma, op.sig, op.sigval = eng, fn, dma, False, 0
        op.gidx = self.n
        self.n += 1
        deps = {}
        for k in r:
            lw = self.lastw.get(k)
            if lw is not None:
                deps[id(lw)] = lw
        for k in w:
            lw = self.lastw.get(k)
            if lw is not None:
                deps[id(lw)] = lw
            rd = self.readers.get(k)
            if rd:
                for o in rd[0].values():
                    deps[id(o)] = o
                for o in rd[1]:
                    deps[id(o)] = o
        if dma:
            slots = self.dma_slots[eng]
            s = slots[self.dma_rr[eng] % len(slots)]
            self.dma_rr[eng] += 1
            op.sem = s
            prev = self.slot_last.get(s)
            if prev is not None:
                deps[id(prev)] = prev
            self.slot_last[s] = op
            self.slot_count[s] = self.slot_count.get(s, 0) + 1
            op.sigval = 16 * self.slot_count[s]
            op.sig = True
        else:
            op.sem = "c_" + eng
        for k in w:
            self.lastw[k] = op
            self.readers[k] = ({}, [])
        for k in r:
            rd = self.readers.get(k)
            if rd is None:
                rd = ({}, [])
                self.readers[k] = rd
            if dma:
                rd[1].append(op)
            else:
                rd[0][eng] = op
        deps.pop(id(op), None)
        dl = []
        for d in deps.values():
            if (not d.dma) and (not dma) and d.eng == eng and eng == "pe":
                continue
            dl.append(d)
        op.deps = dl
        self.ops[eng].append(op)
        return op

    def finalize(self):
        for e in ENGS:
            for op in self.ops[e]:
                for d in op.deps:
                    d.sig = True
        for e in ENGS:
            c = 0
            for op in self.ops[e]:
                if op.dma:
                    continue
                if op.sig:
                    c += 1
                    op.sigval = c

    def sem_names(self):
        names = ["c_" + e for e in ("pe", "act", "dve", "pool")]
        for e in ("sp", "pool", "act"):
            names += self.dma_slots[e]
        return names

    def emit(self, e, eng, sems, final_wait=False):
        waited = {}
        for op in self.ops[e]:
            for d in op.deps:
                if waited.get(d.sem, 0) < d.sigval:
                    eng.wait_ge(sems[d.sem], d.sigval)
                    waited[d.sem] = d.sigval
            inst = op.fn(eng)
            if op.dma:
                inst.then_inc(sems[op.sem], 16)
            elif op.sig:
                inst.then_inc(sems[op.sem], 1)
        if final_wait:
            for s, c in self.slot_count.items():
                if waited.get(s, 0) < 16 * c:
                    eng.wait_ge(sems[s], 16 * c)


RG = [(g * 128, 128) for g in range(48)] + [(6144, 96), (6240, 96), (6336, 128), (6464, 128)]
NRG = len(RG)
CDEC = float(np.exp(-0.5))
WNAMES = [("ffn1_norm", [D]), ("ffn1_w1", [D, FF]), ("ffn1_w3", [D, FF]), ("ffn1_w2", [FF, D]),
          ("mix_norm", [D]), ("w_in", [D, P_TOT]), ("mu_shift", [P_RWKV]), ("w0", [D]),
          ("w_up", [R_W, D]), ("a0", [D]), ("a_up", [R_A, D]), ("g_up", [R_G, D]), ("k_k", [D]),
          ("k_a", [D]), ("r_k", [D]), ("gn_g", [D]), ("gn_b", [D]), ("w_o_a", [D, D]),
          ("b_conv_in", [2 * D]), ("conv_w", [CONV_K, D]), ("conv_b", [D]), ("conv_ln_g", [D]),
          ("conv_ln_b", [D]), ("w_o_c", [D, D]), ("b_o_c", [D]), ("b_gate", [2 * D]),
          ("w_out", [D, D]), ("ffn2_norm", [D]), ("ffn2_w1", [D, FF]), ("ffn2_w3", [D, FF]),
          ("ffn2_w2", [FF, D])]


def build_program(cfg):
    n_pseq, plen, n_sseq, slen = cfg["n_pseq"], cfg["plen"], cfg["n_sseq"], cfg["slen"]
    L = cfg["depth"]
    phases = cfg.get("phases", ("ffn1", "mix", "ffn2"))
    TT = 512
    NSEG = max(1, n_sseq)

    nc = bass.Bass("TRN2", target_bir_lowering=False)

    def din(name, shape):
        return nc.dram_tensor(name, list(shape), F32, kind="ExternalInput").ap()

    def dout(name, shape):
        return nc.dram_tensor(name, list(shape), F32, kind="ExternalOutput").ap()

    x_p = din("x_p", [n_pseq, plen, D])
    x_s = din("x_s", [max(1, n_sseq), slen, D])
    st_wkv = din("state_wkv", [L, max(1, n_sseq), NH, HD, HD])
    st_shift = din("state_shift", [L, max(1, n_sseq), P_RWKV])
    st_conv = din("state_conv", [L, max(1, n_sseq), CONV_K - 1, D])
    W = {}
    for nm, shp in WNAMES:
        W[nm] = din(nm, [L] + shp)
    W["final_norm"] = din("final_norm", [1, D])
    y_p = dout("y_p", [n_pseq, plen, D])
    y_s = dout("y_s", [max(1, n_sseq), slen, D])
    o_wkv = {"p": dout("wkv_p", [L, n_pseq, NH, HD, HD]), "s": dout("wkv_s", [L, max(1, n_sseq), NH, HD, HD])}
    o_shift = {"p": dout("shift_p", [L, n_pseq, P_RWKV]), "s": dout("shift_s", [L, max(1, n_sseq), P_RWKV])}
    o_conv = {"p": dout("conv_p", [L, n_pseq, CONV_K - 1, D]),
              "s": dout("conv_s", [L, max(1, n_sseq), CONV_K - 1, D])}

    P = Prog()
    st = contextlib.ExitStack()

    def sb(name, shape, dt):
        return st.enter_context(nc.sbuf_tensor(name, list(shape), dt))

    X = sb("X", [128, KC, TT], F32)
    H = sb("H", [128, KC, TT], BF16)
    HID = sb("HID", [128, FC * TT], BF16)
    HIDf = HID.bitcast(F32)
    NWU = 48
    WBUF = sb("WBUF", [128, NWU * 512], BF16)
    SQ = [sb("SQ%d" % i, [128, TT], BF16) for i in range(2)]
    SIL = [sb("SIL%d" % i, [128, TT], F32) for i in range(2)]
    RSTD = sb("RSTD", [128, TT], F32)
    TMPF = [sb("TMPF%d" % i, [128, TT], F32) for i in range(2)]
    ONESB = sb("ONESB", [128, 128], BF16)
    ONESF = sb("ONESF", [128, 128], F32)
    IDF = sb("IDF", [128, 128], F32)
    IDB = sb("IDB", [128, 128], BF16)
    BLKONES = sb("BLKONES", [128, 128], BF16)
    MASK2 = sb("MASK2", [128, 2, 128], F32)
    MASKL = sb("MASKL", [128, 128], F32)
    NV = sb("NV", [128, 2 * L + 1, KC], F32)
    PS = [st.enter_context(nc.psum_tensor("PS%d" % i, [128, 512], F32)) for i in range(7)]
    PSB = st.enter_context(nc.psum_tensor("PSB", [128, 1024], BF16))

    cvcols = {}
    off = 0
    for nm, n in [("mix_norm", 16), ("mu", NRG), ("omu", NRG), ("w0", 16), ("a0", 16), ("k_k", 16), ("k_a", 16),
                  ("r_k", 16), ("gn_g", 16), ("gn_b", 16), ("b_ci", 32), ("conv_w", 31 * 16), ("conv_b", 16),
                  ("ln_g", 16), ("ln_b", 16), ("b_oc", 16), ("b_gate", 32)]:
        cvcols[nm] = off
        off += n
    NCV = off
    CV = [sb("CV%d" % l, [128, NCV], F32) for l in range(L)]
    ST = [sb("ST%d" % l, [128, 16, 64], F32) for l in range(L)]
    SBD = sb("SBD", [128, 16, 128], BF16)
    SBLK = sb("SBLK", [128, 128], F32)
    STG = sb("STG", [128, 64], F32)
    OSTG = sb("OSTG", [128, 64], F32)
    SH = [[sb("SH%d_%d" % (l, s), [128, NRG], F32) for s in range(NSEG)] for l in range(L)]
    CT = [[sb("CT%d_%d" % (l, s), [128, 16, 30], BF16) for s in range(NSEG)] for l in range(L)]
    LORA = sb("LORA", [128, 4, TT], BF16)
    LUB = [sb("LUB%d" % i, [128, 512], BF16) for i in range(2)]
    TOK = sb("TOK", [128, 4, 3, 128], BF16)
    DG = sb("DG", [128, 8, 128], BF16)
    WCL = sb("WCL", [128, 4], F32)
    MA = [sb("MA%d" % h, [128, 2, 128], BF16) for h in range(4)]
    MB = [sb("MB%d" % h, [128, 2, 128], BF16) for h in range(4)]
    MC = [sb("MC%d" % h, [128, 128], BF16) for h in range(4)]
    TT_ = [[sb("TI%d_%d" % (h, i), [128, 128], BF16) for i in range(2)] for h in range(4)]
    PP = [[sb("PP%d_%d" % (h, i), [128, 256], BF16) for i in range(2)] for h in range(4)]
    YS = sb("YS", [128, 4, 128], F32)
    XT = sb("XT", [128, 128], BF16)
    UT = sb("UT", [128, 128], BF16)
    YN = sb("YN", [128, 128], F32)
    Y1 = sb("Y1", [128, 128], F32)
    BNS = sb("BNS", [128, 2, 6], F32)
    MV = sb("MV", [128, 2, 2], F32)
    RS2 = sb("RS2", [128, 2], F32)

    def hid_keys(b0, nbytes):
        return [("HID", i) for i in range(b0 // 1024, (b0 + nbytes + 1023) // 1024)]

    def psk(bank, c0=0, n=512):
        return [("PS", bank)]

    def hbf(b0, n):
        return HID[:, b0 // 2:b0 // 2 + n]

    def hf32(b0, n):
        return HIDf[:, b0 // 4:b0 // 4 + n]

    ring = [0]

    def walloc(nunits):
        if ring[0] + nunits > NWU:
            ring[0] = 0
        u0 = ring[0]
        ring[0] += nunits
        return u0, [("WB", u) for u in range(u0, u0 + nunits)]

    def wview(u0, kc, ncols):
        return WBUF[:, u0 * 512:u0 * 512 + kc * ncols].rearrange("p (kc n) -> p kc n", kc=kc)

    def load_w(src3, kc, ncols):
        nun = (kc * ncols + 511) // 512
        u0, keys = walloc(nun)
        v = wview(u0, kc, ncols)
        P.add("pool", (lambda e, v=v, src3=src3: e.dma_start(out=v, in_=src3)), w=keys, dma=True)
        return v, keys

    def cvc(l, nm, i=0):
        c = cvcols[nm] + i
        return CV[l][:, c:c + 1]

    def cvk(l, nm):
        return ("CV", l, nm)

    P.add("dve", lambda e: e.memset(ONESB[:], 1.0), w=[("ONESB",)])
    P.add("dve", lambda e: e.memset(ONESF[:], 1.0), w=[("ONESF",)])
    P.add("pool", lambda e: e.memset(IDF[:], 1.0), w=[("IDF",)])
    P.add("pool", lambda e: e.affine_select(out=IDF[:], in_=IDF[:], pattern=[[-1, 128]],
                                            compare_op=ALU.is_equal, fill=0.0, base=0,
                                            channel_multiplier=1), r=[("IDF",)], w=[("IDF",)])
    P.add("dve", lambda e: e.tensor_copy(out=IDB[:], in_=IDF[:]), r=[("IDF",)], w=[("IDB",)])
    P.add("dve", lambda e: e.memset(BLKONES[:], 0.0), w=[("BLKONES",)])
    P.add("dve", lambda e: e.memset(BLKONES[0:64, 0:64], 1.0), w=[("BLKONES",)])
    P.add("dve", lambda e: e.memset(BLKONES[64:128, 64:128], 1.0), w=[("BLKONES",)])
    P.add("dve", lambda e: e.memset(SBLK[:], 0.0), w=[("SBLK",)])
    for i_ in range(2):
        P.add("dve", (lambda e, i_=i_: e.memset(LUB[i_][:], 0.0)), w=[("LUB", i_)])
    P.add("pool", lambda e: e.memset(MASK2[:], 1.0), w=[("MASK2",)])
    P.add("pool", lambda e: e.affine_select(out=MASK2[:, 0, :], in_=MASK2[:, 0, :], pattern=[[1, 128]],
                                            compare_op=ALU.is_gt, fill=0.0, base=0, channel_multiplier=-1),
          r=[("MASK2",)], w=[("MASK2",)])
    P.add("pool", lambda e: e.affine_select(out=MASK2[:, 1, :], in_=MASK2[:, 1, :], pattern=[[1, 128]],
                                            compare_op=ALU.is_ge, fill=0.0, base=0, channel_multiplier=-1),
          r=[("MASK2",)], w=[("MASK2",)])
    P.add("pool", lambda e: e.memset(MASKL[:], 1.0), w=[("MASKL",)])
    P.add("pool", lambda e: e.affine_select(out=MASKL[:], in_=MASKL[:], pattern=[[-1, 128]],
                                            compare_op=ALU.is_gt, fill=0.0, base=0, channel_multiplier=1),
          r=[("MASKL",)], w=[("MASKL",)])

    def load_vec(dst_tile, col0, vec1d, n, key):
        src = vec1d.rearrange("(c p) -> p c", p=128)
        P.add("sp", (lambda e, src=src: e.dma_start(out=dst_tile[:, col0:col0 + n], in_=src)), w=[key], dma=True)

    def load_rg(dst_tile, col0, vec1d, key, store=False):
        parts = [(dst_tile[:, col0:col0 + 48], vec1d[0:6144].rearrange("(c p) -> p c", p=128)),
                 (dst_tile[0:96, col0 + 48:col0 + 49], vec1d[6144:6240].rearrange("(c p) -> p c", p=96)),
                 (dst_tile[0:96, col0 + 49:col0 + 50], vec1d[6240:6336].rearrange("(c p) -> p c", p=96)),
                 (dst_tile[:, col0 + 50:col0 + 52], vec1d[6336:6592].rearrange("(c p) -> p c", p=128))]
        for (sbap, drap) in parts:
            if store:
                P.add("sp", (lambda e, sbap=sbap, drap=drap: e.dma_start(out=drap, in_=sbap)), r=[key], dma=True)
            else:
                P.add("sp", (lambda e, sbap=sbap, drap=drap: e.dma_start(out=sbap, in_=drap)), w=[key], dma=True)

    nv_idx = {}
    i = 0
    for l in range(L):
        for nm in ("ffn1_norm", "ffn2_norm"):
            nv_idx[(nm, l)] = i
            load_vec(NV[:, i, :], 0, W[nm][l], 16, ("NV", i))
            i += 1
    nv_idx[("final_norm", 0)] = i
    load_vec(NV[:, i, :], 0, W["final_norm"][0], 16, ("NV", i))
    if "mix" in phases:
        for l in range(L):
            for nm, src, n in [("mix_norm", "mix_norm", 16), ("w0", "w0", 16), ("a0", "a0", 16), ("k_k", "k_k", 16),
                               ("k_a", "k_a", 16), ("r_k", "r_k", 16), ("gn_g", "gn_g", 16), ("gn_b", "gn_b", 16),
                               ("b_ci", "b_conv_in", 32), ("conv_b", "conv_b", 16), ("ln_g", "conv_ln_g", 16),
                               ("ln_b", "conv_ln_b", 16), ("b_oc", "b_o_c", 16), ("b_gate", "b_gate", 32)]:
                load_vec(CV[l], cvcols[nm], W[src][l], n, cvk(l, nm))
            for k in range(CONV_K):
                load_vec(CV[l], cvcols["conv_w"] + k * 16, W["conv_w"][l, k], 16, cvk(l, "conv_w"))
            P.add("dve", (lambda e, l=l: e.memset(CV[l][:, cvcols["mu"]:cvcols["mu"] + NRG], 0.0)), w=[cvk(l, "mu")])
            load_rg(CV[l], cvcols["mu"], W["mu_shift"][l], cvk(l, "mu"))
            P.add("dve", (lambda e, l=l: e.tensor_scalar(
                out=CV[l][:, cvcols["omu"]:cvcols["omu"] + NRG], in0=CV[l][:, cvcols["mu"]:cvcols["mu"] + NRG],
                scalar1=-1.0, scalar2=1.0, op0=ALU.mult, op1=ALU.add)), r=[cvk(l, "mu")], w=[cvk(l, "omu")])

    def load_x(xsrc_rows, ntok):
        nb = len(xsrc_rows)
        IO = HIDf[:, 0:nb * D].rearrange("p (a b) -> p a b", a=nb)
        for tb, (src, n) in enumerate(xsrc_rows):
            P.add("sp", (lambda e, tb=tb, src=src, n=n: e.dma_start(out=IO[0:n, tb, :], in_=src)),
                  w=hid_keys(tb * 8192, 8192), dma=True)
        for c in range(KC):
            bank = c % 2
            for tb, (src, n) in enumerate(xsrc_rows):
                P.add("pe", (lambda e, tb=tb, n=n, c=c, bank=bank: e.transpose(
                    PS[bank][:, tb * 128:tb * 128 + n], IO[0:n, tb, c * 128:(c + 1) * 128],
                    IDF[0:n, 0:n])),
                    r=hid_keys(tb * 8192, 8192) + [("IDF",)], w=psk(bank))
            P.add("act" if c % 2 else "dve",
                  (lambda e, c=c, bank=bank: (e.activation(out=X[:, c, 0:ntok], in_=PS[bank][:, 0:ntok],
                                                           func=AF.Copy)
                                              if c % 2 else
                                              e.tensor_copy(out=X[:, c, 0:ntok], in_=PS[bank][:, 0:ntok]))),
                  r=psk(bank), w=[("X", c)])

    def rms_stats(ntok):
        for c in range(KC):
            s = c % 2
            P.add("act", (lambda e, c=c, s=s: e.activation(out=SQ[s][:, 0:ntok], in_=X[:, c, 0:ntok],
                                                           func=AF.Square)),
                  r=[("X", c)], w=[("SQ", s)])
            P.add("pe", (lambda e, c=c, s=s: e.matmul(PS[6][:, 0:ntok], lhsT=ONESB[:], rhs=SQ[s][:, 0:ntok],
                                                      start=(c == 0), stop=(c == KC - 1))),
                  r=[("SQ", s), ("ONESB",)], w=psk(6))
        P.add("act", (lambda e: e.activation(out=RSTD[:, 0:ntok], in_=PS[6][:, 0:ntok], func=AF.Sqrt,
                                             scale=1.0 / D, bias=EPS_RMS)),
              r=psk(6), w=[("RSTD",)])
        P.add("dve", (lambda e: e.reciprocal(out=RSTD[:, 0:ntok], in_=RSTD[:, 0:ntok])),
              r=[("RSTD",)], w=[("RSTD",)])

    def rmsnorm(ntok, gap_fn, gkey):
        rms_stats(ntok)
        for c in range(KC):
            P.add("dve", (lambda e, c=c: e.scalar_tensor_tensor(
                out=H[:, c, 0:ntok], in0=X[:, c, 0:ntok], scalar=gap_fn(c),
                in1=RSTD[:, 0:ntok], op0=ALU.mult, op1=ALU.mult)),
                r=[("X", c), ("RSTD",), gkey], w=[("H", c)])

    def proj16(bank, ntok, wv, wkeys, col0, m, rows=128):
        for kc in range(KC):
            P.add("pe", (lambda e, kc=kc: e.matmul(
                PS[bank][0:rows, 0:ntok], lhsT=wv[:, kc, col0:col0 + rows], rhs=H[:, kc, 0:ntok],
                start=(kc == 0), stop=(kc == KC - 1))),
                r=wkeys + [("H", kc)], w=psk(bank))

    def ffn(ntok, l, pre):
        w1 = W[pre + "_w1"][l].rearrange("(kc p) n -> p kc n", p=128)
        w3 = W[pre + "_w3"][l].rearrange("(kc p) n -> p kc n", p=128)
        w2 = W[pre + "_w2"][l].rearrange("(kc p) n -> p kc n", p=128)
        for fb in range(FF // 512):
            v1, k1 = load_w(w1[:, :, fb * 512:(fb + 1) * 512], KC, 512)
            v3, k3 = load_w(w3[:, :, fb * 512:(fb + 1) * 512], KC, 512)
            for m in range(4):
                f = fb * 4 + m
                ba, bb = 2 * (f % 2), 2 * (f % 2) + 1
                proj16(ba, ntok, v1, k1, m * 128, m)
                proj16(bb, ntok, v3, k3, m * 128, m)
                sl = f % 2
                P.add("act", (lambda e, sl=sl, ba=ba: e.activation(out=SIL[sl][:, 0:ntok], in_=PS[ba][:, 0:ntok],
                                                                   func=AF.Silu)),
                      r=psk(ba), w=[("SIL", sl)])
                P.add("dve", (lambda e, sl=sl, bb=bb, f=f: e.tensor_tensor(
                    out=HID[:, f * TT:f * TT + ntok], in0=SIL[sl][:, 0:ntok], in1=PS[bb][:, 0:ntok], op=ALU.mult)),
                    r=[("SIL", sl)] + psk(bb), w=[("HID", f)])
        for dg in range(4):
            banks = [0, 1, 2, 3] if dg % 2 == 0 else [4, 5, 6, 3]
            for fq in range(4):
                v2, k2 = load_w(w2[:, fq * 11:(fq + 1) * 11, dg * 512:(dg + 1) * 512], 11, 512)
                for dd in range(4):
                    for ff in range(11):
                        f = fq * 11 + ff
                        P.add("pe", (lambda e, dd=dd, ff=ff, f=f, v2=v2, bk=banks[dd]: e.matmul(
                            PS[bk][:, 0:ntok], lhsT=v2[:, ff, dd * 128:(dd + 1) * 128],
                            rhs=HID[:, f * TT:f * TT + ntok], start=(f == 0), stop=(f == FC - 1))),
                            r=k2 + [("HID", f)], w=psk(banks[dd]))
            for dd in range(4):
                c = dg * 4 + dd
                P.add("dve", (lambda e, c=c, bk=banks[dd]: e.scalar_tensor_tensor(
                    out=X[:, c, 0:ntok], in0=PS[bk][:, 0:ntok], scalar=0.5, in1=X[:, c, 0:ntok],
                    op0=ALU.mult, op1=ALU.add)),
                    r=psk(banks[dd]) + [("X", c)], w=[("X", c)])

    def add_out_proj(ntok, l, zsrc_fn, zkeys_fn):
        wo = W["w_out"][l].rearrange("(kc p) n -> p kc n", p=128)
        for blk in range(4):
            v, keys = load_w(wo[:, :, blk * 512:(blk + 1) * 512], KC, 512)
            for m in range(4):
                c = blk * 4 + m
                bank = c % 2
                for kc in range(KC):
                    P.add("pe", (lambda e, kc=kc, m=m, v=v, bank=bank: e.matmul(
                        PS[bank][:, 0:ntok], lhsT=v[:, kc, m * 128:(m + 1) * 128], rhs=zsrc_fn(kc),
                        start=(kc == 0), stop=(kc == KC - 1))),
                        r=keys + zkeys_fn(kc), w=psk(bank))
                P.add("dve", (lambda e, c=c, bank=bank: e.tensor_tensor(
                    out=X[:, c, 0:ntok], in0=PS[bank][:, 0:ntok], in1=X[:, c, 0:ntok], op=ALU.add)),
                    r=psk(bank) + [("X", c)], w=[("X", c)])

    def shift_evac(l, bank, rows, g, out_ap, out_keys, segs, ntok):
        mu = CV[l][0:rows, cvcols["mu"] + g:cvcols["mu"] + g + 1]
        omu = CV[l][0:rows, cvcols["omu"] + g:cvcols["omu"] + g + 1]
        ps = PS[bank]
        P.add("act", (lambda e: e.activation(out=out_ap[0:rows, 0:ntok], in_=ps[0:rows, 0:ntok], func=AF.Copy,
                                             scale=omu)),
              r=psk(bank) + [cvk(l, "omu")], w=out_keys)
        se = cfg.get("se", 99)
        for s, (c0, n) in enumerate(segs):
            if se <= 1:
                break
            P.add("dve", (lambda e, c0=c0, n=n: e.scalar_tensor_tensor(
                out=out_ap[0:rows, c0 + 1:c0 + n], in0=ps[0:rows, c0:c0 + n - 1], scalar=mu,
                in1=out_ap[0:rows, c0 + 1:c0 + n], op0=ALU.mult, op1=ALU.add)),
                r=psk(bank) + out_keys + [cvk(l, "mu")], w=out_keys)
            if se <= 2:
                continue
            P.add("dve", (lambda e, c0=c0, s=s: e.scalar_tensor_tensor(
                out=out_ap[0:rows, c0:c0 + 1], in0=SH[l][s][0:rows, g:g + 1], scalar=mu,
                in1=out_ap[0:rows, c0:c0 + 1], op0=ALU.mult, op1=ALU.add)),
                r=[("SH", l, s, g), cvk(l, "mu")] + out_keys, w=out_keys)
            if se <= 3:
                continue
            P.add("dve", (lambda e, c0=c0, n=n, s=s: e.tensor_copy(
                out=SH[l][s][0:rows, g:g + 1], in_=ps[0:rows, c0 + n - 1:c0 + n])),
                r=psk(bank), w=[("SH", l, s, g)])

    def mix(l, ti):
        ntok, segs, C, nq = ti["ntok"], ti["segs"], ti["C"], ti["nq"]
        kind = ti["kind"]
        stop = cfg.get("stop", 99)
        win = W["w_in"][l].rearrange("(kc p) n -> p kc n", p=128)
        nseg = len(segs)
        for s in range(nseg):
            if kind == "p":
                if ti["first"]:
                    P.add("dve", (lambda e, s=s: e.memset(SH[l][s][:], 0.0)),
                          w=[("SH", l, s, g) for g in range(NRG)])
                    P.add("dve", (lambda e, s=s: e.memset(CT[l][s][:], 0.0)), w=[("CT", l, s)])
            else:
                P.add("dve", (lambda e, s=s: e.memset(SH[l][s][:], 0.0)), w=[("SH", l, s, g) for g in range(NRG)])
                parts_key = "SHLOAD"
                vec = st_shift[l, ti["sq"][s]]
                for (sbap, drap) in [
                        (SH[l][s][:, 0:48], vec[0:6144].rearrange("(c p) -> p c", p=128)),
                        (SH[l][s][0:96, 48:49], vec[6144:6240].rearrange("(c p) -> p c", p=96)),
                        (SH[l][s][0:96, 49:50], vec[6240:6336].rearrange("(c p) -> p c", p=96)),
                        (SH[l][s][:, 50:52], vec[6336:6592].rearrange("(c p) -> p c", p=128))]:
                    P.add("sp", (lambda e, sbap=sbap, drap=drap: e.dma_start(out=sbap, in_=drap)),
                          w=[("SH", l, s, g) for g in range(NRG)], dma=True)
                stg = hf32(20480, D)
                P.add("sp", (lambda e, s=s, stg=stg: e.dma_start(out=stg[0:30, :], in_=st_conv[l, ti["sq"][s]])),
                      w=hid_keys(20480, 8192), dma=True)
                for c in range(KC):
                    P.add("pe", (lambda e, c=c, stg=stg: e.transpose(
                        PS[c % 2][:, 0:30], stg[0:30, c * 128:(c + 1) * 128], IDF[0:30, 0:30])),
                        r=hid_keys(20480, 8192) + [("IDF",)], w=psk(c % 2, 0, 30))
                    P.add("dve", (lambda e, c=c, s=s: e.tensor_copy(out=CT[l][s][:, c, :], in_=PS[c % 2][:, 0:30])),
                          r=psk(c % 2, 0, 30), w=[("CT", l, s)])
        if stop <= 0:
            return
        rmsnorm(ntok, lambda c: cvc(l, "mix_norm", c), cvk(l, "mix_norm"))

        if stop <= 1:
            return
        GW = sum(30 + n for (_, n) in segs)
        gbase = []
        o = 0
        for (_, n) in segs:
            gbase.append(o)
            o += 30 + n
        GLU = HID[:, 0:16 * GW].rearrange("p (c w) -> p c w", c=16)

        def glu_keys(c):
            return hid_keys(c * GW * 2, GW * 2)
        DWB0 = 12288

        def dw_ap(c):
            return hf32(DWB0 + c * ntok * 4, ntok)

        def dw_keys(c):
            return hid_keys(DWB0 + c * ntok * 4, ntok * 4)
        need_cs = (kind == "s") or ti["last"]
        CTF0 = 40960
        CTF = hf32(CTF0, 16 * nseg * 30).rearrange("p (c s t) -> p c s t", c=16, s=nseg)
        for s in range(nseg):
            P.add("dve", (lambda e, s=s: e.tensor_copy(out=GLU[:, :, gbase[s]:gbase[s] + 30], in_=CT[l][s][:, :, :])),
                  r=[("CT", l, s)], w=hid_keys(0, 16 * GW * 2))
        for blk in range(4):
            c0w = P_RWKV + blk * 512
            vv, kv = load_w(win[:, :, c0w:c0w + 512], KC, 512)
            vg, kg = load_w(win[:, :, c0w + D:c0w + D + 512], KC, 512)
            for m in range(4):
                c = blk * 4 + m
                ba, bb = 2 * (c % 2), 2 * (c % 2) + 1
                proj16(ba, ntok, vv, kv, m * 128, m)
                proj16(bb, ntok, vg, kg, m * 128, m)
                sl = c % 2
                P.add("act", (lambda e, sl=sl, bb=bb, c=c: e.activation(
                    out=SIL[sl][:, 0:ntok], in_=PS[bb][:, 0:ntok], func=AF.Sigmoid, bias=cvc(l, "b_ci", 16 + c))),
                    r=psk(bb) + [cvk(l, "b_ci")], w=[("SIL", sl)])
                for s, (c0, n) in enumerate(segs):
                    P.add("dve", (lambda e, sl=sl, ba=ba, c=c, c0=c0, n=n, s=s: e.scalar_tensor_tensor(
                        out=GLU[:, c, gbase[s] + 30:gbase[s] + 30 + n], in0=PS[ba][:, c0:c0 + n],
                        scalar=cvc(l, "b_ci", c), in1=SIL[sl][:, c0:c0 + n], op0=ALU.add, op1=ALU.mult)),
                        r=psk(ba) + [("SIL", sl), cvk(l, "b_ci")], w=glu_keys(c))
                    if need_cs:
                        P.add("dve", (lambda e, sl=sl, ba=ba, c=c, c0=c0, n=n, s=s: e.scalar_tensor_tensor(
                            out=CTF[:, c, s, :], in0=PS[ba][:, c0 + n - 30:c0 + n],
                            scalar=cvc(l, "b_ci", c), in1=SIL[sl][:, c0 + n - 30:c0 + n], op0=ALU.add, op1=ALU.mult)),
                            r=psk(ba) + [("SIL", sl), cvk(l, "b_ci")], w=hid_keys(CTF0, 16 * nseg * 120))
        if stop <= 2:
            return
        for s, (c0, n) in enumerate(segs):
            P.add("dve", (lambda e, s=s, n=n: e.tensor_copy(out=CT[l][s][:, :, :],
                                                            in_=GLU[:, :, gbase[s] + n:gbase[s] + n + 30])),
                  r=hid_keys(0, 16 * GW * 2), w=[("CT", l, s)])
        if stop <= 3:
            return
        if need_cs:
            stg = hf32(20480, D)
            for s in range(nseg):
                for c in range(KC):
                    P.add("pe", (lambda e, c=c, s=s: e.transpose(
                        PS[4 + c % 2][0:30, 0:128], CTF[:, c, s, :], IDF[:, :])),
                        r=hid_keys(CTF0, 16 * nseg * 120) + [("IDF",)], w=psk(4 + c % 2, 0, 128))
                    P.add("dve", (lambda e, c=c, stg=stg: e.tensor_copy(out=stg[0:30, c * 128:(c + 1) * 128],
                                                                        in_=PS[4 + c % 2][0:30, 0:128])),
                          r=psk(4 + c % 2, 0, 128), w=hid_keys(20480, 8192))
                dst = o_conv[kind][l, ti["sq"][s]]
                P.add("sp", (lambda e, dst=dst, stg=stg: e.dma_start(out=dst, in_=stg[0:30, :])),
                      r=hid_keys(20480, 8192), dma=True)
        if stop <= 4:
            return
        dgc = [0]
        S1B, S2B = 5, 6
        for ci, c in enumerate(range(KC - 1, -1, -1)):
            cbanks = [(2 * (ci % 2)) + s for s in range(nseg)]
            for k in range(CONV_K):
                slot = dgc[0] % 8
                dgc[0] += 1
                eng = "act" if k % 2 else "pool"
                if eng == "act":
                    P.add("act", (lambda e, slot=slot, k=k, c=c: e.activation(
                        out=DG[:, slot, :], in_=IDB[:], func=AF.Copy, scale=cvc(l, "conv_w", k * 16 + c))),
                        r=[("IDB",), cvk(l, "conv_w")], w=[("DG", slot)])
                else:
                    P.add("pool", (lambda e, slot=slot, k=k, c=c: e.tensor_scalar(
                        out=DG[:, slot, :], in0=IDB[:], scalar1=cvc(l, "conv_w", k * 16 + c), scalar2=None,
                        op0=ALU.mult)),
                        r=[("IDB",), cvk(l, "conv_w")], w=[("DG", slot)])
                for s, (c0, n) in enumerate(segs):
                    P.add("pe", (lambda e, slot=slot, k=k, c=c, s=s, c0=c0, n=n, bk=cbanks[s]: e.matmul(
                        PS[bk][:, 0:n], lhsT=DG[:, slot, :], rhs=GLU[:, c, gbase[s] + k:gbase[s] + k + n],
                        start=(k == 0), stop=(k == CONV_K - 1))),
                        r=[("DG", slot)] + glu_keys(c), w=psk(cbanks[s], 0, n))
            for s, (c0, n) in enumerate(segs):
                bk = cbanks[s]
                P.add("act", (lambda e, c=c, c0=c0, n=n, bk=bk: e.activation(
                    out=dw_ap(c)[:, c0:c0 + n], in_=PS[bk][:, 0:n], func=AF.Identity, bias=cvc(l, "conv_b", c))),
                    r=psk(bk, 0, n) + [cvk(l, "conv_b")], w=dw_keys(c))
                P.add("act", (lambda e, c=c, c0=c0, n=n, bk=bk, ci=ci: e.activation(
                    out=SQ[1][:, c0:c0 + n], in_=PS[bk][:, 0:n], func=AF.Square, bias=cvc(l, "conv_b", c))),
                    r=psk(bk, 0, n) + [cvk(l, "conv_b")], w=[("SQ", 1)])
            P.add("dve", (lambda e, c=c: e.tensor_copy(out=SQ[0][:, 0:ntok], in_=dw_ap(c)[:, 0:ntok])),
                  r=dw_keys(c), w=[("SQ", 0)])
            P.add("pe", (lambda e, ci=ci: e.matmul(PS[S1B][:, 0:ntok], lhsT=ONESB[:], rhs=SQ[0][:, 0:ntok],
                                                   start=(ci == 0), stop=(ci == KC - 1))),
                  r=[("SQ", 0), ("ONESB",)], w=psk(S1B))
            P.add("pe", (lambda e, ci=ci: e.matmul(PS[S2B][:, 0:ntok], lhsT=ONESB[:], rhs=SQ[1][:, 0:ntok],
                                                   start=(ci == 0), stop=(ci == KC - 1))),
                  r=[("SQ", 1), ("ONESB",)], w=psk(S2B))
        if stop <= 5:
            return
        P.add("act", (lambda e: e.activation(out=SIL[0][:, 0:ntok], in_=PS[S1B][:, 0:ntok], func=AF.Copy,
                                             scale=1.0 / D)), r=psk(S1B), w=[("SIL", 0)])
        P.add("dve", (lambda e: e.tensor_tensor(out=TMPF[0][:, 0:ntok], in0=SIL[0][:, 0:ntok], in1=SIL[0][:, 0:ntok],
                                                op=ALU.mult)), r=[("SIL", 0)], w=[("TMPF", 0)])
        P.add("dve", (lambda e: e.scalar_tensor_tensor(out=SIL[1][:, 0:ntok], in0=PS[S2B][:, 0:ntok],
                                                       scalar=1.0 / D, in1=TMPF[0][:, 0:ntok],
                                                       op0=ALU.mult, op1=ALU.subtract)),
              r=psk(S2B) + [("TMPF", 0)], w=[("SIL", 1)])
        P.add("act", (lambda e: e.activation(out=SIL[1][:, 0:ntok], in_=SIL[1][:, 0:ntok], func=AF.Sqrt,
                                             bias=EPS_LN)), r=[("SIL", 1)], w=[("SIL", 1)])
        P.add("dve", (lambda e: e.reciprocal(out=SIL[1][:, 0:ntok], in_=SIL[1][:, 0:ntok])),
              r=[("SIL", 1)], w=[("SIL", 1)])
        HC0 = 0

        def hc_ap(c):
            return hbf(HC0 + c * ntok * 2, ntok)

        def hc_keys(c):
            return hid_keys(HC0 + c * ntok * 2, ntok * 2)
        for c in range(KC):
            t = TMPF[c % 2]
            P.add("dve", (lambda e, c=c, t=t: e.tensor_tensor(out=t[:, 0:ntok], in0=dw_ap(c)[:, 0:ntok],
                                                              in1=SIL[0][:, 0:ntok], op=ALU.subtract)),
                  r=dw_keys(c) + [("SIL", 0)], w=[("TMPF", c % 2)])
            P.add("dve", (lambda e, c=c, t=t: e.tensor_tensor(out=t[:, 0:ntok], in0=t[:, 0:ntok],
                                                              in1=SIL[1][:, 0:ntok], op=ALU.mult)),
                  r=[("TMPF", c % 2), ("SIL", 1)], w=[("TMPF", c % 2)])
            P.add("act", (lambda e, c=c, t=t: e.activation(out=hc_ap(c), in_=t[:, 0:ntok], func=AF.Silu,
                                                           scale=cvc(l, "ln_g", c), bias=cvc(l, "ln_b", c))),
                  r=[("TMPF", c % 2), cvk(l, "ln_g"), cvk(l, "ln_b")], w=hc_keys(c))
        if stop <= 6:
            return
        ZC0 = 16384

        def zc_ap(c):
            return hbf(ZC0 + c * ntok * 2, ntok)

        def zc_keys(c):
            return hid_keys(ZC0 + c * ntok * 2, ntok * 2)
        woc = W["w_o_c"][l].rearrange("(kc p) n -> p kc n", p=128)
        for blk in range(4):
            vo, ko = load_w(woc[:, :, blk * 512:(blk + 1) * 512], KC, 512)
            cg = P_RWKV + 2 * D + D + blk * 512
            vg, kg = load_w(win[:, :, cg:cg + 512], KC, 512)
            for m in range(4):
                c = blk * 4 + m
                ba, bb = 2 * (c % 2), 2 * (c % 2) + 1
                for kc in range(KC):
                    P.add("pe", (lambda e, kc=kc, m=m, vo=vo, ba=ba: e.matmul(
                        PS[ba][:, 0:ntok], lhsT=vo[:, kc, m * 128:(m + 1) * 128], rhs=hc_ap(kc),
                        start=(kc == 0), stop=(kc == KC - 1))),
                        r=ko + hc_keys(kc), w=psk(ba))
                proj16(bb, ntok, vg, kg, m * 128, m)
                sl = c % 2
                P.add("act", (lambda e, sl=sl, bb=bb, c=c: e.activation(
                    out=SIL[sl][:, 0:ntok], in_=PS[bb][:, 0:ntok], func=AF.Sigmoid, bias=cvc(l, "b_gate", 16 + c))),
                    r=psk(bb) + [cvk(l, "b_gate")], w=[("SIL", sl)])
                P.add("dve", (lambda e, sl=sl, ba=ba, c=c: e.scalar_tensor_tensor(
                    out=zc_ap(c), in0=PS[ba][:, 0:ntok], scalar=cvc(l, "b_oc", c), in1=SIL[sl][:, 0:ntok],
                    op0=ALU.add, op1=ALU.mult)),
                    r=psk(ba) + [("SIL", sl), cvk(l, "b_oc")], w=zc_keys(c))
        add_out_proj(ntok, l, zc_ap, zc_keys)

        if stop <= 7:
            return
        YA0 = 0

        def ya_ap(c):
            return hbf(YA0 + c * ntok * 2, ntok)

        def ya_keys(c):
            return hid_keys(YA0 + c * ntok * 2, ntok * 2)
        B0 = 16384
        offs = {}
        for i_, nm in enumerate(["RM", "KM", "VM", "AL", "SG", "LC", "E0", "E1", "KK", "T"]):
            offs[nm] = B0 + i_ * 2048
        offs["AR"] = B0 + 20480
        offs["BT"] = offs["AR"] + 2048
        offs["KT"] = offs["BT"] + 1024
        offs["VB"] = offs["KT"] + 1024
        offs["BHF"] = offs["VB"] + 1024
        offs["KHF"] = offs["BHF"] + 1024

        def f32t(nm):
            return hf32(offs[nm], TT), hid_keys(offs[nm], 2048)

        def bft(nm):
            return hbf(offs[nm], TT), hid_keys(offs[nm], 1024)
        RM, kRM = f32t("RM")
        KM, kKM = f32t("KM")
        VM, kVM = f32t("VM")
        AL, kAL = f32t("AL")
        SG, kSG = f32t("SG")
        LC, kLC = f32t("LC")
        E0, kE0 = f32t("E0")
        E1, kE1 = f32t("E1")
        KK, kKK = f32t("KK")
        T_, kT = f32t("T")
        AR = hbf(offs["AR"], 2 * TT).rearrange("p (a t) -> p a t", a=2)
        kAR = hid_keys(offs["AR"], 2048)
        BT, kBT = bft("BT")
        KT, kKT = bft("KT")
        VB, kVB = bft("VB")
        BHF, kBHF = bft("BHF")
        KHF, kKHF = bft("KHF")

        vs, ks = load_w(win[:, :, 6144:6272], KC, 128)
        vsa, ksa = load_w(win[:, :, 6240:6368], KC, 128)
        vgx, kgx = load_w(win[:, :, 6336:6592], KC, 256)
        sub = cfg.get("sub", 99)
        if sub <= 1:
            return
        proj16(0, ntok, vs, ks, 0, 0)
        if sub <= 2:
            return
        shift_evac(l, 0, 128, 48, T_, kT, segs, ntok)
        if sub <= 3:
            return
        P.add("act", (lambda e: e.activation(out=LORA[:, 0, 0:ntok], in_=T_[:, 0:ntok], func=AF.Tanh)),
              r=kT, w=[("LORA", 0)])
        proj16(1, ntok, vsa, ksa, 0, 0)
        shift_evac(l, 1, 128, 49, E0, kE0, segs, ntok)
        P.add("dve", (lambda e: e.tensor_copy(out=LORA[:, 1, 0:ntok], in_=E0[:, 0:ntok])),
              r=kE0, w=[("LORA", 1)])
        for j in range(2):
            tt = E1 if j == 0 else KK
            kt = kE1 if j == 0 else kKK
            proj16(2 + j, ntok, vgx, kgx, j * 128, 0)
            shift_evac(l, 2 + j, 128, 50 + j, tt, kt, segs, ntok)
            P.add("act", (lambda e, j=j, tt=tt: e.activation(out=LORA[:, 2 + j, 0:ntok], in_=tt[:, 0:ntok],
                                                             func=AF.Sigmoid)),
                  r=kt, w=[("LORA", 2 + j)])

        if stop <= 8:
            return
        wup = W["w_up"][l]
        aup = W["a_up"][l]
        gup = W["g_up"][l].rearrange("(kc p) n -> p kc n", p=128)
        for c in range(KC):
            u0, kw = walloc(12)
            wv = wview(u0, KC, 384)
            for j in range(3):
                P.add("pool", (lambda e, j=j, wv=wv, c=c: e.dma_start(
                    out=wv[:, :, j * 128:(j + 1) * 128], in_=win[:, :, j * D + c * 128:j * D + (c + 1) * 128])),
                    w=kw, dma=True)
            LU = LUB[c % 2]
            kl = [("LUB", c % 2)]
            P.add("pool", (lambda e, LU=LU, c=c: e.dma_start(out=LU[0:96, 0:128], in_=wup[:, c * 128:(c + 1) * 128])),
                  w=kl, dma=True)
            P.add("pool", (lambda e, LU=LU, c=c: e.dma_start(out=LU[0:96, 128:256], in_=aup[:, c * 128:(c + 1) * 128])),
                  w=kl, dma=True)
            P.add("pool", (lambda e, LU=LU, c=c: e.dma_start(
                out=LU[:, 256:512].rearrange("p (k n) -> p k n", k=2), in_=gup[:, :, c * 128:(c + 1) * 128])),
                w=kl, dma=True)
            for j, (dst, kd) in enumerate([(RM, kRM), (KM, kKM), (VM, kVM)]):
                proj16(j, ntok, wv, kw, j * 128, 0)
                shift_evac(l, j, 128, j * 16 + c, dst, kd, segs, ntok)
            P.add("pe", (lambda e, LU=LU: e.matmul(PS[3][:, 0:ntok], lhsT=LU[:, 0:128], rhs=LORA[:, 0, 0:ntok],
                                                   start=True, stop=True)),
                  r=kl + [("LORA", 0)], w=psk(3))
            P.add("pe", (lambda e, LU=LU: e.matmul(PS[4][:, 0:ntok], lhsT=LU[:, 128:256], rhs=LORA[:, 1, 0:ntok],
                                                   start=True, stop=True)),
                  r=kl + [("LORA", 1)], w=psk(4))
            for j in range(2):
                P.add("pe", (lambda e, LU=LU, j=j: e.matmul(
                    PS[5][:, 0:ntok], lhsT=LU[:, 256 + j * 128:256 + (j + 1) * 128], rhs=LORA[:, 2 + j, 0:ntok],
                    start=(j == 0), stop=(j == 1))),
                    r=kl + [("LORA", 2 + j)], w=psk(5))
            P.add("act", (lambda e, c=c: e.activation(out=SG[:, 0:ntok], in_=PS[3][:, 0:ntok], func=AF.Sigmoid,
                                                      bias=cvc(l, "w0", c))),
                  r=psk(3) + [cvk(l, "w0")], w=kSG)
            P.add("act", (lambda e, c=c: e.activation(out=AL[:, 0:ntok], in_=PS[4][:, 0:ntok], func=AF.Sigmoid,
                                                      bias=cvc(l, "a0", c))),
                  r=psk(4) + [cvk(l, "a0")], w=kAL)
            for q in range(nq):
                P.add("dve", (lambda e, q=q: e.tensor_tensor_scan(
                    out=LC[:, q * C:(q + 1) * C], data0=ONESF[:, 0:C], data1=SG[:, q * C:(q + 1) * C], initial=0.0,
                    op0=ALU.mult, op1=ALU.add)),
                    r=kSG + [("ONESF",)], w=kLC)
            P.add("dve", (lambda e: e.tensor_tensor(out=SG[:, 0:ntok], in0=LC[:, 0:ntok], in1=SG[:, 0:ntok],
                                                    op=ALU.subtract)), r=kLC + kSG, w=kSG)
            LCend = LC[:, 0:ntok].rearrange("p (q t) -> p q t", t=C)[:, :, C - 1]
            P.add("act", (lambda e, LCend=LCend: e.activation(out=WCL[:, 0:nq], in_=LCend, func=AF.Exp, scale=-CDEC)),
                  r=kLC, w=[("WCL",)])
            P.add("dve", (lambda e, c=c: e.tensor_scalar(out=KK[:, 0:ntok], in0=KM[:, 0:ntok], scalar1=cvc(l, "k_k", c),
                                                         scalar2=None, op0=ALU.mult)),
                  r=kKM + [cvk(l, "k_k")], w=kKK)
            P.add("act", (lambda e: e.activation(out=SQ[0][:, 0:ntok], in_=KK[:, 0:ntok], func=AF.Square)),
                  r=kKK, w=[("SQ", 0)])
            P.add("pe", (lambda e: e.matmul(PS[6][:, 0:ntok], lhsT=BLKONES[:], rhs=SQ[0][:, 0:ntok], start=True,
                                            stop=True)), r=[("SQ", 0), ("BLKONES",)], w=psk(6))
            P.add("dve", (lambda e: e.tensor_scalar(out=T_[:, 0:ntok], in0=PS[6][:, 0:ntok], scalar1=1e-24, scalar2=None,
                                                    op0=ALU.max)), r=psk(6), w=kT)
            P.add("act", (lambda e: e.activation(out=T_[:, 0:ntok], in_=T_[:, 0:ntok], func=AF.Sqrt)), r=kT, w=kT)
            P.add("dve", (lambda e: e.reciprocal(out=T_[:, 0:ntok], in_=T_[:, 0:ntok])), r=kT, w=kT)
            P.add("dve", (lambda e: e.tensor_tensor(out=KK[:, 0:ntok], in0=KK[:, 0:ntok], in1=T_[:, 0:ntok],
                                                    op=ALU.mult)), r=kKK + kT, w=kKK)
            P.add("dve", (lambda e, c=c: e.tensor_scalar(out=T_[:, 0:ntok], in0=AL[:, 0:ntok], scalar1=-1.0,
                                                         scalar2=cvc(l, "k_a", c), op0=ALU.add, op1=ALU.mult)),
                  r=kAL + [cvk(l, "k_a")], w=kT)
            P.add("dve", (lambda e: e.scalar_tensor_tensor(out=KM[:, 0:ntok], in0=T_[:, 0:ntok], scalar=1.0,
                                                           in1=KM[:, 0:ntok], op0=ALU.add, op1=ALU.mult)),
                  r=kT + kKM, w=kKM)
            P.add("dve", (lambda e: e.tensor_tensor(out=AL[:, 0:ntok], in0=KK[:, 0:ntok], in1=AL[:, 0:ntok],
                                                    op=ALU.mult)), r=kKK + kAL, w=kAL)
            P.add("dve", (lambda e, c=c: e.scalar_tensor_tensor(out=SQ[1][:, 0:ntok], in0=RM[:, 0:ntok],
                                                                scalar=cvc(l, "r_k", c), in1=KM[:, 0:ntok],
                                                                op0=ALU.mult, op1=ALU.mult)),
                  r=kRM + kKM + [cvk(l, "r_k")], w=[("SQ", 1)])
            P.add("pe", (lambda e: e.matmul(PS[6][:, 0:ntok], lhsT=BLKONES[:], rhs=SQ[1][:, 0:ntok], start=True,
                                            stop=True)), r=[("SQ", 1), ("BLKONES",)], w=psk(6))
            P.add("act", (lambda e: e.activation(out=VB[:, 0:ntok], in_=VM[:, 0:ntok], func=AF.Copy)), r=kVM, w=kVB)
            P.add("dve", (lambda e: e.tensor_tensor(out=VM[:, 0:ntok], in0=PS[6][:, 0:ntok], in1=VM[:, 0:ntok],
                                                    op=ALU.mult)), r=psk(6) + kVM + kVB, w=kVM)
            P.add("act", (lambda e: e.activation(out=E0[:, 0:ntok], in_=LC[:, 0:ntok], func=AF.Exp, scale=-CDEC)),
                  r=kLC, w=kE0)
            P.add("dve", (lambda e: e.tensor_tensor(out=AR[:, 1, 0:ntok], in0=RM[:, 0:ntok], in1=E0[:, 0:ntok],
                                                    op=ALU.mult)), r=kRM + kE0, w=kAR)
            P.add("act", (lambda e: e.activation(out=E1[:, 0:ntok], in_=LC[:, 0:ntok], func=AF.Exp, scale=CDEC)),
                  r=kLC, w=kE1)
            P.add("dve", (lambda e: e.tensor_tensor(out=KT[:, 0:ntok], in0=KM[:, 0:ntok], in1=E1[:, 0:ntok],
                                                    op=ALU.mult)), r=kKM + kE1, w=kKT)
            P.add("dve", (lambda e: e.tensor_tensor(out=BT[:, 0:ntok], in0=AL[:, 0:ntok], in1=E1[:, 0:ntok],
                                                    op=ALU.mult)), r=kAL + kE1, w=kBT)
            P.add("act", (lambda e: e.activation(out=E0[:, 0:ntok], in_=SG[:, 0:ntok], func=AF.Exp, scale=-CDEC)),
                  r=kSG + kAR, w=kE0)
            P.add("dve", (lambda e: e.scalar_tensor_tensor(out=AR[:, 0, 0:ntok], in0=KK[:, 0:ntok], scalar=-1.0,
                                                           in1=E0[:, 0:ntok], op0=ALU.mult, op1=ALU.mult)),
                  r=kKK + kE0, w=kAR)
            for q in range(nq):
                P.add("act", (lambda e, q=q: e.activation(out=BHF[:, q * C:(q + 1) * C], in_=BT[:, q * C:(q + 1) * C],
                                                          func=AF.Copy, scale=WCL[:, q:q + 1])),
                      r=kBT + [("WCL",)], w=kBHF)
                P.add("act", (lambda e, q=q: e.activation(out=KHF[:, q * C:(q + 1) * C], in_=KT[:, q * C:(q + 1) * C],
                                                          func=AF.Copy, scale=WCL[:, q:q + 1])),
                      r=kKT + [("WCL",)], w=kKHF)
            for q in range(nq):
                for j, (src, ksrc) in enumerate([(VB, kVB), (BHF, kBHF), (KHF, kKHF)]):
                    P.add("pe", (lambda e, q=q, j=j, src=src: e.transpose(
                        PSB[0:C, j * 128:(j + 1) * 128], src[:, q * C:(q + 1) * C], IDB[:, :])),
                        r=ksrc + [("IDB",)], w=[("PSB",)])
                P.add("dve", (lambda e, q=q: e.tensor_copy(
                    out=TOK[0:C, q, :, :], in_=PSB[0:C, 0:384].rearrange("p (a b) -> p a b", a=3))),
                    r=[("PSB",)], w=[("TOK", q)])
            if stop > 9:
                wkv_pair(l, c, C, nq, ti, AR, kAR, BT, kBT, KT, kKT, VM, kVM, ya_ap, ya_keys)
        if stop <= 10:
            return
        ZA0 = 16384

        def za_ap(c):
            return hbf(ZA0 + c * ntok * 2, ntok)

        def za_keys(c):
            return hid_keys(ZA0 + c * ntok * 2, ntok * 2)
        woa = W["w_o_a"][l].rearrange("(kc p) n -> p kc n", p=128)
        for blk in range(4):
            vo, ko = load_w(woa[:, :, blk * 512:(blk + 1) * 512], KC, 512)
            cg = P_RWKV + 2 * D + blk * 512
            vg, kg = load_w(win[:, :, cg:cg + 512], KC, 512)
            for m in range(4):
                c = blk * 4 + m
                ba, bb = 2 * (c % 2), 2 * (c % 2) + 1
                for kc in range(KC):
                    P.add("pe", (lambda e, kc=kc, m=m, vo=vo, ba=ba: e.matmul(
                        PS[ba][:, 0:ntok], lhsT=vo[:, kc, m * 128:(m + 1) * 128], rhs=ya_ap(kc),
                        start=(kc == 0), stop=(kc == KC - 1))),
                        r=ko + ya_keys(kc), w=psk(ba))
                proj16(bb, ntok, vg, kg, m * 128, m)
                sl = c % 2
                P.add("act", (lambda e, sl=sl, bb=bb, c=c: e.activation(
                    out=SIL[sl][:, 0:ntok], in_=PS[bb][:, 0:ntok], func=AF.Sigmoid, bias=cvc(l, "b_gate", c))),
                    r=psk(bb) + [cvk(l, "b_gate")], w=[("SIL", sl)])
                P.add("dve", (lambda e, sl=sl, ba=ba, c=c: e.tensor_tensor(
                    out=za_ap(c), in0=PS[ba][:, 0:ntok], in1=SIL[sl][:, 0:ntok], op=ALU.mult)),
                    r=psk(ba) + [("SIL", sl)], w=za_keys(c))
        add_out_proj(ntok, l, za_ap, za_keys)
        if kind == "s" or ti["last"]:
            for s in range(nseg):
                vec = o_shift[kind][l, ti["sq"][s]]
                for (sbap, drap) in [
                        (SH[l][s][:, 0:48], vec[0:6144].rearrange("(c p) -> p c", p=128)),
                        (SH[l][s][0:96, 48:49], vec[6144:6240].rearrange("(c p) -> p c", p=96)),
                        (SH[l][s][0:96, 49:50], vec[6240:6336].rearrange("(c p) -> p c", p=96)),
                        (SH[l][s][:, 50:52], vec[6336:6592].rearrange("(c p) -> p c", p=128))]:
                    P.add("sp", (lambda e, sbap=sbap, drap=drap: e.dma_start(out=drap, in_=sbap)),
                          r=[("SH", l, s, g) for g in range(NRG)], dma=True)

    def wkv_pair(l, c, C, nq, ti, AR, kAR, BT, kBT, KT, kKT, VM, kVM, ya_ap, ya_keys):
        kind = ti["kind"]
        nit = int(np.log2(C)) - 1
        kST = ("ST", l, c)
        kSBD = ("SBD", c)
        CB = [0, 1, 3, 4]
        groups = [list(range(g0, min(g0 + 2, nq))) for g0 in range(0, nq, 2)]

        def state_init(q):
            fresh = (kind == "p" and ti["first"] and q == 0)
            if fresh:
                P.add("dve", (lambda e: e.memset(ST[l][:, c, :], 0.0)), w=[kST])
                P.add("dve", (lambda e: e.memset(SBD[:, c, :], 0.0)), w=[kSBD])
            elif kind == "s":
                b = ti["sq"][q]
                P.add("sp", (lambda e, b=b: e.dma_start(
                    out=STG[:, :], in_=st_wkv[l, b, 2 * c:2 * c + 2].rearrange("h i j -> (h i) j"))),
                    w=[("STG",)], dma=True)
                for h in range(2):
                    P.add("dve", (lambda e, h=h: e.tensor_copy(out=SBLK[h * 64:(h + 1) * 64, h * 64:(h + 1) * 64],
                                                               in_=STG[h * 64:(h + 1) * 64, :])),
                          r=[("STG",)], w=[("SBLK",)])
                P.add("pe", (lambda e: e.transpose(PS[2][:, 0:128], SBLK[:, :], IDF[:, :])),
                      r=[("SBLK",), ("IDF",)], w=psk(2))
                for h in range(2):
                    P.add("dve", (lambda e, h=h: e.tensor_copy(out=ST[l][h * 64:(h + 1) * 64, c, :],
                                                               in_=PS[2][h * 64:(h + 1) * 64, h * 64:(h + 1) * 64])),
                          r=psk(2), w=[kST])
                P.add("dve", (lambda e: e.tensor_copy(out=SBD[:, c, :], in_=PS[2][:, 0:128])),
                      r=psk(2), w=[kSBD])
            elif q == 0:
                P.add("dve", (lambda e: e.memset(SBD[:, c, :], 0.0)), w=[kSBD])
                for h in range(2):
                    P.add("act", (lambda e, h=h: e.activation(out=SBD[h * 64:(h + 1) * 64, c, h * 64:(h + 1) * 64],
                                                              in_=ST[l][h * 64:(h + 1) * 64, c, :], func=AF.Copy)),
                          r=[kST], w=[kSBD])

        def state_out(q):
            b = ti["sq"][q] if kind == "s" else ti["sq"][0]
            for h in range(2):
                hs = slice(h * 64, (h + 1) * 64)
                P.add("dve", (lambda e, h=h, hs=hs: e.tensor_copy(out=SBLK[hs, h * 64:(h + 1) * 64], in_=ST[l][hs, c, :])),
                      r=[kST], w=[("SBLK",)])
            P.add("pe", (lambda e: e.transpose(PS[2][:, 0:128], SBLK[:, :], IDF[:, :])), r=[("SBLK",), ("IDF",)],
                  w=psk(2))
            for h in range(2):
                P.add("dve", (lambda e, h=h: e.tensor_copy(out=OSTG[h * 64:(h + 1) * 64, :],
                                                           in_=PS[2][h * 64:(h + 1) * 64, h * 64:(h + 1) * 64])),
                      r=psk(2), w=[("OSTG",)])
            P.add("sp", (lambda e, b=b: e.dma_start(
                out=o_wkv[kind][l, b, 2 * c:2 * c + 2].rearrange("h i j -> (h i) j"), in_=OSTG[:, :])),
                r=[("OSTG",)], dma=True)

        for grp in groups:
            chains = [(q, h) for q in grp for h in range(2)]
            for i, (q, h) in enumerate(chains):
                cs = slice(q * C, (q + 1) * C)
                hs = slice(h * 64, (h + 1) * 64)
                bk = CB[i]
                psA = PS[bk][0:C, 0:2 * C].rearrange("p (a t) -> p a t", a=2)
                psB = PS[bk][0:C, 256:256 + 2 * C].rearrange("p (a t) -> p a t", a=2)
                P.add("pe", (lambda e, hs=hs, cs=cs, psA=psA: e.matmul(psA, lhsT=BT[hs, cs], rhs=AR[hs, :, cs], start=True,
                                                                       stop=True)), r=kBT + kAR, w=psk(bk))
                P.add("pe", (lambda e, hs=hs, cs=cs, psB=psB: e.matmul(psB, lhsT=KT[hs, cs], rhs=AR[hs, :, cs], start=True,
                                                                       stop=True)), r=kKT + kAR, w=psk(bk))
            for i, (q, h) in enumerate(chains):
                bk = CB[i]
                psA = PS[bk][0:C, 0:2 * C].rearrange("p (a t) -> p a t", a=2)
                psB = PS[bk][0:C, 256:256 + 2 * C].rearrange("p (a t) -> p a t", a=2)
                P.add("dve", (lambda e, i=i, psA=psA: e.tensor_tensor(out=MA[i][0:C, :, 0:C], in0=psA,
                                                                      in1=MASK2[0:C, :, 0:C], op=ALU.mult)),
                      r=psk(bk) + [("MASK2",)], w=[("MA", i)])
                P.add("dve", (lambda e, i=i, psB=psB: e.tensor_tensor(out=MB[i][0:C, :, 0:C], in0=psB,
                                                                      in1=MASK2[0:C, :, 0:C], op=ALU.mult)),
                      r=psk(bk) + [("MASK2",)], w=[("MB", i)])
            for i, (q, h) in enumerate(chains):
                cs = slice(q * C, (q + 1) * C)
                hs = slice(h * 64, (h + 1) * 64)
                bk = CB[i]
                P.add("pe", (lambda e, hs=hs, cs=cs, bk=bk: e.matmul(PS[bk][0:C, 0:C], lhsT=AR[hs, 0, cs], rhs=BT[hs, cs],
                                                                     start=True, stop=True)), r=kBT + kAR, w=psk(bk))
            for i, (q, h) in enumerate(chains):
                bk = CB[i]
                P.add("dve", (lambda e, i=i, bk=bk: e.tensor_tensor(out=MC[i][0:C, 0:C], in0=PS[bk][0:C, 0:C],
                                                                    in1=MASKL[0:C, 0:C], op=ALU.mult)),
                      r=psk(bk) + [("MASKL",)], w=[("MC", i)])
                P.add("dve", (lambda e, i=i: e.tensor_tensor(out=TT_[i][0][0:C, 0:C], in0=MA[i][0:C, 0, 0:C],
                                                             in1=IDB[0:C, 0:C], op=ALU.add)),
                      r=[("MA", i), ("IDB",)], w=[("TI", i, 0)])
            Pm = [MA[i][0:C, 0, 0:C] for i in range(len(chains))]
            PTm = [MC[i][0:C, 0:C] for i in range(len(chains))]
            kPm = [[("MA", i), ("MC", i)] for i in range(len(chains))]
            cur = [0] * len(chains)
            for s in range(nit + 1):
                do_sq = s < nit
                do_t = s >= 1
                for i in range(len(chains)):
                    bk = CB[i]
                    if do_sq:
                        if s < nit - 1:
                            P.add("pe", (lambda e, i=i, bk=bk, a=PTm[i], b_=Pm[i]: e.matmul(
                                PS[bk][0:C, 256:256 + C], lhsT=a, rhs=b_, start=True, stop=True)),
                                r=kPm[i], w=psk(bk))
                        P.add("pe", (lambda e, i=i, bk=bk, a=Pm[i], b_=PTm[i]: e.matmul(
                            PS[bk][0:C, 256 + C:256 + 2 * C], lhsT=a, rhs=b_, start=True, stop=True)),
                            r=kPm[i], w=psk(bk))
                    if do_t:
                        Tc = TT_[i][cur[i]][0:C, 0:C]
                        P.add("pe", (lambda e, i=i, bk=bk, a=PTm[i], Tc=Tc: e.matmul(
                            PS[bk][0:C, 128:128 + C], lhsT=a, rhs=Tc, start=True, stop=True)),
                            r=kPm[i] + [("TI", i, cur[i])], w=psk(bk))
                newP = []
                for i in range(len(chains)):
                    bk = CB[i]
                    if do_t:
                        Tc = TT_[i][cur[i]][0:C, 0:C]
                        P.add("dve", (lambda e, i=i, bk=bk, Tc=Tc, nc_=1 - cur[i]: e.tensor_tensor(
                            out=TT_[i][nc_][0:C, 0:C], in0=PS[bk][0:C, 128:128 + C], in1=Tc, op=ALU.add)),
                            r=psk(bk) + [("TI", i, cur[i])], w=[("TI", i, 1 - cur[i])])
                        cur[i] = 1 - cur[i]
                    if do_sq:
                        nx = s % 2
                        P.add("dve", (lambda e, i=i, bk=bk, nx=nx: e.tensor_copy(
                            out=PP[i][nx][0:C, 0:2 * C], in_=PS[bk][0:C, 256:256 + 2 * C])),
                            r=psk(bk), w=[("PP", i, nx)])
                        newP.append((PP[i][nx][0:C, 0:C], PP[i][nx][0:C, C:2 * C], [("PP", i, nx)]))
                if do_sq:
                    for i in range(len(chains)):
                        Pm[i], PTm[i], kPm[i] = newP[i]
            for gi, q in enumerate(grp):
                cs = slice(q * C, (q + 1) * C)
                state_init(q)
                VTq = TOK[0:C, q, 0, :]
                BHq = TOK[0:C, q, 1, :]
                KHq = TOK[0:C, q, 2, :]
                kTOK = [("TOK", q)]
                ci = [2 * gi, 2 * gi + 1]
                P.add("pe", (lambda e, cs=cs: e.matmul(PS[2][0:C, 0:128], lhsT=AR[:, 0, cs], rhs=SBD[:, c, :], start=True,
                                                       stop=False)), r=kAR + [kSBD], w=psk(2))
                for h in range(2):
                    hc = slice(h * 64, (h + 1) * 64)
                    P.add("pe", (lambda e, h=h, hc=hc, i=ci[h], VTq=VTq: e.matmul(
                        PS[2][0:C, hc], lhsT=MB[i][0:C, 0, 0:C], rhs=VTq[:, hc], start=False, stop=(h == 1))),
                        r=[("MB", ci[h])] + kTOK, w=psk(2))
                P.add("dve", (lambda e: e.tensor_copy(out=XT[0:C, :], in_=PS[2][0:C, 0:128])), r=psk(2), w=[("XT",)])
                for h in range(2):
                    hc = slice(h * 64, (h + 1) * 64)
                    Tf = TT_[ci[h]][cur[ci[h]]][0:C, 0:C]
                    P.add("pe", (lambda e, h=h, hc=hc, Tf=Tf: e.matmul(PS[2][0:C, 128 + h * 64:128 + (h + 1) * 64], lhsT=Tf,
                                                                       rhs=XT[0:C, hc], start=True, stop=True)),
                          r=[("TI", ci[h], cur[ci[h]]), ("XT",)], w=psk(2))
                P.add("dve", (lambda e: e.tensor_copy(out=UT[0:C, :], in_=PS[2][0:C, 128:256])), r=psk(2), w=[("UT",)])
                P.add("pe", (lambda e, cs=cs: e.matmul(PS[6][0:C, 0:128], lhsT=AR[:, 1, cs], rhs=SBD[:, c, :], start=True,
                                                       stop=False)), r=kAR + [kSBD], w=psk(6))
                for h in range(2):
                    hc = slice(h * 64, (h + 1) * 64)
                    P.add("pe", (lambda e, h=h, hc=hc, i=ci[h]: e.matmul(PS[6][0:C, hc], lhsT=MA[i][0:C, 1, 0:C],
                                                                         rhs=UT[0:C, hc], start=False, stop=False)),
                          r=[("MA", ci[h]), ("UT",)], w=psk(6))
                    P.add("pe", (lambda e, h=h, hc=hc, i=ci[h], VTq=VTq: e.matmul(
                        PS[6][0:C, hc], lhsT=MB[i][0:C, 1, 0:C], rhs=VTq[:, hc], start=False, stop=(h == 1))),
                        r=[("MB", ci[h])] + kTOK, w=psk(6))
                P.add("pe", (lambda e, BHq=BHq: e.matmul(PS[6][:, 128:256], lhsT=BHq, rhs=UT[0:C, :], start=True,
                                                         stop=False)), r=kTOK + [("UT",)], w=psk(6))
                P.add("pe", (lambda e, KHq=KHq, VTq=VTq: e.matmul(PS[6][:, 128:256], lhsT=KHq, rhs=VTq, start=False,
                                                                  stop=True)), r=kTOK, w=psk(6))
                for h in range(2):
                    hs = slice(h * 64, (h + 1) * 64)
                    P.add("dve", (lambda e, h=h, hs=hs, q=q: e.scalar_tensor_tensor(
                        out=ST[l][hs, c, :], in0=ST[l][hs, c, :], scalar=WCL[hs, q:q + 1],
                        in1=PS[6][hs, 128 + h * 64:128 + (h + 1) * 64], op0=ALU.mult, op1=ALU.add)),
                        r=[kST, ("WCL",)] + psk(6) + [kSBD], w=[kST])
                    P.add("act", (lambda e, h=h, hs=hs: e.activation(out=SBD[hs, c, h * 64:(h + 1) * 64],
                                                                     in_=ST[l][hs, c, :], func=AF.Copy)),
                          r=[kST], w=[kSBD])
                P.add("dve", (lambda e, q=q: e.tensor_copy(out=YS[0:C, q, :], in_=PS[6][0:C, 0:128])), r=psk(6),
                      w=[("YS", q)])
                if (kind == "s") or (ti["last"] and q == nq - 1):
                    state_out(q)
            for q in grp:
                cs = slice(q * C, (q + 1) * C)
                for h in range(2):
                    P.add("dve", (lambda e, h=h, q=q: e.bn_stats(out=BNS[0:C, h, :], in_=YS[0:C, q, h * 64:(h + 1) * 64])),
                          r=[("YS", q)], w=[("BNS", h)])
                    P.add("dve", (lambda e, h=h: e.bn_aggr(out=MV[0:C, h, :], in_=BNS[0:C, h, :])),
                          r=[("BNS", h)], w=[("MV", h)])
                P.add("act", (lambda e: e.activation(out=RS2[0:C, :], in_=MV[0:C, :, 1], func=AF.Sqrt, bias=EPS_GN)),
                      r=[("MV", 0), ("MV", 1)], w=[("RS2",)])
                P.add("dve", (lambda e: e.reciprocal(out=RS2[0:C, :], in_=RS2[0:C, :])), r=[("RS2",)], w=[("RS2",)])
                for h in range(2):
                    P.add("dve", (lambda e, h=h, q=q: e.tensor_scalar(
                        out=YN[0:C, h * 64:(h + 1) * 64], in0=YS[0:C, q, h * 64:(h + 1) * 64],
                        scalar1=MV[0:C, h, 0:1], scalar2=RS2[0:C, h:h + 1], op0=ALU.subtract, op1=ALU.mult)),
                        r=[("YS", q), ("MV", h), ("RS2",)], w=[("YN",)])
                P.add("pe", (lambda e: e.transpose(PS[2][:, 256:256 + C], YN[0:C, :], IDF[0:C, 0:C])),
                      r=[("YN",), ("IDF",)], w=psk(2))
                P.add("act", (lambda e: e.activation(out=Y1[:, 0:C], in_=PS[2][:, 256:256 + C], func=AF.Identity,
                                                     scale=cvc(l, "gn_g", c), bias=cvc(l, "gn_b", c))),
                      r=psk(2) + [cvk(l, "gn_g"), cvk(l, "gn_b")], w=[("Y1",)])
                P.add("dve", (lambda e, cs=cs: e.tensor_tensor(out=Y1[:, 0:C], in0=Y1[:, 0:C], in1=VM[:, cs], op=ALU.add)),
                      r=[("Y1",)] + kVM, w=[("Y1",)])
                P.add("dve", (lambda e, cs=cs: e.tensor_tensor(out=ya_ap(c)[:, cs], in0=Y1[:, 0:C], in1=PS[5][:, cs],
                                                               op=ALU.mult)),
                      r=[("Y1",)] + psk(5), w=ya_keys(c))

    def final_and_store(ydst_rows, ntok):
        gidx = nv_idx[("final_norm", 0)]
        rms_stats(ntok)
        YF = HIDf[:, 0:KC * TT].rearrange("p (c t) -> p c t", c=KC)
        for c in range(KC):
            P.add("dve", (lambda e, c=c: e.scalar_tensor_tensor(
                out=YF[:, c, 0:ntok], in0=X[:, c, 0:ntok], scalar=NV[:, gidx, c:c + 1],
                in1=RSTD[:, 0:ntok], op0=ALU.mult, op1=ALU.mult)),
                r=[("X", c), ("RSTD",), ("NV", gidx)], w=hid_keys(c * 2048, 2048))
        IO = HIDf[:, KC * TT:KC * TT + D]
        for tb, (dst, n) in enumerate(ydst_rows):
            for cg in range(4):
                bank = cg % 2
                for cc in range(4):
                    c = cg * 4 + cc
                    P.add("pe", (lambda e, c=c, cc=cc, tb=tb, n=n, bank=bank: e.transpose(
                        PS[bank][0:n, cc * 128:(cc + 1) * 128], YF[:, c, tb * 128:tb * 128 + n], IDF[:, :])),
                        r=hid_keys(c * 2048, 2048) + [("IDF",)], w=psk(bank))
                P.add("act" if cg % 2 else "dve",
                      (lambda e, cg=cg, n=n, bank=bank: (
                          e.activation(out=IO[0:n, cg * 512:(cg + 1) * 512], in_=PS[bank][0:n, :], func=AF.Copy)
                          if cg % 2 else
                          e.tensor_copy(out=IO[0:n, cg * 512:(cg + 1) * 512], in_=PS[bank][0:n, :]))),
                      r=psk(bank), w=hid_keys(32768 + cg * 2048, 2048))
            P.add("sp", (lambda e, dst=dst, n=n: e.dma_start(out=dst, in_=IO[0:n, :])),
                  r=hid_keys(32768, 8192), dma=True)

    tiles = []
    for sq in range(n_pseq):
        nt = plen // TT
        for it in range(nt):
            tiles.append(dict(kind="p", sq=[sq], t0=it * TT, ntok=TT, segs=[(0, TT)], C=128, nq=TT // 128,
                              first=(it == 0), last=(it == nt - 1)))
    if n_sseq and not cfg.get("nos", 0):
        tiles.append(dict(kind="s", sq=list(range(n_sseq)), t0=0, ntok=n_sseq * slen,
                          segs=[(s * slen, slen) for s in range(n_sseq)], C=slen, nq=n_sseq, first=True, last=True))

    for ti in tiles:
        ntok = ti["ntok"]
        if ti["kind"] == "p":
            sq, t0 = ti["sq"][0], ti["t0"]
            rows = [(x_p[sq, t0 + tb * 128:t0 + (tb + 1) * 128, :], 128) for tb in range(TT // 128)]
            orow = [(y_p[sq, t0 + tb * 128:t0 + (tb + 1) * 128, :], 128) for tb in range(TT // 128)]
        else:
            xs2 = x_s.rearrange("b t d -> (b t) d")
            ys2 = y_s.rearrange("b t d -> (b t) d")
            rows, orow = [], []
            for r0 in range(0, ntok, 128):
                n = min(128, ntok - r0)
                rows.append((xs2[r0:r0 + n, :], n))
                orow.append((ys2[r0:r0 + n, :], n))
        load_x(rows, ntok)
        for l in range(L):
            if "ffn1" in phases:
                gi = nv_idx[("ffn1_norm", l)]
                rmsnorm(ntok, (lambda c, gi=gi: NV[:, gi, c:c + 1]), ("NV", gi))
                ffn(ntok, l, "ffn1")
            if "mix" in phases:
                mix(l, ti)
            if "ffn2" in phases:
                gi = nv_idx[("ffn2_norm", l)]
                rmsnorm(ntok, (lambda c, gi=gi: NV[:, gi, c:c + 1]), ("NV", gi))
                ffn(ntok, l, "ffn2")
        final_and_store(orow, ntok)

    P.finalize()
    sems = {nm: st.enter_context(nc.semaphore(nm)) for nm in P.sem_names()}
    with nc.allow_non_contiguous_dma(reason="per-feature vectors / small state layouts"):
        with nc.Block() as block:
            @block.tensor
            def _(eng):
                P.emit("pe", eng, sems)

            @block.scalar
            def _(eng):
                P.emit("act", eng, sems)

            @block.vector
            def _(eng):
                P.emit("dve", eng, sems)

            @block.gpsimd
            def _(eng):
                P.emit("pool", eng, sems)

            @block.sync
            def _(eng):
                P.emit("sp", eng, sems, final_wait=True)
    st.close()
    return nc, P


FULL_CFG = dict(n_pseq=2, plen=2048, n_sseq=2, slen=64, depth=2)
_CACHE = {}


def kernel(**inputs):
    cfg = dict(FULL_CFG)
    if "nc" not in _CACHE:
        _CACHE["nc"] = build_program(cfg)[0]
    nc = _CACHE["nc"]
    n = 8
    f = lambda a: np.ascontiguousarray(a, dtype=np.float32)
    shared = {nm: f(inputs[nm]) for nm, _ in WNAMES if nm != "r_k"}
    shared["r_k"] = f(inputs["r_k"]).reshape(2, D)
    shared["final_norm"] = f(inputs["final_norm"]).reshape(1, D)
    xp, xs = f(inputs["x_prompt"]), f(inputs["x_sample"])
    swkv, sshift, sconv = f(inputs["state_wkv"]), f(inputs["state_shift"]), f(inputs["state_conv"])
    in_maps = []
    for i in range(n):
        m = dict(shared)
        m["x_p"] = xp[2 * i:2 * i + 2]
        m["x_s"] = xs[2 * i:2 * i + 2]
        m["state_wkv"] = f(swkv[:, 2 * i:2 * i + 2])
        m["state_shift"] = f(sshift[:, 2 * i:2 * i + 2])
        m["state_conv"] = f(sconv[:, 2 * i:2 * i + 2])
        in_maps.append(m)
    res = run_bass_kernel_spmd(nc, in_maps, core_ids=list(range(n)))
    cat0 = lambda k: np.concatenate([r[k] for r in res.results], axis=0)
    cat1 = lambda k: np.concatenate([r[k] for r in res.results], axis=1)
    return (cat0("y_p"), cat0("y_s"), cat1("wkv_p"), cat1("shift_p"), cat1("conv_p"),
            cat1("wkv_s"), cat1("shift_s"), cat1("conv_s"))
```

```python
import contextlib
import numpy as np
import concourse.bass as bass
import concourse.mybir as mybir
from concourse.bass_utils import run_bass_kernel_spmd

F32 = mybir.dt.float32
BF16 = mybir.dt.bfloat16
ALU = mybir.AluOpType
AF = mybir.ActivationFunctionType

D = 2048
KC = 16
FF = 5632
FC = 44
NH = 32
HD = 64
R_W, R_A, R_G = 96, 96, 256
P_RWKV = 3 * D + R_W + R_A + R_G
P_TOT = P_RWKV + 2 * D + 2 * D
CONV_K = 31
EPS_RMS = 1e-6
EPS_LN = 1e-5
EPS_GN = 64e-5

ENGS = ("pe", "act", "dve", "pool", "sp")


class Op:
    __slots__ = ("eng", "fn", "deps", "dma", "sem", "sig", "sigval", "gidx")


class Prog:
    def __init__(self):
        self.ops = {e: [] for e in ENGS}
        self.lastw = {}
        self.readers = {}
        self.n = 0
        self.dma_slots = {"sp": ["d_sp%d" % i for i in range(8)],
                          "pool": ["d_pl%d" % i for i in range(8)],
                          "act": ["d_ac%d" % i for i in range(4)]}
        self.dma_rr = {"sp": 0, "pool": 0, "act": 0}
        self.slot_last = {}
        self.slot_count = {}

    def add(self, eng, fn, r=(), w=(), dma=False):
        op = Op()
        op.eng, op.fn, op.dma, op.sig, op.sigval = eng, fn, dma, False, 0
        op.gidx = self.n
        self.n += 1
        deps = {}
        for k in r:
            lw = self.lastw.get(k)
            if lw is not None:
                deps[id(lw)] = lw
        for k in w:
            lw = self.lastw.get(k)
            if lw is not None:
                deps[id(lw)] = lw
            rd = self.readers.get(k)
            if rd:
                for o in rd[0].values():
                    deps[id(o)] = o
                for o in rd[1]:
                    deps[id(o)] = o
        if dma:
            slots = self.dma_slots[eng]
            s = slots[self.dma_rr[eng] % len(slots)]
            self.dma_rr[eng] += 1
            op.sem = s
            prev = self.slot_last.get(s)
            if prev is not None:
                deps[id(prev)] = prev
            self.slot_last[s] = op
            self.slot_count[s] = self.slot_count.get(s, 0) + 1
            op.sigval = 16 * self.slot_count[s]
            op.sig = True
        else:
            op.sem = "c_" + eng
        for k in w:
            self.lastw[k] = op
            self.readers[k] = ({}, [])
        for k in r:
            rd = self.readers.get(k)
            if rd is None:
                rd = ({}, [])
                self.readers[k] = rd
            if dma:
                rd[1].append(op)
            else:
                rd[0][eng] = op
        deps.pop(id(op), None)
        dl = []
        for d in deps.values():
            if (not d.dma) and (not dma) and d.eng == eng and eng == "pe":
                continue
            dl.append(d)
        op.deps = dl
        self.ops[eng].append(op)
        return op

    def finalize(self):
        for e in ENGS:
            for op in self.ops[e]:
                for d in op.deps:
                    d.sig = True
        for e in ENGS:
            c = 0
            for op in self.ops[e]:
                if op.dma:
                    continue
                if op.sig:
                    c += 1
                    op.sigval = c

    def sem_names(self):
        names = ["c_" + e for e in ("pe", "act", "dve", "pool")]
        for e in ("sp", "pool", "act"):
            names += self.dma_slots[e]
        return names

    def emit(self, e, eng, sems, final_wait=False):
        waited = {}
        for op in self.ops[e]:
            for d in op.deps:
                if waited.get(d.sem, 0) < d.sigval:
                    eng.wait_ge(sems[d.sem], d.sigval)
                    waited[d.sem] = d.sigval
            inst = op.fn(eng)
            if op.dma:
                inst.then_inc(sems[op.sem], 16)
            elif op.sig:
                inst.then_inc(sems[op.sem], 1)
        if final_wait:
            for s, c in self.slot_count.items():
                if waited.get(s, 0) < 16 * c:
                    eng.wait_ge(sems[s], 16 * c)


RG = [(g * 128, 128) for g in range(48)] + [(6144, 96), (6240, 96), (6336, 128), (6464, 128)]
NRG = len(RG)
CDEC = float(np.exp(-0.5))
WNAMES = [("ffn1_norm", [D]), ("ffn1_w1", [D, FF]), ("ffn1_w3", [D, FF]), ("ffn1_w2", [FF, D]),
          ("mix_norm", [D]), ("w_in", [D, P_TOT]), ("mu_shift", [P_RWKV]), ("w0", [D]),
          ("w_up", [R_W, D]), ("a0", [D]), ("a_up", [R_A, D]), ("g_up", [R_G, D]), ("k_k", [D]),
          ("k_a", [D]), ("r_k", [D]), ("gn_g", [D]), ("gn_b", [D]), ("w_o_a", [D, D]),
          ("b_conv_in", [2 * D]), ("conv_w", [CONV_K, D]), ("conv_b", [D]), ("conv_ln_g", [D]),
          ("conv_ln_b", [D]), ("w_o_c", [D, D]), ("b_o_c", [D]), ("b_gate", [2 * D]),
          ("w_out", [D, D]), ("ffn2_norm", [D]), ("ffn2_w1", [D, FF]), ("ffn2_w3", [D, FF]),
          ("ffn2_w2", [FF, D])]


def build_program(cfg):
    n_pseq, plen, n_sseq, slen = cfg["n_pseq"], cfg["plen"], cfg["n_sseq"], cfg["slen"]
    L = cfg["depth"]
    phases = cfg.get("phases", ("ffn1", "mix", "ffn2"))
    TT = 512
    NSEG = max(1, n_sseq)

    nc = bass.Bass("TRN2", target_bir_lowering=False)

    def din(name, shape):
        return nc.dram_tensor(name, list(shape), F32, kind="ExternalInput").ap()

    def dout(name, shape):
        return nc.dram_tensor(name, list(shape), F32, kind="ExternalOutput").ap()

    x_p = din("x_p", [n_pseq, plen, D])
    x_s = din("x_s", [max(1, n_sseq), slen, D])
    st_wkv = din("state_wkv", [L, max(1, n_sseq), NH, HD, HD])
    st_shift = din("state_shift", [L, max(1, n_sseq), P_RWKV])
    st_conv = din("state_conv", [L, max(1, n_sseq), CONV_K - 1, D])
    W = {}
    for nm, shp in WNAMES:
        W[nm] = din(nm, [L] + shp)
    W["final_norm"] = din("final_norm", [1, D])
    y_p = dout("y_p", [n_pseq, plen, D])
    y_s = dout("y_s", [max(1, n_sseq), slen, D])
    o_wkv = {"p": dout("wkv_p", [L, n_pseq, NH, HD, HD]), "s": dout("wkv_s", [L, max(1, n_sseq), NH, HD, HD])}
    o_shift = {"p": dout("shift_p", [L, n_pseq, P_RWKV]), "s": dout("shift_s", [L, max(1, n_sseq), P_RWKV])}
    o_conv = {"p": dout("conv_p", [L, n_pseq, CONV_K - 1, D]),
              "s": dout("conv_s", [L, max(1, n_sseq), CONV_K - 1, D])}

    P = Prog()
    st = contextlib.ExitStack()

    def sb(name, shape, dt):
        return st.enter_context(nc.sbuf_tensor(name, list(shape), dt))

    X = sb("X", [128, KC, TT], F32)
    H = sb("H", [128, KC, TT], BF16)
    HID = sb("HID", [128, FC * TT], BF16)
    HIDf = HID.bitcast(F32)
    NWU = 48
    WBUF = sb("WBUF", [128, NWU * 512], BF16)
    SQ = [sb("SQ%d" % i, [128, TT], BF16) for i in range(2)]
    SIL = [sb("SIL%d" % i, [128, TT], F32) for i in range(2)]
    RSTD = sb("RSTD", [128, TT], F32)
    TMPF = [sb("TMPF%d" % i, [128, TT], F32) for i in range(2)]
    ONESB = sb("ONESB", [128, 128], BF16)
    ONESF = sb("ONESF", [128, 128], F32)
    IDF = sb("IDF", [128, 128], F32)
    IDB = sb("IDB", [128, 128], BF16)
    BLKONES = sb("BLKONES", [128, 128], BF16)
    MASK2 = sb("MASK2", [128, 2, 128], F32)
    MASKL = sb("MASKL", [128, 128], F32)
    NV = sb("NV", [128, 2 * L + 1, KC], F32)
    PS = [st.enter_context(nc.psum_tensor("PS%d" % i, [128, 512], F32)) for i in range(7)]
    PSB = st.enter_context(nc.psum_tensor("PSB", [128, 1024], BF16))

    cvcols = {}
    off = 0
    for nm, n in [("mix_norm", 16), ("mu", NRG), ("omu", NRG), ("w0", 16), ("a0", 16), ("k_k", 16), ("k_a", 16),
                  ("r_k", 16), ("gn_g", 16), ("gn_b", 16), ("b_ci", 32), ("conv_w", 31 * 16), ("conv_b", 16),
                  ("ln_g", 16), ("ln_b", 16), ("b_oc", 16), ("b_gate", 32)]:
        cvcols[nm] = off
        off += n
    NCV = off
    CV = [sb("CV%d" % l, [128, NCV], F32) for l in range(L)]
    ST = [sb("ST%d" % l, [128, 16, 64], F32) for l in range(L)]
    SBD = sb("SBD", [128, 16, 128], BF16)
    SBLK = sb("SBLK", [128, 128], F32)
    STG = sb("STG", [128, 64], F32)
    OSTG = sb("OSTG", [128, 64], F32)
    SH = [[sb("SH%d_%d" % (l, s), [128, NRG], F32) for s in range(NSEG)] for l in range(L)]
    CT = [[sb("CT%d_%d" % (l, s), [128, 16, 30], BF16) for s in range(NSEG)] for l in range(L)]
    LORA = sb("LORA", [128, 4, TT], BF16)
    LUB = [sb("LUB%d" % i, [128, 512], BF16) for i in range(2)]
    TOK = sb("TOK", [128, 4, 3, 128], BF16)
    DG = sb("DG", [128, 8, 128], BF16)
    WCL = sb("WCL", [128, 4], F32)
    MA = [sb("MA%d" % h, [128, 2, 128], BF16) for h in range(4)]
    MB = [sb("MB%d" % h, [128, 2, 128], BF16) for h in range(4)]
    MC = [sb("MC%d" % h, [128, 128], BF16) for h in range(4)]
    TT_ = [[sb("TI%d_%d" % (h, i), [128, 128], BF16) for i in range(2)] for h in range(4)]
    PP = [[sb("PP%d_%d" % (h, i), [128, 256], BF16) for i in range(2)] for h in range(4)]
    YS = sb("YS", [128, 4, 128], F32)
    XT = sb("XT", [128, 128], BF16)
    UT = sb("UT", [128, 128], BF16)
    YN = sb("YN", [128, 128], F32)
    Y1 = sb("Y1", [128, 128], F32)
    BNS = sb("BNS", [128, 2, 6], F32)
    MV = sb("MV", [128, 2, 2], F32)
    RS2 = sb("RS2", [128, 2], F32)

    def hid_keys(b0, nbytes):
        return [("HID", i) for i in range(b0 // 1024, (b0 + nbytes + 1023) // 1024)]

    def psk(bank, c0=0, n=512):
        return [("PS", bank)]

    def hbf(b0, n):
        return HID[:, b0 // 2:b0 // 2 + n]

    def hf32(b0, n):
        return HIDf[:, b0 // 4:b0 // 4 + n]

    ring = [0]

    def walloc(nunits):
        if ring[0] + nunits > NWU:
            ring[0] = 0
        u0 = ring[0]
        ring[0] += nunits
        return u0, [("WB", u) for u in range(u0, u0 + nunits)]

    def wview(u0, kc, ncols):
        return WBUF[:, u0 * 512:u0 * 512 + kc * ncols].rearrange("p (kc n) -> p kc n", kc=kc)

    WC_TOTAL = L * (6 * D * FF + D * (P_TOT + 1024) + 5 * D * D)
    WC_CH = 60 * 1024 * 1024
    n_wc = (WC_TOTAL + WC_CH - 1) // WC_CH + 1
    wcaches = [nc.dram_tensor("wcache%d" % i_, [WC_CH], BF16).ap() for i_ in range(n_wc)]
    wc_state = {"off": 0, "idx": 0, "pass": 0, "tab": [], "t": 0}

    def wc_begin_tile(first):
        wc_state["idx"] = 0
        wc_state["pass"] = 0 if first else 1

    def load_w_multi(srcs, kc, ncols):
        nun = (kc * ncols + 511) // 512
        u0, keys = walloc(nun)
        v = wview(u0, kc, ncols)
        flat = WBUF[:, u0 * 512:u0 * 512 + kc * ncols]
        i = wc_state["idx"]
        wc_state["idx"] += 1
        if wc_state["pass"] == 0:
            if wc_state["off"] + 128 * kc * ncols > WC_CH:
                wc_state["t"] += 1
                wc_state["off"] = 0
            off = wc_state["off"]
            wc_state["off"] += 128 * kc * ncols
            wcache = wcaches[wc_state["t"]]
            wc_state["tab"].append((off, kc * ncols, wc_state["t"]))
            dst = wcache[off:off + 128 * kc * ncols].rearrange("(p n) -> p n", p=128)
            for (src3, c0) in srcs:
                w_ = src3.shape[2]
                P.add("pool", (lambda e, v=v, src3=src3, c0=c0, w_=w_: e.dma_start(out=v[:, :, c0:c0 + w_], in_=src3)),
                      w=keys, dma=True)
            P.add("sp", (lambda e, dst=dst, flat=flat: e.dma_start(out=dst, in_=flat)), r=keys, w=[("WC", i)], dma=True)
        else:
            off, n, t_ = wc_state["tab"][i]
            assert n == kc * ncols
            srcc = wcaches[t_][off:off + 128 * n].rearrange("(p n) -> p n", p=128)
            P.add("sp", (lambda e, srcc=srcc, flat=flat: e.dma_start(out=flat, in_=srcc)), r=[("WC", i)], w=keys,
                  dma=True)
        return v, keys

    def load_w(src3, kc, ncols):
        return load_w_multi([(src3, 0)], kc, ncols)

    def cvc(l, nm, i=0):
        c = cvcols[nm] + i
        return CV[l][:, c:c + 1]

    def cvk(l, nm):
        return ("CV", l, nm)

    P.add("dve", lambda e: e.memset(ONESB[:], 1.0), w=[("ONESB",)])
    P.add("dve", lambda e: e.memset(ONESF[:], 1.0), w=[("ONESF",)])
    P.add("pool", lambda e: e.memset(IDF[:], 1.0), w=[("IDF",)])
    P.add("pool", lambda e: e.affine_select(out=IDF[:], in_=IDF[:], pattern=[[-1, 128]],
                                            compare_op=ALU.is_equal, fill=0.0, base=0,
                                            channel_multiplier=1), r=[("IDF",)], w=[("IDF",)])
    P.add("dve", lambda e: e.tensor_copy(out=IDB[:], in_=IDF[:]), r=[("IDF",)], w=[("IDB",)])
    P.add("dve", lambda e: e.memset(BLKONES[:], 0.0), w=[("BLKONES",)])
    P.add("dve", lambda e: e.memset(BLKONES[0:64, 0:64], 1.0), w=[("BLKONES",)])
    P.add("dve", lambda e: e.memset(BLKONES[64:128, 64:128], 1.0), w=[("BLKONES",)])
    P.add("dve", lambda e: e.memset(SBLK[:], 0.0), w=[("SBLK",)])
    for i_ in range(2):
        P.add("dve", (lambda e, i_=i_: e.memset(LUB[i_][:], 0.0)), w=[("LUB", i_)])
    P.add("pool", lambda e: e.memset(MASK2[:], 1.0), w=[("MASK2",)])
    P.add("pool", lambda e: e.affine_select(out=MASK2[:, 0, :], in_=MASK2[:, 0, :], pattern=[[1, 128]],
                                            compare_op=ALU.is_gt, fill=0.0, base=0, channel_multiplier=-1),
          r=[("MASK2",)], w=[("MASK2",)])
    P.add("pool", lambda e: e.affine_select(out=MASK2[:, 1, :], in_=MASK2[:, 1, :], pattern=[[1, 128]],
                                            compare_op=ALU.is_ge, fill=0.0, base=0, channel_multiplier=-1),
          r=[("MASK2",)], w=[("MASK2",)])
    P.add("pool", lambda e: e.memset(MASKL[:], 1.0), w=[("MASKL",)])
    P.add("pool", lambda e: e.affine_select(out=MASKL[:], in_=MASKL[:], pattern=[[-1, 128]],
                                            compare_op=ALU.is_gt, fill=0.0, base=0, channel_multiplier=1),
          r=[("MASKL",)], w=[("MASKL",)])

    def load_vec(dst_tile, col0, vec1d, n, key):
        src = vec1d.rearrange("(c p) -> p c", p=128)
        P.add("sp", (lambda e, src=src: e.dma_start(out=dst_tile[:, col0:col0 + n], in_=src)), w=[key], dma=True)

    def load_rg(dst_tile, col0, vec1d, key, store=False):
        parts = [(dst_tile[:, col0:col0 + 48], vec1d[0:6144].rearrange("(c p) -> p c", p=128)),
                 (dst_tile[0:96, col0 + 48:col0 + 49], vec1d[6144:6240].rearrange("(c p) -> p c", p=96)),
                 (dst_tile[0:96, col0 + 49:col0 + 50], vec1d[6240:6336].rearrange("(c p) -> p c", p=96)),
                 (dst_tile[:, col0 + 50:col0 + 52], vec1d[6336:6592].rearrange("(c p) -> p c", p=128))]
        for (sbap, drap) in parts:
            if store:
                P.add("sp", (lambda e, sbap=sbap, drap=drap: e.dma_start(out=drap, in_=sbap)), r=[key], dma=True)
            else:
                P.add("sp", (lambda e, sbap=sbap, drap=drap: e.dma_start(out=sbap, in_=drap)), w=[key], dma=True)

    nv_idx = {}
    i = 0
    for l in range(L):
        for nm in ("ffn1_norm", "ffn2_norm"):
            nv_idx[(nm, l)] = i
            load_vec(NV[:, i, :], 0, W[nm][l], 16, ("NV", i))
            i += 1
    nv_idx[("final_norm", 0)] = i
    load_vec(NV[:, i, :], 0, W["final_norm"][0], 16, ("NV", i))
    if "mix" in phases:
        for l in range(L):
            for nm, src, n in [("mix_norm", "mix_norm", 16), ("w0", "w0", 16), ("a0", "a0", 16), ("k_k", "k_k", 16),
                               ("k_a", "k_a", 16), ("r_k", "r_k", 16), ("gn_g", "gn_g", 16), ("gn_b", "gn_b", 16),
                               ("b_ci", "b_conv_in", 32), ("conv_b", "conv_b", 16), ("ln_g", "conv_ln_g", 16),
                               ("ln_b", "conv_ln_b", 16), ("b_oc", "b_o_c", 16), ("b_gate", "b_gate", 32)]:
                load_vec(CV[l], cvcols[nm], W[src][l], n, cvk(l, nm))
            for k in range(CONV_K):
                load_vec(CV[l], cvcols["conv_w"] + k * 16, W["conv_w"][l, k], 16, cvk(l, "conv_w"))
            P.add("dve", (lambda e, l=l: e.memset(CV[l][:, cvcols["mu"]:cvcols["mu"] + NRG], 0.0)), w=[cvk(l, "mu")])
            load_rg(CV[l], cvcols["mu"], W["mu_shift"][l], cvk(l, "mu"))
            P.add("dve", (lambda e, l=l: e.tensor_scalar(
                out=CV[l][:, cvcols["omu"]:cvcols["omu"] + NRG], in0=CV[l][:, cvcols["mu"]:cvcols["mu"] + NRG],
                scalar1=-1.0, scalar2=1.0, op0=ALU.mult, op1=ALU.add)), r=[cvk(l, "mu")], w=[cvk(l, "omu")])

    def load_x(xsrc_rows, ntok):
        nb = len(xsrc_rows)
        IO = HIDf[:, 0:nb * D].rearrange("p (a b) -> p a b", a=nb)
        for tb, (src, n) in enumerate(xsrc_rows):
            P.add("sp", (lambda e, tb=tb, src=src, n=n: e.dma_start(out=IO[0:n, tb, :], in_=src)),
                  w=hid_keys(tb * 8192, 8192), dma=True)
        for c in range(KC):
            bank = c % 2
            for tb, (src, n) in enumerate(xsrc_rows):
                P.add("pe", (lambda e, tb=tb, n=n, c=c, bank=bank: e.transpose(
                    PS[bank][:, tb * 128:tb * 128 + n], IO[0:n, tb, c * 128:(c + 1) * 128],
                    IDF[0:n, 0:n])),
                    r=hid_keys(tb * 8192, 8192) + [("IDF",)], w=psk(bank))
            P.add("act" if c % 2 else "dve",
                  (lambda e, c=c, bank=bank: (e.activation(out=X[:, c, 0:ntok], in_=PS[bank][:, 0:ntok],
                                                           func=AF.Copy)
                                              if c % 2 else
                                              e.tensor_copy(out=X[:, c, 0:ntok], in_=PS[bank][:, 0:ntok]))),
                  r=psk(bank), w=[("X", c)])

    def rms_stats(ntok):
        for c in range(KC):
            s = c % 2
            P.add("act", (lambda e, c=c, s=s: e.activation(out=SQ[s][:, 0:ntok], in_=X[:, c, 0:ntok],
                                                           func=AF.Square)),
                  r=[("X", c)], w=[("SQ", s)])
            P.add("pe", (lambda e, c=c, s=s: e.matmul(PS[6][:, 0:ntok], lhsT=ONESB[:], rhs=SQ[s][:, 0:ntok],
                                                      start=(c == 0), stop=(c == KC - 1))),
                  r=[("SQ", s), ("ONESB",)], w=psk(6))
        P.add("act", (lambda e: e.activation(out=RSTD[:, 0:ntok], in_=PS[6][:, 0:ntok], func=AF.Sqrt,
                                             scale=1.0 / D, bias=EPS_RMS)),
              r=psk(6), w=[("RSTD",)])
        P.add("dve", (lambda e: e.reciprocal(out=RSTD[:, 0:ntok], in_=RSTD[:, 0:ntok])),
              r=[("RSTD",)], w=[("RSTD",)])

    def rmsnorm(ntok, gap_fn, gkey):
        rms_stats(ntok)
        for c in range(KC):
            P.add("dve", (lambda e, c=c: e.scalar_tensor_tensor(
                out=H[:, c, 0:ntok], in0=X[:, c, 0:ntok], scalar=gap_fn(c),
                in1=RSTD[:, 0:ntok], op0=ALU.mult, op1=ALU.mult)),
                r=[("X", c), ("RSTD",), gkey], w=[("H", c)])

    def proj16(bank, ntok, wv, wkeys, col0, m, rows=128):
        for kc in range(KC):
            P.add("pe", (lambda e, kc=kc: e.matmul(
                PS[bank][0:rows, 0:ntok], lhsT=wv[:, kc, col0:col0 + rows], rhs=H[:, kc, 0:ntok],
                start=(kc == 0), stop=(kc == KC - 1))),
                r=wkeys + [("H", kc)], w=psk(bank))

    def ffn(ntok, l, pre):
        w1 = W[pre + "_w1"][l].rearrange("(kc p) n -> p kc n", p=128)
        w3 = W[pre + "_w3"][l].rearrange("(kc p) n -> p kc n", p=128)
        w2 = W[pre + "_w2"][l].rearrange("(kc p) n -> p kc n", p=128)
        for fb in range(FF // 512):
            v1, k1 = load_w(w1[:, :, fb * 512:(fb + 1) * 512], KC, 512)
            v3, k3 = load_w(w3[:, :, fb * 512:(fb + 1) * 512], KC, 512)
            for m in range(4):
                f = fb * 4 + m
                ba, bb = 2 * (f % 2), 2 * (f % 2) + 1
                proj16(ba, ntok, v1, k1, m * 128, m)
                proj16(bb, ntok, v3, k3, m * 128, m)
                sl = f % 2
                P.add("act", (lambda e, sl=sl, ba=ba: e.activation(out=SIL[sl][:, 0:ntok], in_=PS[ba][:, 0:ntok],
                                                                   func=AF.Silu)),
                      r=psk(ba), w=[("SIL", sl)])
                P.add("dve", (lambda e, sl=sl, bb=bb, f=f: e.tensor_tensor(
                    out=HID[:, f * TT:f * TT + ntok], in0=SIL[sl][:, 0:ntok], in1=PS[bb][:, 0:ntok], op=ALU.mult)),
                    r=[("SIL", sl)] + psk(bb), w=[("HID", f)])
        for dg in range(4):
            banks = [0, 1, 2, 3] if dg % 2 == 0 else [4, 5, 6, 3]
            for fq in range(4):
                v2, k2 = load_w(w2[:, fq * 11:(fq + 1) * 11, dg * 512:(dg + 1) * 512], 11, 512)
                for dd in range(4):
                    for ff in range(11):
                        f = fq * 11 + ff
                        P.add("pe", (lambda e, dd=dd, ff=ff, f=f, v2=v2, bk=banks[dd]: e.matmul(
                            PS[bk][:, 0:ntok], lhsT=v2[:, ff, dd * 128:(dd + 1) * 128],
                            rhs=HID[:, f * TT:f * TT + ntok], start=(f == 0), stop=(f == FC - 1))),
                            r=k2 + [("HID", f)], w=psk(banks[dd]))
            for dd in range(4):
                c = dg * 4 + dd
                P.add("dve", (lambda e, c=c, bk=banks[dd]: e.scalar_tensor_tensor(
                    out=X[:, c, 0:ntok], in0=PS[bk][:, 0:ntok], scalar=0.5, in1=X[:, c, 0:ntok],
                    op0=ALU.mult, op1=ALU.add)),
                    r=psk(banks[dd]) + [("X", c)], w=[("X", c)])

    def add_out_proj(ntok, l, zsrc_fn, zkeys_fn):
        wo = W["w_out"][l].rearrange("(kc p) n -> p kc n", p=128)
        for blk in range(4):
            v, keys = load_w(wo[:, :, blk * 512:(blk + 1) * 512], KC, 512)
            for m in range(4):
                c = blk * 4 + m
                bank = c % 2
                for kc in range(KC):
                    P.add("pe", (lambda e, kc=kc, m=m, v=v, bank=bank: e.matmul(
                        PS[bank][:, 0:ntok], lhsT=v[:, kc, m * 128:(m + 1) * 128], rhs=zsrc_fn(kc),
                        start=(kc == 0), stop=(kc == KC - 1))),
                        r=keys + zkeys_fn(kc), w=psk(bank))
                P.add("dve", (lambda e, c=c, bank=bank: e.tensor_tensor(
                    out=X[:, c, 0:ntok], in0=PS[bank][:, 0:ntok], in1=X[:, c, 0:ntok], op=ALU.add)),
                    r=psk(bank) + [("X", c)], w=[("X", c)])

    def shift_evac(l, bank, rows, g, out_ap, out_keys, segs, ntok):
        mu = CV[l][0:rows, cvcols["mu"] + g:cvcols["mu"] + g + 1]
        omu = CV[l][0:rows, cvcols["omu"] + g:cvcols["omu"] + g + 1]
        ps = PS[bank]
        P.add("act", (lambda e: e.activation(out=out_ap[0:rows, 0:ntok], in_=ps[0:rows, 0:ntok], func=AF.Copy,
                                             scale=omu)),
              r=psk(bank) + [cvk(l, "omu")], w=out_keys)
        se = cfg.get("se", 99)
        for s, (c0, n) in enumerate(segs):
            if se <= 1:
                break
            P.add("dve", (lambda e, c0=c0, n=n: e.scalar_tensor_tensor(
                out=out_ap[0:rows, c0 + 1:c0 + n], in0=ps[0:rows, c0:c0 + n - 1], scalar=mu,
                in1=out_ap[0:rows, c0 + 1:c0 + n], op0=ALU.mult, op1=ALU.add)),
                r=psk(bank) + out_keys + [cvk(l, "mu")], w=out_keys)
            if se <= 2:
                continue
            P.add("dve", (lambda e, c0=c0, s=s: e.scalar_tensor_tensor(
                out=out_ap[0:rows, c0:c0 + 1], in0=SH[l][s][0:rows, g:g + 1], scalar=mu,
                in1=out_ap[0:rows, c0:c0 + 1], op0=ALU.mult, op1=ALU.add)),
                r=[("SH", l, s, g), cvk(l, "mu")] + out_keys, w=out_keys)
            if se <= 3:
                continue
            P.add("dve", (lambda e, c0=c0, n=n, s=s: e.tensor_copy(
                out=SH[l][s][0:rows, g:g + 1], in_=ps[0:rows, c0 + n - 1:c0 + n])),
                r=psk(bank), w=[("SH", l, s, g)])

    def mix(l, ti):
        ntok, segs, C, nq = ti["ntok"], ti["segs"], ti["C"], ti["nq"]
        kind = ti["kind"]
        stop = cfg.get("stop", 99)
        win = W["w_in"][l].rearrange("(kc p) n -> p kc n", p=128)
        nseg = len(segs)
        for s in range(nseg):
            if kind == "p":
                if ti["first"]:
                    P.add("dve", (lambda e, s=s: e.memset(SH[l][s][:], 0.0)),
                          w=[("SH", l, s, g) for g in range(NRG)])
                    P.add("dve", (lambda e, s=s: e.memset(CT[l][s][:], 0.0)), w=[("CT", l, s)])
            else:
                P.add("dve", (lambda e, s=s: e.memset(SH[l][s][:], 0.0)), w=[("SH", l, s, g) for g in range(NRG)])
                parts_key = "SHLOAD"
                vec = st_shift[l, ti["sq"][s]]
                for (sbap, drap) in [
                        (SH[l][s][:, 0:48], vec[0:6144].rearrange("(c p) -> p c", p=128)),
                        (SH[l][s][0:96, 48:49], vec[6144:6240].rearrange("(c p) -> p c", p=96)),
                        (SH[l][s][0:96, 49:50], vec[6240:6336].rearrange("(c p) -> p c", p=96)),
                        (SH[l][s][:, 50:52], vec[6336:6592].rearrange("(c p) -> p c", p=128))]:
                    P.add("sp", (lambda e, sbap=sbap, drap=drap: e.dma_start(out=sbap, in_=drap)),
                          w=[("SH", l, s, g) for g in range(NRG)], dma=True)
                stg = hf32(20480, D)
                P.add("sp", (lambda e, s=s, stg=stg: e.dma_start(out=stg[0:30, :], in_=st_conv[l, ti["sq"][s]])),
                      w=hid_keys(20480, 8192), dma=True)
                for c in range(KC):
                    P.add("pe", (lambda e, c=c, stg=stg: e.transpose(
                        PS[c % 2][:, 0:30], stg[0:30, c * 128:(c + 1) * 128], IDF[0:30, 0:30])),
                        r=hid_keys(20480, 8192) + [("IDF",)], w=psk(c % 2, 0, 30))
                    P.add("dve", (lambda e, c=c, s=s: e.tensor_copy(out=CT[l][s][:, c, :], in_=PS[c % 2][:, 0:30])),
                          r=psk(c % 2, 0, 30), w=[("CT", l, s)])
        if stop <= 0:
            return
        rmsnorm(ntok, lambda c: cvc(l, "mix_norm", c), cvk(l, "mix_norm"))

        if stop <= 1:
            return
        GW = sum(30 + n for (_, n) in segs)
        gbase = []
        o = 0
        for (_, n) in segs:
            gbase.append(o)
            o += 30 + n
        GLU = HID[:, 0:16 * GW].rearrange("p (c w) -> p c w", c=16)

        def glu_keys(c):
            return hid_keys(c * GW * 2, GW * 2)
        DWB0 = 12288

        def dw_ap(c):
            return hf32(DWB0 + c * ntok * 4, ntok)

        def dw_keys(c):
            return hid_keys(DWB0 + c * ntok * 4, ntok * 4)
        need_cs = (kind == "s") or ti["last"]
        CTF0 = 40960
        CTF = hf32(CTF0, 16 * nseg * 30).rearrange("p (c s t) -> p c s t", c=16, s=nseg)
        for s in range(nseg):
            P.add("dve", (lambda e, s=s: e.tensor_copy(out=GLU[:, :, gbase[s]:gbase[s] + 30], in_=CT[l][s][:, :, :])),
                  r=[("CT", l, s)], w=hid_keys(0, 16 * GW * 2))
        for blk in range(4):
            c0w = P_RWKV + blk * 512
            vv, kv = load_w(win[:, :, c0w:c0w + 512], KC, 512)
            vg, kg = load_w(win[:, :, c0w + D:c0w + D + 512], KC, 512)
            for m in range(4):
                c = blk * 4 + m
                ba, bb = 2 * (c % 2), 2 * (c % 2) + 1
                proj16(ba, ntok, vv, kv, m * 128, m)
                proj16(bb, ntok, vg, kg, m * 128, m)
                sl = c % 2
                P.add("act", (lambda e, sl=sl, bb=bb, c=c: e.activation(
                    out=SIL[sl][:, 0:ntok], in_=PS[bb][:, 0:ntok], func=AF.Sigmoid, bias=cvc(l, "b_ci", 16 + c))),
                    r=psk(bb) + [cvk(l, "b_ci")], w=[("SIL", sl)])
                for s, (c0, n) in enumerate(segs):
                    P.add("dve", (lambda e, sl=sl, ba=ba, c=c, c0=c0, n=n, s=s: e.scalar_tensor_tensor(
                        out=GLU[:, c, gbase[s] + 30:gbase[s] + 30 + n], in0=PS[ba][:, c0:c0 + n],
                        scalar=cvc(l, "b_ci", c), in1=SIL[sl][:, c0:c0 + n], op0=ALU.add, op1=ALU.mult)),
                        r=psk(ba) + [("SIL", sl), cvk(l, "b_ci")], w=glu_keys(c))
                    if need_cs:
                        P.add("dve", (lambda e, sl=sl, ba=ba, c=c, c0=c0, n=n, s=s: e.scalar_tensor_tensor(
                            out=CTF[:, c, s, :], in0=PS[ba][:, c0 + n - 30:c0 + n],
                            scalar=cvc(l, "b_ci", c), in1=SIL[sl][:, c0 + n - 30:c0 + n], op0=ALU.add, op1=ALU.mult)),
                            r=psk(ba) + [("SIL", sl), cvk(l, "b_ci")], w=hid_keys(CTF0, 16 * nseg * 120))
        if stop <= 2:
            return
        for s, (c0, n) in enumerate(segs):
            P.add("dve", (lambda e, s=s, n=n: e.tensor_copy(out=CT[l][s][:, :, :],
                                                            in_=GLU[:, :, gbase[s] + n:gbase[s] + n + 30])),
                  r=hid_keys(0, 16 * GW * 2), w=[("CT", l, s)])
        if stop <= 3:
            return
        if need_cs:
            stg = hf32(20480, D)
            for s in range(nseg):
                for c in range(KC):
                    P.add("pe", (lambda e, c=c, s=s: e.transpose(
                        PS[4 + c % 2][0:30, 0:128], CTF[:, c, s, :], IDF[:, :])),
                        r=hid_keys(CTF0, 16 * nseg * 120) + [("IDF",)], w=psk(4 + c % 2, 0, 128))
                    P.add("dve", (lambda e, c=c, stg=stg: e.tensor_copy(out=stg[0:30, c * 128:(c + 1) * 128],
                                                                        in_=PS[4 + c % 2][0:30, 0:128])),
                          r=psk(4 + c % 2, 0, 128), w=hid_keys(20480, 8192))
                dst = o_conv[kind][l, ti["sq"][s]]
                P.add("sp", (lambda e, dst=dst, stg=stg: e.dma_start(out=dst, in_=stg[0:30, :])),
                      r=hid_keys(20480, 8192), dma=True)
        if stop <= 4:
            return
        dgc = [0]
        S1B, S2B = 5, 6
        for ci, c in enumerate(range(KC - 1, -1, -1)):
            cbanks = [(2 * (ci % 2)) + s for s in range(nseg)]
            for k in range(CONV_K):
                slot = dgc[0] % 8
                dgc[0] += 1
                eng = "act" if k % 2 else "dve"
                if eng == "act":
                    P.add("act", (lambda e, slot=slot, k=k, c=c: e.activation(
                        out=DG[:, slot, :], in_=IDB[:], func=AF.Copy, scale=cvc(l, "conv_w", k * 16 + c))),
                        r=[("IDB",), cvk(l, "conv_w")], w=[("DG", slot)])
                else:
                    P.add("dve", (lambda e, slot=slot, k=k, c=c: e.tensor_scalar(
                        out=DG[:, slot, :], in0=IDB[:], scalar1=cvc(l, "conv_w", k * 16 + c), scalar2=None,
                        op0=ALU.mult)),
                        r=[("IDB",), cvk(l, "conv_w")], w=[("DG", slot)])
                for s, (c0, n) in enumerate(segs):
                    P.add("pe", (lambda e, slot=slot, k=k, c=c, s=s, c0=c0, n=n, bk=cbanks[s]: e.matmul(
                        PS[bk][:, 0:n], lhsT=DG[:, slot, :], rhs=GLU[:, c, gbase[s] + k:gbase[s] + k + n],
                        start=(k == 0), stop=(k == CONV_K - 1))),
                        r=[("DG", slot)] + glu_keys(c), w=psk(cbanks[s], 0, n))
            for s, (c0, n) in enumerate(segs):
                bk = cbanks[s]
                P.add("act", (lambda e, c=c, c0=c0, n=n, bk=bk: e.activation(
                    out=dw_ap(c)[:, c0:c0 + n], in_=PS[bk][:, 0:n], func=AF.Identity, bias=cvc(l, "conv_b", c))),
                    r=psk(bk, 0, n) + [cvk(l, "conv_b")], w=dw_keys(c))
                P.add("act", (lambda e, c=c, c0=c0, n=n, bk=bk, ci=ci: e.activation(
                    out=SQ[1][:, c0:c0 + n], in_=PS[bk][:, 0:n], func=AF.Square, bias=cvc(l, "conv_b", c))),
                    r=psk(bk, 0, n) + [cvk(l, "conv_b")], w=[("SQ", 1)])
            P.add("dve", (lambda e, c=c: e.tensor_copy(out=SQ[0][:, 0:ntok], in_=dw_ap(c)[:, 0:ntok])),
                  r=dw_keys(c), w=[("SQ", 0)])
            P.add("pe", (lambda e, ci=ci: e.matmul(PS[S1B][:, 0:ntok], lhsT=ONESB[:], rhs=SQ[0][:, 0:ntok],
                                                   start=(ci == 0), stop=(ci == KC - 1))),
                  r=[("SQ", 0), ("ONESB",)], w=psk(S1B))
            P.add("pe", (lambda e, ci=ci: e.matmul(PS[S2B][:, 0:ntok], lhsT=ONESB[:], rhs=SQ[1][:, 0:ntok],
                                                   start=(ci == 0), stop=(ci == KC - 1))),
                  r=[("SQ", 1), ("ONESB",)], w=psk(S2B))
        if stop <= 5:
            return
        P.add("act", (lambda e: e.activation(out=SIL[0][:, 0:ntok], in_=PS[S1B][:, 0:ntok], func=AF.Copy,
                                             scale=1.0 / D)), r=psk(S1B), w=[("SIL", 0)])
        P.add("dve", (lambda e: e.tensor_tensor(out=TMPF[0][:, 0:ntok], in0=SIL[0][:, 0:ntok], in1=SIL[0][:, 0:ntok],
                                                op=ALU.mult)), r=[("SIL", 0)], w=[("TMPF", 0)])
        P.add("dve", (lambda e: e.scalar_tensor_tensor(out=SIL[1][:, 0:ntok], in0=PS[S2B][:, 0:ntok],
                                                       scalar=1.0 / D, in1=TMPF[0][:, 0:ntok],
                                                       op0=ALU.mult, op1=ALU.subtract)),
              r=psk(S2B) + [("TMPF", 0)], w=[("SIL", 1)])
        P.add("act", (lambda e: e.activation(out=SIL[1][:, 0:ntok], in_=SIL[1][:, 0:ntok], func=AF.Sqrt,
                                             bias=EPS_LN)), r=[("SIL", 1)], w=[("SIL", 1)])
        P.add("dve", (lambda e: e.reciprocal(out=SIL[1][:, 0:ntok], in_=SIL[1][:, 0:ntok])),
              r=[("SIL", 1)], w=[("SIL", 1)])
        HC0 = 0

        def hc_ap(c):
            return hbf(HC0 + c * ntok * 2, ntok)

        def hc_keys(c):
            return hid_keys(HC0 + c * ntok * 2, ntok * 2)
        for c in range(KC):
            t = TMPF[c % 2]
            P.add("dve", (lambda e, c=c, t=t: e.tensor_tensor(out=t[:, 0:ntok], in0=dw_ap(c)[:, 0:ntok],
                                                              in1=SIL[0][:, 0:ntok], op=ALU.subtract)),
                  r=dw_keys(c) + [("SIL", 0)], w=[("TMPF", c % 2)])
            P.add("dve", (lambda e, c=c, t=t: e.tensor_tensor(out=t[:, 0:ntok], in0=t[:, 0:ntok],
                                                              in1=SIL[1][:, 0:ntok], op=ALU.mult)),
                  r=[("TMPF", c % 2), ("SIL", 1)], w=[("TMPF", c % 2)])
            P.add("act", (lambda e, c=c, t=t: e.activation(out=hc_ap(c), in_=t[:, 0:ntok], func=AF.Silu,
                                                           scale=cvc(l, "ln_g", c), bias=cvc(l, "ln_b", c))),
                  r=[("TMPF", c % 2), cvk(l, "ln_g"), cvk(l, "ln_b")], w=hc_keys(c))
        if stop <= 6:
            return
        ZC0 = 16384

        def zc_ap(c):
            return hbf(ZC0 + c * ntok * 2, ntok)

        def zc_keys(c):
            return hid_keys(ZC0 + c * ntok * 2, ntok * 2)
        woc = W["w_o_c"][l].rearrange("(kc p) n -> p kc n", p=128)
        for blk in range(4):
            vo, ko = load_w(woc[:, :, blk * 512:(blk + 1) * 512], KC, 512)
            cg = P_RWKV + 2 * D + D + blk * 512
            vg, kg = load_w(win[:, :, cg:cg + 512], KC, 512)
            for m in range(4):
                c = blk * 4 + m
                ba, bb = 2 * (c % 2), 2 * (c % 2) + 1
                for kc in range(KC):
                    P.add("pe", (lambda e, kc=kc, m=m, vo=vo, ba=ba: e.matmul(
                        PS[ba][:, 0:ntok], lhsT=vo[:, kc, m * 128:(m + 1) * 128], rhs=hc_ap(kc),
                        start=(kc == 0), stop=(kc == KC - 1))),
                        r=ko + hc_keys(kc), w=psk(ba))
                proj16(bb, ntok, vg, kg, m * 128, m)
                sl = c % 2
                P.add("act", (lambda e, sl=sl, bb=bb, c=c: e.activation(
                    out=SIL[sl][:, 0:ntok], in_=PS[bb][:, 0:ntok], func=AF.Sigmoid, bias=cvc(l, "b_gate", 16 + c))),
                    r=psk(bb) + [cvk(l, "b_gate")], w=[("SIL", sl)])
                P.add("dve", (lambda e, sl=sl, ba=ba, c=c: e.scalar_tensor_tensor(
                    out=zc_ap(c), in0=PS[ba][:, 0:ntok], scalar=cvc(l, "b_oc", c), in1=SIL[sl][:, 0:ntok],
                    op0=ALU.add, op1=ALU.mult)),
                    r=psk(ba) + [("SIL", sl), cvk(l, "b_oc")], w=zc_keys(c))
        add_out_proj(ntok, l, zc_ap, zc_keys)

        if stop <= 7:
            return
        YA0 = 0

        def ya_ap(c):
            return hbf(YA0 + c * ntok * 2, ntok)

        def ya_keys(c):
            return hid_keys(YA0 + c * ntok * 2, ntok * 2)
        B0 = 16384
        offs = {}
        for i_, nm in enumerate(["RM", "KM", "VM", "AL", "SG", "LC", "E0", "E1", "KK", "T"]):
            offs[nm] = B0 + i_ * 2048
        offs["AR"] = B0 + 20480
        offs["BT"] = offs["AR"] + 2048
        offs["KT"] = offs["BT"] + 1024
        offs["VB"] = offs["KT"] + 1024
        offs["BHF"] = offs["VB"] + 1024
        offs["KHF"] = offs["BHF"] + 1024

        def f32t(nm):
            return hf32(offs[nm], TT), hid_keys(offs[nm], 2048)

        def bft(nm):
            return hbf(offs[nm], TT), hid_keys(offs[nm], 1024)
        RM, kRM = f32t("RM")
        KM, kKM = f32t("KM")
        VM, kVM = f32t("VM")
        AL, kAL = f32t("AL")
        SG, kSG = f32t("SG")
        LC, kLC = f32t("LC")
        E0, kE0 = f32t("E0")
        E1, kE1 = f32t("E1")
        KK, kKK = f32t("KK")
        T_, kT = f32t("T")
        AR = hbf(offs["AR"], 2 * TT).rearrange("p (a t) -> p a t", a=2)
        kAR = hid_keys(offs["AR"], 2048)
        BT, kBT = bft("BT")
        KT, kKT = bft("KT")
        VB, kVB = bft("VB")
        BHF, kBHF = bft("BHF")
        KHF, kKHF = bft("KHF")

        vs, ks = load_w(win[:, :, 6144:6272], KC, 128)
        vsa, ksa = load_w(win[:, :, 6240:6368], KC, 128)
        vgx, kgx = load_w(win[:, :, 6336:6592], KC, 256)
        sub = cfg.get("sub", 99)
        if sub <= 1:
            return
        proj16(0, ntok, vs, ks, 0, 0)
        if sub <= 2:
            return
        shift_evac(l, 0, 128, 48, T_, kT, segs, ntok)
        if sub <= 3:
            return
        P.add("act", (lambda e: e.activation(out=LORA[:, 0, 0:ntok], in_=T_[:, 0:ntok], func=AF.Tanh)),
              r=kT, w=[("LORA", 0)])
        proj16(1, ntok, vsa, ksa, 0, 0)
        shift_evac(l, 1, 128, 49, E0, kE0, segs, ntok)
        P.add("dve", (lambda e: e.tensor_copy(out=LORA[:, 1, 0:ntok], in_=E0[:, 0:ntok])),
              r=kE0, w=[("LORA", 1)])
        for j in range(2):
            tt = E1 if j == 0 else KK
            kt = kE1 if j == 0 else kKK
            proj16(2 + j, ntok, vgx, kgx, j * 128, 0)
            shift_evac(l, 2 + j, 128, 50 + j, tt, kt, segs, ntok)
            P.add("act", (lambda e, j=j, tt=tt: e.activation(out=LORA[:, 2 + j, 0:ntok], in_=tt[:, 0:ntok],
                                                             func=AF.Sigmoid)),
                  r=kt, w=[("LORA", 2 + j)])

        if stop <= 8:
            return
        wup = W["w_up"][l]
        aup = W["a_up"][l]
        gup = W["g_up"][l].rearrange("(kc p) n -> p kc n", p=128)
        for c in range(KC):
            wv, kw = load_w_multi([(win[:, :, j * D + c * 128:j * D + (c + 1) * 128], j * 128) for j in range(3)],
                                  KC, 384)
            LU = LUB[c % 2]
            kl = [("LUB", c % 2)]
            P.add("pool", (lambda e, LU=LU, c=c: e.dma_start(out=LU[0:96, 0:128], in_=wup[:, c * 128:(c + 1) * 128])),
                  w=kl, dma=True)
            P.add("pool", (lambda e, LU=LU, c=c: e.dma_start(out=LU[0:96, 128:256], in_=aup[:, c * 128:(c + 1) * 128])),
                  w=kl, dma=True)
            P.add("pool", (lambda e, LU=LU, c=c: e.dma_start(
                out=LU[:, 256:512].rearrange("p (k n) -> p k n", k=2), in_=gup[:, :, c * 128:(c + 1) * 128])),
                w=kl, dma=True)
            for j, (dst, kd) in enumerate([(RM, kRM), (KM, kKM), (VM, kVM)]):
                proj16(j, ntok, wv, kw, j * 128, 0)
                shift_evac(l, j, 128, j * 16 + c, dst, kd, segs, ntok)
            P.add("pe", (lambda e, LU=LU: e.matmul(PS[3][:, 0:ntok], lhsT=LU[:, 0:128], rhs=LORA[:, 0, 0:ntok],
                                                   start=True, stop=True)),
                  r=kl + [("LORA", 0)], w=psk(3))
            P.add("pe", (lambda e, LU=LU: e.matmul(PS[4][:, 0:ntok], lhsT=LU[:, 128:256], rhs=LORA[:, 1, 0:ntok],
                                                   start=True, stop=True)),
                  r=kl + [("LORA", 1)], w=psk(4))
            for j in range(2):
                P.add("pe", (lambda e, LU=LU, j=j: e.matmul(
                    PS[5][:, 0:ntok], lhsT=LU[:, 256 + j * 128:256 + (j + 1) * 128], rhs=LORA[:, 2 + j, 0:ntok],
                    start=(j == 0), stop=(j == 1))),
                    r=kl + [("LORA", 2 + j)], w=psk(5))
            P.add("act", (lambda e, c=c: e.activation(out=SG[:, 0:ntok], in_=PS[3][:, 0:ntok], func=AF.Sigmoid,
                                                      bias=cvc(l, "w0", c))),
                  r=psk(3) + [cvk(l, "w0")], w=kSG)
            P.add("act", (lambda e, c=c: e.activation(out=AL[:, 0:ntok], in_=PS[4][:, 0:ntok], func=AF.Sigmoid,
                                                      bias=cvc(l, "a0", c))),
                  r=psk(4) + [cvk(l, "a0")], w=kAL)
            for q in range(nq):
                P.add("dve", (lambda e, q=q: e.tensor_tensor_scan(
                    out=LC[:, q * C:(q + 1) * C], data0=ONESF[:, 0:C], data1=SG[:, q * C:(q + 1) * C], initial=0.0,
                    op0=ALU.mult, op1=ALU.add)),
                    r=kSG + [("ONESF",)], w=kLC)
            P.add("dve", (lambda e: e.tensor_tensor(out=SG[:, 0:ntok], in0=LC[:, 0:ntok], in1=SG[:, 0:ntok],
                                                    op=ALU.subtract)), r=kLC + kSG, w=kSG)
            LCend = LC[:, 0:ntok].rearrange("p (q t) -> p q t", t=C)[:, :, C - 1]
            P.add("act", (lambda e, LCend=LCend: e.activation(out=WCL[:, 0:nq], in_=LCend, func=AF.Exp, scale=-CDEC)),
                  r=kLC, w=[("WCL",)])
            P.add("dve", (lambda e, c=c: e.tensor_scalar(out=KK[:, 0:ntok], in0=KM[:, 0:ntok], scalar1=cvc(l, "k_k", c),
                                                         scalar2=None, op0=ALU.mult)),
                  r=kKM + [cvk(l, "k_k")], w=kKK)
            P.add("act", (lambda e: e.activation(out=SQ[0][:, 0:ntok], in_=KK[:, 0:ntok], func=AF.Square)),
                  r=kKK, w=[("SQ", 0)])
            P.add("pe", (lambda e: e.matmul(PS[6][:, 0:ntok], lhsT=BLKONES[:], rhs=SQ[0][:, 0:ntok], start=True,
                                            stop=True)), r=[("SQ", 0), ("BLKONES",)], w=psk(6))
            P.add("dve", (lambda e: e.tensor_scalar(out=T_[:, 0:ntok], in0=PS[6][:, 0:ntok], scalar1=1e-24, scalar2=None,
                                                    op0=ALU.max)), r=psk(6), w=kT)
            P.add("act", (lambda e: e.activation(out=T_[:, 0:ntok], in_=T_[:, 0:ntok], func=AF.Sqrt)), r=kT, w=kT)
            P.add("dve", (lambda e: e.reciprocal(out=T_[:, 0:ntok], in_=T_[:, 0:ntok])), r=kT, w=kT)
            P.add("dve", (lambda e: e.tensor_tensor(out=KK[:, 0:ntok], in0=KK[:, 0:ntok], in1=T_[:, 0:ntok],
                                                    op=ALU.mult)), r=kKK + kT, w=kKK)
            P.add("dve", (lambda e, c=c: e.tensor_scalar(out=T_[:, 0:ntok], in0=AL[:, 0:ntok], scalar1=-1.0,
                                                         scalar2=cvc(l, "k_a", c), op0=ALU.add, op1=ALU.mult)),
                  r=kAL + [cvk(l, "k_a")], w=kT)
            P.add("dve", (lambda e: e.scalar_tensor_tensor(out=KM[:, 0:ntok], in0=T_[:, 0:ntok], scalar=1.0,
                                                           in1=KM[:, 0:ntok], op0=ALU.add, op1=ALU.mult)),
                  r=kT + kKM, w=kKM)
            P.add("dve", (lambda e: e.tensor_tensor(out=AL[:, 0:ntok], in0=KK[:, 0:ntok], in1=AL[:, 0:ntok],
                                                    op=ALU.mult)), r=kKK + kAL, w=kAL)
            P.add("dve", (lambda e, c=c: e.scalar_tensor_tensor(out=SQ[1][:, 0:ntok], in0=RM[:, 0:ntok],
                                                                scalar=cvc(l, "r_k", c), in1=KM[:, 0:ntok],
                                                                op0=ALU.mult, op1=ALU.mult)),
                  r=kRM + kKM + [cvk(l, "r_k")], w=[("SQ", 1)])
            P.add("pe", (lambda e: e.matmul(PS[6][:, 0:ntok], lhsT=BLKONES[:], rhs=SQ[1][:, 0:ntok], start=True,
                                            stop=True)), r=[("SQ", 1), ("BLKONES",)], w=psk(6))
            P.add("act", (lambda e: e.activation(out=VB[:, 0:ntok], in_=VM[:, 0:ntok], func=AF.Copy)), r=kVM, w=kVB)
            P.add("dve", (lambda e: e.tensor_tensor(out=VM[:, 0:ntok], in0=PS[6][:, 0:ntok], in1=VM[:, 0:ntok],
                                                    op=ALU.mult)), r=psk(6) + kVM + kVB, w=kVM)
            P.add("act", (lambda e: e.activation(out=E0[:, 0:ntok], in_=LC[:, 0:ntok], func=AF.Exp, scale=-CDEC)),
                  r=kLC, w=kE0)
            P.add("dve", (lambda e: e.tensor_tensor(out=AR[:, 1, 0:ntok], in0=RM[:, 0:ntok], in1=E0[:, 0:ntok],
                                                    op=ALU.mult)), r=kRM + kE0, w=kAR)
            P.add("act", (lambda e: e.activation(out=E1[:, 0:ntok], in_=LC[:, 0:ntok], func=AF.Exp, scale=CDEC)),
                  r=kLC, w=kE1)
            P.add("dve", (lambda e: e.tensor_tensor(out=KT[:, 0:ntok], in0=KM[:, 0:ntok], in1=E1[:, 0:ntok],
                                                    op=ALU.mult)), r=kKM + kE1, w=kKT)
            P.add("dve", (lambda e: e.tensor_tensor(out=BT[:, 0:ntok], in0=AL[:, 0:ntok], in1=E1[:, 0:ntok],
                                                    op=ALU.mult)), r=kAL + kE1, w=kBT)
            P.add("act", (lambda e: e.activation(out=E0[:, 0:ntok], in_=SG[:, 0:ntok], func=AF.Exp, scale=-CDEC)),
                  r=kSG + kAR, w=kE0)
            P.add("dve", (lambda e: e.scalar_tensor_tensor(out=AR[:, 0, 0:ntok], in0=KK[:, 0:ntok], scalar=-1.0,
                                                           in1=E0[:, 0:ntok], op0=ALU.mult, op1=ALU.mult)),
                  r=kKK + kE0, w=kAR)
            for q in range(nq):
                P.add("act", (lambda e, q=q: e.activation(out=BHF[:, q * C:(q + 1) * C], in_=BT[:, q * C:(q + 1) * C],
                                                          func=AF.Copy, scale=WCL[:, q:q + 1])),
                      r=kBT + [("WCL",)], w=kBHF)
                P.add("act", (lambda e, q=q: e.activation(out=KHF[:, q * C:(q + 1) * C], in_=KT[:, q * C:(q + 1) * C],
                                                          func=AF.Copy, scale=WCL[:, q:q + 1])),
                      r=kKT + [("WCL",)], w=kKHF)
            for q in range(nq):
                for j, (src, ksrc) in enumerate([(VB, kVB), (BHF, kBHF), (KHF, kKHF)]):
                    P.add("pe", (lambda e, q=q, j=j, src=src: e.transpose(
                        PSB[0:C, j * 128:(j + 1) * 128], src[:, q * C:(q + 1) * C], IDB[:, :])),
                        r=ksrc + [("IDB",)], w=[("PSB",)])
                P.add("dve", (lambda e, q=q: e.tensor_copy(
                    out=TOK[0:C, q, :, :], in_=PSB[0:C, 0:384].rearrange("p (a b) -> p a b", a=3))),
                    r=[("PSB",)], w=[("TOK", q)])
            if stop > 9:
                wkv_pair(l, c, C, nq, ti, AR, kAR, BT, kBT, KT, kKT, VM, kVM, ya_ap, ya_keys)
        if stop <= 10:
            return
        ZA0 = 16384

        def za_ap(c):
            return hbf(ZA0 + c * ntok * 2, ntok)

        def za_keys(c):
            return hid_keys(ZA0 + c * ntok * 2, ntok * 2)
        woa = W["w_o_a"][l].rearrange("(kc p) n -> p kc n", p=128)
        for blk in range(4):
            vo, ko = load_w(woa[:, :, blk * 512:(blk + 1) * 512], KC, 512)
            cg = P_RWKV + 2 * D + blk * 512
            vg, kg = load_w(win[:, :, cg:cg + 512], KC, 512)
            for m in range(4):
                c = blk * 4 + m
                ba, bb = 2 * (c % 2), 2 * (c % 2) + 1
                for kc in range(KC):
                    P.add("pe", (lambda e, kc=kc, m=m, vo=vo, ba=ba: e.matmul(
                        PS[ba][:, 0:ntok], lhsT=vo[:, kc, m * 128:(m + 1) * 128], rhs=ya_ap(kc),
                        start=(kc == 0), stop=(kc == KC - 1))),
                        r=ko + ya_keys(kc), w=psk(ba))
                proj16(bb, ntok, vg, kg, m * 128, m)
                sl = c % 2
                P.add("act", (lambda e, sl=sl, bb=bb, c=c: e.activation(
                    out=SIL[sl][:, 0:ntok], in_=PS[bb][:, 0:ntok], func=AF.Sigmoid, bias=cvc(l, "b_gate", c))),
                    r=psk(bb) + [cvk(l, "b_gate")], w=[("SIL", sl)])
                P.add("dve", (lambda e, sl=sl, ba=ba, c=c: e.tensor_tensor(
                    out=za_ap(c), in0=PS[ba][:, 0:ntok], in1=SIL[sl][:, 0:ntok], op=ALU.mult)),
                    r=psk(ba) + [("SIL", sl)], w=za_keys(c))
        add_out_proj(ntok, l, za_ap, za_keys)
        if kind == "s" or ti["last"]:
            for s in range(nseg):
                vec = o_shift[kind][l, ti["sq"][s]]
                for (sbap, drap) in [
                        (SH[l][s][:, 0:48], vec[0:6144].rearrange("(c p) -> p c", p=128)),
                        (SH[l][s][0:96, 48:49], vec[6144:6240].rearrange("(c p) -> p c", p=96)),
                        (SH[l][s][0:96, 49:50], vec[6240:6336].rearrange("(c p) -> p c", p=96)),
                        (SH[l][s][:, 50:52], vec[6336:6592].rearrange("(c p) -> p c", p=128))]:
                    P.add("sp", (lambda e, sbap=sbap, drap=drap: e.dma_start(out=drap, in_=sbap)),
                          r=[("SH", l, s, g) for g in range(NRG)], dma=True)

    def wkv_pair(l, c, C, nq, ti, AR, kAR, BT, kBT, KT, kKT, VM, kVM, ya_ap, ya_keys):
        kind = ti["kind"]
        nit = int(np.log2(C)) - 1
        kST = ("ST", l, c)
        kSBD = ("SBD", c)
        CB = [0, 1, 3, 4]
        groups = [list(range(g0, min(g0 + 2, nq))) for g0 in range(0, nq, 2)]

        def state_init(q):
            fresh = (kind == "p" and ti["first"] and q == 0)
            if fresh:
                P.add("dve", (lambda e: e.memset(ST[l][:, c, :], 0.0)), w=[kST])
                P.add("dve", (lambda e: e.memset(SBD[:, c, :], 0.0)), w=[kSBD])
            elif kind == "s":
                b = ti["sq"][q]
                P.add("sp", (lambda e, b=b: e.dma_start(
                    out=STG[:, :], in_=st_wkv[l, b, 2 * c:2 * c + 2].rearrange("h i j -> (h i) j"))),
                    w=[("STG",)], dma=True)
                for h in range(2):
                    P.add("dve", (lambda e, h=h: e.tensor_copy(out=SBLK[h * 64:(h + 1) * 64, h * 64:(h + 1) * 64],
                                                               in_=STG[h * 64:(h + 1) * 64, :])),
                          r=[("STG",)], w=[("SBLK",)])
                P.add("pe", (lambda e: e.transpose(PS[2][:, 0:128], SBLK[:, :], IDF[:, :])),
                      r=[("SBLK",), ("IDF",)], w=psk(2))
                for h in range(2):
                    P.add("dve", (lambda e, h=h: e.tensor_copy(out=ST[l][h * 64:(h + 1) * 64, c, :],
                                                               in_=PS[2][h * 64:(h + 1) * 64, h * 64:(h + 1) * 64])),
                          r=psk(2), w=[kST])
                P.add("dve", (lambda e: e.tensor_copy(out=SBD[:, c, :], in_=PS[2][:, 0:128])),
                      r=psk(2), w=[kSBD])
            elif q == 0:
                P.add("dve", (lambda e: e.memset(SBD[:, c, :], 0.0)), w=[kSBD])
                for h in range(2):
                    P.add("act", (lambda e, h=h: e.activation(out=SBD[h * 64:(h + 1) * 64, c, h * 64:(h + 1) * 64],
                                                              in_=ST[l][h * 64:(h + 1) * 64, c, :], func=AF.Copy)),
                          r=[kST], w=[kSBD])

        def state_out(q):
            b = ti["sq"][q] if kind == "s" else ti["sq"][0]
            for h in range(2):
                hs = slice(h * 64, (h + 1) * 64)
                P.add("dve", (lambda e, h=h, hs=hs: e.tensor_copy(out=SBLK[hs, h * 64:(h + 1) * 64], in_=ST[l][hs, c, :])),
                      r=[kST], w=[("SBLK",)])
            P.add("pe", (lambda e: e.transpose(PS[2][:, 0:128], SBLK[:, :], IDF[:, :])), r=[("SBLK",), ("IDF",)],
                  w=psk(2))
            for h in range(2):
                P.add("dve", (lambda e, h=h: e.tensor_copy(out=OSTG[h * 64:(h + 1) * 64, :],
                                                           in_=PS[2][h * 64:(h + 1) * 64, h * 64:(h + 1) * 64])),
                      r=psk(2), w=[("OSTG",)])
            P.add("sp", (lambda e, b=b: e.dma_start(
                out=o_wkv[kind][l, b, 2 * c:2 * c + 2].rearrange("h i j -> (h i) j"), in_=OSTG[:, :])),
                r=[("OSTG",)], dma=True)

        for grp in groups:
            chains = [(q, h) for q in grp for h in range(2)]
            for i, (q, h) in enumerate(chains):
                cs = slice(q * C, (q + 1) * C)
                hs = slice(h * 64, (h + 1) * 64)
                bk = CB[i]
                psA = PS[bk][0:C, 0:2 * C].rearrange("p (a t) -> p a t", a=2)
                psB = PS[bk][0:C, 256:256 + 2 * C].rearrange("p (a t) -> p a t", a=2)
                P.add("pe", (lambda e, hs=hs, cs=cs, psA=psA: e.matmul(psA, lhsT=BT[hs, cs], rhs=AR[hs, :, cs], start=True,
                                                                       stop=True)), r=kBT + kAR, w=psk(bk))
                P.add("pe", (lambda e, hs=hs, cs=cs, psB=psB: e.matmul(psB, lhsT=KT[hs, cs], rhs=AR[hs, :, cs], start=True,
                                                                       stop=True)), r=kKT + kAR, w=psk(bk))
            for i, (q, h) in enumerate(chains):
                bk = CB[i]
                psA = PS[bk][0:C, 0:2 * C].rearrange("p (a t) -> p a t", a=2)
                psB = PS[bk][0:C, 256:256 + 2 * C].rearrange("p (a t) -> p a t", a=2)
                P.add("dve", (lambda e, i=i, psA=psA: e.tensor_tensor(out=MA[i][0:C, :, 0:C], in0=psA,
                                                                      in1=MASK2[0:C, :, 0:C], op=ALU.mult)),
                      r=psk(bk) + [("MASK2",)], w=[("MA", i)])
                P.add("dve", (lambda e, i=i, psB=psB: e.tensor_tensor(out=MB[i][0:C, :, 0:C], in0=psB,
                                                                      in1=MASK2[0:C, :, 0:C], op=ALU.mult)),
                      r=psk(bk) + [("MASK2",)], w=[("MB", i)])
            for i, (q, h) in enumerate(chains):
                cs = slice(q * C, (q + 1) * C)
                hs = slice(h * 64, (h + 1) * 64)
                bk = CB[i]
                P.add("pe", (lambda e, hs=hs, cs=cs, bk=bk: e.matmul(PS[bk][0:C, 0:C], lhsT=AR[hs, 0, cs], rhs=BT[hs, cs],
                                                                     start=True, stop=True)), r=kBT + kAR, w=psk(bk))
            for i, (q, h) in enumerate(chains):
                bk = CB[i]
                P.add("dve", (lambda e, i=i, bk=bk: e.tensor_tensor(out=MC[i][0:C, 0:C], in0=PS[bk][0:C, 0:C],
                                                                    in1=MASKL[0:C, 0:C], op=ALU.mult)),
                      r=psk(bk) + [("MASKL",)], w=[("MC", i)])
                P.add("dve", (lambda e, i=i: e.tensor_tensor(out=TT_[i][0][0:C, 0:C], in0=MA[i][0:C, 0, 0:C],
                                                             in1=IDB[0:C, 0:C], op=ALU.add)),
                      r=[("MA", i), ("IDB",)], w=[("TI", i, 0)])
            Pm = [MA[i][0:C, 0, 0:C] for i in range(len(chains))]
            PTm = [MC[i][0:C, 0:C] for i in range(len(chains))]
            kPm = [[("MA", i), ("MC", i)] for i in range(len(chains))]
            cur = [0] * len(chains)
            for s in range(nit + 1):
                do_sq = s < nit
                do_t = s >= 1
                for i in range(len(chains)):
                    bk = CB[i]
                    if do_sq:
                        if s < nit - 1:
                            P.add("pe", (lambda e, i=i, bk=bk, a=PTm[i], b_=Pm[i]: e.matmul(
                                PS[bk][0:C, 256:256 + C], lhsT=a, rhs=b_, start=True, stop=True)),
                                r=kPm[i], w=psk(bk))
                        P.add("pe", (lambda e, i=i, bk=bk, a=Pm[i], b_=PTm[i]: e.matmul(
                            PS[bk][0:C, 256 + C:256 + 2 * C], lhsT=a, rhs=b_, start=True, stop=True)),
                            r=kPm[i], w=psk(bk))
                    if do_t:
                        Tc = TT_[i][cur[i]][0:C, 0:C]
                        P.add("pe", (lambda e, i=i, bk=bk, a=PTm[i], Tc=Tc: e.matmul(
                            PS[bk][0:C, 128:128 + C], lhsT=a, rhs=Tc, start=True, stop=True)),
                            r=kPm[i] + [("TI", i, cur[i])], w=psk(bk))
                newP = []
                for i in range(len(chains)):
                    bk = CB[i]
                    if do_t:
                        Tc = TT_[i][cur[i]][0:C, 0:C]
                        P.add("dve", (lambda e, i=i, bk=bk, Tc=Tc, nc_=1 - cur[i]: e.tensor_tensor(
                            out=TT_[i][nc_][0:C, 0:C], in0=PS[bk][0:C, 128:128 + C], in1=Tc, op=ALU.add)),
                            r=psk(bk) + [("TI", i, cur[i])], w=[("TI", i, 1 - cur[i])])
                        cur[i] = 1 - cur[i]
                    if do_sq:
                        nx = s % 2
                        P.add("dve", (lambda e, i=i, bk=bk, nx=nx: e.tensor_copy(
                            out=PP[i][nx][0:C, 0:2 * C], in_=PS[bk][0:C, 256:256 + 2 * C])),
                            r=psk(bk), w=[("PP", i, nx)])
                        newP.append((PP[i][nx][0:C, 0:C], PP[i][nx][0:C, C:2 * C], [("PP", i, nx)]))
                if do_sq:
                    for i in range(len(chains)):
                        Pm[i], PTm[i], kPm[i] = newP[i]
            for gi, q in enumerate(grp):
                cs = slice(q * C, (q + 1) * C)
                state_init(q)
                VTq = TOK[0:C, q, 0, :]
                BHq = TOK[0:C, q, 1, :]
                KHq = TOK[0:C, q, 2, :]
                kTOK = [("TOK", q)]
                ci = [2 * gi, 2 * gi + 1]
                P.add("pe", (lambda e, cs=cs: e.matmul(PS[2][0:C, 0:128], lhsT=AR[:, 0, cs], rhs=SBD[:, c, :], start=True,
                                                       stop=False)), r=kAR + [kSBD], w=psk(2))
                for h in range(2):
                    hc = slice(h * 64, (h + 1) * 64)
                    P.add("pe", (lambda e, h=h, hc=hc, i=ci[h], VTq=VTq: e.matmul(
                        PS[2][0:C, hc], lhsT=MB[i][0:C, 0, 0:C], rhs=VTq[:, hc], start=False, stop=(h == 1))),
                        r=[("MB", ci[h])] + kTOK, w=psk(2))
                P.add("dve", (lambda e: e.tensor_copy(out=XT[0:C, :], in_=PS[2][0:C, 0:128])), r=psk(2), w=[("XT",)])
                for h in range(2):
                    hc = slice(h * 64, (h + 1) * 64)
                    Tf = TT_[ci[h]][cur[ci[h]]][0:C, 0:C]
                    P.add("pe", (lambda e, h=h, hc=hc, Tf=Tf: e.matmul(PS[2][0:C, 128 + h * 64:128 + (h + 1) * 64], lhsT=Tf,
                                                                       rhs=XT[0:C, hc], start=True, stop=True)),
                          r=[("TI", ci[h], cur[ci[h]]), ("XT",)], w=psk(2))
                P.add("dve", (lambda e: e.tensor_copy(out=UT[0:C, :], in_=PS[2][0:C, 128:256])), r=psk(2), w=[("UT",)])
                P.add("pe", (lambda e, cs=cs: e.matmul(PS[6][0:C, 0:128], lhsT=AR[:, 1, cs], rhs=SBD[:, c, :], start=True,
                                                       stop=False)), r=kAR + [kSBD], w=psk(6))
                for h in range(2):
                    hc = slice(h * 64, (h + 1) * 64)
                    P.add("pe", (lambda e, h=h, hc=hc, i=ci[h]: e.matmul(PS[6][0:C, hc], lhsT=MA[i][0:C, 1, 0:C],
                                                                         rhs=UT[0:C, hc], start=False, stop=False)),
                          r=[("MA", ci[h]), ("UT",)], w=psk(6))
                    P.add("pe", (lambda e, h=h, hc=hc, i=ci[h], VTq=VTq: e.matmul(
                        PS[6][0:C, hc], lhsT=MB[i][0:C, 1, 0:C], rhs=VTq[:, hc], start=False, stop=(h == 1))),
                        r=[("MB", ci[h])] + kTOK, w=psk(6))
                P.add("pe", (lambda e, BHq=BHq: e.matmul(PS[6][:, 128:256], lhsT=BHq, rhs=UT[0:C, :], start=True,
                                                         stop=False)), r=kTOK + [("UT",)], w=psk(6))
                P.add("pe", (lambda e, KHq=KHq, VTq=VTq: e.matmul(PS[6][:, 128:256], lhsT=KHq, rhs=VTq, start=False,
                                                                  stop=True)), r=kTOK, w=psk(6))
                for h in range(2):
                    hs = slice(h * 64, (h + 1) * 64)
                    P.add("dve", (lambda e, h=h, hs=hs, q=q: e.scalar_tensor_tensor(
                        out=ST[l][hs, c, :], in0=ST[l][hs, c, :], scalar=WCL[hs, q:q + 1],
                        in1=PS[6][hs, 128 + h * 64:128 + (h + 1) * 64], op0=ALU.mult, op1=ALU.add)),
                        r=[kST, ("WCL",)] + psk(6) + [kSBD], w=[kST])
                    P.add("act", (lambda e, h=h, hs=hs: e.activation(out=SBD[hs, c, h * 64:(h + 1) * 64],
                                                                     in_=ST[l][hs, c, :], func=AF.Copy)),
                          r=[kST], w=[kSBD])
                P.add("dve", (lambda e, q=q: e.tensor_copy(out=YS[0:C, q, :], in_=PS[6][0:C, 0:128])), r=psk(6),
                      w=[("YS", q)])
                if (kind == "s") or (ti["last"] and q == nq - 1):
                    state_out(q)
            for q in grp:
                cs = slice(q * C, (q + 1) * C)
                for h in range(2):
                    P.add("dve", (lambda e, h=h, q=q: e.bn_stats(out=BNS[0:C, h, :], in_=YS[0:C, q, h * 64:(h + 1) * 64])),
                          r=[("YS", q)], w=[("BNS", h)])
                    P.add("dve", (lambda e, h=h: e.bn_aggr(out=MV[0:C, h, :], in_=BNS[0:C, h, :])),
                          r=[("BNS", h)], w=[("MV", h)])
                P.add("act", (lambda e: e.activation(out=RS2[0:C, :], in_=MV[0:C, :, 1], func=AF.Sqrt, bias=EPS_GN)),
                      r=[("MV", 0), ("MV", 1)], w=[("RS2",)])
                P.add("dve", (lambda e: e.reciprocal(out=RS2[0:C, :], in_=RS2[0:C, :])), r=[("RS2",)], w=[("RS2",)])
                for h in range(2):
                    P.add("dve", (lambda e, h=h, q=q: e.tensor_scalar(
                        out=YN[0:C, h * 64:(h + 1) * 64], in0=YS[0:C, q, h * 64:(h + 1) * 64],
                        scalar1=MV[0:C, h, 0:1], scalar2=RS2[0:C, h:h + 1], op0=ALU.subtract, op1=ALU.mult)),
                        r=[("YS", q), ("MV", h), ("RS2",)], w=[("YN",)])
                P.add("pe", (lambda e: e.transpose(PS[2][:, 256:256 + C], YN[0:C, :], IDF[0:C, 0:C])),
                      r=[("YN",), ("IDF",)], w=psk(2))
                P.add("act", (lambda e: e.activation(out=Y1[:, 0:C], in_=PS[2][:, 256:256 + C], func=AF.Identity,
                                                     scale=cvc(l, "gn_g", c), bias=cvc(l, "gn_b", c))),
                      r=psk(2) + [cvk(l, "gn_g"), cvk(l, "gn_b")], w=[("Y1",)])
                P.add("dve", (lambda e, cs=cs: e.tensor_tensor(out=Y1[:, 0:C], in0=Y1[:, 0:C], in1=VM[:, cs], op=ALU.add)),
                      r=[("Y1",)] + kVM, w=[("Y1",)])
                P.add("dve", (lambda e, cs=cs: e.tensor_tensor(out=ya_ap(c)[:, cs], in0=Y1[:, 0:C], in1=PS[5][:, cs],
                                                               op=ALU.mult)),
                      r=[("Y1",)] + psk(5), w=ya_keys(c))

    def final_and_store(ydst_rows, ntok):
        gidx = nv_idx[("final_norm", 0)]
        rms_stats(ntok)
        YF = HIDf[:, 0:KC * TT].rearrange("p (c t) -> p c t", c=KC)
        for c in range(KC):
            P.add("dve", (lambda e, c=c: e.scalar_tensor_tensor(
                out=YF[:, c, 0:ntok], in0=X[:, c, 0:ntok], scalar=NV[:, gidx, c:c + 1],
                in1=RSTD[:, 0:ntok], op0=ALU.mult, op1=ALU.mult)),
                r=[("X", c), ("RSTD",), ("NV", gidx)], w=hid_keys(c * 2048, 2048))
        IO = HIDf[:, KC * TT:KC * TT + D]
        for tb, (dst, n) in enumerate(ydst_rows):
            for cg in range(4):
                bank = cg % 2
                for cc in range(4):
                    c = cg * 4 + cc
                    P.add("pe", (lambda e, c=c, cc=cc, tb=tb, n=n, bank=bank: e.transpose(
                        PS[bank][0:n, cc * 128:(cc + 1) * 128], YF[:, c, tb * 128:tb * 128 + n], IDF[:, :])),
                        r=hid_keys(c * 2048, 2048) + [("IDF",)], w=psk(bank))
                P.add("act" if cg % 2 else "dve",
                      (lambda e, cg=cg, n=n, bank=bank: (
                          e.activation(out=IO[0:n, cg * 512:(cg + 1) * 512], in_=PS[bank][0:n, :], func=AF.Copy)
                          if cg % 2 else
                          e.tensor_copy(out=IO[0:n, cg * 512:(cg + 1) * 512], in_=PS[bank][0:n, :]))),
                      r=psk(bank), w=hid_keys(32768 + cg * 2048, 2048))
            P.add("sp", (lambda e, dst=dst, n=n: e.dma_start(out=dst, in_=IO[0:n, :])),
                  r=hid_keys(32768, 8192), dma=True)

    tiles = []
    for sq in range(n_pseq):
        nt = plen // TT
        for it in range(nt):
            tiles.append(dict(kind="p", sq=[sq], t0=it * TT, ntok=TT, segs=[(0, TT)], C=128, nq=TT // 128,
                              first=(it == 0), last=(it == nt - 1)))
    if n_sseq and not cfg.get("nos", 0):
        tiles.append(dict(kind="s", sq=list(range(n_sseq)), t0=0, ntok=n_sseq * slen,
                          segs=[(s * slen, slen) for s in range(n_sseq)], C=slen, nq=n_sseq, first=True, last=True))

    for tnum, ti in enumerate(tiles):
        ntok = ti["ntok"]
        wc_begin_tile(tnum == 0)
        if ti["kind"] == "p":
            sq, t0 = ti["sq"][0], ti["t0"]
            rows = [(x_p[sq, t0 + tb * 128:t0 + (tb + 1) * 128, :], 128) for tb in range(TT // 128)]
            orow = [(y_p[sq, t0 + tb * 128:t0 + (tb + 1) * 128, :], 128) for tb in range(TT // 128)]
        else:
            xs2 = x_s.rearrange("b t d -> (b t) d")
            ys2 = y_s.rearrange("b t d -> (b t) d")
            rows, orow = [], []
            for r0 in range(0, ntok, 128):
                n = min(128, ntok - r0)
                rows.append((xs2[r0:r0 + n, :], n))
                orow.append((ys2[r0:r0 + n, :], n))
        load_x(rows, ntok)
        for l in range(L):
            if "ffn1" in phases:
                gi = nv_idx[("ffn1_norm", l)]
                rmsnorm(ntok, (lambda c, gi=gi: NV[:, gi, c:c + 1]), ("NV", gi))
                ffn(ntok, l, "ffn1")
            if "mix" in phases:
                mix(l, ti)
            if "ffn2" in phases:
                gi = nv_idx[("ffn2_norm", l)]
                rmsnorm(ntok, (lambda c, gi=gi: NV[:, gi, c:c + 1]), ("NV", gi))
                ffn(ntok, l, "ffn2")
        final_and_store(orow, ntok)

    P.finalize()
    sems = {nm: st.enter_context(nc.semaphore(nm)) for nm in P.sem_names()}
    with nc.allow_non_contiguous_dma(reason="per-feature vectors / small state layouts"):
        with nc.Block() as block:
            @block.tensor
            def _(eng):
                P.emit("pe", eng, sems)

            @block.scalar
            def _(eng):
                P.emit("act", eng, sems)

            @block.vector
            def _(eng):
                P.emit("dve", eng, sems)

            @block.gpsimd
            def _(eng):
                P.emit("pool", eng, sems)

            @block.sync
            def _(eng):
                P.emit("sp", eng, sems, final_wait=True)
    st.close()
    return nc, P


FULL_CFG = dict(n_pseq=2, plen=2048, n_sseq=2, slen=64, depth=2)
_CACHE = {}


def kernel(**inputs):
    cfg = dict(FULL_CFG)
    if "nc" not in _CACHE:
        _CACHE["nc"] = build_program(cfg)[0]
    nc = _CACHE["nc"]
    n = 8
    f = lambda a: np.ascontiguousarray(a, dtype=np.float32)
    shared = {nm: f(inputs[nm]) for nm, _ in WNAMES if nm != "r_k"}
    shared["r_k"] = f(inputs["r_k"]).reshape(2, D)
    shared["final_norm"] = f(inputs["final_norm"]).reshape(1, D)
    xp, xs = f(inputs["x_prompt"]), f(inputs["x_sample"])
    swkv, sshift, sconv = f(inputs["state_wkv"]), f(inputs["state_shift"]), f(inputs["state_conv"])
    in_maps = []
    for i in range(n):
        m = dict(shared)
        m["x_p"] = xp[2 * i:2 * i + 2]
        m["x_s"] = xs[2 * i:2 * i + 2]
        m["state_wkv"] = f(swkv[:, 2 * i:2 * i + 2])
        m["state_shift"] = f(sshift[:, 2 * i:2 * i + 2])
        m["state_conv"] = f(sconv[:, 2 * i:2 * i + 2])
        in_maps.append(m)
    res = run_bass_kernel_spmd(nc, in_maps, core_ids=list(range(n)))
    cat0 = lambda k: np.concatenate([r[k] for r in res.results], axis=0)
    cat1 = lambda k: np.concatenate([r[k] for r in res.results], axis=1)
    return (cat0("y_p"), cat0("y_s"), cat1("wkv_p"), cat1("shift_p"), cat1("conv_p"),
            cat1("wkv_s"), cat1("shift_s"), cat1("conv_s"))
```

```python
import contextlib
import numpy as np
import concourse.bass as bass
import concourse.mybir as mybir
from concourse.bass_utils import run_bass_kernel_spmd

F32 = mybir.dt.float32
BF16 = mybir.dt.bfloat16
ALU = mybir.AluOpType
AF = mybir.ActivationFunctionType

D = 2048
KC = 16
FF = 5632
FC = 44
NH = 32
HD = 64
R_W, R_A, R_G = 96, 96, 256
P_RWKV = 3 * D + R_W + R_A + R_G
P_TOT = P_RWKV + 2 * D + 2 * D
CONV_K = 31
EPS_RMS = 1e-6
EPS_LN = 1e-5
EPS_GN = 64e-5

ENGS = ("pe", "act", "dve", "pool", "sp")


class Op:
    __slots__ = ("eng", "fn", "deps", "dma", "sem", "sig", "sigval", "gidx")


class Prog:
    def __init__(self):
        self.ops = {e: [] for e in ENGS}
        self.lastw = {}
        self.readers = {}
        self.n = 0
        self.dma_slots = {"sp": ["d_sp%d" % i for i in range(8)],
                          "pool": ["d_pl%d" % i for i in range(8)],
                          "act": ["d_ac%d" % i for i in range(4)]}
        self.dma_rr = {"sp": 0, "pool": 0, "act": 0}
        self.slot_last = {}
        self.slot_count = {}

    def add(self, eng, fn, r=(), w=(), dma=False):
        op = Op()
        op.eng, op.fn, op.dma, op.sig, op.sigval = eng, fn, dma, False, 0
        op.gidx = self.n
        self.n += 1
        deps = {}
        for k in r:
            lw = self.lastw.get(k)
            if lw is not None:
                deps[id(lw)] = lw
        for k in w:
            lw = self.lastw.get(k)
            if lw is not None:
                deps[id(lw)] = lw
            rd = self.readers.get(k)
            if rd:
                for o in rd[0].values():
                    deps[id(o)] = o
                for o in rd[1]:
                    deps[id(o)] = o
        if dma:
            slots = self.dma_slots[eng]
            s = slots[self.dma_rr[eng] % len(slots)]
            self.dma_rr[eng] += 1
            op.sem = s
            prev = self.slot_last.get(s)
            if prev is not None:
                deps[id(prev)] = prev
            self.slot_last[s] = op
            self.slot_count[s] = self.slot_count.get(s, 0) + 1
            op.sigval = 16 * self.slot_count[s]
            op.sig = True
        else:
            op.sem = "c_" + eng
        for k in w:
            self.lastw[k] = op
            self.readers[k] = ({}, [])
        for k in r:
            rd = self.readers.get(k)
            if rd is None:
                rd = ({}, [])
                self.readers[k] = rd
            if dma:
                rd[1].append(op)
            else:
                rd[0][eng] = op
        deps.pop(id(op), None)
        dl = []
        for d in deps.values():
            if (not d.dma) and (not dma) and d.eng == eng and eng == "pe":
                continue
            dl.append(d)
        op.deps = dl
        self.ops[eng].append(op)
        return op

    def finalize(self):
        for e in ENGS:
            for op in self.ops[e]:
                for d in op.deps:
                    d.sig = True
        for e in ENGS:
            c = 0
            for op in self.ops[e]:
                if op.dma:
                    continue
                if op.sig:
                    c += 1
                    op.sigval = c

    def sem_names(self):
        names = ["c_" + e for e in ("pe", "act", "dve", "pool")]
        for e in ("sp", "pool", "act"):
            names += self.dma_slots[e]
        return names

    def emit(self, e, eng, sems, final_wait=False):
        waited = {}
        for op in self.ops[e]:
            for d in op.deps:
                if waited.get(d.sem, 0) < d.sigval:
                    eng.wait_ge(sems[d.sem], d.sigval)
                    waited[d.sem] = d.sigval
            inst = op.fn(eng)
            if op.dma:
                inst.then_inc(sems[op.sem], 16)
            elif op.sig:
                inst.then_inc(sems[op.sem], 1)
        if final_wait:
            for s, c in self.slot_count.items():
                if waited.get(s, 0) < 16 * c:
                    eng.wait_ge(sems[s], 16 * c)


RG = [(g * 128, 128) for g in range(48)] + [(6144, 96), (6240, 96), (6336, 128), (6464, 128)]
NRG = len(RG)
CDEC = float(np.exp(-0.5))
WNAMES = [("ffn1_norm", [D]), ("ffn1_w1", [D, FF]), ("ffn1_w3", [D, FF]), ("ffn1_w2", [FF, D]),
          ("mix_norm", [D]), ("w_in", [D, P_TOT]), ("mu_shift", [P_RWKV]), ("w0", [D]),
          ("w_up", [R_W, D]), ("a0", [D]), ("a_up", [R_A, D]), ("g_up", [R_G, D]), ("k_k", [D]),
          ("k_a", [D]), ("r_k", [D]), ("gn_g", [D]), ("gn_b", [D]), ("w_o_a", [D, D]),
          ("b_conv_in", [2 * D]), ("conv_w", [CONV_K, D]), ("conv_b", [D]), ("conv_ln_g", [D]),
          ("conv_ln_b", [D]), ("w_o_c", [D, D]), ("b_o_c", [D]), ("b_gate", [2 * D]),
          ("w_out", [D, D]), ("ffn2_norm", [D]), ("ffn2_w1", [D, FF]), ("ffn2_w3", [D, FF]),
          ("ffn2_w2", [FF, D])]


def build_program(cfg):
    n_pseq, plen, n_sseq, slen = cfg["n_pseq"], cfg["plen"], cfg["n_sseq"], cfg["slen"]
    L = cfg["depth"]
    phases = cfg.get("phases", ("ffn1", "mix", "ffn2"))
    TT = 512
    NSEG = max(1, n_sseq)

    nc = bass.Bass("TRN2", target_bir_lowering=False)

    def din(name, shape):
        return nc.dram_tensor(name, list(shape), F32, kind="ExternalInput").ap()

    def dout(name, shape):
        return nc.dram_tensor(name, list(shape), F32, kind="ExternalOutput").ap()

    x_p = din("x_p", [n_pseq, plen, D])
    x_s = din("x_s", [max(1, n_sseq), slen, D])
    st_wkv = din("state_wkv", [L, max(1, n_sseq), NH, HD, HD])
    st_shift = din("state_shift", [L, max(1, n_sseq), P_RWKV])
    st_conv = din("state_conv", [L, max(1, n_sseq), CONV_K - 1, D])
    W = {}
    for nm, shp in WNAMES:
        W[nm] = din(nm, [L] + shp)
    W["final_norm"] = din("final_norm", [1, D])
    y_p = dout("y_p", [n_pseq, plen, D])
    y_s = dout("y_s", [max(1, n_sseq), slen, D])
    o_wkv = {"p": dout("wkv_p", [L, n_pseq, NH, HD, HD]), "s": dout("wkv_s", [L, max(1, n_sseq), NH, HD, HD])}
    o_shift = {"p": dout("shift_p", [L, n_pseq, P_RWKV]), "s": dout("shift_s", [L, max(1, n_sseq), P_RWKV])}
    o_conv = {"p": dout("conv_p", [L, n_pseq, CONV_K - 1, D]),
              "s": dout("conv_s", [L, max(1, n_sseq), CONV_K - 1, D])}

    P = Prog()
    st = contextlib.ExitStack()

    def sb(name, shape, dt):
        return st.enter_context(nc.sbuf_tensor(name, list(shape), dt))

    X = sb("X", [128, KC, TT], F32)
    H = sb("H", [128, KC, TT], BF16)
    HID = sb("HID", [128, FC * TT], BF16)
    HIDf = HID.bitcast(F32)
    NWU = 48
    WBUF = sb("WBUF", [128, NWU * 512], BF16)
    SQ = [sb("SQ%d" % i, [128, TT], BF16) for i in range(2)]
    SIL = [sb("SIL%d" % i, [128, TT], F32) for i in range(2)]
    RSTD = sb("RSTD", [128, TT], F32)
    TMPF = [sb("TMPF%d" % i, [128, TT], F32) for i in range(2)]
    ONESB = sb("ONESB", [128, 128], BF16)
    ONESF = sb("ONESF", [128, 128], F32)
    IDF = sb("IDF", [128, 128], F32)
    IDB = sb("IDB", [128, 128], BF16)
    BLKONES = sb("BLKONES", [128, 128], BF16)
    MASK2 = sb("MASK2", [128, 2, 128], F32)
    MASKL = sb("MASKL", [128, 128], F32)
    NV = sb("NV", [128, 2 * L + 1, KC], F32)
    PS = [st.enter_context(nc.psum_tensor("PS%d" % i, [128, 512], F32)) for i in range(7)]
    PSB = st.enter_context(nc.psum_tensor("PSB", [128, 1024], BF16))

    cvcols = {}
    off = 0
    for nm, n in [("mix_norm", 16), ("mu", NRG), ("omu", NRG), ("w0", 16), ("a0", 16), ("k_k", 16), ("k_a", 16),
                  ("r_k", 16), ("gn_g", 16), ("gn_b", 16), ("b_ci", 32), ("conv_w", 31 * 16), ("conv_b", 16),
                  ("ln_g", 16), ("ln_b", 16), ("b_oc", 16), ("b_gate", 32)]:
        cvcols[nm] = off
        off += n
    NCV = off
    CV = [sb("CV%d" % l, [128, NCV], F32) for l in range(L)]
    ST = [sb("ST%d" % l, [128, 16, 64], F32) for l in range(L)]
    SBD = sb("SBD", [128, 16, 128], BF16)
    SBLK = sb("SBLK", [128, 128], F32)
    STG = sb("STG", [128, 64], F32)
    OSTG = sb("OSTG", [128, 64], F32)
    SH = [[sb("SH%d_%d" % (l, s), [128, NRG], F32) for s in range(NSEG)] for l in range(L)]
    CT = [[sb("CT%d_%d" % (l, s), [128, 16, 30], BF16) for s in range(NSEG)] for l in range(L)]
    LORA = sb("LORA", [128, 4, TT], BF16)
    LUB = [sb("LUB%d" % i, [128, 512], BF16) for i in range(2)]
    TOK = sb("TOK", [128, 4, 3, 128], BF16)
    DG = sb("DG", [128, 8, 128], BF16)
    WCL = sb("WCL", [128, 4], F32)
    MA = [sb("MA%d" % h, [128, 2, 128], BF16) for h in range(4)]
    MB = [sb("MB%d" % h, [128, 2, 128], BF16) for h in range(4)]
    MC = [sb("MC%d" % h, [128, 128], BF16) for h in range(4)]
    PPT = [[sb("PPT%d_%d" % (h, i), [128, 384], BF16) for i in range(2)] for h in range(4)]
    YS = sb("YS", [128, 4, 128], F32)
    XT = sb("XT", [128, 128], BF16)
    UT = sb("UT", [128, 128], BF16)
    YN = sb("YN", [128, 128], F32)
    Y1 = sb("Y1", [128, 128], F32)
    BNS = sb("BNS", [128, 2, 6], F32)
    MV = sb("MV", [128, 2, 2], F32)
    RS2 = sb("RS2", [128, 2], F32)

    def hid_keys(b0, nbytes):
        return [("HID", i) for i in range(b0 // 1024, (b0 + nbytes + 1023) // 1024)]

    def psk(bank, c0=0, n=512):
        return [("PS", bank)]

    def hbf(b0, n):
        return HID[:, b0 // 2:b0 // 2 + n]

    def hf32(b0, n):
        return HIDf[:, b0 // 4:b0 // 4 + n]

    ring = [0]

    def walloc(nunits):
        if ring[0] + nunits > NWU:
            ring[0] = 0
        u0 = ring[0]
        ring[0] += nunits
        return u0, [("WB", u) for u in range(u0, u0 + nunits)]

    def wview(u0, kc, ncols):
        return WBUF[:, u0 * 512:u0 * 512 + kc * ncols].rearrange("p (kc n) -> p kc n", kc=kc)

    WC_TOTAL = L * (6 * D * FF + D * (P_TOT + 1024) + 5 * D * D)
    WC_CH = 60 * 1024 * 1024
    n_wc = (WC_TOTAL + WC_CH - 1) // WC_CH + 1
    wcaches = [nc.dram_tensor("wcache%d" % i_, [WC_CH], BF16).ap() for i_ in range(n_wc)]
    wc_state = {"off": 0, "idx": 0, "pass": 0, "tab": [], "t": 0}

    def wc_begin_tile(first):
        wc_state["idx"] = 0
        wc_state["pass"] = 0 if first else 1

    def load_w_multi(srcs, kc, ncols):
        nun = (kc * ncols + 511) // 512
        u0, keys = walloc(nun)
        v = wview(u0, kc, ncols)
        flat = WBUF[:, u0 * 512:u0 * 512 + kc * ncols]
        i = wc_state["idx"]
        wc_state["idx"] += 1
        if wc_state["pass"] == 0:
            if wc_state["off"] + 128 * kc * ncols > WC_CH:
                wc_state["t"] += 1
                wc_state["off"] = 0
            off = wc_state["off"]
            wc_state["off"] += 128 * kc * ncols
            wcache = wcaches[wc_state["t"]]
            wc_state["tab"].append((off, kc * ncols, wc_state["t"]))
            dst = wcache[off:off + 128 * kc * ncols].rearrange("(p n) -> p n", p=128)
            for (src3, c0) in srcs:
                w_ = src3.shape[2]
                P.add("pool", (lambda e, v=v, src3=src3, c0=c0, w_=w_: e.dma_start(out=v[:, :, c0:c0 + w_], in_=src3)),
                      w=keys, dma=True)
            P.add("sp", (lambda e, dst=dst, flat=flat: e.dma_start(out=dst, in_=flat)), r=keys, w=[("WC", i)], dma=True)
        else:
            off, n, t_ = wc_state["tab"][i]
            assert n == kc * ncols
            srcc = wcaches[t_][off:off + 128 * n].rearrange("(p n) -> p n", p=128)
            P.add("sp", (lambda e, srcc=srcc, flat=flat: e.dma_start(out=flat, in_=srcc)), r=[("WC", i)], w=keys,
                  dma=True)
        return v, keys

    def load_w(src3, kc, ncols):
        return load_w_multi([(src3, 0)], kc, ncols)

    def cvc(l, nm, i=0):
        c = cvcols[nm] + i
        return CV[l][:, c:c + 1]

    def cvk(l, nm):
        return ("CV", l, nm)

    P.add("dve", lambda e: e.memset(ONESB[:], 1.0), w=[("ONESB",)])
    P.add("dve", lambda e: e.memset(ONESF[:], 1.0), w=[("ONESF",)])
    P.add("pool", lambda e: e.memset(IDF[:], 1.0), w=[("IDF",)])
    P.add("pool", lambda e: e.affine_select(out=IDF[:], in_=IDF[:], pattern=[[-1, 128]],
                                            compare_op=ALU.is_equal, fill=0.0, base=0,
                                            channel_multiplier=1), r=[("IDF",)], w=[("IDF",)])
    P.add("dve", lambda e: e.tensor_copy(out=IDB[:], in_=IDF[:]), r=[("IDF",)], w=[("IDB",)])
    P.add("dve", lambda e: e.memset(BLKONES[:], 0.0), w=[("BLKONES",)])
    P.add("dve", lambda e: e.memset(BLKONES[0:64, 0:64], 1.0), w=[("BLKONES",)])
    P.add("dve", lambda e: e.memset(BLKONES[64:128, 64:128], 1.0), w=[("BLKONES",)])
    P.add("dve", lambda e: e.memset(SBLK[:], 0.0), w=[("SBLK",)])
    for i_ in range(2):
        P.add("dve", (lambda e, i_=i_: e.memset(LUB[i_][:], 0.0)), w=[("LUB", i_)])
    P.add("pool", lambda e: e.memset(MASK2[:], 1.0), w=[("MASK2",)])
    P.add("pool", lambda e: e.affine_select(out=MASK2[:, 0, :], in_=MASK2[:, 0, :], pattern=[[1, 128]],
                                            compare_op=ALU.is_gt, fill=0.0, base=0, channel_multiplier=-1),
          r=[("MASK2",)], w=[("MASK2",)])
    P.add("pool", lambda e: e.affine_select(out=MASK2[:, 1, :], in_=MASK2[:, 1, :], pattern=[[1, 128]],
                                            compare_op=ALU.is_ge, fill=0.0, base=0, channel_multiplier=-1),
          r=[("MASK2",)], w=[("MASK2",)])
    P.add("pool", lambda e: e.memset(MASKL[:], 1.0), w=[("MASKL",)])
    P.add("pool", lambda e: e.affine_select(out=MASKL[:], in_=MASKL[:], pattern=[[-1, 128]],
                                            compare_op=ALU.is_gt, fill=0.0, base=0, channel_multiplier=1),
          r=[("MASKL",)], w=[("MASKL",)])

    def load_vec(dst_tile, col0, vec1d, n, key):
        src = vec1d.rearrange("(c p) -> p c", p=128)
        P.add("sp", (lambda e, src=src: e.dma_start(out=dst_tile[:, col0:col0 + n], in_=src)), w=[key], dma=True)

    def load_rg(dst_tile, col0, vec1d, key, store=False):
        parts = [(dst_tile[:, col0:col0 + 48], vec1d[0:6144].rearrange("(c p) -> p c", p=128)),
                 (dst_tile[0:96, col0 + 48:col0 + 49], vec1d[6144:6240].rearrange("(c p) -> p c", p=96)),
                 (dst_tile[0:96, col0 + 49:col0 + 50], vec1d[6240:6336].rearrange("(c p) -> p c", p=96)),
                 (dst_tile[:, col0 + 50:col0 + 52], vec1d[6336:6592].rearrange("(c p) -> p c", p=128))]
        for (sbap, drap) in parts:
            if store:
                P.add("sp", (lambda e, sbap=sbap, drap=drap: e.dma_start(out=drap, in_=sbap)), r=[key], dma=True)
            else:
                P.add("sp", (lambda e, sbap=sbap, drap=drap: e.dma_start(out=sbap, in_=drap)), w=[key], dma=True)

    nv_idx = {}
    i = 0
    for l in range(L):
        for nm in ("ffn1_norm", "ffn2_norm"):
            nv_idx[(nm, l)] = i
            load_vec(NV[:, i, :], 0, W[nm][l], 16, ("NV", i))
            i += 1
    nv_idx[("final_norm", 0)] = i
    load_vec(NV[:, i, :], 0, W["final_norm"][0], 16, ("NV", i))
    if "mix" in phases:
        for l in range(L):
            for nm, src, n in [("mix_norm", "mix_norm", 16), ("w0", "w0", 16), ("a0", "a0", 16), ("k_k", "k_k", 16),
                               ("k_a", "k_a", 16), ("r_k", "r_k", 16), ("gn_g", "gn_g", 16), ("gn_b", "gn_b", 16),
                               ("b_ci", "b_conv_in", 32), ("conv_b", "conv_b", 16), ("ln_g", "conv_ln_g", 16),
                               ("ln_b", "conv_ln_b", 16), ("b_oc", "b_o_c", 16), ("b_gate", "b_gate", 32)]:
                load_vec(CV[l], cvcols[nm], W[src][l], n, cvk(l, nm))
            for k in range(CONV_K):
                load_vec(CV[l], cvcols["conv_w"] + k * 16, W["conv_w"][l, k], 16, cvk(l, "conv_w"))
            P.add("dve", (lambda e, l=l: e.memset(CV[l][:, cvcols["mu"]:cvcols["mu"] + NRG], 0.0)), w=[cvk(l, "mu")])
            load_rg(CV[l], cvcols["mu"], W["mu_shift"][l], cvk(l, "mu"))
            P.add("dve", (lambda e, l=l: e.tensor_scalar(
                out=CV[l][:, cvcols["omu"]:cvcols["omu"] + NRG], in0=CV[l][:, cvcols["mu"]:cvcols["mu"] + NRG],
                scalar1=-1.0, scalar2=1.0, op0=ALU.mult, op1=ALU.add)), r=[cvk(l, "mu")], w=[cvk(l, "omu")])

    def load_x(xsrc_rows, ntok):
        nb = len(xsrc_rows)
        IO = HIDf[:, 0:nb * D].rearrange("p (a b) -> p a b", a=nb)
        for tb, (src, n) in enumerate(xsrc_rows):
            P.add("sp", (lambda e, tb=tb, src=src, n=n: e.dma_start(out=IO[0:n, tb, :], in_=src)),
                  w=hid_keys(tb * 8192, 8192), dma=True)
        for c in range(KC):
            bank = c % 2
            for tb, (src, n) in enumerate(xsrc_rows):
                P.add("pe", (lambda e, tb=tb, n=n, c=c, bank=bank: e.transpose(
                    PS[bank][:, tb * 128:tb * 128 + n], IO[0:n, tb, c * 128:(c + 1) * 128],
                    IDF[0:n, 0:n])),
                    r=hid_keys(tb * 8192, 8192) + [("IDF",)], w=psk(bank))
            P.add("act" if c % 2 else "dve",
                  (lambda e, c=c, bank=bank: (e.activation(out=X[:, c, 0:ntok], in_=PS[bank][:, 0:ntok],
                                                           func=AF.Copy)
                                              if c % 2 else
                                              e.tensor_copy(out=X[:, c, 0:ntok], in_=PS[bank][:, 0:ntok]))),
                  r=psk(bank), w=[("X", c)])

    def rms_stats(ntok):
        for c in range(KC):
            s = c % 2
            P.add("act", (lambda e, c=c, s=s: e.activation(out=SQ[s][:, 0:ntok], in_=X[:, c, 0:ntok],
                                                           func=AF.Square)),
                  r=[("X", c)], w=[("SQ", s)])
            P.add("pe", (lambda e, c=c, s=s: e.matmul(PS[6][:, 0:ntok], lhsT=ONESB[:], rhs=SQ[s][:, 0:ntok],
                                                      start=(c == 0), stop=(c == KC - 1))),
                  r=[("SQ", s), ("ONESB",)], w=psk(6))
        P.add("act", (lambda e: e.activation(out=RSTD[:, 0:ntok], in_=PS[6][:, 0:ntok], func=AF.Sqrt,
                                             scale=1.0 / D, bias=EPS_RMS)),
              r=psk(6), w=[("RSTD",)])
        P.add("dve", (lambda e: e.reciprocal(out=RSTD[:, 0:ntok], in_=RSTD[:, 0:ntok])),
              r=[("RSTD",)], w=[("RSTD",)])

    def rmsnorm(ntok, gap_fn, gkey):
        rms_stats(ntok)
        for c in range(KC):
            P.add("dve", (lambda e, c=c: e.scalar_tensor_tensor(
                out=H[:, c, 0:ntok], in0=X[:, c, 0:ntok], scalar=gap_fn(c),
                in1=RSTD[:, 0:ntok], op0=ALU.mult, op1=ALU.mult)),
                r=[("X", c), ("RSTD",), gkey], w=[("H", c)])

    def proj16(bank, ntok, wv, wkeys, col0, m, rows=128):
        for kc in range(KC):
            P.add("pe", (lambda e, kc=kc: e.matmul(
                PS[bank][0:rows, 0:ntok], lhsT=wv[:, kc, col0:col0 + rows], rhs=H[:, kc, 0:ntok],
                start=(kc == 0), stop=(kc == KC - 1))),
                r=wkeys + [("H", kc)], w=psk(bank))

    def ffn(ntok, l, pre):
        w1 = W[pre + "_w1"][l].rearrange("(kc p) n -> p kc n", p=128)
        w3 = W[pre + "_w3"][l].rearrange("(kc p) n -> p kc n", p=128)
        w2 = W[pre + "_w2"][l].rearrange("(kc p) n -> p kc n", p=128)
        for fb in range(FF // 512):
            v1, k1 = load_w(w1[:, :, fb * 512:(fb + 1) * 512], KC, 512)
            v3, k3 = load_w(w3[:, :, fb * 512:(fb + 1) * 512], KC, 512)
            for m in range(4):
                f = fb * 4 + m
                ba, bb = 2 * (f % 2), 2 * (f % 2) + 1
                proj16(ba, ntok, v1, k1, m * 128, m)
                proj16(bb, ntok, v3, k3, m * 128, m)
                sl = f % 2
                P.add("act", (lambda e, sl=sl, ba=ba: e.activation(out=SIL[sl][:, 0:ntok], in_=PS[ba][:, 0:ntok],
                                                                   func=AF.Silu)),
                      r=psk(ba), w=[("SIL", sl)])
                P.add("dve", (lambda e, sl=sl, bb=bb, f=f: e.tensor_tensor(
                    out=HID[:, f * TT:f * TT + ntok], in0=SIL[sl][:, 0:ntok], in1=PS[bb][:, 0:ntok], op=ALU.mult)),
                    r=[("SIL", sl)] + psk(bb), w=[("HID", f)])
        for dg in range(4):
            banks = [0, 1, 2, 3] if dg % 2 == 0 else [4, 5, 6, 3]
            for fq in range(4):
                v2, k2 = load_w(w2[:, fq * 11:(fq + 1) * 11, dg * 512:(dg + 1) * 512], 11, 512)
                for dd in range(4):
                    for ff in range(11):
                        f = fq * 11 + ff
                        P.add("pe", (lambda e, dd=dd, ff=ff, f=f, v2=v2, bk=banks[dd]: e.matmul(
                            PS[bk][:, 0:ntok], lhsT=v2[:, ff, dd * 128:(dd + 1) * 128],
                            rhs=HID[:, f * TT:f * TT + ntok], start=(f == 0), stop=(f == FC - 1))),
                            r=k2 + [("HID", f)], w=psk(banks[dd]))
            for dd in range(4):
                c = dg * 4 + dd
                P.add("dve", (lambda e, c=c, bk=banks[dd]: e.scalar_tensor_tensor(
                    out=X[:, c, 0:ntok], in0=PS[bk][:, 0:ntok], scalar=0.5, in1=X[:, c, 0:ntok],
                    op0=ALU.mult, op1=ALU.add)),
                    r=psk(banks[dd]) + [("X", c)], w=[("X", c)])

    def add_out_proj(ntok, l, zsrc_fn, zkeys_fn):
        wo = W["w_out"][l].rearrange("(kc p) n -> p kc n", p=128)
        for blk in range(4):
            v, keys = load_w(wo[:, :, blk * 512:(blk + 1) * 512], KC, 512)
            for m in range(4):
                c = blk * 4 + m
                bank = c % 2
                for kc in range(KC):
                    P.add("pe", (lambda e, kc=kc, m=m, v=v, bank=bank: e.matmul(
                        PS[bank][:, 0:ntok], lhsT=v[:, kc, m * 128:(m + 1) * 128], rhs=zsrc_fn(kc),
                        start=(kc == 0), stop=(kc == KC - 1))),
                        r=keys + zkeys_fn(kc), w=psk(bank))
                P.add("dve", (lambda e, c=c, bank=bank: e.tensor_tensor(
                    out=X[:, c, 0:ntok], in0=PS[bank][:, 0:ntok], in1=X[:, c, 0:ntok], op=ALU.add)),
                    r=psk(bank) + [("X", c)], w=[("X", c)])

    def shift_evac(l, bank, rows, g, out_ap, out_keys, segs, ntok):
        mu = CV[l][0:rows, cvcols["mu"] + g:cvcols["mu"] + g + 1]
        omu = CV[l][0:rows, cvcols["omu"] + g:cvcols["omu"] + g + 1]
        ps = PS[bank]
        P.add("act", (lambda e: e.activation(out=out_ap[0:rows, 0:ntok], in_=ps[0:rows, 0:ntok], func=AF.Copy,
                                             scale=omu)),
              r=psk(bank) + [cvk(l, "omu")], w=out_keys)
        se = cfg.get("se", 99)
        for s, (c0, n) in enumerate(segs):
            if se <= 1:
                break
            P.add("dve", (lambda e, c0=c0, n=n: e.scalar_tensor_tensor(
                out=out_ap[0:rows, c0 + 1:c0 + n], in0=ps[0:rows, c0:c0 + n - 1], scalar=mu,
                in1=out_ap[0:rows, c0 + 1:c0 + n], op0=ALU.mult, op1=ALU.add)),
                r=psk(bank) + out_keys + [cvk(l, "mu")], w=out_keys)
            if se <= 2:
                continue
            P.add("dve", (lambda e, c0=c0, s=s: e.scalar_tensor_tensor(
                out=out_ap[0:rows, c0:c0 + 1], in0=SH[l][s][0:rows, g:g + 1], scalar=mu,
                in1=out_ap[0:rows, c0:c0 + 1], op0=ALU.mult, op1=ALU.add)),
                r=[("SH", l, s, g), cvk(l, "mu")] + out_keys, w=out_keys)
            if se <= 3:
                continue
            P.add("dve", (lambda e, c0=c0, n=n, s=s: e.tensor_copy(
                out=SH[l][s][0:rows, g:g + 1], in_=ps[0:rows, c0 + n - 1:c0 + n])),
                r=psk(bank), w=[("SH", l, s, g)])

    def mix(l, ti):
        ntok, segs, C, nq = ti["ntok"], ti["segs"], ti["C"], ti["nq"]
        kind = ti["kind"]
        stop = cfg.get("stop", 99)
        win = W["w_in"][l].rearrange("(kc p) n -> p kc n", p=128)
        nseg = len(segs)
        for s in range(nseg):
            if kind == "p":
                if ti["first"]:
                    P.add("dve", (lambda e, s=s: e.memset(SH[l][s][:], 0.0)),
                          w=[("SH", l, s, g) for g in range(NRG)])
                    P.add("dve", (lambda e, s=s: e.memset(CT[l][s][:], 0.0)), w=[("CT", l, s)])
            else:
                P.add("dve", (lambda e, s=s: e.memset(SH[l][s][:], 0.0)), w=[("SH", l, s, g) for g in range(NRG)])
                parts_key = "SHLOAD"
                vec = st_shift[l, ti["sq"][s]]
                for (sbap, drap) in [
                        (SH[l][s][:, 0:48], vec[0:6144].rearrange("(c p) -> p c", p=128)),
                        (SH[l][s][0:96, 48:49], vec[6144:6240].rearrange("(c p) -> p c", p=96)),
                        (SH[l][s][0:96, 49:50], vec[6240:6336].rearrange("(c p) -> p c", p=96)),
                        (SH[l][s][:, 50:52], vec[6336:6592].rearrange("(c p) -> p c", p=128))]:
                    P.add("sp", (lambda e, sbap=sbap, drap=drap: e.dma_start(out=sbap, in_=drap)),
                          w=[("SH", l, s, g) for g in range(NRG)], dma=True)
                stg = hf32(20480, D)
                P.add("sp", (lambda e, s=s, stg=stg: e.dma_start(out=stg[0:30, :], in_=st_conv[l, ti["sq"][s]])),
                      w=hid_keys(20480, 8192), dma=True)
                for c in range(KC):
                    P.add("pe", (lambda e, c=c, stg=stg: e.transpose(
                        PS[c % 2][:, 0:30], stg[0:30, c * 128:(c + 1) * 128], IDF[0:30, 0:30])),
                        r=hid_keys(20480, 8192) + [("IDF",)], w=psk(c % 2, 0, 30))
                    P.add("dve", (lambda e, c=c, s=s: e.tensor_copy(out=CT[l][s][:, c, :], in_=PS[c % 2][:, 0:30])),
                          r=psk(c % 2, 0, 30), w=[("CT", l, s)])
        if stop <= 0:
            return
        rmsnorm(ntok, lambda c: cvc(l, "mix_norm", c), cvk(l, "mix_norm"))

        if stop <= 1:
            return
        GW = sum(30 + n for (_, n) in segs)
        gbase = []
        o = 0
        for (_, n) in segs:
            gbase.append(o)
            o += 30 + n
        GLU = HID[:, 0:16 * GW].rearrange("p (c w) -> p c w", c=16)

        def glu_keys(c):
            return hid_keys(c * GW * 2, GW * 2)
        DWB0 = 12288

        def dw_ap(c):
            return hf32(DWB0 + c * ntok * 4, ntok)

        def dw_keys(c):
            return hid_keys(DWB0 + c * ntok * 4, ntok * 4)
        need_cs = (kind == "s") or ti["last"]
        CTF0 = 40960
        CTF = hf32(CTF0, 16 * nseg * 30).rearrange("p (c s t) -> p c s t", c=16, s=nseg)
        for s in range(nseg):
            P.add("dve", (lambda e, s=s: e.tensor_copy(out=GLU[:, :, gbase[s]:gbase[s] + 30], in_=CT[l][s][:, :, :])),
                  r=[("CT", l, s)], w=hid_keys(0, 16 * GW * 2))
        for blk in range(4):
            c0w = P_RWKV + blk * 512
            vv, kv = load_w(win[:, :, c0w:c0w + 512], KC, 512)
            vg, kg = load_w(win[:, :, c0w + D:c0w + D + 512], KC, 512)
            for m in range(4):
                c = blk * 4 + m
                ba, bb = 2 * (c % 2), 2 * (c % 2) + 1
                proj16(ba, ntok, vv, kv, m * 128, m)
                proj16(bb, ntok, vg, kg, m * 128, m)
                sl = c % 2
                P.add("act", (lambda e, sl=sl, bb=bb, c=c: e.activation(
                    out=SIL[sl][:, 0:ntok], in_=PS[bb][:, 0:ntok], func=AF.Sigmoid, bias=cvc(l, "b_ci", 16 + c))),
                    r=psk(bb) + [cvk(l, "b_ci")], w=[("SIL", sl)])
                for s, (c0, n) in enumerate(segs):
                    P.add("dve", (lambda e, sl=sl, ba=ba, c=c, c0=c0, n=n, s=s: e.scalar_tensor_tensor(
                        out=GLU[:, c, gbase[s] + 30:gbase[s] + 30 + n], in0=PS[ba][:, c0:c0 + n],
                        scalar=cvc(l, "b_ci", c), in1=SIL[sl][:, c0:c0 + n], op0=ALU.add, op1=ALU.mult)),
                        r=psk(ba) + [("SIL", sl), cvk(l, "b_ci")], w=glu_keys(c))
                    if need_cs:
                        P.add("dve", (lambda e, sl=sl, ba=ba, c=c, c0=c0, n=n, s=s: e.scalar_tensor_tensor(
                            out=CTF[:, c, s, :], in0=PS[ba][:, c0 + n - 30:c0 + n],
                            scalar=cvc(l, "b_ci", c), in1=SIL[sl][:, c0 + n - 30:c0 + n], op0=ALU.add, op1=ALU.mult)),
                            r=psk(ba) + [("SIL", sl), cvk(l, "b_ci")], w=hid_keys(CTF0, 16 * nseg * 120))
        if stop <= 2:
            return
        for s, (c0, n) in enumerate(segs):
            P.add("dve", (lambda e, s=s, n=n: e.tensor_copy(out=CT[l][s][:, :, :],
                                                            in_=GLU[:, :, gbase[s] + n:gbase[s] + n + 30])),
                  r=hid_keys(0, 16 * GW * 2), w=[("CT", l, s)])
        if stop <= 3:
            return
        if need_cs:
            stg = hf32(20480, D)
            for s in range(nseg):
                for c in range(KC):
                    P.add("pe", (lambda e, c=c, s=s: e.transpose(
                        PS[4 + c % 2][0:30, 0:128], CTF[:, c, s, :], IDF[:, :])),
                        r=hid_keys(CTF0, 16 * nseg * 120) + [("IDF",)], w=psk(4 + c % 2, 0, 128))
                    P.add("dve", (lambda e, c=c, stg=stg: e.tensor_copy(out=stg[0:30, c * 128:(c + 1) * 128],
                                                                        in_=PS[4 + c % 2][0:30, 0:128])),
                          r=psk(4 + c % 2, 0, 128), w=hid_keys(20480, 8192))
                dst = o_conv[kind][l, ti["sq"][s]]
                P.add("sp", (lambda e, dst=dst, stg=stg: e.dma_start(out=dst, in_=stg[0:30, :])),
                      r=hid_keys(20480, 8192), dma=True)
        if stop <= 4:
            return
        dgc = [0]
        S1B, S2B = 5, 6
        for ci, c in enumerate(range(KC - 1, -1, -1)):
            cbanks = [(2 * (ci % 2)) + s for s in range(nseg)]
            for k in range(CONV_K):
                slot = dgc[0] % 8
                dgc[0] += 1
                eng = "act" if k % 2 else "dve"
                if eng == "act":
                    P.add("act", (lambda e, slot=slot, k=k, c=c: e.activation(
                        out=DG[:, slot, :], in_=IDB[:], func=AF.Copy, scale=cvc(l, "conv_w", k * 16 + c))),
                        r=[("IDB",), cvk(l, "conv_w")], w=[("DG", slot)])
                else:
                    P.add("dve", (lambda e, slot=slot, k=k, c=c: e.tensor_scalar(
                        out=DG[:, slot, :], in0=IDB[:], scalar1=cvc(l, "conv_w", k * 16 + c), scalar2=None,
                        op0=ALU.mult)),
                        r=[("IDB",), cvk(l, "conv_w")], w=[("DG", slot)])
                for s, (c0, n) in enumerate(segs):
                    P.add("pe", (lambda e, slot=slot, k=k, c=c, s=s, c0=c0, n=n, bk=cbanks[s]: e.matmul(
                        PS[bk][:, 0:n], lhsT=DG[:, slot, :], rhs=GLU[:, c, gbase[s] + k:gbase[s] + k + n],
                        start=(k == 0), stop=(k == CONV_K - 1))),
                        r=[("DG", slot)] + glu_keys(c), w=psk(cbanks[s], 0, n))
            for s, (c0, n) in enumerate(segs):
                bk = cbanks[s]
                P.add("act", (lambda e, c=c, c0=c0, n=n, bk=bk: e.activation(
                    out=dw_ap(c)[:, c0:c0 + n], in_=PS[bk][:, 0:n], func=AF.Identity, bias=cvc(l, "conv_b", c))),
                    r=psk(bk, 0, n) + [cvk(l, "conv_b")], w=dw_keys(c))
                P.add("act", (lambda e, c=c, c0=c0, n=n, bk=bk, ci=ci: e.activation(
                    out=SQ[1][:, c0:c0 + n], in_=PS[bk][:, 0:n], func=AF.Square, bias=cvc(l, "conv_b", c))),
                    r=psk(bk, 0, n) + [cvk(l, "conv_b")], w=[("SQ", 1)])
            P.add("dve", (lambda e, c=c: e.tensor_copy(out=SQ[0][:, 0:ntok], in_=dw_ap(c)[:, 0:ntok])),
                  r=dw_keys(c), w=[("SQ", 0)])
            P.add("pe", (lambda e, ci=ci: e.matmul(PS[S1B][:, 0:ntok], lhsT=ONESB[:], rhs=SQ[0][:, 0:ntok],
                                                   start=(ci == 0), stop=(ci == KC - 1))),
                  r=[("SQ", 0), ("ONESB",)], w=psk(S1B))
            P.add("pe", (lambda e, ci=ci: e.matmul(PS[S2B][:, 0:ntok], lhsT=ONESB[:], rhs=SQ[1][:, 0:ntok],
                                                   start=(ci == 0), stop=(ci == KC - 1))),
                  r=[("SQ", 1), ("ONESB",)], w=psk(S2B))
        if stop <= 5:
            return
        P.add("act", (lambda e: e.activation(out=SIL[0][:, 0:ntok], in_=PS[S1B][:, 0:ntok], func=AF.Copy,
                                             scale=1.0 / D)), r=psk(S1B), w=[("SIL", 0)])
        P.add("dve", (lambda e: e.tensor_tensor(out=TMPF[0][:, 0:ntok], in0=SIL[0][:, 0:ntok], in1=SIL[0][:, 0:ntok],
                                                op=ALU.mult)), r=[("SIL", 0)], w=[("TMPF", 0)])
        P.add("dve", (lambda e: e.scalar_tensor_tensor(out=SIL[1][:, 0:ntok], in0=PS[S2B][:, 0:ntok],
                                                       scalar=1.0 / D, in1=TMPF[0][:, 0:ntok],
                                                       op0=ALU.mult, op1=ALU.subtract)),
              r=psk(S2B) + [("TMPF", 0)], w=[("SIL", 1)])
        P.add("act", (lambda e: e.activation(out=SIL[1][:, 0:ntok], in_=SIL[1][:, 0:ntok], func=AF.Sqrt,
                                             bias=EPS_LN)), r=[("SIL", 1)], w=[("SIL", 1)])
        P.add("dve", (lambda e: e.reciprocal(out=SIL[1][:, 0:ntok], in_=SIL[1][:, 0:ntok])),
              r=[("SIL", 1)], w=[("SIL", 1)])
        HC0 = 0

        def hc_ap(c):
            return hbf(HC0 + c * ntok * 2, ntok)

        def hc_keys(c):
            return hid_keys(HC0 + c * ntok * 2, ntok * 2)
        for c in range(KC):
            t = TMPF[c % 2]
            P.add("dve", (lambda e, c=c, t=t: e.tensor_tensor(out=t[:, 0:ntok], in0=dw_ap(c)[:, 0:ntok],
                                                              in1=SIL[0][:, 0:ntok], op=ALU.subtract)),
                  r=dw_keys(c) + [("SIL", 0)], w=[("TMPF", c % 2)])
            P.add("dve", (lambda e, c=c, t=t: e.tensor_tensor(out=t[:, 0:ntok], in0=t[:, 0:ntok],
                                                              in1=SIL[1][:, 0:ntok], op=ALU.mult)),
                  r=[("TMPF", c % 2), ("SIL", 1)], w=[("TMPF", c % 2)])
            P.add("act", (lambda e, c=c, t=t: e.activation(out=hc_ap(c), in_=t[:, 0:ntok], func=AF.Silu,
                                                           scale=cvc(l, "ln_g", c), bias=cvc(l, "ln_b", c))),
                  r=[("TMPF", c % 2), cvk(l, "ln_g"), cvk(l, "ln_b")], w=hc_keys(c))
        if stop <= 6:
            return
        ZC0 = 16384

        def zc_ap(c):
            return hbf(ZC0 + c * ntok * 2, ntok)

        def zc_keys(c):
            return hid_keys(ZC0 + c * ntok * 2, ntok * 2)
        woc = W["w_o_c"][l].rearrange("(kc p) n -> p kc n", p=128)
        for blk in range(4):
            vo, ko = load_w(woc[:, :, blk * 512:(blk + 1) * 512], KC, 512)
            cg = P_RWKV + 2 * D + D + blk * 512
            vg, kg = load_w(win[:, :, cg:cg + 512], KC, 512)
            for m in range(4):
                c = blk * 4 + m
                ba, bb = 2 * (c % 2), 2 * (c % 2) + 1
                for kc in range(KC):
                    P.add("pe", (lambda e, kc=kc, m=m, vo=vo, ba=ba: e.matmul(
                        PS[ba][:, 0:ntok], lhsT=vo[:, kc, m * 128:(m + 1) * 128], rhs=hc_ap(kc),
                        start=(kc == 0), stop=(kc == KC - 1))),
                        r=ko + hc_keys(kc), w=psk(ba))
                proj16(bb, ntok, vg, kg, m * 128, m)
                sl = c % 2
                P.add("act", (lambda e, sl=sl, bb=bb, c=c: e.activation(
                    out=SIL[sl][:, 0:ntok], in_=PS[bb][:, 0:ntok], func=AF.Sigmoid, bias=cvc(l, "b_gate", 16 + c))),
                    r=psk(bb) + [cvk(l, "b_gate")], w=[("SIL", sl)])
                P.add("dve", (lambda e, sl=sl, ba=ba, c=c: e.scalar_tensor_tensor(
                    out=zc_ap(c), in0=PS[ba][:, 0:ntok], scalar=cvc(l, "b_oc", c), in1=SIL[sl][:, 0:ntok],
                    op0=ALU.add, op1=ALU.mult)),
                    r=psk(ba) + [("SIL", sl), cvk(l, "b_oc")], w=zc_keys(c))
        add_out_proj(ntok, l, zc_ap, zc_keys)

        if stop <= 7:
            return
        YA0 = 0

        def ya_ap(c):
            return hbf(YA0 + c * ntok * 2, ntok)

        def ya_keys(c):
            return hid_keys(YA0 + c * ntok * 2, ntok * 2)
        B0 = 16384
        offs = {}
        for i_, nm in enumerate(["RM", "KM", "VM", "AL", "SG", "LC", "E0", "E1", "KK", "T"]):
            offs[nm] = B0 + i_ * 2048
        offs["AR"] = B0 + 20480
        offs["BT"] = offs["AR"] + 2048
        offs["KT"] = offs["BT"] + 1024
        offs["VB"] = offs["KT"] + 1024
        offs["BHF"] = offs["VB"] + 1024
        offs["KHF"] = offs["BHF"] + 1024

        def f32t(nm):
            return hf32(offs[nm], TT), hid_keys(offs[nm], 2048)

        def bft(nm):
            return hbf(offs[nm], TT), hid_keys(offs[nm], 1024)
        RM, kRM = f32t("RM")
        KM, kKM = f32t("KM")
        VM, kVM = f32t("VM")
        AL, kAL = f32t("AL")
        SG, kSG = f32t("SG")
        LC, kLC = f32t("LC")
        E0, kE0 = f32t("E0")
        E1, kE1 = f32t("E1")
        KK, kKK = f32t("KK")
        T_, kT = f32t("T")
        AR = hbf(offs["AR"], 2 * TT).rearrange("p (a t) -> p a t", a=2)
        kAR = hid_keys(offs["AR"], 2048)
        BT, kBT = bft("BT")
        KT, kKT = bft("KT")
        VB, kVB = bft("VB")
        BHF, kBHF = bft("BHF")
        KHF, kKHF = bft("KHF")

        vs, ks = load_w(win[:, :, 6144:6272], KC, 128)
        vsa, ksa = load_w(win[:, :, 6240:6368], KC, 128)
        vgx, kgx = load_w(win[:, :, 6336:6592], KC, 256)
        sub = cfg.get("sub", 99)
        if sub <= 1:
            return
        proj16(0, ntok, vs, ks, 0, 0)
        if sub <= 2:
            return
        shift_evac(l, 0, 128, 48, T_, kT, segs, ntok)
        if sub <= 3:
            return
        P.add("act", (lambda e: e.activation(out=LORA[:, 0, 0:ntok], in_=T_[:, 0:ntok], func=AF.Tanh)),
              r=kT, w=[("LORA", 0)])
        proj16(1, ntok, vsa, ksa, 0, 0)
        shift_evac(l, 1, 128, 49, E0, kE0, segs, ntok)
        P.add("dve", (lambda e: e.tensor_copy(out=LORA[:, 1, 0:ntok], in_=E0[:, 0:ntok])),
              r=kE0, w=[("LORA", 1)])
        for j in range(2):
            tt = E1 if j == 0 else KK
            kt = kE1 if j == 0 else kKK
            proj16(2 + j, ntok, vgx, kgx, j * 128, 0)
            shift_evac(l, 2 + j, 128, 50 + j, tt, kt, segs, ntok)
            P.add("act", (lambda e, j=j, tt=tt: e.activation(out=LORA[:, 2 + j, 0:ntok], in_=tt[:, 0:ntok],
                                                             func=AF.Sigmoid)),
                  r=kt, w=[("LORA", 2 + j)])

        if stop <= 8:
            return
        wup = W["w_up"][l]
        aup = W["a_up"][l]
        gup = W["g_up"][l].rearrange("(kc p) n -> p kc n", p=128)
        for c in range(KC):
            wv, kw = load_w_multi([(win[:, :, j * D + c * 128:j * D + (c + 1) * 128], j * 128) for j in range(3)],
                                  KC, 384)
            LU = LUB[c % 2]
            kl = [("LUB", c % 2)]
            P.add("pool", (lambda e, LU=LU, c=c: e.dma_start(out=LU[0:96, 0:128], in_=wup[:, c * 128:(c + 1) * 128])),
                  w=kl, dma=True)
            P.add("pool", (lambda e, LU=LU, c=c: e.dma_start(out=LU[0:96, 128:256], in_=aup[:, c * 128:(c + 1) * 128])),
                  w=kl, dma=True)
            P.add("pool", (lambda e, LU=LU, c=c: e.dma_start(
                out=LU[:, 256:512].rearrange("p (k n) -> p k n", k=2), in_=gup[:, :, c * 128:(c + 1) * 128])),
                w=kl, dma=True)
            for j, (dst, kd) in enumerate([(RM, kRM), (KM, kKM), (VM, kVM)]):
                proj16(j, ntok, wv, kw, j * 128, 0)
                shift_evac(l, j, 128, j * 16 + c, dst, kd, segs, ntok)
            P.add("pe", (lambda e, LU=LU: e.matmul(PS[3][:, 0:ntok], lhsT=LU[:, 0:128], rhs=LORA[:, 0, 0:ntok],
                                                   start=True, stop=True)),
                  r=kl + [("LORA", 0)], w=psk(3))
            P.add("pe", (lambda e, LU=LU: e.matmul(PS[4][:, 0:ntok], lhsT=LU[:, 128:256], rhs=LORA[:, 1, 0:ntok],
                                                   start=True, stop=True)),
                  r=kl + [("LORA", 1)], w=psk(4))
            for j in range(2):
                P.add("pe", (lambda e, LU=LU, j=j: e.matmul(
                    PS[5][:, 0:ntok], lhsT=LU[:, 256 + j * 128:256 + (j + 1) * 128], rhs=LORA[:, 2 + j, 0:ntok],
                    start=(j == 0), stop=(j == 1))),
                    r=kl + [("LORA", 2 + j)], w=psk(5))
            P.add("act", (lambda e, c=c: e.activation(out=SG[:, 0:ntok], in_=PS[3][:, 0:ntok], func=AF.Sigmoid,
                                                      bias=cvc(l, "w0", c))),
                  r=psk(3) + [cvk(l, "w0")], w=kSG)
            P.add("act", (lambda e, c=c: e.activation(out=AL[:, 0:ntok], in_=PS[4][:, 0:ntok], func=AF.Sigmoid,
                                                      bias=cvc(l, "a0", c))),
                  r=psk(4) + [cvk(l, "a0")], w=kAL)
            for q in range(nq):
                P.add("dve", (lambda e, q=q: e.tensor_tensor_scan(
                    out=LC[:, q * C:(q + 1) * C], data0=ONESF[:, 0:C], data1=SG[:, q * C:(q + 1) * C], initial=0.0,
                    op0=ALU.mult, op1=ALU.add)),
                    r=kSG + [("ONESF",)], w=kLC)
            P.add("dve", (lambda e: e.tensor_tensor(out=SG[:, 0:ntok], in0=LC[:, 0:ntok], in1=SG[:, 0:ntok],
                                                    op=ALU.subtract)), r=kLC + kSG, w=kSG)
            LCend = LC[:, 0:ntok].rearrange("p (q t) -> p q t", t=C)[:, :, C - 1]
            P.add("act", (lambda e, LCend=LCend: e.activation(out=WCL[:, 0:nq], in_=LCend, func=AF.Exp, scale=-CDEC)),
                  r=kLC, w=[("WCL",)])
            P.add("dve", (lambda e, c=c: e.tensor_scalar(out=KK[:, 0:ntok], in0=KM[:, 0:ntok], scalar1=cvc(l, "k_k", c),
                                                         scalar2=None, op0=ALU.mult)),
                  r=kKM + [cvk(l, "k_k")], w=kKK)
            P.add("act", (lambda e: e.activation(out=SQ[0][:, 0:ntok], in_=KK[:, 0:ntok], func=AF.Square)),
                  r=kKK, w=[("SQ", 0)])
            P.add("pe", (lambda e: e.matmul(PS[6][:, 0:ntok], lhsT=BLKONES[:], rhs=SQ[0][:, 0:ntok], start=True,
                                            stop=True)), r=[("SQ", 0), ("BLKONES",)], w=psk(6))
            P.add("dve", (lambda e: e.tensor_scalar(out=T_[:, 0:ntok], in0=PS[6][:, 0:ntok], scalar1=1e-24, scalar2=None,
                                                    op0=ALU.max)), r=psk(6), w=kT)
            P.add("act", (lambda e: e.activation(out=T_[:, 0:ntok], in_=T_[:, 0:ntok], func=AF.Sqrt)), r=kT, w=kT)
            P.add("dve", (lambda e: e.reciprocal(out=T_[:, 0:ntok], in_=T_[:, 0:ntok])), r=kT, w=kT)
            P.add("dve", (lambda e: e.tensor_tensor(out=KK[:, 0:ntok], in0=KK[:, 0:ntok], in1=T_[:, 0:ntok],
                                                    op=ALU.mult)), r=kKK + kT, w=kKK)
            P.add("dve", (lambda e, c=c: e.tensor_scalar(out=T_[:, 0:ntok], in0=AL[:, 0:ntok], scalar1=-1.0,
                                                         scalar2=cvc(l, "k_a", c), op0=ALU.add, op1=ALU.mult)),
                  r=kAL + [cvk(l, "k_a")], w=kT)
            P.add("dve", (lambda e: e.scalar_tensor_tensor(out=KM[:, 0:ntok], in0=T_[:, 0:ntok], scalar=1.0,
                                                           in1=KM[:, 0:ntok], op0=ALU.add, op1=ALU.mult)),
                  r=kT + kKM, w=kKM)
            P.add("dve", (lambda e: e.tensor_tensor(out=AL[:, 0:ntok], in0=KK[:, 0:ntok], in1=AL[:, 0:ntok],
                                                    op=ALU.mult)), r=kKK + kAL, w=kAL)
            P.add("dve", (lambda e, c=c: e.scalar_tensor_tensor(out=SQ[1][:, 0:ntok], in0=RM[:, 0:ntok],
                                                                scalar=cvc(l, "r_k", c), in1=KM[:, 0:ntok],
                                                                op0=ALU.mult, op1=ALU.mult)),
                  r=kRM + kKM + [cvk(l, "r_k")], w=[("SQ", 1)])
            P.add("pe", (lambda e: e.matmul(PS[6][:, 0:ntok], lhsT=BLKONES[:], rhs=SQ[1][:, 0:ntok], start=True,
                                            stop=True)), r=[("SQ", 1), ("BLKONES",)], w=psk(6))
            P.add("act", (lambda e: e.activation(out=VB[:, 0:ntok], in_=VM[:, 0:ntok], func=AF.Copy)), r=kVM, w=kVB)
            P.add("dve", (lambda e: e.tensor_tensor(out=VM[:, 0:ntok], in0=PS[6][:, 0:ntok], in1=VM[:, 0:ntok],
                                                    op=ALU.mult)), r=psk(6) + kVM + kVB, w=kVM)
            P.add("act", (lambda e: e.activation(out=E0[:, 0:ntok], in_=LC[:, 0:ntok], func=AF.Exp, scale=-CDEC)),
                  r=kLC, w=kE0)
            P.add("dve", (lambda e: e.tensor_tensor(out=AR[:, 1, 0:ntok], in0=RM[:, 0:ntok], in1=E0[:, 0:ntok],
                                                    op=ALU.mult)), r=kRM + kE0, w=kAR)
            P.add("act", (lambda e: e.activation(out=E1[:, 0:ntok], in_=LC[:, 0:ntok], func=AF.Exp, scale=CDEC)),
                  r=kLC, w=kE1)
            P.add("dve", (lambda e: e.tensor_tensor(out=KT[:, 0:ntok], in0=KM[:, 0:ntok], in1=E1[:, 0:ntok],
                                                    op=ALU.mult)), r=kKM + kE1, w=kKT)
            P.add("dve", (lambda e: e.tensor_tensor(out=BT[:, 0:ntok], in0=AL[:, 0:ntok], in1=E1[:, 0:ntok],
                                                    op=ALU.mult)), r=kAL + kE1, w=kBT)
            P.add("act", (lambda e: e.activation(out=E0[:, 0:ntok], in_=SG[:, 0:ntok], func=AF.Exp, scale=-CDEC)),
                  r=kSG + kAR, w=kE0)
            P.add("dve", (lambda e: e.scalar_tensor_tensor(out=AR[:, 0, 0:ntok], in0=KK[:, 0:ntok], scalar=-1.0,
                                                           in1=E0[:, 0:ntok], op0=ALU.mult, op1=ALU.mult)),
                  r=kKK + kE0, w=kAR)
            for q in range(nq):
                P.add("act", (lambda e, q=q: e.activation(out=BHF[:, q * C:(q + 1) * C], in_=BT[:, q * C:(q + 1) * C],
                                                          func=AF.Copy, scale=WCL[:, q:q + 1])),
                      r=kBT + [("WCL",)], w=kBHF)
                P.add("act", (lambda e, q=q: e.activation(out=KHF[:, q * C:(q + 1) * C], in_=KT[:, q * C:(q + 1) * C],
                                                          func=AF.Copy, scale=WCL[:, q:q + 1])),
                      r=kKT + [("WCL",)], w=kKHF)
            for q in range(nq):
                for j, (src, ksrc) in enumerate([(VB, kVB), (BHF, kBHF), (KHF, kKHF)]):
                    P.add("pe", (lambda e, q=q, j=j, src=src: e.transpose(
                        PSB[0:C, j * 128:(j + 1) * 128], src[:, q * C:(q + 1) * C], IDB[:, :])),
                        r=ksrc + [("IDB",)], w=[("PSB",)])
                P.add("dve", (lambda e, q=q: e.tensor_copy(
                    out=TOK[0:C, q, :, :], in_=PSB[0:C, 0:384].rearrange("p (a b) -> p a b", a=3))),
                    r=[("PSB",)], w=[("TOK", q)])
            if stop > 9:
                wkv_pair(l, c, C, nq, ti, AR, kAR, BT, kBT, KT, kKT, VM, kVM, ya_ap, ya_keys)
        if stop <= 10:
            return
        ZA0 = 16384

        def za_ap(c):
            return hbf(ZA0 + c * ntok * 2, ntok)

        def za_keys(c):
            return hid_keys(ZA0 + c * ntok * 2, ntok * 2)
        woa = W["w_o_a"][l].rearrange("(kc p) n -> p kc n", p=128)
        for blk in range(4):
            vo, ko = load_w(woa[:, :, blk * 512:(blk + 1) * 512], KC, 512)
            cg = P_RWKV + 2 * D + blk * 512
            vg, kg = load_w(win[:, :, cg:cg + 512], KC, 512)
            for m in range(4):
                c = blk * 4 + m
                ba, bb = 2 * (c % 2), 2 * (c % 2) + 1
                for kc in range(KC):
                    P.add("pe", (lambda e, kc=kc, m=m, vo=vo, ba=ba: e.matmul(
                        PS[ba][:, 0:ntok], lhsT=vo[:, kc, m * 128:(m + 1) * 128], rhs=ya_ap(kc),
                        start=(kc == 0), stop=(kc == KC - 1))),
                        r=ko + ya_keys(kc), w=psk(ba))
                proj16(bb, ntok, vg, kg, m * 128, m)
                sl = c % 2
                P.add("act", (lambda e, sl=sl, bb=bb, c=c: e.activation(
                    out=SIL[sl][:, 0:ntok], in_=PS[bb][:, 0:ntok], func=AF.Sigmoid, bias=cvc(l, "b_gate", c))),
                    r=psk(bb) + [cvk(l, "b_gate")], w=[("SIL", sl)])
                P.add("dve", (lambda e, sl=sl, ba=ba, c=c: e.tensor_tensor(
                    out=za_ap(c), in0=PS[ba][:, 0:ntok], in1=SIL[sl][:, 0:ntok], op=ALU.mult)),
                    r=psk(ba) + [("SIL", sl)], w=za_keys(c))
        add_out_proj(ntok, l, za_ap, za_keys)
        if kind == "s" or ti["last"]:
            for s in range(nseg):
                vec = o_shift[kind][l, ti["sq"][s]]
                for (sbap, drap) in [
                        (SH[l][s][:, 0:48], vec[0:6144].rearrange("(c p) -> p c", p=128)),
                        (SH[l][s][0:96, 48:49], vec[6144:6240].rearrange("(c p) -> p c", p=96)),
                        (SH[l][s][0:96, 49:50], vec[6240:6336].rearrange("(c p) -> p c", p=96)),
                        (SH[l][s][:, 50:52], vec[6336:6592].rearrange("(c p) -> p c", p=128))]:
                    P.add("sp", (lambda e, sbap=sbap, drap=drap: e.dma_start(out=drap, in_=sbap)),
                          r=[("SH", l, s, g) for g in range(NRG)], dma=True)

    def wkv_pair(l, c, C, nq, ti, AR, kAR, BT, kBT, KT, kKT, VM, kVM, ya_ap, ya_keys):
        kind = ti["kind"]
        nit = int(np.log2(C)) - 1
        kST = ("ST", l, c)
        kSBD = ("SBD", c)
        CB = [0, 1, 3, 4]
        groups = [list(range(g0, min(g0 + 2, nq))) for g0 in range(0, nq, 2)]

        def state_init(q):
            fresh = (kind == "p" and ti["first"] and q == 0)
            if fresh:
                P.add("dve", (lambda e: e.memset(ST[l][:, c, :], 0.0)), w=[kST])
                P.add("dve", (lambda e: e.memset(SBD[:, c, :], 0.0)), w=[kSBD])
            elif kind == "s":
                b = ti["sq"][q]
                P.add("sp", (lambda e, b=b: e.dma_start(
                    out=STG[:, :], in_=st_wkv[l, b, 2 * c:2 * c + 2].rearrange("h i j -> (h i) j"))),
                    w=[("STG",)], dma=True)
                for h in range(2):
                    P.add("dve", (lambda e, h=h: e.tensor_copy(out=SBLK[h * 64:(h + 1) * 64, h * 64:(h + 1) * 64],
                                                               in_=STG[h * 64:(h + 1) * 64, :])),
                          r=[("STG",)], w=[("SBLK",)])
                P.add("pe", (lambda e: e.transpose(PS[2][:, 0:128], SBLK[:, :], IDF[:, :])),
                      r=[("SBLK",), ("IDF",)], w=psk(2))
                for h in range(2):
                    P.add("dve", (lambda e, h=h: e.tensor_copy(out=ST[l][h * 64:(h + 1) * 64, c, :],
                                                               in_=PS[2][h * 64:(h + 1) * 64, h * 64:(h + 1) * 64])),
                          r=psk(2), w=[kST])
                P.add("dve", (lambda e: e.tensor_copy(out=SBD[:, c, :], in_=PS[2][:, 0:128])),
                      r=psk(2), w=[kSBD])
            elif q == 0:
                P.add("dve", (lambda e: e.memset(SBD[:, c, :], 0.0)), w=[kSBD])
                for h in range(2):
                    P.add("act", (lambda e, h=h: e.activation(out=SBD[h * 64:(h + 1) * 64, c, h * 64:(h + 1) * 64],
                                                              in_=ST[l][h * 64:(h + 1) * 64, c, :], func=AF.Copy)),
                          r=[kST], w=[kSBD])

        def state_out(q):
            b = ti["sq"][q] if kind == "s" else ti["sq"][0]
            for h in range(2):
                hs = slice(h * 64, (h + 1) * 64)
                P.add("dve", (lambda e, h=h, hs=hs: e.tensor_copy(out=SBLK[hs, h * 64:(h + 1) * 64], in_=ST[l][hs, c, :])),
                      r=[kST], w=[("SBLK",)])
            P.add("pe", (lambda e: e.transpose(PS[2][:, 0:128], SBLK[:, :], IDF[:, :])), r=[("SBLK",), ("IDF",)],
                  w=psk(2))
            for h in range(2):
                P.add("dve", (lambda e, h=h: e.tensor_copy(out=OSTG[h * 64:(h + 1) * 64, :],
                                                           in_=PS[2][h * 64:(h + 1) * 64, h * 64:(h + 1) * 64])),
                      r=psk(2), w=[("OSTG",)])
            P.add("sp", (lambda e, b=b: e.dma_start(
                out=o_wkv[kind][l, b, 2 * c:2 * c + 2].rearrange("h i j -> (h i) j"), in_=OSTG[:, :])),
                r=[("OSTG",)], dma=True)

        for grp in groups:
            chains = [(q, h) for q in grp for h in range(2)]
            for i, (q, h) in enumerate(chains):
                cs = slice(q * C, (q + 1) * C)
                hs = slice(h * 64, (h + 1) * 64)
                bk = CB[i]
                psA = PS[bk][0:C, 0:2 * C].rearrange("p (a t) -> p a t", a=2)
                psB = PS[bk][0:C, 256:256 + 2 * C].rearrange("p (a t) -> p a t", a=2)
                P.add("pe", (lambda e, hs=hs, cs=cs, psA=psA: e.matmul(psA, lhsT=BT[hs, cs], rhs=AR[hs, :, cs], start=True,
                                                                       stop=True)), r=kBT + kAR, w=psk(bk))
                P.add("pe", (lambda e, hs=hs, cs=cs, psB=psB: e.matmul(psB, lhsT=KT[hs, cs], rhs=AR[hs, :, cs], start=True,
                                                                       stop=True)), r=kKT + kAR, w=psk(bk))
            for i, (q, h) in enumerate(chains):
                bk = CB[i]
                psA = PS[bk][0:C, 0:2 * C].rearrange("p (a t) -> p a t", a=2)
                psB = PS[bk][0:C, 256:256 + 2 * C].rearrange("p (a t) -> p a t", a=2)
                P.add("dve", (lambda e, i=i, psA=psA: e.tensor_tensor(out=MA[i][0:C, :, 0:C], in0=psA,
                                                                      in1=MASK2[0:C, :, 0:C], op=ALU.mult)),
                      r=psk(bk) + [("MASK2",)], w=[("MA", i)])
                P.add("dve", (lambda e, i=i, psB=psB: e.tensor_tensor(out=MB[i][0:C, :, 0:C], in0=psB,
                                                                      in1=MASK2[0:C, :, 0:C], op=ALU.mult)),
                      r=psk(bk) + [("MASK2",)], w=[("MB", i)])
            for i, (q, h) in enumerate(chains):
                cs = slice(q * C, (q + 1) * C)
                hs = slice(h * 64, (h + 1) * 64)
                bk = CB[i]
                P.add("pe", (lambda e, hs=hs, cs=cs, bk=bk: e.matmul(PS[bk][0:C, 0:C], lhsT=AR[hs, 0, cs], rhs=BT[hs, cs],
                                                                     start=True, stop=True)), r=kBT + kAR, w=psk(bk))
            for i, (q, h) in enumerate(chains):
                bk = CB[i]
                P.add("dve", (lambda e, i=i, bk=bk: e.tensor_tensor(out=MC[i][0:C, 0:C], in0=PS[bk][0:C, 0:C],
                                                                    in1=MASKL[0:C, 0:C], op=ALU.mult)),
                      r=psk(bk) + [("MASKL",)], w=[("MC", i)])
                P.add("dve", (lambda e, i=i: e.tensor_tensor(out=PPT[i][1][0:C, 0:C], in0=MA[i][0:C, 0, 0:C],
                                                             in1=IDB[0:C, 0:C], op=ALU.add)),
                      r=[("MA", i), ("IDB",)], w=[("PPT", i, 1)])
            nch = len(chains)
            Pm = [MA[i][0:C, 0, 0:C] for i in range(nch)]
            PTm = [MC[i][0:C, 0:C] for i in range(nch)]
            kPm = [[("MA", i), ("MC", i)] for i in range(nch)]
            cur = [1] * nch
            for s in range(nit + 1):
                do_sq = s < nit
                do_t = s >= 1
                for i in range(nch):
                    bk = CB[i]
                    if do_sq:
                        if s < nit - 1:
                            P.add("pe", (lambda e, bk=bk, a=PTm[i], b_=Pm[i]: e.matmul(
                                PS[bk][0:C, 2 * C:3 * C], lhsT=a, rhs=b_, start=True, stop=True)),
                                r=kPm[i], w=psk(bk))
                        P.add("pe", (lambda e, bk=bk, a=Pm[i], b_=PTm[i]: e.matmul(
                            PS[bk][0:C, 3 * C:4 * C], lhsT=a, rhs=b_, start=True, stop=True)),
                            r=kPm[i], w=psk(bk))
                    if do_t:
                        Tc = PPT[i][cur[i]][0:C, 0:C]
                        kTc = [("PPT", i, cur[i])]
                        P.add("pe", (lambda e, bk=bk, a=PTm[i], Tc=Tc: e.matmul(
                            PS[bk][0:C, C:2 * C], lhsT=a, rhs=Tc, start=True, stop=False)),
                            r=kPm[i] + kTc, w=psk(bk))
                        P.add("pe", (lambda e, bk=bk, Tc=Tc: e.matmul(
                            PS[bk][0:C, C:2 * C], lhsT=IDB[0:C, 0:C], rhs=Tc, start=False, stop=True)),
                            r=kTc + [("IDB",)], w=psk(bk))
                for i in range(nch):
                    bk = CB[i]
                    if s == 0:
                        nt = cur[i]
                        P.add("dve", (lambda e, i=i, bk=bk, nt=nt: e.tensor_copy(
                            out=PPT[i][nt][0:C, C:3 * C], in_=PS[bk][0:C, 2 * C:4 * C])),
                            r=psk(bk), w=[("PPT", i, nt)])
                    else:
                        nt = 1 - cur[i]
                        w_ = 3 * C if do_sq else C
                        P.add("dve", (lambda e, i=i, bk=bk, nt=nt, w_=w_: e.tensor_copy(
                            out=PPT[i][nt][0:C, 0:w_], in_=PS[bk][0:C, C:C + w_])),
                            r=psk(bk), w=[("PPT", i, nt)])
                        cur[i] = nt
                    if do_sq:
                        Pm[i], PTm[i] = PPT[i][cur[i]][0:C, C:2 * C], PPT[i][cur[i]][0:C, 2 * C:3 * C]
                        kPm[i] = [("PPT", i, cur[i])]
            for gi, q in enumerate(grp):
                cs = slice(q * C, (q + 1) * C)
                state_init(q)
                VTq = TOK[0:C, q, 0, :]
                BHq = TOK[0:C, q, 1, :]
                KHq = TOK[0:C, q, 2, :]
                kTOK = [("TOK", q)]
                ci = [2 * gi, 2 * gi + 1]
                P.add("pe", (lambda e, cs=cs: e.matmul(PS[2][0:C, 0:128], lhsT=AR[:, 0, cs], rhs=SBD[:, c, :], start=True,
                                                       stop=False)), r=kAR + [kSBD], w=psk(2))
                for h in range(2):
                    hc = slice(h * 64, (h + 1) * 64)
                    P.add("pe", (lambda e, h=h, hc=hc, i=ci[h], VTq=VTq: e.matmul(
                        PS[2][0:C, hc], lhsT=MB[i][0:C, 0, 0:C], rhs=VTq[:, hc], start=False, stop=(h == 1))),
                        r=[("MB", ci[h])] + kTOK, w=psk(2))
                P.add("dve", (lambda e: e.tensor_copy(out=XT[0:C, :], in_=PS[2][0:C, 0:128])), r=psk(2), w=[("XT",)])
                for h in range(2):
                    hc = slice(h * 64, (h + 1) * 64)
                    Tf = PPT[ci[h]][cur[ci[h]]][0:C, 0:C]
                    P.add("pe", (lambda e, h=h, hc=hc, Tf=Tf: e.matmul(PS[2][0:C, 128 + h * 64:128 + (h + 1) * 64], lhsT=Tf,
                                                                       rhs=XT[0:C, hc], start=True, stop=True)),
                          r=[("PPT", ci[h], cur[ci[h]]), ("XT",)], w=psk(2))
                P.add("dve", (lambda e: e.tensor_copy(out=UT[0:C, :], in_=PS[2][0:C, 128:256])), r=psk(2), w=[("UT",)])
                P.add("pe", (lambda e, cs=cs: e.matmul(PS[6][0:C, 0:128], lhsT=AR[:, 1, cs], rhs=SBD[:, c, :], start=True,
                                                       stop=False)), r=kAR + [kSBD], w=psk(6))
                for h in range(2):
                    hc = slice(h * 64, (h + 1) * 64)
                    P.add("pe", (lambda e, h=h, hc=hc, i=ci[h]: e.matmul(PS[6][0:C, hc], lhsT=MA[i][0:C, 1, 0:C],
                                                                         rhs=UT[0:C, hc], start=False, stop=False)),
                          r=[("MA", ci[h]), ("UT",)], w=psk(6))
                    P.add("pe", (lambda e, h=h, hc=hc, i=ci[h], VTq=VTq: e.matmul(
                        PS[6][0:C, hc], lhsT=MB[i][0:C, 1, 0:C], rhs=VTq[:, hc], start=False, stop=(h == 1))),
                        r=[("MB", ci[h])] + kTOK, w=psk(6))
                P.add("pe", (lambda e, BHq=BHq: e.matmul(PS[6][:, 128:256], lhsT=BHq, rhs=UT[0:C, :], start=True,
                                                         stop=False)), r=kTOK + [("UT",)], w=psk(6))
                P.add("pe", (lambda e, KHq=KHq, VTq=VTq: e.matmul(PS[6][:, 128:256], lhsT=KHq, rhs=VTq, start=False,
                                                                  stop=True)), r=kTOK, w=psk(6))
                for h in range(2):
                    hs = slice(h * 64, (h + 1) * 64)
                    P.add("dve", (lambda e, h=h, hs=hs, q=q: e.scalar_tensor_tensor(
                        out=ST[l][hs, c, :], in0=ST[l][hs, c, :], scalar=WCL[hs, q:q + 1],
                        in1=PS[6][hs, 128 + h * 64:128 + (h + 1) * 64], op0=ALU.mult, op1=ALU.add)),
                        r=[kST, ("WCL",)] + psk(6) + [kSBD], w=[kST])
                    P.add("act", (lambda e, h=h, hs=hs: e.activation(out=SBD[hs, c, h * 64:(h + 1) * 64],
                                                                     in_=ST[l][hs, c, :], func=AF.Copy)),
                          r=[kST], w=[kSBD])
                P.add("dve", (lambda e, q=q: e.tensor_copy(out=YS[0:C, q, :], in_=PS[6][0:C, 0:128])), r=psk(6),
                      w=[("YS", q)])
                if (kind == "s") or (ti["last"] and q == nq - 1):
                    state_out(q)
            for q in grp:
                cs = slice(q * C, (q + 1) * C)
                for h in range(2):
                    P.add("dve", (lambda e, h=h, q=q: e.bn_stats(out=BNS[0:C, h, :], in_=YS[0:C, q, h * 64:(h + 1) * 64])),
                          r=[("YS", q)], w=[("BNS", h)])
                    P.add("dve", (lambda e, h=h: e.bn_aggr(out=MV[0:C, h, :], in_=BNS[0:C, h, :])),
                          r=[("BNS", h)], w=[("MV", h)])
                P.add("act", (lambda e: e.activation(out=RS2[0:C, :], in_=MV[0:C, :, 1], func=AF.Sqrt, bias=EPS_GN)),
                      r=[("MV", 0), ("MV", 1)], w=[("RS2",)])
                P.add("dve", (lambda e: e.reciprocal(out=RS2[0:C, :], in_=RS2[0:C, :])), r=[("RS2",)], w=[("RS2",)])
                for h in range(2):
                    P.add("dve", (lambda e, h=h, q=q: e.tensor_scalar(
                        out=YN[0:C, h * 64:(h + 1) * 64], in0=YS[0:C, q, h * 64:(h + 1) * 64],
                        scalar1=MV[0:C, h, 0:1], scalar2=RS2[0:C, h:h + 1], op0=ALU.subtract, op1=ALU.mult)),
                        r=[("YS", q), ("MV", h), ("RS2",)], w=[("YN",)])
                P.add("pe", (lambda e: e.transpose(PS[2][:, 256:256 + C], YN[0:C, :], IDF[0:C, 0:C])),
                      r=[("YN",), ("IDF",)], w=psk(2))
                P.add("act", (lambda e: e.activation(out=Y1[:, 0:C], in_=PS[2][:, 256:256 + C], func=AF.Identity,
                                                     scale=cvc(l, "gn_g", c), bias=cvc(l, "gn_b", c))),
                      r=psk(2) + [cvk(l, "gn_g"), cvk(l, "gn_b")], w=[("Y1",)])
                P.add("dve", (lambda e, cs=cs: e.tensor_tensor(out=Y1[:, 0:C], in0=Y1[:, 0:C], in1=VM[:, cs], op=ALU.add)),
                      r=[("Y1",)] + kVM, w=[("Y1",)])
                P.add("dve", (lambda e, cs=cs: e.tensor_tensor(out=ya_ap(c)[:, cs], in0=Y1[:, 0:C], in1=PS[5][:, cs],
                                                               op=ALU.mult)),
                      r=[("Y1",)] + psk(5), w=ya_keys(c))

    def final_and_store(ydst_rows, ntok):
        gidx = nv_idx[("final_norm", 0)]
        rms_stats(ntok)
        YF = HIDf[:, 0:KC * TT].rearrange("p (c t) -> p c t", c=KC)
        for c in range(KC):
            P.add("dve", (lambda e, c=c: e.scalar_tensor_tensor(
                out=YF[:, c, 0:ntok], in0=X[:, c, 0:ntok], scalar=NV[:, gidx, c:c + 1],
                in1=RSTD[:, 0:ntok], op0=ALU.mult, op1=ALU.mult)),
                r=[("X", c), ("RSTD",), ("NV", gidx)], w=hid_keys(c * 2048, 2048))
        IO = HIDf[:, KC * TT:KC * TT + D]
        for tb, (dst, n) in enumerate(ydst_rows):
            for cg in range(4):
                bank = cg % 2
                for cc in range(4):
                    c = cg * 4 + cc
                    P.add("pe", (lambda e, c=c, cc=cc, tb=tb, n=n, bank=bank: e.transpose(
                        PS[bank][0:n, cc * 128:(cc + 1) * 128], YF[:, c, tb * 128:tb * 128 + n], IDF[:, :])),
                        r=hid_keys(c * 2048, 2048) + [("IDF",)], w=psk(bank))
                P.add("act" if cg % 2 else "dve",
                      (lambda e, cg=cg, n=n, bank=bank: (
                          e.activation(out=IO[0:n, cg * 512:(cg + 1) * 512], in_=PS[bank][0:n, :], func=AF.Copy)
                          if cg % 2 else
                          e.tensor_copy(out=IO[0:n, cg * 512:(cg + 1) * 512], in_=PS[bank][0:n, :]))),
                      r=psk(bank), w=hid_keys(32768 + cg * 2048, 2048))
            P.add("sp", (lambda e, dst=dst, n=n: e.dma_start(out=dst, in_=IO[0:n, :])),
                  r=hid_keys(32768, 8192), dma=True)

    tiles = []
    for sq in range(n_pseq):
        nt = plen // TT
        for it in range(nt):
            tiles.append(dict(kind="p", sq=[sq], t0=it * TT, ntok=TT, segs=[(0, TT)], C=128, nq=TT // 128,
                              first=(it == 0), last=(it == nt - 1)))
    if n_sseq and not cfg.get("nos", 0):
        tiles.append(dict(kind="s", sq=list(range(n_sseq)), t0=0, ntok=n_sseq * slen,
                          segs=[(s * slen, slen) for s in range(n_sseq)], C=slen, nq=n_sseq, first=True, last=True))

    for tnum, ti in enumerate(tiles):
        ntok = ti["ntok"]
        wc_begin_tile(tnum == 0)
        if ti["kind"] == "p":
            sq, t0 = ti["sq"][0], ti["t0"]
            rows = [(x_p[sq, t0 + tb * 128:t0 + (tb + 1) * 128, :], 128) for tb in range(TT // 128)]
            orow = [(y_p[sq, t0 + tb * 128:t0 + (tb + 1) * 128, :], 128) for tb in range(TT // 128)]
        else:
            xs2 = x_s.rearrange("b t d -> (b t) d")
            ys2 = y_s.rearrange("b t d -> (b t) d")
            rows, orow = [], []
            for r0 in range(0, ntok, 128):
                n = min(128, ntok - r0)
                rows.append((xs2[r0:r0 + n, :], n))
                orow.append((ys2[r0:r0 + n, :], n))
        load_x(rows, ntok)
        for l in range(L):
            if "ffn1" in phases:
                gi = nv_idx[("ffn1_norm", l)]
                rmsnorm(ntok, (lambda c, gi=gi: NV[:, gi, c:c + 1]), ("NV", gi))
                ffn(ntok, l, "ffn1")
            if "mix" in phases:
                mix(l, ti)
            if "ffn2" in phases:
                gi = nv_idx[("ffn2_norm", l)]
                rmsnorm(ntok, (lambda c, gi=gi: NV[:, gi, c:c + 1]), ("NV", gi))
                ffn(ntok, l, "ffn2")
        final_and_store(orow, ntok)

    P.finalize()
    sems = {nm: st.enter_context(nc.semaphore(nm)) for nm in P.sem_names()}
    with nc.allow_non_contiguous_dma(reason="per-feature vectors / small state layouts"):
        with nc.Block() as block:
            @block.tensor
            def _(eng):
                P.emit("pe", eng, sems)

            @block.scalar
            def _(eng):
                P.emit("act", eng, sems)

            @block.vector
            def _(eng):
                P.emit("dve", eng, sems)

            @block.gpsimd
            def _(eng):
                P.emit("pool", eng, sems)

            @block.sync
            def _(eng):
                P.emit("sp", eng, sems, final_wait=True)
    st.close()
    return nc, P


FULL_CFG = dict(n_pseq=2, plen=2048, n_sseq=2, slen=64, depth=2)
_CACHE = {}


def kernel(**inputs):
    cfg = dict(FULL_CFG)
    if "nc" not in _CACHE:
        _CACHE["nc"] = build_program(cfg)[0]
    nc = _CACHE["nc"]
    n = 8
    f = lambda a: np.ascontiguousarray(a, dtype=np.float32)
    shared = {nm: f(inputs[nm]) for nm, _ in WNAMES if nm != "r_k"}
    shared["r_k"] = f(inputs["r_k"]).reshape(2, D)
    shared["final_norm"] = f(inputs["final_norm"]).reshape(1, D)
    xp, xs = f(inputs["x_prompt"]), f(inputs["x_sample"])
    swkv, sshift, sconv = f(inputs["state_wkv"]), f(inputs["state_shift"]), f(inputs["state_conv"])
    in_maps = []
    for i in range(n):
        m = dict(shared)
        m["x_p"] = xp[2 * i:2 * i + 2]
        m["x_s"] = xs[2 * i:2 * i + 2]
        m["state_wkv"] = f(swkv[:, 2 * i:2 * i + 2])
        m["state_shift"] = f(sshift[:, 2 * i:2 * i + 2])
        m["state_conv"] = f(sconv[:, 2 * i:2 * i + 2])
        in_maps.append(m)
    res = run_bass_kernel_spmd(nc, in_maps, core_ids=list(range(n)))
    cat0 = lambda k: np.concatenate([r[k] for r in res.results], axis=0)
    cat1 = lambda k: np.concatenate([r[k] for r in res.results], axis=1)
    return (cat0("y_p"), cat0("y_s"), cat1("wkv_p"), cat1("shift_p"), cat1("conv_p"),
            cat1("wkv_s"), cat1("shift_s"), cat1("conv_s"))
```

```python
import contextlib
import numpy as np
import concourse.bass as bass
import concourse.mybir as mybir
from concourse.bass_utils import run_bass_kernel_spmd

F32 = mybir.dt.float32
BF16 = mybir.dt.bfloat16
ALU = mybir.AluOpType
AF = mybir.ActivationFunctionType

D = 2048
KC = 16
FF = 5632
FC = 44
NH = 32
HD = 64
R_W, R_A, R_G = 96, 96, 256
P_RWKV = 3 * D + R_W + R_A + R_G
P_TOT = P_RWKV + 2 * D + 2 * D
CONV_K = 31
EPS_RMS = 1e-6
EPS_LN = 1e-5
EPS_GN = 64e-5

ENGS = ("pe", "act", "dve", "pool", "sp")


class Op:
    __slots__ = ("eng", "fn", "deps", "dma", "sem", "sig", "sigval", "gidx")


class Prog:
    def __init__(self):
        self.ops = {e: [] for e in ENGS}
        self.lastw = {}
        self.readers = {}
        self.n = 0
        self.dma_slots = {"sp": ["d_sp%d" % i for i in range(8)],
                          "pool": ["d_pl%d" % i for i in range(8)],
                          "act": ["d_ac%d" % i for i in range(4)]}
        self.dma_rr = {"sp": 0, "pool": 0, "act": 0}
        self.slot_last = {}
        self.slot_count = {}

    def add(self, eng, fn, r=(), w=(), dma=False):
        op = Op()
        op.eng, op.fn, op.dma, op.sig, op.sigval = eng, fn, dma, False, 0
        op.gidx = self.n
        self.n += 1
        deps = {}
        for k in r:
            lw = self.lastw.get(k)
            if lw is not None:
                deps[id(lw)] = lw
        for k in w:
            lw = self.lastw.get(k)
            if lw is not None:
                deps[id(lw)] = lw
            rd = self.readers.get(k)
            if rd:
                for o in rd[0].values():
                    deps[id(o)] = o
                for o in rd[1]:
                    deps[id(o)] = o
        if dma:
            slots = self.dma_slots[eng]
            s = slots[self.dma_rr[eng] % len(slots)]
            self.dma_rr[eng] += 1
            op.sem = s
            prev = self.slot_last.get(s)
            if prev is not None:
                deps[id(prev)] = prev
            self.slot_last[s] = op
            self.slot_count[s] = self.slot_count.get(s, 0) + 1
            op.sigval = 16 * self.slot_count[s]
            op.sig = True
        else:
            op.sem = "c_" + eng
        for k in w:
            self.lastw[k] = op
            self.readers[k] = ({}, [])
        for k in r:
            rd = self.readers.get(k)
            if rd is None:
                rd = ({}, [])
                self.readers[k] = rd
            if dma:
                rd[1].append(op)
            else:
                rd[0][eng] = op
        deps.pop(id(op), None)
        dl = []
        for d in deps.values():
            if (not d.dma) and (not dma) and d.eng == eng and eng == "pe":
                continue
            dl.append(d)
        op.deps = dl
        self.ops[eng].append(op)
        return op

    def finalize(self):
        for e in ENGS:
            for op in self.ops[e]:
                for d in op.deps:
                    d.sig = True
        for e in ENGS:
            c = 0
            for op in self.ops[e]:
                if op.dma:
                    continue
                if op.sig:
                    c += 1
                    op.sigval = c

    def sem_names(self):
        names = ["c_" + e for e in ("pe", "act", "dve", "pool")]
        for e in ("sp", "pool", "act"):
            names += self.dma_slots[e]
        return names

    def emit(self, e, eng, sems, final_wait=False):
        waited = {}
        for op in self.ops[e]:
            for d in op.deps:
                if waited.get(d.sem, 0) < d.sigval:
                    eng.wait_ge(sems[d.sem], d.sigval)
                    waited[d.sem] = d.sigval
            inst = op.fn(eng)
            if op.dma:
                inst.then_inc(sems[op.sem], 16)
            elif op.sig:
                inst.then_inc(sems[op.sem], 1)
        if final_wait:
            for s, c in self.slot_count.items():
                if waited.get(s, 0) < 16 * c:
                    eng.wait_ge(sems[s], 16 * c)


RG = [(g * 128, 128) for g in range(48)] + [(6144, 96), (6240, 96), (6336, 128), (6464, 128)]
NRG = len(RG)
CDEC = float(np.exp(-0.5))
WNAMES = [("ffn1_norm", [D]), ("ffn1_w1", [D, FF]), ("ffn1_w3", [D, FF]), ("ffn1_w2", [FF, D]),
          ("mix_norm", [D]), ("w_in", [D, P_TOT]), ("mu_shift", [P_RWKV]), ("w0", [D]),
          ("w_up", [R_W, D]), ("a0", [D]), ("a_up", [R_A, D]), ("g_up", [R_G, D]), ("k_k", [D]),
          ("k_a", [D]), ("r_k", [D]), ("gn_g", [D]), ("gn_b", [D]), ("w_o_a", [D, D]),
          ("b_conv_in", [2 * D]), ("conv_w", [CONV_K, D]), ("conv_b", [D]), ("conv_ln_g", [D]),
          ("conv_ln_b", [D]), ("w_o_c", [D, D]), ("b_o_c", [D]), ("b_gate", [2 * D]),
          ("w_out", [D, D]), ("ffn2_norm", [D]), ("ffn2_w1", [D, FF]), ("ffn2_w3", [D, FF]),
          ("ffn2_w2", [FF, D])]


def build_program(cfg):
    n_pseq, plen, n_sseq, slen = cfg["n_pseq"], cfg["plen"], cfg["n_sseq"], cfg["slen"]
    L = cfg["depth"]
    phases = cfg.get("phases", ("ffn1", "mix", "ffn2"))
    TT = 512
    NSEG = max(1, n_sseq)

    nc = bass.Bass("TRN2", target_bir_lowering=False)

    def din(name, shape):
        return nc.dram_tensor(name, list(shape), F32, kind="ExternalInput").ap()

    def dout(name, shape):
        return nc.dram_tensor(name, list(shape), F32, kind="ExternalOutput").ap()

    x_p = din("x_p", [n_pseq, plen, D])
    x_s = din("x_s", [max(1, n_sseq), slen, D])
    st_wkv = din("state_wkv", [L, max(1, n_sseq), NH, HD, HD])
    st_shift = din("state_shift", [L, max(1, n_sseq), P_RWKV])
    st_conv = din("state_conv", [L, max(1, n_sseq), CONV_K - 1, D])
    W = {}
    for nm, shp in WNAMES:
        W[nm] = din(nm, [L] + shp)
    W["final_norm"] = din("final_norm", [1, D])
    y_p = dout("y_p", [n_pseq, plen, D])
    y_s = dout("y_s", [max(1, n_sseq), slen, D])
    o_wkv = {"p": dout("wkv_p", [L, n_pseq, NH, HD, HD]), "s": dout("wkv_s", [L, max(1, n_sseq), NH, HD, HD])}
    o_shift = {"p": dout("shift_p", [L, n_pseq, P_RWKV]), "s": dout("shift_s", [L, max(1, n_sseq), P_RWKV])}
    o_conv = {"p": dout("conv_p", [L, n_pseq, CONV_K - 1, D]),
              "s": dout("conv_s", [L, max(1, n_sseq), CONV_K - 1, D])}

    P = Prog()
    st = contextlib.ExitStack()

    def sb(name, shape, dt):
        return st.enter_context(nc.sbuf_tensor(name, list(shape), dt))

    X = sb("X", [128, KC, TT], F32)
    H = sb("H", [128, KC, TT], BF16)
    HID = sb("HID", [128, FC * TT], BF16)
    HIDf = HID.bitcast(F32)
    NWU = 48
    WBUF = sb("WBUF", [128, NWU * 512], BF16)
    SQ = [sb("SQ%d" % i, [128, TT], BF16) for i in range(2)]
    SIL = [sb("SIL%d" % i, [128, TT], F32) for i in range(2)]
    RSTD = sb("RSTD", [128, TT], F32)
    TMPF = [sb("TMPF%d" % i, [128, TT], F32) for i in range(2)]
    ONESB = sb("ONESB", [128, 128], BF16)
    ONESF = sb("ONESF", [128, 128], F32)
    IDF = sb("IDF", [128, 128], F32)
    IDB = sb("IDB", [128, 128], BF16)
    BLKONES = sb("BLKONES", [128, 128], BF16)
    MASK2 = sb("MASK2", [128, 4, 128], F32)
    MASKL = sb("MASKL", [128, 128], F32)
    NV = sb("NV", [128, 2 * L + 1, KC], F32)
    PS = [st.enter_context(nc.psum_tensor("PS%d" % i, [128, 512], F32)) for i in range(7)]
    PSB = st.enter_context(nc.psum_tensor("PSB", [128, 1024], BF16))

    cvcols = {}
    off = 0
    for nm, n in [("mix_norm", 16), ("mu", NRG), ("omu", NRG), ("w0", 16), ("a0", 16), ("k_k", 16), ("k_a", 16),
                  ("r_k", 16), ("gn_g", 16), ("gn_b", 16), ("b_ci", 32), ("conv_w", 31 * 16), ("conv_b", 16),
                  ("ln_g", 16), ("ln_b", 16), ("b_oc", 16), ("b_gate", 32)]:
        cvcols[nm] = off
        off += n
    NCV = off
    CV = [sb("CV%d" % l, [128, NCV], F32) for l in range(L)]
    ST = [sb("ST%d" % l, [128, 16, 64], F32) for l in range(L)]
    SBD = sb("SBD", [128, 16, 128], BF16)
    SBLK = sb("SBLK", [128, 128], F32)
    STG = sb("STG", [128, 64], F32)
    OSTG = sb("OSTG", [128, 64], F32)
    SH = [[sb("SH%d_%d" % (l, s), [128, NRG], F32) for s in range(NSEG)] for l in range(L)]
    CT = [[sb("CT%d_%d" % (l, s), [128, 16, 30], BF16) for s in range(NSEG)] for l in range(L)]
    LORA = sb("LORA", [128, 4, TT], BF16)
    LUB = [sb("LUB%d" % i, [128, 512], BF16) for i in range(2)]
    TOK = sb("TOK", [128, 4, 3, 128], BF16)
    DG = sb("DG", [128, 8, 128], BF16)
    WCL = sb("WCL", [128, 4], F32)
    MAB = [sb("MAB%d" % h, [128, 4, 128], BF16) for h in range(4)]
    MA = [MAB[h][:, 0:2, :] for h in range(4)]
    MB = [MAB[h][:, 2:4, :] for h in range(4)]
    MC = [sb("MC%d" % h, [128, 128], BF16) for h in range(4)]
    PPT = [[sb("PPT%d_%d" % (h, i), [128, 384], BF16) for i in range(2)] for h in range(4)]
    YS = sb("YS", [128, 4, 128], F32)
    XT = sb("XT", [128, 128], BF16)
    UT = sb("UT", [128, 128], BF16)
    YN = sb("YN", [128, 128], F32)
    Y1 = sb("Y1", [128, 128], F32)
    BNS = sb("BNS", [128, 2, 6], F32)
    MV = sb("MV", [128, 2, 2], F32)
    RS2 = sb("RS2", [128, 2], F32)

    def hid_keys(b0, nbytes):
        return [("HID", i) for i in range(b0 // 1024, (b0 + nbytes + 1023) // 1024)]

    def psk(bank, c0=0, n=512):
        return [("PS", bank)]

    def hbf(b0, n):
        return HID[:, b0 // 2:b0 // 2 + n]

    def hf32(b0, n):
        return HIDf[:, b0 // 4:b0 // 4 + n]

    ring = [0]

    def walloc(nunits):
        if ring[0] + nunits > NWU:
            ring[0] = 0
        u0 = ring[0]
        ring[0] += nunits
        return u0, [("WB", u) for u in range(u0, u0 + nunits)]

    def wview(u0, kc, ncols):
        return WBUF[:, u0 * 512:u0 * 512 + kc * ncols].rearrange("p (kc n) -> p kc n", kc=kc)

    WC_TOTAL = L * (6 * D * FF + D * (P_TOT + 1024) + 5 * D * D)
    WC_CH = 60 * 1024 * 1024
    n_wc = (WC_TOTAL + WC_CH - 1) // WC_CH + 1
    wcaches = [nc.dram_tensor("wcache%d" % i_, [WC_CH], BF16).ap() for i_ in range(n_wc)]
    wc_state = {"off": 0, "idx": 0, "pass": 0, "tab": [], "t": 0}

    def wc_begin_tile(first):
        wc_state["idx"] = 0
        wc_state["pass"] = 0 if first else 1

    def load_w_multi(srcs, kc, ncols):
        nun = (kc * ncols + 511) // 512
        u0, keys = walloc(nun)
        v = wview(u0, kc, ncols)
        flat = WBUF[:, u0 * 512:u0 * 512 + kc * ncols]
        i = wc_state["idx"]
        wc_state["idx"] += 1
        if wc_state["pass"] == 0:
            if wc_state["off"] + 128 * kc * ncols > WC_CH:
                wc_state["t"] += 1
                wc_state["off"] = 0
            off = wc_state["off"]
            wc_state["off"] += 128 * kc * ncols
            wcache = wcaches[wc_state["t"]]
            wc_state["tab"].append((off, kc * ncols, wc_state["t"]))
            dst = wcache[off:off + 128 * kc * ncols].rearrange("(p n) -> p n", p=128)
            for (src3, c0) in srcs:
                w_ = src3.shape[2]
                P.add("pool", (lambda e, v=v, src3=src3, c0=c0, w_=w_: e.dma_start(out=v[:, :, c0:c0 + w_], in_=src3)),
                      w=keys, dma=True)
            P.add("sp", (lambda e, dst=dst, flat=flat: e.dma_start(out=dst, in_=flat)), r=keys, w=[("WC", i)], dma=True)
        else:
            off, n, t_ = wc_state["tab"][i]
            assert n == kc * ncols
            srcc = wcaches[t_][off:off + 128 * n].rearrange("(p n) -> p n", p=128)
            P.add("sp", (lambda e, srcc=srcc, flat=flat: e.dma_start(out=flat, in_=srcc)), r=[("WC", i)], w=keys,
                  dma=True)
        return v, keys

    def load_w(src3, kc, ncols):
        return load_w_multi([(src3, 0)], kc, ncols)

    def cvc(l, nm, i=0):
        c = cvcols[nm] + i
        return CV[l][:, c:c + 1]

    def cvk(l, nm):
        return ("CV", l, nm)

    P.add("dve", lambda e: e.memset(ONESB[:], 1.0), w=[("ONESB",)])
    P.add("dve", lambda e: e.memset(ONESF[:], 1.0), w=[("ONESF",)])
    P.add("pool", lambda e: e.memset(IDF[:], 1.0), w=[("IDF",)])
    P.add("pool", lambda e: e.affine_select(out=IDF[:], in_=IDF[:], pattern=[[-1, 128]],
                                            compare_op=ALU.is_equal, fill=0.0, base=0,
                                            channel_multiplier=1), r=[("IDF",)], w=[("IDF",)])
    P.add("dve", lambda e: e.tensor_copy(out=IDB[:], in_=IDF[:]), r=[("IDF",)], w=[("IDB",)])
    P.add("dve", lambda e: e.memset(BLKONES[:], 0.0), w=[("BLKONES",)])
    P.add("dve", lambda e: e.memset(BLKONES[0:64, 0:64], 1.0), w=[("BLKONES",)])
    P.add("dve", lambda e: e.memset(BLKONES[64:128, 64:128], 1.0), w=[("BLKONES",)])
    P.add("dve", lambda e: e.memset(SBLK[:], 0.0), w=[("SBLK",)])
    for i_ in range(2):
        P.add("dve", (lambda e, i_=i_: e.memset(LUB[i_][:], 0.0)), w=[("LUB", i_)])
    P.add("pool", lambda e: e.memset(MASK2[:], 1.0), w=[("MASK2",)])
    for a_ in range(4):
        P.add("pool", (lambda e, a_=a_: e.affine_select(
            out=MASK2[:, a_, :], in_=MASK2[:, a_, :], pattern=[[1, 128]],
            compare_op=(ALU.is_gt if a_ % 2 == 0 else ALU.is_ge), fill=0.0, base=0, channel_multiplier=-1)),
            r=[("MASK2",)], w=[("MASK2",)])
    P.add("pool", lambda e: e.memset(MASKL[:], 1.0), w=[("MASKL",)])
    P.add("pool", lambda e: e.affine_select(out=MASKL[:], in_=MASKL[:], pattern=[[-1, 128]],
                                            compare_op=ALU.is_gt, fill=0.0, base=0, channel_multiplier=1),
          r=[("MASKL",)], w=[("MASKL",)])

    def load_vec(dst_tile, col0, vec1d, n, key):
        src = vec1d.rearrange("(c p) -> p c", p=128)
        P.add("sp", (lambda e, src=src: e.dma_start(out=dst_tile[:, col0:col0 + n], in_=src)), w=[key], dma=True)

    def load_rg(dst_tile, col0, vec1d, key, store=False):
        parts = [(dst_tile[:, col0:col0 + 48], vec1d[0:6144].rearrange("(c p) -> p c", p=128)),
                 (dst_tile[0:96, col0 + 48:col0 + 49], vec1d[6144:6240].rearrange("(c p) -> p c", p=96)),
                 (dst_tile[0:96, col0 + 49:col0 + 50], vec1d[6240:6336].rearrange("(c p) -> p c", p=96)),
                 (dst_tile[:, col0 + 50:col0 + 52], vec1d[6336:6592].rearrange("(c p) -> p c", p=128))]
        for (sbap, drap) in parts:
            if store:
                P.add("sp", (lambda e, sbap=sbap, drap=drap: e.dma_start(out=drap, in_=sbap)), r=[key], dma=True)
            else:
                P.add("sp", (lambda e, sbap=sbap, drap=drap: e.dma_start(out=sbap, in_=drap)), w=[key], dma=True)

    nv_idx = {}
    i = 0
    for l in range(L):
        for nm in ("ffn1_norm", "ffn2_norm"):
            nv_idx[(nm, l)] = i
            load_vec(NV[:, i, :], 0, W[nm][l], 16, ("NV", i))
            i += 1
    nv_idx[("final_norm", 0)] = i
    load_vec(NV[:, i, :], 0, W["final_norm"][0], 16, ("NV", i))
    if "mix" in phases:
        for l in range(L):
            for nm, src, n in [("mix_norm", "mix_norm", 16), ("w0", "w0", 16), ("a0", "a0", 16), ("k_k", "k_k", 16),
                               ("k_a", "k_a", 16), ("r_k", "r_k", 16), ("gn_g", "gn_g", 16), ("gn_b", "gn_b", 16),
                               ("b_ci", "b_conv_in", 32), ("conv_b", "conv_b", 16), ("ln_g", "conv_ln_g", 16),
                               ("ln_b", "conv_ln_b", 16), ("b_oc", "b_o_c", 16), ("b_gate", "b_gate", 32)]:
                load_vec(CV[l], cvcols[nm], W[src][l], n, cvk(l, nm))
            for k in range(CONV_K):
                load_vec(CV[l], cvcols["conv_w"] + k * 16, W["conv_w"][l, k], 16, cvk(l, "conv_w"))
            P.add("dve", (lambda e, l=l: e.memset(CV[l][:, cvcols["mu"]:cvcols["mu"] + NRG], 0.0)), w=[cvk(l, "mu")])
            load_rg(CV[l], cvcols["mu"], W["mu_shift"][l], cvk(l, "mu"))
            P.add("dve", (lambda e, l=l: e.tensor_scalar(
                out=CV[l][:, cvcols["omu"]:cvcols["omu"] + NRG], in0=CV[l][:, cvcols["mu"]:cvcols["mu"] + NRG],
                scalar1=-1.0, scalar2=1.0, op0=ALU.mult, op1=ALU.add)), r=[cvk(l, "mu")], w=[cvk(l, "omu")])

    def load_x(xsrc_rows, ntok):
        nb = len(xsrc_rows)
        IO = HIDf[:, 0:nb * D].rearrange("p (a b) -> p a b", a=nb)
        for tb, (src, n) in enumerate(xsrc_rows):
            P.add("sp", (lambda e, tb=tb, src=src, n=n: e.dma_start(out=IO[0:n, tb, :], in_=src)),
                  w=hid_keys(tb * 8192, 8192), dma=True)
        for c in range(KC):
            bank = c % 2
            for tb, (src, n) in enumerate(xsrc_rows):
                P.add("pe", (lambda e, tb=tb, n=n, c=c, bank=bank: e.transpose(
                    PS[bank][:, tb * 128:tb * 128 + n], IO[0:n, tb, c * 128:(c + 1) * 128],
                    IDF[0:n, 0:n])),
                    r=hid_keys(tb * 8192, 8192) + [("IDF",)], w=psk(bank))
            P.add("act" if c % 2 else "dve",
                  (lambda e, c=c, bank=bank: (e.activation(out=X[:, c, 0:ntok], in_=PS[bank][:, 0:ntok],
                                                           func=AF.Copy)
                                              if c % 2 else
                                              e.tensor_copy(out=X[:, c, 0:ntok], in_=PS[bank][:, 0:ntok]))),
                  r=psk(bank), w=[("X", c)])

    def rms_stats(ntok):
        for c in range(KC):
            s = c % 2
            P.add("act", (lambda e, c=c, s=s: e.activation(out=SQ[s][:, 0:ntok], in_=X[:, c, 0:ntok],
                                                           func=AF.Square)),
                  r=[("X", c)], w=[("SQ", s)])
            P.add("pe", (lambda e, c=c, s=s: e.matmul(PS[6][:, 0:ntok], lhsT=ONESB[:], rhs=SQ[s][:, 0:ntok],
                                                      start=(c == 0), stop=(c == KC - 1))),
                  r=[("SQ", s), ("ONESB",)], w=psk(6))
        P.add("act", (lambda e: e.activation(out=RSTD[:, 0:ntok], in_=PS[6][:, 0:ntok], func=AF.Sqrt,
                                             scale=1.0 / D, bias=EPS_RMS)),
              r=psk(6), w=[("RSTD",)])
        P.add("dve", (lambda e: e.reciprocal(out=RSTD[:, 0:ntok], in_=RSTD[:, 0:ntok])),
              r=[("RSTD",)], w=[("RSTD",)])

    def rmsnorm(ntok, gap_fn, gkey):
        rms_stats(ntok)
        for c in range(KC):
            P.add("dve", (lambda e, c=c: e.scalar_tensor_tensor(
                out=H[:, c, 0:ntok], in0=X[:, c, 0:ntok], scalar=gap_fn(c),
                in1=RSTD[:, 0:ntok], op0=ALU.mult, op1=ALU.mult)),
                r=[("X", c), ("RSTD",), gkey], w=[("H", c)])

    def proj16(bank, ntok, wv, wkeys, col0, m, rows=128):
        for kc in range(KC):
            P.add("pe", (lambda e, kc=kc: e.matmul(
                PS[bank][0:rows, 0:ntok], lhsT=wv[:, kc, col0:col0 + rows], rhs=H[:, kc, 0:ntok],
                start=(kc == 0), stop=(kc == KC - 1))),
                r=wkeys + [("H", kc)], w=psk(bank))

    def ffn(ntok, l, pre):
        w1 = W[pre + "_w1"][l].rearrange("(kc p) n -> p kc n", p=128)
        w3 = W[pre + "_w3"][l].rearrange("(kc p) n -> p kc n", p=128)
        w2 = W[pre + "_w2"][l].rearrange("(kc p) n -> p kc n", p=128)
        for fb in range(FF // 512):
            v1, k1 = load_w(w1[:, :, fb * 512:(fb + 1) * 512], KC, 512)
            v3, k3 = load_w(w3[:, :, fb * 512:(fb + 1) * 512], KC, 512)
            for m in range(4):
                f = fb * 4 + m
                ba, bb = 2 * (f % 2), 2 * (f % 2) + 1
                proj16(ba, ntok, v1, k1, m * 128, m)
                proj16(bb, ntok, v3, k3, m * 128, m)
                sl = f % 2
                P.add("act", (lambda e, sl=sl, ba=ba: e.activation(out=SIL[sl][:, 0:ntok], in_=PS[ba][:, 0:ntok],
                                                                   func=AF.Silu)),
                      r=psk(ba), w=[("SIL", sl)])
                P.add("dve", (lambda e, sl=sl, bb=bb, f=f: e.tensor_tensor(
                    out=HID[:, f * TT:f * TT + ntok], in0=SIL[sl][:, 0:ntok], in1=PS[bb][:, 0:ntok], op=ALU.mult)),
                    r=[("SIL", sl)] + psk(bb), w=[("HID", f)])
        for dg in range(4):
            banks = [0, 1, 2, 3] if dg % 2 == 0 else [4, 5, 6, 3]
            for fq in range(4):
                v2, k2 = load_w(w2[:, fq * 11:(fq + 1) * 11, dg * 512:(dg + 1) * 512], 11, 512)
                for dd in range(4):
                    for ff in range(11):
                        f = fq * 11 + ff
                        P.add("pe", (lambda e, dd=dd, ff=ff, f=f, v2=v2, bk=banks[dd]: e.matmul(
                            PS[bk][:, 0:ntok], lhsT=v2[:, ff, dd * 128:(dd + 1) * 128],
                            rhs=HID[:, f * TT:f * TT + ntok], start=(f == 0), stop=(f == FC - 1))),
                            r=k2 + [("HID", f)], w=psk(banks[dd]))
            for dd in range(4):
                c = dg * 4 + dd
                P.add("dve", (lambda e, c=c, bk=banks[dd]: e.scalar_tensor_tensor(
                    out=X[:, c, 0:ntok], in0=PS[bk][:, 0:ntok], scalar=0.5, in1=X[:, c, 0:ntok],
                    op0=ALU.mult, op1=ALU.add)),
                    r=psk(banks[dd]) + [("X", c)], w=[("X", c)])

    def add_out_proj(ntok, l, zsrc_fn, zkeys_fn):
        wo = W["w_out"][l].rearrange("(kc p) n -> p kc n", p=128)
        for blk in range(4):
            v, keys = load_w(wo[:, :, blk * 512:(blk + 1) * 512], KC, 512)
            for m in range(4):
                c = blk * 4 + m
                bank = c % 2
                for kc in range(KC):
                    P.add("pe", (lambda e, kc=kc, m=m, v=v, bank=bank: e.matmul(
                        PS[bank][:, 0:ntok], lhsT=v[:, kc, m * 128:(m + 1) * 128], rhs=zsrc_fn(kc),
                        start=(kc == 0), stop=(kc == KC - 1))),
                        r=keys + zkeys_fn(kc), w=psk(bank))
                P.add("dve", (lambda e, c=c, bank=bank: e.tensor_tensor(
                    out=X[:, c, 0:ntok], in0=PS[bank][:, 0:ntok], in1=X[:, c, 0:ntok], op=ALU.add)),
                    r=psk(bank) + [("X", c)], w=[("X", c)])

    def shift_evac(l, bank, rows, g, out_ap, out_keys, segs, ntok):
        mu = CV[l][0:rows, cvcols["mu"] + g:cvcols["mu"] + g + 1]
        omu = CV[l][0:rows, cvcols["omu"] + g:cvcols["omu"] + g + 1]
        ps = PS[bank]
        P.add("act", (lambda e: e.activation(out=out_ap[0:rows, 0:ntok], in_=ps[0:rows, 0:ntok], func=AF.Copy,
                                             scale=omu)),
              r=psk(bank) + [cvk(l, "omu")], w=out_keys)
        se = cfg.get("se", 99)
        for s, (c0, n) in enumerate(segs):
            if se <= 1:
                break
            P.add("dve", (lambda e, c0=c0, n=n: e.scalar_tensor_tensor(
                out=out_ap[0:rows, c0 + 1:c0 + n], in0=ps[0:rows, c0:c0 + n - 1], scalar=mu,
                in1=out_ap[0:rows, c0 + 1:c0 + n], op0=ALU.mult, op1=ALU.add)),
                r=psk(bank) + out_keys + [cvk(l, "mu")], w=out_keys)
            if se <= 2:
                continue
            P.add("dve", (lambda e, c0=c0, s=s: e.scalar_tensor_tensor(
                out=out_ap[0:rows, c0:c0 + 1], in0=SH[l][s][0:rows, g:g + 1], scalar=mu,
                in1=out_ap[0:rows, c0:c0 + 1], op0=ALU.mult, op1=ALU.add)),
                r=[("SH", l, s, g), cvk(l, "mu")] + out_keys, w=out_keys)
            if se <= 3:
                continue
            P.add("dve", (lambda e, c0=c0, n=n, s=s: e.tensor_copy(
                out=SH[l][s][0:rows, g:g + 1], in_=ps[0:rows, c0 + n - 1:c0 + n])),
                r=psk(bank), w=[("SH", l, s, g)])

    def mix(l, ti):
        ntok, segs, C, nq = ti["ntok"], ti["segs"], ti["C"], ti["nq"]
        kind = ti["kind"]
        stop = cfg.get("stop", 99)
        win = W["w_in"][l].rearrange("(kc p) n -> p kc n", p=128)
        nseg = len(segs)
        for s in range(nseg):
            if kind == "p":
                if ti["first"]:
                    P.add("dve", (lambda e, s=s: e.memset(SH[l][s][:], 0.0)),
                          w=[("SH", l, s, g) for g in range(NRG)])
                    P.add("dve", (lambda e, s=s: e.memset(CT[l][s][:], 0.0)), w=[("CT", l, s)])
            else:
                P.add("dve", (lambda e, s=s: e.memset(SH[l][s][:], 0.0)), w=[("SH", l, s, g) for g in range(NRG)])
                parts_key = "SHLOAD"
                vec = st_shift[l, ti["sq"][s]]
                for (sbap, drap) in [
                        (SH[l][s][:, 0:48], vec[0:6144].rearrange("(c p) -> p c", p=128)),
                        (SH[l][s][0:96, 48:49], vec[6144:6240].rearrange("(c p) -> p c", p=96)),
                        (SH[l][s][0:96, 49:50], vec[6240:6336].rearrange("(c p) -> p c", p=96)),
                        (SH[l][s][:, 50:52], vec[6336:6592].rearrange("(c p) -> p c", p=128))]:
                    P.add("sp", (lambda e, sbap=sbap, drap=drap: e.dma_start(out=sbap, in_=drap)),
                          w=[("SH", l, s, g) for g in range(NRG)], dma=True)
                stg = hf32(20480, D)
                P.add("sp", (lambda e, s=s, stg=stg: e.dma_start(out=stg[0:30, :], in_=st_conv[l, ti["sq"][s]])),
                      w=hid_keys(20480, 8192), dma=True)
                for c in range(KC):
                    P.add("pe", (lambda e, c=c, stg=stg: e.transpose(
                        PS[c % 2][:, 0:30], stg[0:30, c * 128:(c + 1) * 128], IDF[0:30, 0:30])),
                        r=hid_keys(20480, 8192) + [("IDF",)], w=psk(c % 2, 0, 30))
                    P.add("dve", (lambda e, c=c, s=s: e.tensor_copy(out=CT[l][s][:, c, :], in_=PS[c % 2][:, 0:30])),
                          r=psk(c % 2, 0, 30), w=[("CT", l, s)])
        if stop <= 0:
            return
        rmsnorm(ntok, lambda c: cvc(l, "mix_norm", c), cvk(l, "mix_norm"))

        if stop <= 1:
            return
        GW = sum(30 + n for (_, n) in segs)
        gbase = []
        o = 0
        for (_, n) in segs:
            gbase.append(o)
            o += 30 + n
        GLU = HID[:, 0:16 * GW].rearrange("p (c w) -> p c w", c=16)

        def glu_keys(c):
            return hid_keys(c * GW * 2, GW * 2)
        DWB0 = 12288

        def dw_ap(c):
            return hf32(DWB0 + c * ntok * 4, ntok)

        def dw_keys(c):
            return hid_keys(DWB0 + c * ntok * 4, ntok * 4)
        need_cs = (kind == "s") or ti["last"]
        CTF0 = 40960
        CTF = hf32(CTF0, 16 * nseg * 30).rearrange("p (c s t) -> p c s t", c=16, s=nseg)
        for s in range(nseg):
            P.add("dve", (lambda e, s=s: e.tensor_copy(out=GLU[:, :, gbase[s]:gbase[s] + 30], in_=CT[l][s][:, :, :])),
                  r=[("CT", l, s)], w=hid_keys(0, 16 * GW * 2))
        for blk in range(4):
            c0w = P_RWKV + blk * 512
            vv, kv = load_w(win[:, :, c0w:c0w + 512], KC, 512)
            vg, kg = load_w(win[:, :, c0w + D:c0w + D + 512], KC, 512)
            for m in range(4):
                c = blk * 4 + m
                ba, bb = 2 * (c % 2), 2 * (c % 2) + 1
                proj16(ba, ntok, vv, kv, m * 128, m)
                proj16(bb, ntok, vg, kg, m * 128, m)
                sl = c % 2
                P.add("act", (lambda e, sl=sl, bb=bb, c=c: e.activation(
                    out=SIL[sl][:, 0:ntok], in_=PS[bb][:, 0:ntok], func=AF.Sigmoid, bias=cvc(l, "b_ci", 16 + c))),
                    r=psk(bb) + [cvk(l, "b_ci")], w=[("SIL", sl)])
                for s, (c0, n) in enumerate(segs):
                    P.add("dve", (lambda e, sl=sl, ba=ba, c=c, c0=c0, n=n, s=s: e.scalar_tensor_tensor(
                        out=GLU[:, c, gbase[s] + 30:gbase[s] + 30 + n], in0=PS[ba][:, c0:c0 + n],
                        scalar=cvc(l, "b_ci", c), in1=SIL[sl][:, c0:c0 + n], op0=ALU.add, op1=ALU.mult)),
                        r=psk(ba) + [("SIL", sl), cvk(l, "b_ci")], w=glu_keys(c))
                    if need_cs:
                        P.add("dve", (lambda e, sl=sl, ba=ba, c=c, c0=c0, n=n, s=s: e.scalar_tensor_tensor(
                            out=CTF[:, c, s, :], in0=PS[ba][:, c0 + n - 30:c0 + n],
                            scalar=cvc(l, "b_ci", c), in1=SIL[sl][:, c0 + n - 30:c0 + n], op0=ALU.add, op1=ALU.mult)),
                            r=psk(ba) + [("SIL", sl), cvk(l, "b_ci")], w=hid_keys(CTF0, 16 * nseg * 120))
        if stop <= 2:
            return
        for s, (c0, n) in enumerate(segs):
            P.add("dve", (lambda e, s=s, n=n: e.tensor_copy(out=CT[l][s][:, :, :],
                                                            in_=GLU[:, :, gbase[s] + n:gbase[s] + n + 30])),
                  r=hid_keys(0, 16 * GW * 2), w=[("CT", l, s)])
        if stop <= 3:
            return
        if need_cs:
            stg = hf32(20480, D)
            for s in range(nseg):
                for c in range(KC):
                    P.add("pe", (lambda e, c=c, s=s: e.transpose(
                        PS[4 + c % 2][0:30, 0:128], CTF[:, c, s, :], IDF[:, :])),
                        r=hid_keys(CTF0, 16 * nseg * 120) + [("IDF",)], w=psk(4 + c % 2, 0, 128))
                    P.add("dve", (lambda e, c=c, stg=stg: e.tensor_copy(out=stg[0:30, c * 128:(c + 1) * 128],
                                                                        in_=PS[4 + c % 2][0:30, 0:128])),
                          r=psk(4 + c % 2, 0, 128), w=hid_keys(20480, 8192))
                dst = o_conv[kind][l, ti["sq"][s]]
                P.add("sp", (lambda e, dst=dst, stg=stg: e.dma_start(out=dst, in_=stg[0:30, :])),
                      r=hid_keys(20480, 8192), dma=True)
        if stop <= 4:
            return
        dgc = [0]
        S1B, S2B = 5, 6
        for ci, c in enumerate(range(KC - 1, -1, -1)):
            cbanks = [(2 * (ci % 2)) + s for s in range(nseg)]
            for k in range(CONV_K):
                slot = dgc[0] % 8
                dgc[0] += 1
                eng = "act" if k % 2 else "dve"
                if eng == "act":
                    P.add("act", (lambda e, slot=slot, k=k, c=c: e.activation(
                        out=DG[:, slot, :], in_=IDB[:], func=AF.Copy, scale=cvc(l, "conv_w", k * 16 + c))),
                        r=[("IDB",), cvk(l, "conv_w")], w=[("DG", slot)])
                else:
                    P.add("dve", (lambda e, slot=slot, k=k, c=c: e.tensor_scalar(
                        out=DG[:, slot, :], in0=IDB[:], scalar1=cvc(l, "conv_w", k * 16 + c), scalar2=None,
                        op0=ALU.mult)),
                        r=[("IDB",), cvk(l, "conv_w")], w=[("DG", slot)])
                for s, (c0, n) in enumerate(segs):
                    P.add("pe", (lambda e, slot=slot, k=k, c=c, s=s, c0=c0, n=n, bk=cbanks[s]: e.matmul(
                        PS[bk][:, 0:n], lhsT=DG[:, slot, :], rhs=GLU[:, c, gbase[s] + k:gbase[s] + k + n],
                        start=(k == 0), stop=(k == CONV_K - 1))),
                        r=[("DG", slot)] + glu_keys(c), w=psk(cbanks[s], 0, n))
            for s, (c0, n) in enumerate(segs):
                bk = cbanks[s]
                P.add("act", (lambda e, c=c, c0=c0, n=n, bk=bk: e.activation(
                    out=dw_ap(c)[:, c0:c0 + n], in_=PS[bk][:, 0:n], func=AF.Identity, bias=cvc(l, "conv_b", c))),
                    r=psk(bk, 0, n) + [cvk(l, "conv_b")], w=dw_keys(c))
                P.add("act", (lambda e, c=c, c0=c0, n=n, bk=bk, ci=ci: e.activation(
                    out=SQ[1][:, c0:c0 + n], in_=PS[bk][:, 0:n], func=AF.Square, bias=cvc(l, "conv_b", c))),
                    r=psk(bk, 0, n) + [cvk(l, "conv_b")], w=[("SQ", 1)])
            P.add("dve", (lambda e, c=c: e.tensor_copy(out=SQ[0][:, 0:ntok], in_=dw_ap(c)[:, 0:ntok])),
                  r=dw_keys(c), w=[("SQ", 0)])
            P.add("pe", (lambda e, ci=ci: e.matmul(PS[S1B][:, 0:ntok], lhsT=ONESB[:], rhs=SQ[0][:, 0:ntok],
                                                   start=(ci == 0), stop=(ci == KC - 1))),
                  r=[("SQ", 0), ("ONESB",)], w=psk(S1B))
            P.add("pe", (lambda e, ci=ci: e.matmul(PS[S2B][:, 0:ntok], lhsT=ONESB[:], rhs=SQ[1][:, 0:ntok],
                                                   start=(ci == 0), stop=(ci == KC - 1))),
                  r=[("SQ", 1), ("ONESB",)], w=psk(S2B))
        if stop <= 5:
            return
        P.add("act", (lambda e: e.activation(out=SIL[0][:, 0:ntok], in_=PS[S1B][:, 0:ntok], func=AF.Copy,
                                             scale=1.0 / D)), r=psk(S1B), w=[("SIL", 0)])
        P.add("dve", (lambda e: e.tensor_tensor(out=TMPF[0][:, 0:ntok], in0=SIL[0][:, 0:ntok], in1=SIL[0][:, 0:ntok],
                                                op=ALU.mult)), r=[("SIL", 0)], w=[("TMPF", 0)])
        P.add("dve", (lambda e: e.scalar_tensor_tensor(out=SIL[1][:, 0:ntok], in0=PS[S2B][:, 0:ntok],
                                                       scalar=1.0 / D, in1=TMPF[0][:, 0:ntok],
                                                       op0=ALU.mult, op1=ALU.subtract)),
              r=psk(S2B) + [("TMPF", 0)], w=[("SIL", 1)])
        P.add("act", (lambda e: e.activation(out=SIL[1][:, 0:ntok], in_=SIL[1][:, 0:ntok], func=AF.Sqrt,
                                             bias=EPS_LN)), r=[("SIL", 1)], w=[("SIL", 1)])
        P.add("dve", (lambda e: e.reciprocal(out=SIL[1][:, 0:ntok], in_=SIL[1][:, 0:ntok])),
              r=[("SIL", 1)], w=[("SIL", 1)])
        HC0 = 0

        def hc_ap(c):
            return hbf(HC0 + c * ntok * 2, ntok)

        def hc_keys(c):
            return hid_keys(HC0 + c * ntok * 2, ntok * 2)
        for c in range(KC):
            t = TMPF[c % 2]
            P.add("dve", (lambda e, c=c, t=t: e.tensor_tensor(out=t[:, 0:ntok], in0=dw_ap(c)[:, 0:ntok],
                                                              in1=SIL[0][:, 0:ntok], op=ALU.subtract)),
                  r=dw_keys(c) + [("SIL", 0)], w=[("TMPF", c % 2)])
            P.add("dve", (lambda e, c=c, t=t: e.tensor_tensor(out=t[:, 0:ntok], in0=t[:, 0:ntok],
                                                              in1=SIL[1][:, 0:ntok], op=ALU.mult)),
                  r=[("TMPF", c % 2), ("SIL", 1)], w=[("TMPF", c % 2)])
            P.add("act", (lambda e, c=c, t=t: e.activation(out=hc_ap(c), in_=t[:, 0:ntok], func=AF.Silu,
                                                           scale=cvc(l, "ln_g", c), bias=cvc(l, "ln_b", c))),
                  r=[("TMPF", c % 2), cvk(l, "ln_g"), cvk(l, "ln_b")], w=hc_keys(c))
        if stop <= 6:
            return
        ZC0 = 16384

        def zc_ap(c):
            return hbf(ZC0 + c * ntok * 2, ntok)

        def zc_keys(c):
            return hid_keys(ZC0 + c * ntok * 2, ntok * 2)
        woc = W["w_o_c"][l].rearrange("(kc p) n -> p kc n", p=128)
        for blk in range(4):
            vo, ko = load_w(woc[:, :, blk * 512:(blk + 1) * 512], KC, 512)
            cg = P_RWKV + 2 * D + D + blk * 512
            vg, kg = load_w(win[:, :, cg:cg + 512], KC, 512)
            for m in range(4):
                c = blk * 4 + m
                ba, bb = 2 * (c % 2), 2 * (c % 2) + 1
                for kc in range(KC):
                    P.add("pe", (lambda e, kc=kc, m=m, vo=vo, ba=ba: e.matmul(
                        PS[ba][:, 0:ntok], lhsT=vo[:, kc, m * 128:(m + 1) * 128], rhs=hc_ap(kc),
                        start=(kc == 0), stop=(kc == KC - 1))),
                        r=ko + hc_keys(kc), w=psk(ba))
                proj16(bb, ntok, vg, kg, m * 128, m)
                sl = c % 2
                P.add("act", (lambda e, sl=sl, bb=bb, c=c: e.activation(
                    out=SIL[sl][:, 0:ntok], in_=PS[bb][:, 0:ntok], func=AF.Sigmoid, bias=cvc(l, "b_gate", 16 + c))),
                    r=psk(bb) + [cvk(l, "b_gate")], w=[("SIL", sl)])
                P.add("dve", (lambda e, sl=sl, ba=ba, c=c: e.scalar_tensor_tensor(
                    out=zc_ap(c), in0=PS[ba][:, 0:ntok], scalar=cvc(l, "b_oc", c), in1=SIL[sl][:, 0:ntok],
                    op0=ALU.add, op1=ALU.mult)),
                    r=psk(ba) + [("SIL", sl), cvk(l, "b_oc")], w=zc_keys(c))
        add_out_proj(ntok, l, zc_ap, zc_keys)

        if stop <= 7:
            return
        YA0 = 0

        def ya_ap(c):
            return hbf(YA0 + c * ntok * 2, ntok)

        def ya_keys(c):
            return hid_keys(YA0 + c * ntok * 2, ntok * 2)
        B0 = 16384
        offs = {}
        for i_, nm in enumerate(["RM", "KM", "VM", "AL", "SG", "LC", "E0", "E1", "KK", "T"]):
            offs[nm] = B0 + i_ * 2048
        offs["AR"] = B0 + 20480
        offs["BT"] = offs["AR"] + 2048
        offs["KT"] = offs["BT"] + 1024
        offs["VB"] = offs["KT"] + 1024
        offs["BHF"] = offs["VB"] + 1024
        offs["KHF"] = offs["BHF"] + 1024

        def f32t(nm):
            return hf32(offs[nm], TT), hid_keys(offs[nm], 2048)

        def bft(nm):
            return hbf(offs[nm], TT), hid_keys(offs[nm], 1024)
        RM, kRM = f32t("RM")
        KM, kKM = f32t("KM")
        VM, kVM = f32t("VM")
        AL, kAL = f32t("AL")
        SG, kSG = f32t("SG")
        LC, kLC = f32t("LC")
        E0, kE0 = f32t("E0")
        E1, kE1 = f32t("E1")
        KK, kKK = f32t("KK")
        T_, kT = f32t("T")
        AR = hbf(offs["AR"], 2 * TT).rearrange("p (a t) -> p a t", a=2)
        kAR = hid_keys(offs["AR"], 2048)
        BT, kBT = bft("BT")
        KT, kKT = bft("KT")
        VB, kVB = bft("VB")
        BHF, kBHF = bft("BHF")
        KHF, kKHF = bft("KHF")

        vs, ks = load_w(win[:, :, 6144:6272], KC, 128)
        vsa, ksa = load_w(win[:, :, 6240:6368], KC, 128)
        vgx, kgx = load_w(win[:, :, 6336:6592], KC, 256)
        sub = cfg.get("sub", 99)
        if sub <= 1:
            return
        proj16(0, ntok, vs, ks, 0, 0)
        if sub <= 2:
            return
        shift_evac(l, 0, 128, 48, T_, kT, segs, ntok)
        if sub <= 3:
            return
        P.add("act", (lambda e: e.activation(out=LORA[:, 0, 0:ntok], in_=T_[:, 0:ntok], func=AF.Tanh)),
              r=kT, w=[("LORA", 0)])
        proj16(1, ntok, vsa, ksa, 0, 0)
        shift_evac(l, 1, 128, 49, E0, kE0, segs, ntok)
        P.add("dve", (lambda e: e.tensor_copy(out=LORA[:, 1, 0:ntok], in_=E0[:, 0:ntok])),
              r=kE0, w=[("LORA", 1)])
        for j in range(2):
            tt = E1 if j == 0 else KK
            kt = kE1 if j == 0 else kKK
            proj16(2 + j, ntok, vgx, kgx, j * 128, 0)
            shift_evac(l, 2 + j, 128, 50 + j, tt, kt, segs, ntok)
            P.add("act", (lambda e, j=j, tt=tt: e.activation(out=LORA[:, 2 + j, 0:ntok], in_=tt[:, 0:ntok],
                                                             func=AF.Sigmoid)),
                  r=kt, w=[("LORA", 2 + j)])

        if stop <= 8:
            return
        wup = W["w_up"][l]
        aup = W["a_up"][l]
        gup = W["g_up"][l].rearrange("(kc p) n -> p kc n", p=128)
        for c in range(KC):
            wv, kw = load_w_multi([(win[:, :, j * D + c * 128:j * D + (c + 1) * 128], j * 128) for j in range(3)],
                                  KC, 384)
            LU = LUB[c % 2]
            kl = [("LUB", c % 2)]
            P.add("pool", (lambda e, LU=LU, c=c: e.dma_start(out=LU[0:96, 0:128], in_=wup[:, c * 128:(c + 1) * 128])),
                  w=kl, dma=True)
            P.add("pool", (lambda e, LU=LU, c=c: e.dma_start(out=LU[0:96, 128:256], in_=aup[:, c * 128:(c + 1) * 128])),
                  w=kl, dma=True)
            P.add("pool", (lambda e, LU=LU, c=c: e.dma_start(
                out=LU[:, 256:512].rearrange("p (k n) -> p k n", k=2), in_=gup[:, :, c * 128:(c + 1) * 128])),
                w=kl, dma=True)
            for j, (dst, kd) in enumerate([(RM, kRM), (KM, kKM), (VM, kVM)]):
                proj16(j, ntok, wv, kw, j * 128, 0)
                shift_evac(l, j, 128, j * 16 + c, dst, kd, segs, ntok)
            P.add("pe", (lambda e, LU=LU: e.matmul(PS[3][:, 0:ntok], lhsT=LU[:, 0:128], rhs=LORA[:, 0, 0:ntok],
                                                   start=True, stop=True)),
                  r=kl + [("LORA", 0)], w=psk(3))
            P.add("pe", (lambda e, LU=LU: e.matmul(PS[4][:, 0:ntok], lhsT=LU[:, 128:256], rhs=LORA[:, 1, 0:ntok],
                                                   start=True, stop=True)),
                  r=kl + [("LORA", 1)], w=psk(4))
            for j in range(2):
                P.add("pe", (lambda e, LU=LU, j=j: e.matmul(
                    PS[5][:, 0:ntok], lhsT=LU[:, 256 + j * 128:256 + (j + 1) * 128], rhs=LORA[:, 2 + j, 0:ntok],
                    start=(j == 0), stop=(j == 1))),
                    r=kl + [("LORA", 2 + j)], w=psk(5))
            P.add("act", (lambda e, c=c: e.activation(out=SG[:, 0:ntok], in_=PS[3][:, 0:ntok], func=AF.Sigmoid,
                                                      bias=cvc(l, "w0", c))),
                  r=psk(3) + [cvk(l, "w0")], w=kSG)
            P.add("act", (lambda e, c=c: e.activation(out=AL[:, 0:ntok], in_=PS[4][:, 0:ntok], func=AF.Sigmoid,
                                                      bias=cvc(l, "a0", c))),
                  r=psk(4) + [cvk(l, "a0")], w=kAL)
            for q in range(nq):
                P.add("dve", (lambda e, q=q: e.tensor_tensor_scan(
                    out=LC[:, q * C:(q + 1) * C], data0=ONESF[:, 0:C], data1=SG[:, q * C:(q + 1) * C], initial=0.0,
                    op0=ALU.mult, op1=ALU.add)),
                    r=kSG + [("ONESF",)], w=kLC)
            P.add("dve", (lambda e: e.tensor_tensor(out=SG[:, 0:ntok], in0=LC[:, 0:ntok], in1=SG[:, 0:ntok],
                                                    op=ALU.subtract)), r=kLC + kSG, w=kSG)
            LCend = LC[:, 0:ntok].rearrange("p (q t) -> p q t", t=C)[:, :, C - 1]
            P.add("act", (lambda e, LCend=LCend: e.activation(out=WCL[:, 0:nq], in_=LCend, func=AF.Exp, scale=-CDEC)),
                  r=kLC, w=[("WCL",)])
            P.add("dve", (lambda e, c=c: e.tensor_scalar(out=KK[:, 0:ntok], in0=KM[:, 0:ntok], scalar1=cvc(l, "k_k", c),
                                                         scalar2=None, op0=ALU.mult)),
                  r=kKM + [cvk(l, "k_k")], w=kKK)
            P.add("act", (lambda e: e.activation(out=SQ[0][:, 0:ntok], in_=KK[:, 0:ntok], func=AF.Square)),
                  r=kKK, w=[("SQ", 0)])
            P.add("pe", (lambda e: e.matmul(PS[6][:, 0:ntok], lhsT=BLKONES[:], rhs=SQ[0][:, 0:ntok], start=True,
                                            stop=True)), r=[("SQ", 0), ("BLKONES",)], w=psk(6))
            P.add("dve", (lambda e: e.tensor_scalar(out=T_[:, 0:ntok], in0=PS[6][:, 0:ntok], scalar1=1e-24, scalar2=None,
                                                    op0=ALU.max)), r=psk(6), w=kT)
            P.add("act", (lambda e: e.activation(out=T_[:, 0:ntok], in_=T_[:, 0:ntok], func=AF.Sqrt)), r=kT, w=kT)
            P.add("dve", (lambda e: e.reciprocal(out=T_[:, 0:ntok], in_=T_[:, 0:ntok])), r=kT, w=kT)
            P.add("dve", (lambda e: e.tensor_tensor(out=KK[:, 0:ntok], in0=KK[:, 0:ntok], in1=T_[:, 0:ntok],
                                                    op=ALU.mult)), r=kKK + kT, w=kKK)
            P.add("dve", (lambda e, c=c: e.tensor_scalar(out=T_[:, 0:ntok], in0=AL[:, 0:ntok], scalar1=-1.0,
                                                         scalar2=cvc(l, "k_a", c), op0=ALU.add, op1=ALU.mult)),
                  r=kAL + [cvk(l, "k_a")], w=kT)
            P.add("dve", (lambda e: e.scalar_tensor_tensor(out=KM[:, 0:ntok], in0=T_[:, 0:ntok], scalar=1.0,
                                                           in1=KM[:, 0:ntok], op0=ALU.add, op1=ALU.mult)),
                  r=kT + kKM, w=kKM)
            P.add("dve", (lambda e: e.tensor_tensor(out=AL[:, 0:ntok], in0=KK[:, 0:ntok], in1=AL[:, 0:ntok],
                                                    op=ALU.mult)), r=kKK + kAL, w=kAL)
            P.add("dve", (lambda e, c=c: e.scalar_tensor_tensor(out=SQ[1][:, 0:ntok], in0=RM[:, 0:ntok],
                                                                scalar=cvc(l, "r_k", c), in1=KM[:, 0:ntok],
                                                                op0=ALU.mult, op1=ALU.mult)),
                  r=kRM + kKM + [cvk(l, "r_k")], w=[("SQ", 1)])
            P.add("pe", (lambda e: e.matmul(PS[6][:, 0:ntok], lhsT=BLKONES[:], rhs=SQ[1][:, 0:ntok], start=True,
                                            stop=True)), r=[("SQ", 1), ("BLKONES",)], w=psk(6))
            P.add("act", (lambda e: e.activation(out=VB[:, 0:ntok], in_=VM[:, 0:ntok], func=AF.Copy)), r=kVM, w=kVB)
            P.add("dve", (lambda e: e.tensor_tensor(out=VM[:, 0:ntok], in0=PS[6][:, 0:ntok], in1=VM[:, 0:ntok],
                                                    op=ALU.mult)), r=psk(6) + kVM + kVB, w=kVM)
            P.add("act", (lambda e: e.activation(out=E0[:, 0:ntok], in_=LC[:, 0:ntok], func=AF.Exp, scale=-CDEC)),
                  r=kLC, w=kE0)
            P.add("dve", (lambda e: e.tensor_tensor(out=AR[:, 1, 0:ntok], in0=RM[:, 0:ntok], in1=E0[:, 0:ntok],
                                                    op=ALU.mult)), r=kRM + kE0, w=kAR)
            P.add("act", (lambda e: e.activation(out=E1[:, 0:ntok], in_=LC[:, 0:ntok], func=AF.Exp, scale=CDEC)),
                  r=kLC, w=kE1)
            P.add("dve", (lambda e: e.tensor_tensor(out=KT[:, 0:ntok], in0=KM[:, 0:ntok], in1=E1[:, 0:ntok],
                                                    op=ALU.mult)), r=kKM + kE1, w=kKT)
            P.add("dve", (lambda e: e.tensor_tensor(out=BT[:, 0:ntok], in0=AL[:, 0:ntok], in1=E1[:, 0:ntok],
                                                    op=ALU.mult)), r=kAL + kE1, w=kBT)
            P.add("act", (lambda e: e.activation(out=E0[:, 0:ntok], in_=SG[:, 0:ntok], func=AF.Exp, scale=-CDEC)),
                  r=kSG + kAR, w=kE0)
            P.add("dve", (lambda e: e.scalar_tensor_tensor(out=AR[:, 0, 0:ntok], in0=KK[:, 0:ntok], scalar=-1.0,
                                                           in1=E0[:, 0:ntok], op0=ALU.mult, op1=ALU.mult)),
                  r=kKK + kE0, w=kAR)
            for q in range(nq):
                P.add("act", (lambda e, q=q: e.activation(out=BHF[:, q * C:(q + 1) * C], in_=BT[:, q * C:(q + 1) * C],
                                                          func=AF.Copy, scale=WCL[:, q:q + 1])),
                      r=kBT + [("WCL",)], w=kBHF)
                P.add("act", (lambda e, q=q: e.activation(out=KHF[:, q * C:(q + 1) * C], in_=KT[:, q * C:(q + 1) * C],
                                                          func=AF.Copy, scale=WCL[:, q:q + 1])),
                      r=kKT + [("WCL",)], w=kKHF)
            for q in range(nq):
                for j, (src, ksrc) in enumerate([(VB, kVB), (BHF, kBHF), (KHF, kKHF)]):
                    P.add("pe", (lambda e, q=q, j=j, src=src: e.transpose(
                        PSB[0:C, j * 128:(j + 1) * 128], src[:, q * C:(q + 1) * C], IDB[:, :])),
                        r=ksrc + [("IDB",)], w=[("PSB",)])
                P.add("dve", (lambda e, q=q: e.tensor_copy(
                    out=TOK[0:C, q, :, :], in_=PSB[0:C, 0:384].rearrange("p (a b) -> p a b", a=3))),
                    r=[("PSB",)], w=[("TOK", q)])
            if stop > 9:
                wkv_pair(l, c, C, nq, ti, AR, kAR, BT, kBT, KT, kKT, VM, kVM, ya_ap, ya_keys)
        if stop <= 10:
            return
        ZA0 = 16384

        def za_ap(c):
            return hbf(ZA0 + c * ntok * 2, ntok)

        def za_keys(c):
            return hid_keys(ZA0 + c * ntok * 2, ntok * 2)
        woa = W["w_o_a"][l].rearrange("(kc p) n -> p kc n", p=128)
        for blk in range(4):
            vo, ko = load_w(woa[:, :, blk * 512:(blk + 1) * 512], KC, 512)
            cg = P_RWKV + 2 * D + blk * 512
            vg, kg = load_w(win[:, :, cg:cg + 512], KC, 512)
            for m in range(4):
                c = blk * 4 + m
                ba, bb = 2 * (c % 2), 2 * (c % 2) + 1
                for kc in range(KC):
                    P.add("pe", (lambda e, kc=kc, m=m, vo=vo, ba=ba: e.matmul(
                        PS[ba][:, 0:ntok], lhsT=vo[:, kc, m * 128:(m + 1) * 128], rhs=ya_ap(kc),
                        start=(kc == 0), stop=(kc == KC - 1))),
                        r=ko + ya_keys(kc), w=psk(ba))
                proj16(bb, ntok, vg, kg, m * 128, m)
                sl = c % 2
                P.add("act", (lambda e, sl=sl, bb=bb, c=c: e.activation(
                    out=SIL[sl][:, 0:ntok], in_=PS[bb][:, 0:ntok], func=AF.Sigmoid, bias=cvc(l, "b_gate", c))),
                    r=psk(bb) + [cvk(l, "b_gate")], w=[("SIL", sl)])
                P.add("dve", (lambda e, sl=sl, ba=ba, c=c: e.tensor_tensor(
                    out=za_ap(c), in0=PS[ba][:, 0:ntok], in1=SIL[sl][:, 0:ntok], op=ALU.mult)),
                    r=psk(ba) + [("SIL", sl)], w=za_keys(c))
        add_out_proj(ntok, l, za_ap, za_keys)
        if kind == "s" or ti["last"]:
            for s in range(nseg):
                vec = o_shift[kind][l, ti["sq"][s]]
                for (sbap, drap) in [
                        (SH[l][s][:, 0:48], vec[0:6144].rearrange("(c p) -> p c", p=128)),
                        (SH[l][s][0:96, 48:49], vec[6144:6240].rearrange("(c p) -> p c", p=96)),
                        (SH[l][s][0:96, 49:50], vec[6240:6336].rearrange("(c p) -> p c", p=96)),
                        (SH[l][s][:, 50:52], vec[6336:6592].rearrange("(c p) -> p c", p=128))]:
                    P.add("sp", (lambda e, sbap=sbap, drap=drap: e.dma_start(out=drap, in_=sbap)),
                          r=[("SH", l, s, g) for g in range(NRG)], dma=True)

    def wkv_pair(l, c, C, nq, ti, AR, kAR, BT, kBT, KT, kKT, VM, kVM, ya_ap, ya_keys):
        kind = ti["kind"]
        nit = int(np.log2(C)) - 1
        kST = ("ST", l, c)
        kSBD = ("SBD", c)
        CB = [0, 1, 3, 4]
        groups = [list(range(g0, min(g0 + 2, nq))) for g0 in range(0, nq, 2)]

        def state_init(q):
            fresh = (kind == "p" and ti["first"] and q == 0)
            if fresh:
                P.add("dve", (lambda e: e.memset(ST[l][:, c, :], 0.0)), w=[kST])
                P.add("dve", (lambda e: e.memset(SBD[:, c, :], 0.0)), w=[kSBD])
            elif kind == "s":
                b = ti["sq"][q]
                P.add("sp", (lambda e, b=b: e.dma_start(
                    out=STG[:, :], in_=st_wkv[l, b, 2 * c:2 * c + 2].rearrange("h i j -> (h i) j"))),
                    w=[("STG",)], dma=True)
                for h in range(2):
                    P.add("dve", (lambda e, h=h: e.tensor_copy(out=SBLK[h * 64:(h + 1) * 64, h * 64:(h + 1) * 64],
                                                               in_=STG[h * 64:(h + 1) * 64, :])),
                          r=[("STG",)], w=[("SBLK",)])
                P.add("pe", (lambda e: e.transpose(PS[2][:, 0:128], SBLK[:, :], IDF[:, :])),
                      r=[("SBLK",), ("IDF",)], w=psk(2))
                for h in range(2):
                    P.add("dve", (lambda e, h=h: e.tensor_copy(out=ST[l][h * 64:(h + 1) * 64, c, :],
                                                               in_=PS[2][h * 64:(h + 1) * 64, h * 64:(h + 1) * 64])),
                          r=psk(2), w=[kST])
                P.add("dve", (lambda e: e.tensor_copy(out=SBD[:, c, :], in_=PS[2][:, 0:128])),
                      r=psk(2), w=[kSBD])
            elif q == 0:
                P.add("dve", (lambda e: e.memset(SBD[:, c, :], 0.0)), w=[kSBD])
                for h in range(2):
                    P.add("act", (lambda e, h=h: e.activation(out=SBD[h * 64:(h + 1) * 64, c, h * 64:(h + 1) * 64],
                                                              in_=ST[l][h * 64:(h + 1) * 64, c, :], func=AF.Copy)),
                          r=[kST], w=[kSBD])

        def state_out(q):
            b = ti["sq"][q] if kind == "s" else ti["sq"][0]
            for h in range(2):
                hs = slice(h * 64, (h + 1) * 64)
                P.add("dve", (lambda e, h=h, hs=hs: e.tensor_copy(out=SBLK[hs, h * 64:(h + 1) * 64], in_=ST[l][hs, c, :])),
                      r=[kST], w=[("SBLK",)])
            P.add("pe", (lambda e: e.transpose(PS[2][:, 0:128], SBLK[:, :], IDF[:, :])), r=[("SBLK",), ("IDF",)],
                  w=psk(2))
            for h in range(2):
                P.add("dve", (lambda e, h=h: e.tensor_copy(out=OSTG[h * 64:(h + 1) * 64, :],
                                                           in_=PS[2][h * 64:(h + 1) * 64, h * 64:(h + 1) * 64])),
                      r=psk(2), w=[("OSTG",)])
            P.add("sp", (lambda e, b=b: e.dma_start(
                out=o_wkv[kind][l, b, 2 * c:2 * c + 2].rearrange("h i j -> (h i) j"), in_=OSTG[:, :])),
                r=[("OSTG",)], dma=True)

        for grp in groups:
            chains = [(q, h) for q in grp for h in range(2)]
            for i, (q, h) in enumerate(chains):
                cs = slice(q * C, (q + 1) * C)
                hs = slice(h * 64, (h + 1) * 64)
                bk = CB[i]
                psA = PS[bk][0:C, 0:2 * C].rearrange("p (a t) -> p a t", a=2)
                psB = PS[bk][0:C, 2 * C:4 * C].rearrange("p (a t) -> p a t", a=2)
                P.add("pe", (lambda e, hs=hs, cs=cs, psA=psA: e.matmul(psA, lhsT=BT[hs, cs], rhs=AR[hs, :, cs], start=True,
                                                                       stop=True)), r=kBT + kAR, w=psk(bk))
                P.add("pe", (lambda e, hs=hs, cs=cs, psB=psB: e.matmul(psB, lhsT=KT[hs, cs], rhs=AR[hs, :, cs], start=True,
                                                                       stop=True)), r=kKT + kAR, w=psk(bk))
            for i, (q, h) in enumerate(chains):
                bk = CB[i]
                psAB = PS[bk][0:C, 0:4 * C].rearrange("p (a t) -> p a t", a=4)
                P.add("dve", (lambda e, i=i, psAB=psAB: e.tensor_tensor(out=MAB[i][0:C, :, 0:C], in0=psAB,
                                                                        in1=MASK2[0:C, :, 0:C], op=ALU.mult)),
                      r=psk(bk) + [("MASK2",)], w=[("MA", i), ("MB", i)])
            for i, (q, h) in enumerate(chains):
                cs = slice(q * C, (q + 1) * C)
                hs = slice(h * 64, (h + 1) * 64)
                bk = CB[i]
                P.add("pe", (lambda e, hs=hs, cs=cs, bk=bk: e.matmul(PS[bk][0:C, 0:C], lhsT=AR[hs, 0, cs], rhs=BT[hs, cs],
                                                                     start=True, stop=True)), r=kBT + kAR, w=psk(bk))
            for i, (q, h) in enumerate(chains):
                bk = CB[i]
                P.add("dve", (lambda e, i=i, bk=bk: e.tensor_tensor(out=MC[i][0:C, 0:C], in0=PS[bk][0:C, 0:C],
                                                                    in1=MASKL[0:C, 0:C], op=ALU.mult)),
                      r=psk(bk) + [("MASKL",)], w=[("MC", i)])
                P.add("dve", (lambda e, i=i: e.tensor_tensor(out=PPT[i][1][0:C, 0:C], in0=MA[i][0:C, 0, 0:C],
                                                             in1=IDB[0:C, 0:C], op=ALU.add)),
                      r=[("MA", i), ("IDB",)], w=[("PPT", i, 1)])
            nch = len(chains)
            Pm = [MA[i][0:C, 0, 0:C] for i in range(nch)]
            PTm = [MC[i][0:C, 0:C] for i in range(nch)]
            kPm = [[("MA", i), ("MC", i)] for i in range(nch)]
            cur = [1] * nch
            for s in range(nit + 1):
                do_sq = s < nit
                do_t = s >= 1
                for i in range(nch):
                    bk = CB[i]
                    if do_sq:
                        if s < nit - 1:
                            P.add("pe", (lambda e, bk=bk, a=PTm[i], b_=Pm[i]: e.matmul(
                                PS[bk][0:C, 2 * C:3 * C], lhsT=a, rhs=b_, start=True, stop=True)),
                                r=kPm[i], w=psk(bk))
                        P.add("pe", (lambda e, bk=bk, a=Pm[i], b_=PTm[i]: e.matmul(
                            PS[bk][0:C, 3 * C:4 * C], lhsT=a, rhs=b_, start=True, stop=True)),
                            r=kPm[i], w=psk(bk))
                    if do_t:
                        Tc = PPT[i][cur[i]][0:C, 0:C]
                        kTc = [("PPT", i, cur[i])]
                        P.add("pe", (lambda e, bk=bk, a=PTm[i], Tc=Tc: e.matmul(
                            PS[bk][0:C, C:2 * C], lhsT=a, rhs=Tc, start=True, stop=False)),
                            r=kPm[i] + kTc, w=psk(bk))
                        P.add("pe", (lambda e, bk=bk, Tc=Tc: e.matmul(
                            PS[bk][0:C, C:2 * C], lhsT=IDB[0:C, 0:C], rhs=Tc, start=False, stop=True)),
                            r=kTc + [("IDB",)], w=psk(bk))
                for i in range(nch):
                    bk = CB[i]
                    if s == 0:
                        nt = cur[i]
                        P.add("dve", (lambda e, i=i, bk=bk, nt=nt: e.tensor_copy(
                            out=PPT[i][nt][0:C, C:3 * C], in_=PS[bk][0:C, 2 * C:4 * C])),
                            r=psk(bk), w=[("PPT", i, nt)])
                    else:
                        nt = 1 - cur[i]
                        w_ = 3 * C if do_sq else C
                        P.add("dve", (lambda e, i=i, bk=bk, nt=nt, w_=w_: e.tensor_copy(
                            out=PPT[i][nt][0:C, 0:w_], in_=PS[bk][0:C, C:C + w_])),
                            r=psk(bk), w=[("PPT", i, nt)])
                        cur[i] = nt
                    if do_sq:
                        Pm[i], PTm[i] = PPT[i][cur[i]][0:C, C:2 * C], PPT[i][cur[i]][0:C, 2 * C:3 * C]
                        kPm[i] = [("PPT", i, cur[i])]
            for gi, q in enumerate(grp):
                cs = slice(q * C, (q + 1) * C)
                state_init(q)
                VTq = TOK[0:C, q, 0, :]
                BHq = TOK[0:C, q, 1, :]
                KHq = TOK[0:C, q, 2, :]
                kTOK = [("TOK", q)]
                ci = [2 * gi, 2 * gi + 1]
                P.add("pe", (lambda e, cs=cs: e.matmul(PS[2][0:C, 0:128], lhsT=AR[:, 0, cs], rhs=SBD[:, c, :], start=True,
                                                       stop=False)), r=kAR + [kSBD], w=psk(2))
                for h in range(2):
                    hc = slice(h * 64, (h + 1) * 64)
                    P.add("pe", (lambda e, h=h, hc=hc, i=ci[h], VTq=VTq: e.matmul(
                        PS[2][0:C, hc], lhsT=MB[i][0:C, 0, 0:C], rhs=VTq[:, hc], start=False, stop=(h == 1))),
                        r=[("MB", ci[h])] + kTOK, w=psk(2))
                P.add("dve", (lambda e: e.tensor_copy(out=XT[0:C, :], in_=PS[2][0:C, 0:128])), r=psk(2), w=[("XT",)])
                for h in range(2):
                    hc = slice(h * 64, (h + 1) * 64)
                    Tf = PPT[ci[h]][cur[ci[h]]][0:C, 0:C]
                    P.add("pe", (lambda e, h=h, hc=hc, Tf=Tf: e.matmul(PS[2][0:C, 128 + h * 64:128 + (h + 1) * 64], lhsT=Tf,
                                                                       rhs=XT[0:C, hc], start=True, stop=True)),
                          r=[("PPT", ci[h], cur[ci[h]]), ("XT",)], w=psk(2))
                P.add("dve", (lambda e: e.tensor_copy(out=UT[0:C, :], in_=PS[2][0:C, 128:256])), r=psk(2), w=[("UT",)])
                P.add("pe", (lambda e, cs=cs: e.matmul(PS[6][0:C, 0:128], lhsT=AR[:, 1, cs], rhs=SBD[:, c, :], start=True,
                                                       stop=False)), r=kAR + [kSBD], w=psk(6))
                for h in range(2):
                    hc = slice(h * 64, (h + 1) * 64)
                    P.add("pe", (lambda e, h=h, hc=hc, i=ci[h]: e.matmul(PS[6][0:C, hc], lhsT=MA[i][0:C, 1, 0:C],
                                                                         rhs=UT[0:C, hc], start=False, stop=False)),
                          r=[("MA", ci[h]), ("UT",)], w=psk(6))
                    P.add("pe", (lambda e, h=h, hc=hc, i=ci[h], VTq=VTq: e.matmul(
                        PS[6][0:C, hc], lhsT=MB[i][0:C, 1, 0:C], rhs=VTq[:, hc], start=False, stop=(h == 1))),
                        r=[("MB", ci[h])] + kTOK, w=psk(6))
                P.add("pe", (lambda e, BHq=BHq: e.matmul(PS[6][:, 128:256], lhsT=BHq, rhs=UT[0:C, :], start=True,
                                                         stop=False)), r=kTOK + [("UT",)], w=psk(6))
                P.add("pe", (lambda e, KHq=KHq, VTq=VTq: e.matmul(PS[6][:, 128:256], lhsT=KHq, rhs=VTq, start=False,
                                                                  stop=True)), r=kTOK, w=psk(6))
                for h in range(2):
                    hs = slice(h * 64, (h + 1) * 64)
                    P.add("dve", (lambda e, h=h, hs=hs, q=q: e.scalar_tensor_tensor(
                        out=ST[l][hs, c, :], in0=ST[l][hs, c, :], scalar=WCL[hs, q:q + 1],
                        in1=PS[6][hs, 128 + h * 64:128 + (h + 1) * 64], op0=ALU.mult, op1=ALU.add)),
                        r=[kST, ("WCL",)] + psk(6) + [kSBD], w=[kST])
                    P.add("act", (lambda e, h=h, hs=hs: e.activation(out=SBD[hs, c, h * 64:(h + 1) * 64],
                                                                     in_=ST[l][hs, c, :], func=AF.Copy)),
                          r=[kST], w=[kSBD])
                P.add("dve", (lambda e, q=q: e.tensor_copy(out=YS[0:C, q, :], in_=PS[6][0:C, 0:128])), r=psk(6),
                      w=[("YS", q)])
                if (kind == "s") or (ti["last"] and q == nq - 1):
                    state_out(q)
            for q in grp:
                cs = slice(q * C, (q + 1) * C)
                for h in range(2):
                    P.add("dve", (lambda e, h=h, q=q: e.bn_stats(out=BNS[0:C, h, :], in_=YS[0:C, q, h * 64:(h + 1) * 64])),
                          r=[("YS", q)], w=[("BNS", h)])
                    P.add("dve", (lambda e, h=h: e.bn_aggr(out=MV[0:C, h, :], in_=BNS[0:C, h, :])),
                          r=[("BNS", h)], w=[("MV", h)])
                P.add("act", (lambda e: e.activation(out=RS2[0:C, :], in_=MV[0:C, :, 1], func=AF.Sqrt, bias=EPS_GN)),
                      r=[("MV", 0), ("MV", 1)], w=[("RS2",)])
                P.add("dve", (lambda e: e.reciprocal(out=RS2[0:C, :], in_=RS2[0:C, :])), r=[("RS2",)], w=[("RS2",)])
                for h in range(2):
                    P.add("dve", (lambda e, h=h, q=q: e.tensor_scalar(
                        out=YN[0:C, h * 64:(h + 1) * 64], in0=YS[0:C, q, h * 64:(h + 1) * 64],
                        scalar1=MV[0:C, h, 0:1], scalar2=RS2[0:C, h:h + 1], op0=ALU.subtract, op1=ALU.mult)),
                        r=[("YS", q), ("MV", h), ("RS2",)], w=[("YN",)])
                P.add("pe", (lambda e: e.transpose(PS[2][:, 256:256 + C], YN[0:C, :], IDF[0:C, 0:C])),
                      r=[("YN",), ("IDF",)], w=psk(2))
                P.add("act", (lambda e: e.activation(out=Y1[:, 0:C], in_=PS[2][:, 256:256 + C], func=AF.Identity,
                                                     scale=cvc(l, "gn_g", c), bias=cvc(l, "gn_b", c))),
                      r=psk(2) + [cvk(l, "gn_g"), cvk(l, "gn_b")], w=[("Y1",)])
                P.add("dve", (lambda e, cs=cs: e.tensor_tensor(out=Y1[:, 0:C], in0=Y1[:, 0:C], in1=VM[:, cs], op=ALU.add)),
                      r=[("Y1",)] + kVM, w=[("Y1",)])
                P.add("dve", (lambda e, cs=cs: e.tensor_tensor(out=ya_ap(c)[:, cs], in0=Y1[:, 0:C], in1=PS[5][:, cs],
                                                               op=ALU.mult)),
                      r=[("Y1",)] + psk(5), w=ya_keys(c))

    def final_and_store(ydst_rows, ntok):
        gidx = nv_idx[("final_norm", 0)]
        rms_stats(ntok)
        YF = HIDf[:, 0:KC * TT].rearrange("p (c t) -> p c t", c=KC)
        for c in range(KC):
            P.add("dve", (lambda e, c=c: e.scalar_tensor_tensor(
                out=YF[:, c, 0:ntok], in0=X[:, c, 0:ntok], scalar=NV[:, gidx, c:c + 1],
                in1=RSTD[:, 0:ntok], op0=ALU.mult, op1=ALU.mult)),
                r=[("X", c), ("RSTD",), ("NV", gidx)], w=hid_keys(c * 2048, 2048))
        IO = HIDf[:, KC * TT:KC * TT + D]
        for tb, (dst, n) in enumerate(ydst_rows):
            for cg in range(4):
                bank = cg % 2
                for cc in range(4):
                    c = cg * 4 + cc
                    P.add("pe", (lambda e, c=c, cc=cc, tb=tb, n=n, bank=bank: e.transpose(
                        PS[bank][0:n, cc * 128:(cc + 1) * 128], YF[:, c, tb * 128:tb * 128 + n], IDF[:, :])),
                        r=hid_keys(c * 2048, 2048) + [("IDF",)], w=psk(bank))
                P.add("act" if cg % 2 else "dve",
                      (lambda e, cg=cg, n=n, bank=bank: (
                          e.activation(out=IO[0:n, cg * 512:(cg + 1) * 512], in_=PS[bank][0:n, :], func=AF.Copy)
                          if cg % 2 else
                          e.tensor_copy(out=IO[0:n, cg * 512:(cg + 1) * 512], in_=PS[bank][0:n, :]))),
                      r=psk(bank), w=hid_keys(32768 + cg * 2048, 2048))
            P.add("sp", (lambda e, dst=dst, n=n: e.dma_start(out=dst, in_=IO[0:n, :])),
                  r=hid_keys(32768, 8192), dma=True)

    tiles = []
    for sq in range(n_pseq):
        nt = plen // TT
        for it in range(nt):
            tiles.append(dict(kind="p", sq=[sq], t0=it * TT, ntok=TT, segs=[(0, TT)], C=128, nq=TT // 128,
                              first=(it == 0), last=(it == nt - 1)))
    if n_sseq and not cfg.get("nos", 0):
        tiles.append(dict(kind="s", sq=list(range(n_sseq)), t0=0, ntok=n_sseq * slen,
                          segs=[(s * slen, slen) for s in range(n_sseq)], C=slen, nq=n_sseq, first=True, last=True))

    for tnum, ti in enumerate(tiles):
        ntok = ti["ntok"]
        wc_begin_tile(tnum == 0)
        if ti["kind"] == "p":
            sq, t0 = ti["sq"][0], ti["t0"]
            rows = [(x_p[sq, t0 + tb * 128:t0 + (tb + 1) * 128, :], 128) for tb in range(TT // 128)]
            orow = [(y_p[sq, t0 + tb * 128:t0 + (tb + 1) * 128, :], 128) for tb in range(TT // 128)]
        else:
            xs2 = x_s.rearrange("b t d -> (b t) d")
            ys2 = y_s.rearrange("b t d -> (b t) d")
            rows, orow = [], []
            for r0 in range(0, ntok, 128):
                n = min(128, ntok - r0)
                rows.append((xs2[r0:r0 + n, :], n))
                orow.append((ys2[r0:r0 + n, :], n))
        load_x(rows, ntok)
        for l in range(L):
            if "ffn1" in phases:
                gi = nv_idx[("ffn1_norm", l)]
                rmsnorm(ntok, (lambda c, gi=gi: NV[:, gi, c:c + 1]), ("NV", gi))
                ffn(ntok, l, "ffn1")
            if "mix" in phases:
                mix(l, ti)
            if "ffn2" in phases:
                gi = nv_idx[("ffn2_norm", l)]
                rmsnorm(ntok, (lambda c, gi=gi: NV[:, gi, c:c + 1]), ("NV", gi))
                ffn(ntok, l, "ffn2")
        final_and_store(orow, ntok)

    P.finalize()
    sems = {nm: st.enter_context(nc.semaphore(nm)) for nm in P.sem_names()}
    with nc.allow_non_contiguous_dma(reason="per-feature vectors / small state layouts"):
        with nc.Block() as block:
            @block.tensor
            def _(eng):
                P.emit("pe", eng, sems)

            @block.scalar
            def _(eng):
                P.emit("act", eng, sems)

            @block.vector
            def _(eng):
                P.emit("dve", eng, sems)

            @block.gpsimd
            def _(eng):
                P.emit("pool", eng, sems)

            @block.sync
            def _(eng):
                P.emit("sp", eng, sems, final_wait=True)
    st.close()
    return nc, P


FULL_CFG = dict(n_pseq=2, plen=2048, n_sseq=2, slen=64, depth=2)
_CACHE = {}


def kernel(**inputs):
    cfg = dict(FULL_CFG)
    if "nc" not in _CACHE:
        _CACHE["nc"] = build_program(cfg)[0]
    nc = _CACHE["nc"]
    n = 8
    f = lambda a: np.ascontiguousarray(a, dtype=np.float32)
    shared = {nm: f(inputs[nm]) for nm, _ in WNAMES if nm != "r_k"}
    shared["r_k"] = f(inputs["r_k"]).reshape(2, D)
    shared["final_norm"] = f(inputs["final_norm"]).reshape(1, D)
    xp, xs = f(inputs["x_prompt"]), f(inputs["x_sample"])
    swkv, sshift, sconv = f(inputs["state_wkv"]), f(inputs["state_shift"]), f(inputs["state_conv"])
    in_maps = []
    for i in range(n):
        m = dict(shared)
        m["x_p"] = xp[2 * i:2 * i + 2]
        m["x_s"] = xs[2 * i:2 * i + 2]
        m["state_wkv"] = f(swkv[:, 2 * i:2 * i + 2])
        m["state_shift"] = f(sshift[:, 2 * i:2 * i + 2])
        m["state_conv"] = f(sconv[:, 2 * i:2 * i + 2])
        in_maps.append(m)
    res = run_bass_kernel_spmd(nc, in_maps, core_ids=list(range(n)))
    cat0 = lambda k: np.concatenate([r[k] for r in res.results], axis=0)
    cat1 = lambda k: np.concatenate([r[k] for r in res.results], axis=1)
    return (cat0("y_p"), cat0("y_s"), cat1("wkv_p"), cat1("shift_p"), cat1("conv_p"),
            cat1("wkv_s"), cat1("shift_s"), cat1("conv_s"))
```
